# Optimizing a Trainium2 kernel written in Bass

```python
import math
import jax, jax.numpy as jnp
from jax import lax
import numpy as np

D_MODEL = 1024
BATCH = 1
SEQ = 16384
DEPTH = 1
DEC_BATCH = 8
DEC_SEQ = 2048
PAST_LEN = 128

N_META = 16
D_A = 512
HEAD_A = 64
H_A = D_A // HEAD_A
LORA_W = 64
LORA_A = 64
LORA_G = 128
D_B = 512
HEAD_B = 64
H_B = D_B // (2 * HEAD_B)
D_FF = 4 * D_MODEL
Q_BLOCK = 128
NORM_EPS = 1e-6
SUBLN_EPS = 1e-5
GN_EPS = 64e-5
RWKV_COLS = 3 * D_A + LORA_W + LORA_A + LORA_G
ATTN_COLS = 3 * D_B
GATE_COLS = 2 * D_MODEL
D_IN = RWKV_COLS + ATTN_COLS + GATE_COLS

kernel_name = 'hybrid_rwkv7_diffattn_encoder'


def _rmsnorm(x, g, eps=NORM_EPS):
    xf = x.astype(jnp.float32)
    y = xf * lax.rsqrt(jnp.mean(xf * xf, axis=-1, keepdims=True) + eps)
    return (y * g.astype(jnp.float32)).astype(x.dtype)


def _token_shift(p, mu_prev, mu_next):
    prev = jnp.pad(p[:, :-1], ((0, 0), (1, 0), (0, 0)))
    nxt = jnp.pad(p[:, 1:], ((0, 0), (0, 1), (0, 0)))
    return p + mu_prev * (prev - p) + mu_next * (nxt - p)


def _heads_a(z):
    return z.reshape(z.shape[:-1] + (H_A, HEAD_A))


def _orient(z):
    z = jnp.stack([z[0], jnp.flip(z[1], axis=1)])
    return jnp.moveaxis(z, 2, 0)


def _both(z):
    return jnp.broadcast_to(z[None], (2,) + z.shape)


def _rwkv7_bidir(p, w0, w_up, a0, a_up, g_up, k_k, k_a, r_k, ln_w, ln_b):
    B, T, _ = p.shape
    f32 = jnp.float32
    r, k, v, wd, ad, gd = jnp.split(
        p.astype(f32), [D_A, 2 * D_A, 3 * D_A, 3 * D_A + LORA_W, 3 * D_A + LORA_W + LORA_A], axis=-1)
    w_pre = w0[:, None, None, :] + jnp.einsum('btr,drc->dbtc', jnp.tanh(wd), w_up)
    decay = jnp.exp(-jnp.exp(-jax.nn.softplus(-w_pre) - 0.5))
    a = jax.nn.sigmoid(a0[:, None, None, :] + jnp.einsum('btr,drc->dbtc', ad, a_up))
    g = jnp.einsum('btr,rc->btc', jax.nn.sigmoid(gd), g_up)
    kk = _heads_a(k * k_k)
    kk = kk * lax.rsqrt(jnp.maximum(jnp.sum(kk * kk, axis=-1, keepdims=True), 1e-24))
    k_dir = _heads_a(k[None] * (1.0 + (a - 1.0) * k_a))
    rh, vh = _heads_a(r), _heads_a(v)
    xs = (_orient(_both(rh)), _orient(_heads_a(decay)), _orient(k_dir),
          _orient(_both(vh)), _orient(_both(kk)), _orient(_heads_a(a)))

    def step(S, inp):
        r_t, w_t, k_t, v_t, kk_t, a_t = inp
        sa = jnp.einsum('dbhvk,dbhk->dbhv', S, -kk_t)
        S = (S * w_t[..., None, :] + sa[..., None] * (kk_t * a_t)[..., None, :]
             + v_t[..., None] * k_t[..., None, :])
        return S, jnp.einsum('dbhvk,dbhk->dbhv', S, r_t)

    S0 = jnp.zeros((2, B, H_A, HEAD_A, HEAD_A), f32)
    _, ys = lax.scan(step, S0, xs)
    ys = jnp.moveaxis(ys, 0, 2)
    y = ys[0] + jnp.flip(ys[1], axis=1)
    mu = jnp.mean(y, axis=-1, keepdims=True)
    var = jnp.mean(jnp.square(y - mu), axis=-1, keepdims=True)
    y = ((y - mu) * lax.rsqrt(var + GN_EPS)).reshape(B, T, D_A) * ln_w + ln_b
    bonus = jnp.sum(jnp.sum(rh[None] * k_dir * r_k, axis=-1, keepdims=True), axis=0) * vh
    y = (y + bonus.reshape(B, T, D_A)) * g
    return y.astype(p.dtype)


def _diff_attention(p, lam_q1, lam_k1, lam_q2, lam_k2, subln_g, lambda_init):
    B, T, _ = p.shape
    f32 = jnp.float32
    q, k, v = jnp.split(p, [D_B, 2 * D_B], axis=-1)
    q = q.reshape(B, T, H_B, 2, HEAD_B)
    k = k.reshape(B, T, H_B, 2, HEAD_B)
    v = v.reshape(B, T, H_B, 2 * HEAD_B)
    lam = (jnp.exp(jnp.sum(lam_q1.astype(f32) * lam_k1.astype(f32)))
           - jnp.exp(jnp.sum(lam_q2.astype(f32) * lam_k2.astype(f32))) + lambda_init)
    slopes = jnp.exp2(-8.0 * jnp.arange(1, H_B + 1, dtype=f32) / H_B)
    nblk = -(-T // Q_BLOCK)
    t_pad = nblk * Q_BLOCK
    qb = jnp.pad(q, ((0, 0), (0, t_pad - T), (0, 0), (0, 0), (0, 0)))
    qb = jnp.moveaxis(qb.reshape(B, nblk, Q_BLOCK, H_B, 2, HEAD_B), 1, 0)
    qpos = jnp.arange(t_pad, dtype=jnp.int32).reshape(nblk, Q_BLOCK)
    kpos = jnp.arange(T, dtype=jnp.int32)
    scale = HEAD_B ** -0.5

    def block(args):
        q_blk, pos = args
        s = jnp.einsum('bqhmd,bshmd->bhmqs', q_blk, k).astype(f32) * scale
        dist = jnp.abs(pos[:, None] - kpos[None, :]).astype(f32)
        s = s - slopes[None, :, None, None, None] * dist
        prob = jax.nn.softmax(s, axis=-1)
        att = prob[:, :, 0] - lam * prob[:, :, 1]
        return jnp.einsum('bhqs,bshe->bqhe', att.astype(v.dtype), v)

    o = lax.map(block, (qb, qpos))
    o = jnp.moveaxis(o, 0, 1).reshape(B, t_pad, H_B, 2 * HEAD_B)[:, :T]
    o = _rmsnorm(o, subln_g, SUBLN_EPS) * (1.0 - lambda_init)
    return o.reshape(B, T, D_B)


def _trunk(x, weights):
    (meta_tokens, g_mix, w_in, mu_prev, mu_next, w0, w_up, a0, a_up, g_up, k_k, k_a, r_k,
     ln_x_w, ln_x_b, lam_q1, lam_k1, lam_q2, lam_k2, subln_g, w_up_a, w_up_b, w_out,
     g_ffn, w_ff1, w_ff2, g_final) = weights
    B = x.shape[0]
    meta = jnp.broadcast_to(meta_tokens[None].astype(x.dtype), (B, N_META, D_MODEL))
    h = jnp.concatenate([meta, x], axis=1)
    for l in range(DEPTH):
        lambda_init = 0.8 - 0.6 * math.exp(-0.3 * l)
        n = _rmsnorm(h, g_mix[l])
        proj = jnp.einsum('btd,dc->btc', n, w_in[l])
        p_a, p_b, gates = jnp.split(proj, [RWKV_COLS, RWKV_COLS + ATTN_COLS], axis=-1)
        o_a = _rwkv7_bidir(_token_shift(p_a, mu_prev[l], mu_next[l]), w0[l], w_up[l], a0[l],
                           a_up[l], g_up[l], k_k[l], k_a[l], r_k[l], ln_x_w[l], ln_x_b[l])
        o_b = _diff_attention(p_b, lam_q1[l], lam_k1[l], lam_q2[l], lam_k2[l], subln_g[l],
                              lambda_init)
        g_a, g_b = jnp.split(gates, 2, axis=-1)
        merged = (jax.nn.sigmoid(g_a) * jnp.einsum('btc,cd->btd', o_a, w_up_a[l])
                  + jax.nn.sigmoid(g_b) * jnp.einsum('btc,cd->btd', o_b, w_up_b[l]))
        h = h + jnp.einsum('btd,de->bte', merged, w_out[l])
        m = _rmsnorm(h, g_ffn[l])
        hid = jnp.square(jax.nn.relu(jnp.einsum('btd,df->btf', m, w_ff1[l])))
        h = h + jnp.einsum('btf,fd->btd', hid, w_ff2[l])
    return _rmsnorm(h, g_final)[:, N_META:]


def setup_inputs(seed: int = 0) -> dict:
    key = jax.random.key(seed)
    ks = iter(jax.random.split(key, 40))
    f32 = jnp.float32

    def nrm(shape, scale):
        return scale * jax.random.normal(next(ks), shape, f32)

    def uni(shape, lo, hi):
        return jax.random.uniform(next(ks), shape, f32, lo, hi)

    L = DEPTH
    return {
        'x_prompt': nrm((BATCH, SEQ, D_MODEL), 1.0),
        'x_sample': nrm((DEC_BATCH, DEC_SEQ, D_MODEL), 1.0),
        'meta_tokens': nrm((N_META, D_MODEL), 1.0),
        'g_mix': 1.0 + nrm((L, D_MODEL), 0.02),
        'w_in': nrm((L, D_MODEL, D_IN), D_MODEL ** -0.5),
        'mu_prev': uni((L, RWKV_COLS), 0.0, 0.5),
        'mu_next': uni((L, RWKV_COLS), 0.0, 0.5),
        'w0': nrm((L, 2, D_A), 0.5),
        'w_up': nrm((L, 2, LORA_W, D_A), 0.5 * LORA_W ** -0.5),
        'a0': nrm((L, 2, D_A), 0.5),
        'a_up': nrm((L, 2, LORA_A, D_A), LORA_A ** -0.5),
        'g_up': nrm((L, LORA_G, D_A), LORA_G ** -0.5),
        'k_k': 0.85 + nrm((L, D_A), 0.05),
        'k_a': 1.0 + nrm((L, D_A), 0.05),
        'r_k': nrm((L, H_A, HEAD_A), 0.1),
        'ln_x_w': 1.0 + nrm((L, D_A), 0.02),
        'ln_x_b': nrm((L, D_A), 0.02),
        'lam_q1': nrm((L, HEAD_B), 0.1),
        'lam_k1': nrm((L, HEAD_B), 0.1),
        'lam_q2': nrm((L, HEAD_B), 0.1),
        'lam_k2': nrm((L, HEAD_B), 0.1),
        'subln_g': 1.0 + nrm((L, 2 * HEAD_B), 0.02),
        'w_up_a': nrm((L, D_A, D_MODEL), D_A ** -0.5),
        'w_up_b': nrm((L, D_B, D_MODEL), D_B ** -0.5),
        'w_out': nrm((L, D_MODEL, D_MODEL), D_MODEL ** -0.5),
        'g_ffn': 1.0 + nrm((L, D_MODEL), 0.02),
        'w_ff1': nrm((L, D_MODEL, D_FF), D_MODEL ** -0.5),
        'w_ff2': nrm((L, D_FF, D_MODEL), D_FF ** -0.5),
        'g_final': 1.0 + nrm((D_MODEL,), 0.02),
    }


def reference(x_prompt, x_sample, meta_tokens, g_mix, w_in, mu_prev, mu_next, w0, w_up, a0,
              a_up, g_up, k_k, k_a, r_k, ln_x_w, ln_x_b, lam_q1, lam_k1, lam_q2, lam_k2,
              subln_g, w_up_a, w_up_b, w_out, g_ffn, w_ff1, w_ff2, g_final):
    weights = (meta_tokens, g_mix, w_in, mu_prev, mu_next, w0, w_up, a0, a_up, g_up, k_k, k_a,
               r_k, ln_x_w, ln_x_b, lam_q1, lam_k1, lam_q2, lam_k2, subln_g, w_up_a, w_up_b,
               w_out, g_ffn, w_ff1, w_ff2, g_final)
    y_prompt = _trunk(x_prompt, weights)
    y_sample = _trunk(x_sample, weights)
    return (y_prompt, y_sample)
```

```python
import math
import numpy as np
from contextlib import ExitStack
import concourse.bass as bass
import concourse.mybir as mybir
from concourse.bass_utils import run_bass_kernel_spmd

F32 = mybir.dt.float32
BF16 = mybir.dt.bfloat16
AF = mybir.ActivationFunctionType
ALU = mybir.AluOpType
ENGS = ("pe", "act", "dve", "pool", "sp")
NDMA = 8

D = 1024
NCORE = 8
T_P, T_S = 16512, 2176
PC = 704
K0 = math.exp(-0.5)
LAMBDA_INIT = 0.2
NPP = 32


class Buf:
    __slots__ = ("w", "r", "name")

    def __init__(self, name=""):
        self.w = None
        self.r = []
        self.name = name


class Prog:
    def __init__(self, nc):
        self.nc = nc
        self.q = {e: [] for e in ENGS}
        self.cnt = {e: 0 for e in ENGS}
        self.seen = {e: {} for e in ENGS}
        self.dman = {e: 0 for e in ENGS}
        self.dma_last = {}

    def _wait(self, eng, key, val):
        if val <= 0:
            return
        s = self.seen[eng]
        if s.get(key, 0) >= val:
            return
        s[key] = val
        self.q[eng].append(("wait", key, val))

    def _deps(self, eng, reads, writes):
        for b in reads:
            if b.w is not None:
                k, v = b.w
                if k == eng and eng == "pe":
                    continue
                self._wait(eng, k, v)
        for b in writes:
            if b.w is not None:
                k, v = b.w
                if not (k == eng and eng == "pe"):
                    self._wait(eng, k, v)
            for (k, v) in b.r:
                if not (k == eng and eng == "pe"):
                    self._wait(eng, k, v)

    def _mark(self, ticket, reads, writes):
        k = ticket[0]
        for b in reads:
            b.r = [t for t in b.r if t[0] != k]
            b.r.append(ticket)
        for b in writes:
            b.w = ticket
            b.r = []

    def op(self, eng, fn, reads=(), writes=(), after_readers=()):
        self._deps(eng, reads, writes)
        for b in after_readers:
            for (k, v) in b.r:
                if not (k == eng and eng == "pe"):
                    self._wait(eng, k, v)
        self.cnt[eng] += 1
        t = (eng, self.cnt[eng])
        self.q[eng].append(("op", fn, eng, 1))
        self._mark(t, reads, writes)
        return t

    def dma(self, fn, reads=(), writes=(), eng="sp"):
        self._deps(eng, reads, writes)
        j = self.dman[eng]
        self.dman[eng] += 1
        slot = j % NDMA
        key = ("d", eng, slot)
        self._wait(eng, key, 16 * (j // NDMA))
        val = 16 * (j // NDMA + 1)
        self.q[eng].append(("op", fn, key, 16))
        self.dma_last[key] = val
        t = (key, val)
        self._mark(t, reads, writes)
        return t

    def barrier(self):
        keys = [(e, self.cnt[e]) for e in ENGS] + list(self.dma_last.items())
        for e in ENGS:
            for (k, v) in keys:
                if k != e:
                    self._wait(e, k, v)

    def finish(self):
        for (k, v) in list(self.dma_last.items()):
            self._wait("sp", k, v)
        for e in ENGS:
            if e != "sp":
                self._wait("sp", e, self.cnt[e])

    def emit(self, stack):
        nc = self.nc
        sems = {}
        for e in ENGS:
            sems[e] = stack.enter_context(nc.semaphore("ps_" + e))
        for k in self.dma_last:
            sems[k] = stack.enter_context(nc.semaphore("ds_%s_%d" % (k[1], k[2])))
        handles = {"pe": "tensor", "act": "scalar", "dve": "vector", "pool": "gpsimd", "sp": "sync"}
        block = stack.enter_context(nc.Block())

        def replay(engname):
            def body(engh):
                for it in self.q[engname]:
                    if it[0] == "wait":
                        engh.wait_ge(sems[it[1]], it[2])
                    else:
                        ins = it[1](engh)
                        ins.then_inc(sems[it[2]], it[3])
            return body

        for e in ENGS:
            getattr(block, handles[e])(replay(e))


def _consts():
    c = np.zeros((128, 1024), np.float32)
    c[:, 0:128] = np.eye(128)
    s = np.arange(64)
    mf = np.zeros((128, 128), np.float32)
    mb = np.zeros((128, 128), np.float32)
    for a in range(2):
        for b in range(2):
            if b == 0:
                mf[a * 64:(a + 1) * 64, 0:64] = (s[:, None] < s[None, :])
                mb[a * 64:(a + 1) * 64, 0:64] = (s[:, None] > s[None, :])
            else:
                mf[a * 64:(a + 1) * 64, 64:128] = (s[:, None] <= s[None, :])
                mb[a * 64:(a + 1) * 64, 64:128] = (s[:, None] >= s[None, :])
    c[:, 128:256] = mf
    c[:, 256:384] = mb
    c[0:64, 384:448] = (s[None, :] < s[:, None])
    c[64:128, 384:448] = (s[None, :] > s[:, None])
    c[0:64, 512:576] = 1.0
    c[64:128, 576:640] = 1.0
    c[0:64, 640:704] = np.eye(64)
    c[64:128, 640:704] = np.eye(64)
    c[:, 704:832] = 1.0
    c[0:64, 832:896] = 1.0 / 64
    c[:, 896:1024] = 1.0 / 128
    return c


def _alibi_static(hb):
    m = 2.0 ** (-8.0 * (hb + 1) / 4)
    sl = np.arange(128, dtype=np.float64)
    tq = np.arange(512, dtype=np.float64)
    DM = np.zeros((128, 5, 512))
    for dl in range(5):
        DM[:, dl, :] = -m * np.abs(16.0 + tq[None, :] - 128.0 * dl - sl[:, None])
    hi = -m * 16.0 * np.floor(tq / 16)
    lo = -m * (tq % 16)
    qb = np.stack([np.stack([hi, lo]), np.stack([-hi, -lo])])
    return DM.astype(np.float32), qb.astype(np.float32)


def _alibi_core(c):
    bc = np.zeros((2, 4, 128, 4, 129), np.float32)
    ks = np.ones((2, 2, T_P), np.float32)
    kval = np.zeros((2, 128, 129), np.float32)
    sl = np.arange(128, dtype=np.float64)
    for kind, (T, nreal, rot, own0) in enumerate(((T_P, 16400, 16 * c, 16 + 2048 * c), (T_S, 2064, 0, 16))):
        nkt = T // 128
        for r in range(nkt):
            i = (r + rot) % nkt
            s = 128.0 * i + sl
            kval[kind, :, r] = (s < nreal)
            if r > 16:
                left = (128 * i + 127) < own0
                ks[kind, :, r * 128:(r + 1) * 128] = 1.0 if left else -1.0
            for hb in range(4):
                m = 2.0 ** (-8.0 * (hb + 1) / 4)
                for jl in range(4):
                    t0 = own0 + 512 * jl
                    if 128 * i + 127 < t0:
                        bc[kind, hb, :, jl, r] = -m * (t0 - s)
                    elif 128 * i >= t0 + 512:
                        bc[kind, hb, :, jl, r] = -m * (s - t0)
    return bc, ks, kval


def build_program(dbg=None):
    dbg = dbg or {}
    pieces_run = dbg.get("pieces", list(range(8)))
    do_cast = dbg.get("cast", True)
    do_p1 = dbg.get("p1", True)
    do_attn = dbg.get("attn", True)
    do_rwkv = dbg.get("rwkv", True)
    do_xchg = dbg.get("xchg", True)
    do_post = dbg.get("post", True)
    dbg_out = dbg.get("dbg_out", False)
    skind = "ExternalOutput" if dbg_out else "Internal"

    nc = bass.Bass("TRN2", target_bir_lowering=False)

    def din(name, shape):
        return nc.dram_tensor(name, list(shape), F32, kind="ExternalInput").ap()

    def dscr(name, shape, dt):
        if name in dbg.get("outs", ()):
            return nc.dram_tensor(name, list(shape), dt, kind="ExternalOutput").ap()
        return nc.dram_tensor(name, list(shape), dt).ap()

    hp = din("hp", [T_P, D])
    hs = din("hs", [T_S, D])
    xpp = din("xpp", [2048, D])
    wpc = din("wpc", [8, D, PC])
    pp_d = din("pp", [8, 128, NPP])
    lw_d = din("lw", [8, 128, 64])
    la_d = din("la", [8, 128, 64])
    lg_d = din("lg", [8, 128, 64])
    bc_d = din("bc", [2, 4, 128, 4, 129])
    dm_d = din("dm", [4, 128, 5, 512])
    qb_d = din("qb", [4, 2, 2, 512])
    ks_d = din("ks", [2, 2, T_P])
    wg_d = din("wg", [D, 2048])
    wua_d = din("wua", [512, D])
    wub_d = din("wub", [512, D])
    wout_d = din("wout", [D, D])
    wf1_d = din("wf1", [D, 4096])
    wf2_d = din("wf2", [4096, D])
    gm_d = din("gm", [128, 8])
    gf_d = din("gf", [128, 8])
    gfin_d = din("gfin", [128, D])
    slg_d = din("slg", [128, 1])
    lam_d = din("lam", [128, 4, 64])
    cst_d = din("cst", [128, 1024])
    kval_d = din("kval", [2, 128, 129])

    yp = nc.dram_tensor("yp", [2048, D], F32, kind="ExternalOutput").ap()
    ys = nc.dram_tensor("ys", [2048, D], F32, kind="ExternalOutput").ap()

    wpc_b = dscr("wpc_b", [8, D, PC], BF16)
    wg_b = dscr("wg_b", [D, 2048], BF16)
    wua_b = dscr("wua_b", [512, D], BF16)
    wub_b = dscr("wub_b", [512, D], BF16)
    wout_b = dscr("wout_b", [D, D], BF16)
    wf1_b = dscr("wf1_b", [D, 4096], BF16)
    wf2_b = dscr("wf2_b", [4096, D], BF16)
    TK = (T_P, T_S)
    PA = [dscr("PA%d" % k, [8, 192, TK[k] + 2], F32) for k in range(2)]
    PAL = [dscr("PAL%d" % k, [256, TK[k] + 2], F32) for k in range(2)]
    KTD = [dscr("KTD%d" % k, [8, 64, 2 * TK[k]], BF16) for k in range(2)]
    VD = [dscr("VD%d" % k, [8, 2 * TK[k], 128], BF16) for k in range(2)]
    QTD = [dscr("QTD%d" % k, [8, 64, TK[k]], BF16) for k in range(2)]
    OA = [dscr("OA%d" % k, [8, 64, TK[k]], BF16) for k in range(2)]
    XR = [dscr("XR%d" % k, [8, 128, 2048], BF16) for k in range(2)]
    KTR = [dscr("KTR%d" % k, [8, 64, TK[k]], BF16) for k in range(2)]
    VR = [dscr("VR%d" % k, [8, TK[k], 128], BF16) for k in range(2)]
    QO = [dscr("QO%d" % k, [8, 64, 2048], BF16) for k in range(2)]
    OAO = [dscr("OAO%d" % k, [8, 64, 2048], BF16) for k in range(2)]
    YD = [dscr("YD%d" % k, [8, 128, TK[k]], F32) for k in range(2)]
    PAO = [dscr("PAO%d" % k, [8, 192, 2050], F32) for k in range(2)]
    PALO = [dscr("PALO%d" % k, [256, 2050], F32) for k in range(2)]
    YDO = [dscr("YDO%d" % k, [8, 128, 2048], F32) for k in range(2)]
    b_PA, b_KV, b_wsc, b_ROT, b_OAO, b_YD, b_PAO, b_YDO = Buf(), Buf(), Buf(), Buf(), Buf(), Buf(), Buf(), Buf()
    b_XR = [Buf(), Buf()]
    b_OA = [Buf(), Buf()]

    P = Prog(nc)
    with ExitStack() as top:
        _uid = [0]

        def sb(st, name, shape, dt):
            _uid[0] += 1
            return st.enter_context(nc.sbuf_tensor("s%d_%s" % (_uid[0], name), list(shape), dt))

        PB = [top.enter_context(nc.psum_tensor("pb%d" % i, [128, 512], F32)) for i in range(8)]
        bPB = [Buf("pb%d" % i) for i in range(8)]

        cst = sb(top, "cst", [128, 1024], F32); b_cst = Buf()
        cstb = sb(top, "cstb", [128, 128], BF16)
        gm = sb(top, "gm", [128, 8], F32)
        gf = sb(top, "gf", [128, 8], F32)
        zb = sb(top, "zb", [128, 512], BF16)
        zf = sb(top, "zf", [128, 8], F32)
        P.dma(lambda e: e.dma_start(out=cst[:], in_=cst_d), writes=[b_cst])
        P.dma(lambda e: e.dma_start(out=gm[:], in_=gm_d), writes=[b_cst])
        P.dma(lambda e: e.dma_start(out=gf[:], in_=gf_d), writes=[b_cst])
        P.op("dve", lambda e: e.tensor_copy(out=cstb[:], in_=cst[:, 0:128]), reads=[b_cst], writes=[b_cst])
        P.op("dve", lambda e: e.memset(zb[:], 0.0), writes=[b_cst])
        P.op("dve", lambda e: e.memset(zf[:], 0.0), writes=[b_cst])
        P.barrier()
        ident = cst[:, 0:128]
        MF, MB = cst[:, 128:256], cst[:, 256:384]
        MAF, MAB = cst[0:64, 384:448], cst[0:64, 448:512]
        BLK = cst[:, 512:640]
        SEL = cst[:, 640:704]
        ONES = cst[:, 704:832]
        O64 = cst[0:64, 832:896]
        O128 = cst[:, 896:1024]

        def castw(src, dst, rows, cols, scale_cols=None):
            with ExitStack() as st:
                nb = 3
                fin = [sb(st, "cw_f%d" % i, [128, 2048], F32) for i in range(nb)]
                fout = [sb(st, "cw_b%d" % i, [128, 2048], BF16) for i in range(nb)]
                bi = [Buf() for _ in range(nb)]
                bo = [Buf() for _ in range(nb)]
                it = 0
                for kc in range(rows // 128):
                    for c0 in range(0, cols, 2048):
                        cw = min(2048, cols - c0)
                        s = it % nb
                        P.dma(lambda e, s=s, kc=kc, c0=c0, cw=cw: e.dma_start(out=fin[s][:, 0:cw], in_=src[kc * 128:(kc + 1) * 128, c0:c0 + cw]),
                              writes=[bi[s]])
                        eng = "dve" if it % 2 == 0 else "pool"
                        if scale_cols is not None:
                            P.op(eng, lambda e, s=s, kc=kc, cw=cw: e.tensor_scalar(out=fout[s][:, 0:cw], in0=fin[s][:, 0:cw],
                                                                                  scalar1=scale_cols[:, kc % 8:kc % 8 + 1], scalar2=None, op0=ALU.mult),
                                 reads=[bi[s]], writes=[bo[s]])
                        else:
                            P.op(eng, lambda e, s=s, cw=cw: e.tensor_copy(out=fout[s][:, 0:cw], in_=fin[s][:, 0:cw]),
                                 reads=[bi[s]], writes=[bo[s]])
                        P.dma(lambda e, s=s, kc=kc, c0=c0, cw=cw: e.dma_start(out=dst[kc * 128:(kc + 1) * 128, c0:c0 + cw], in_=fout[s][:, 0:cw]),
                              reads=[bo[s]], writes=[b_wsc], eng="act")
                        it += 1
                P.barrier()

        if do_cast:
            for p in range(8):
                castw(wpc[p], wpc_b[p], D, PC, gm)
            if do_post:
                castw(wg_d, wg_b, D, 2048, gm)
                castw(wua_d, wua_b, 512, D)
                castw(wub_d, wub_b, 512, D)
                castw(wout_d, wout_b, D, D)
                castw(wf1_d, wf1_b, D, 4096, gf)
                castw(wf2_d, wf2_b, 4096, D)

        def norm_T(st_tmp, src_ap, b_src, dst_ap, b_dst, pbank, tag):
            sq, ss, xn, bt = st_tmp
            P.op("act", lambda e: e.activation(out=sq[:], in_=src_ap, func=AF.Square, accum_out=ss[:, 0:1]),
                 reads=[b_src], writes=[bt])
            P.op("act", lambda e: e.activation(out=ss[:, 1:2], in_=ss[:, 0:1], func=AF.Sqrt, scale=1.0 / D, bias=zf[:, 0:1]),
                 reads=[bt], writes=[bt])
            P.op("dve", lambda e: e.tensor_scalar(out=ss[:, 1:2], in0=ss[:, 1:2], scalar1=1e-12, scalar2=None, op0=ALU.max),
                 reads=[bt], writes=[bt])
            P.op("dve", lambda e: e.reciprocal(out=ss[:, 2:3], in_=ss[:, 1:2]), reads=[bt], writes=[bt])
            P.op("dve", lambda e: e.tensor_scalar(out=xn[:], in0=src_ap, scalar1=ss[:, 2:3], scalar2=None, op0=ALU.mult),
                 reads=[b_src, bt], writes=[bt])
            pt = PB[pbank].bitcast(BF16)
            for k in range(8):
                P.op("pe", lambda e, k=k: e.transpose(out=pt[:, k * 128:(k + 1) * 128], in_=xn[:, k * 128:(k + 1) * 128], identity=cstb[:]),
                     reads=[bt], writes=[bPB[pbank]])
            P.op("act", lambda e: e.copy(out=dst_ap, in_=pt[:].rearrange("p (k t) -> p k t", k=8)), reads=[bPB[pbank]], writes=[b_dst])

        epsb = sb(top, "epsb", [128, 4], F32)
        P.op("dve", lambda e: e.memset(epsb[:, 0:1], 1e-6), writes=[b_cst])
        P.op("dve", lambda e: e.memset(epsb[:, 1:2], 1e-5), writes=[b_cst])
        P.op("dve", lambda e: e.memset(epsb[:, 2:3], 64e-5), writes=[b_cst])
        P.barrier()

        def norm_T2(st_tmp, src_ap, b_src, dst_ap, b_dst, pbank):
            sq, ss, xn, bt = st_tmp
            P.op("act", lambda e: e.activation(out=sq[:], in_=src_ap, func=AF.Square, accum_out=ss[:, 0:1]),
                 reads=[b_src], writes=[bt])
            P.op("act", lambda e: e.activation(out=ss[:, 1:2], in_=ss[:, 0:1], func=AF.Sqrt, scale=1.0 / D, bias=epsb[:, 0:1]),
                 reads=[bt], writes=[bt])
            P.op("dve", lambda e: e.reciprocal(out=ss[:, 2:3], in_=ss[:, 1:2]), reads=[bt], writes=[bt])
            P.op("dve", lambda e: e.tensor_scalar(out=xn[:], in0=src_ap, scalar1=ss[:, 2:3], scalar2=None, op0=ALU.mult),
                 reads=[b_src, bt], writes=[bt])
            pt = PB[pbank].bitcast(BF16)
            for k in range(8):
                P.op("pe", lambda e, k=k: e.transpose(out=pt[:, k * 128:(k + 1) * 128], in_=xn[:, k * 128:(k + 1) * 128], identity=cstb[:]),
                     reads=[bt], writes=[bPB[pbank]])
            P.op("act", lambda e: e.copy(out=dst_ap, in_=pt[:].rearrange("p (k t) -> p k t", k=8)), reads=[bPB[pbank]], writes=[b_dst])

        _pid = {}

        def core_off(e):
            k = id(e)
            if k not in _pid:
                _pid[k] = e.snap(e.partition_id() * 2048) if hasattr(e, "snap") else e.partition_id() * 2048
            return _pid[k]

        def p1_phase(kind):
            T = TK[kind]
            hsrc = hp if kind == 0 else hs
            tiles = [(t0, min(512, T - t0)) for t0 in range(0, T, 512)]
            tg = "a%d" % kind
            with ExitStack() as s1:
                wp = sb(s1, tg + "wp", [128, 8, 8, PC], BF16); b_wp = Buf()
                for p in range(8):
                    P.dma(lambda e, p=p: e.dma_start(out=wp[:, p, :, :], in_=wpc_b[p].rearrange("(k p) m -> p k m", p=128)), reads=[b_wsc], writes=[b_wp])
                    for (r0, nr) in ((0, 128), (128, 64)):
                        for col in (0, T + 1):
                            P.dma(lambda e, p=p, r0=r0, nr=nr, col=col: e.dma_start(out=PA[kind][p, r0:r0 + nr, col:col + 1], in_=zf[0:nr, 0:1], allow_slow_non_contiguous=True), writes=[b_PA])
                for r0 in (0, 128):
                    for col in (0, T + 1):
                        P.dma(lambda e, r0=r0, col=col: e.dma_start(out=PAL[kind][r0:r0 + 128, col:col + 1], in_=zf[:, 0:1], allow_slow_non_contiguous=True), writes=[b_PA])
                NXB = 3
                xt = [sb(s1, tg + "xt%d" % i, [128, D], F32) for i in range(NXB)]
                b_xt = [Buf() for _ in range(NXB)]
                sqs = sb(s1, tg + "sq", [128, D], F32)
                tmp = [(sqs, sb(s1, tg + "ss%d" % i, [128, 4], F32), sb(s1, tg + "xn%d" % i, [128, D], BF16), Buf()) for i in range(2)]
                nT = [sb(s1, tg + "nT%d" % i, [128, 8, 512], BF16) for i in range(2)]
                b_nT = [Buf() for _ in range(2)]
                stg = [sb(s1, tg + "stg%d" % i, [128, 4, 512], F32) for i in range(2)]
                b_stg = [Buf() for _ in range(2)]
                qk = [sb(s1, tg + "qk%d" % i, [64, 2, 512], BF16) for i in range(2)]
                b_qk = [Buf() for _ in range(2)]
                vst = [sb(s1, tg + "vst%d" % i, [128, 4, 128], BF16) for i in range(2)]
                b_vst = [Buf() for _ in range(2)]
                cnt = {"sub": 0, "it": 0}

                def do_tile(ti, t0, w):
                    nb = ti % 2
                    nsub = w // 128
                    for s in range(nsub):
                        xb = cnt["sub"] % NXB
                        P.dma(lambda e, xb=xb, s=s: e.dma_start(out=xt[xb][:], in_=hsrc[t0 + s * 128:t0 + (s + 1) * 128, :]), writes=[b_xt[xb]])
                        norm_T2(tmp[cnt["sub"] % 2], xt[xb][:], b_xt[xb], nT[nb][:, :, s * 128:(s + 1) * 128], b_nT[nb], cnt["sub"] % 2)
                        cnt["sub"] += 1
                    sbl = cnt["it"] % 2
                    cnt["it"] += 1
                    for ci, c0 in enumerate((192, 320)):
                        bk = 2 + (ci % 2)
                        for k in range(8):
                            P.op("pe", lambda e, k=k, c0=c0, bk=bk: e.matmul(out=PB[bk][:, 0:w], lhsT=wp[:, 0, k, c0:c0 + 128], rhs=nT[nb][:, k, 0:w], start=(k == 0), stop=(k == 7)),
                                 reads=[b_wp, b_nT[nb]], writes=[bPB[bk]])
                        if ci == 0:
                            P.op("act", lambda e, bk=bk, ci=ci: e.copy(out=stg[sbl][:, ci, 0:w], in_=PB[bk][:, 0:w]), reads=[bPB[bk]], writes=[b_stg[sbl]])
                        else:
                            P.op("dve", lambda e, bk=bk, ci=ci: e.tensor_copy(out=stg[sbl][:, ci, 0:w], in_=PB[bk][:, 0:w]), reads=[bPB[bk]], writes=[b_stg[sbl]])
                    P.dma(lambda e: e.dma_start(out=PAL[kind][:, 1 + t0:1 + t0 + w].rearrange("(c p) t -> p c t", p=128), in_=stg[sbl][:, 0:2, 0:w]),
                          reads=[b_stg[sbl]], writes=[b_PA])
                    for p in range(8):
                        sbi = cnt["it"] % 2
                        cnt["it"] += 1
                        for ci, (c0, cw) in enumerate([(0, 128), (128, 64)]):
                            bk = 2 + (ci % 2)
                            for k in range(8):
                                P.op("pe", lambda e, k=k, c0=c0, cw=cw, bk=bk, p=p: e.matmul(
                                    out=PB[bk][0:cw, 0:w], lhsT=wp[:, p, k, c0:c0 + cw], rhs=nT[nb][:, k, 0:w], start=(k == 0), stop=(k == 7)),
                                    reads=[b_wp, b_nT[nb]], writes=[bPB[bk]])
                            if ci % 2 == 0:
                                P.op("act", lambda e, cw=cw, bk=bk, ci=ci, sbi=sbi: e.copy(out=stg[sbi][0:cw, ci, 0:w], in_=PB[bk][0:cw, 0:w]),
                                     reads=[bPB[bk]], writes=[b_stg[sbi]])
                            else:
                                P.op("dve", lambda e, cw=cw, bk=bk, ci=ci, sbi=sbi: e.tensor_copy(out=stg[sbi][0:cw, ci, 0:w], in_=PB[bk][0:cw, 0:w]),
                                     reads=[bPB[bk]], writes=[b_stg[sbi]])
                        P.dma(lambda e, p=p, sbi=sbi: e.dma_start(out=PA[kind][p, 0:128, 1 + t0:1 + t0 + w], in_=stg[sbi][:, 0, 0:w]),
                              reads=[b_stg[sbi]], writes=[b_PA])
                        P.dma(lambda e, p=p, sbi=sbi: e.dma_start(out=PA[kind][p, 128:192, 1 + t0:1 + t0 + w], in_=stg[sbi][0:64, 1, 0:w]),
                              reads=[b_stg[sbi]], writes=[b_PA])
                        for k in range(8):
                            P.op("pe", lambda e, k=k, p=p: e.matmul(out=PB[4][0:64, 0:w], lhsT=wp[:, p, k, 448:512], rhs=nT[nb][:, k, 0:w], start=(k == 0), stop=(k == 7)),
                                 reads=[b_wp, b_nT[nb]], writes=[bPB[4]])
                        P.op("act", lambda e, sbi=sbi: e.activation(out=qk[sbi][:, 0, 0:w], in_=PB[4][0:64, 0:w], func=AF.Copy, scale=0.125),
                             reads=[bPB[4]], writes=[b_qk[sbi]])
                        for k in range(8):
                            P.op("pe", lambda e, k=k, p=p: e.matmul(out=PB[5][0:64, 0:w], lhsT=wp[:, p, k, 512:576], rhs=nT[nb][:, k, 0:w], start=(k == 0), stop=(k == 7)),
                                 reads=[b_wp, b_nT[nb]], writes=[bPB[5]])
                        P.op("dve", lambda e, sbi=sbi: e.tensor_copy(out=qk[sbi][:, 1, 0:w], in_=PB[5][0:64, 0:w]), reads=[bPB[5]], writes=[b_qk[sbi]])
                        P.dma(lambda e, p=p, sbi=sbi: e.dma_start(out=QTD[kind][p, :, t0:t0 + w], in_=qk[sbi][:, 0, 0:w]), reads=[b_qk[sbi]], writes=[b_KV])
                        for rep_ in range(2):
                            P.dma(lambda e, p=p, sbi=sbi, rep_=rep_: e.dma_start(out=KTD[kind][p, :, rep_ * T + t0:rep_ * T + t0 + w], in_=qk[sbi][:, 1, 0:w]),
                                  reads=[b_qk[sbi]], writes=[b_KV])
                        if p % 2 == 1:
                            continue
                        for s in range(nsub):
                            for k in range(8):
                                P.op("pe", lambda e, k=k, s=s, p=p: e.matmul(out=PB[6][:, s * 128:(s + 1) * 128], lhsT=nT[nb][:, k, s * 128:(s + 1) * 128],
                                                                             rhs=wp[:, p, k, 576:704], start=(k == 0), stop=(k == 7)),
                                     reads=[b_wp, b_nT[nb]], writes=[bPB[6]])
                        P.op("act", lambda e, sbi=sbi: e.copy(out=vst[sbi][:, 0:nsub, :], in_=PB[6][:, 0:nsub * 128].rearrange("p (s e) -> p s e", s=nsub)),
                             reads=[bPB[6]], writes=[b_vst[sbi]])
                        for rep_ in range(2):
                            P.dma(lambda e, p=p, sbi=sbi, rep_=rep_: e.dma_start(
                                out=VD[kind][p, rep_ * T + t0:rep_ * T + t0 + w, :].rearrange("(s q) e -> q s e", q=128), in_=vst[sbi][:, 0:nsub, :]),
                                reads=[b_vst[sbi]], writes=[b_KV])

                for ti, (t0, w) in enumerate(tiles[:dbg.get("p1_tiles", 10 ** 9)]):
                    do_tile(ti, t0, w)
                P.barrier()

        def rotate_phase(kind):
            T = TK[kind]

            def off(e, extra):
                return (core_off(e) + extra) if kind == 0 else extra
            P.dma(lambda e: e.dma_start(out=KTR[kind][:, :, :], in_=KTD[kind][:, :, bass.ds(off(e, 0), T)]), reads=[b_KV], writes=[b_ROT])
            P.dma(lambda e: e.dma_start(out=VR[kind].rearrange("p t e -> p (t e)"),
                                        in_=VD[kind].rearrange("p t e -> p (t e)")[:, bass.ds(off(e, 0) * 128, T * 128)]), reads=[b_KV], writes=[b_ROT])
            P.dma(lambda e: e.dma_start(out=QO[kind][:, :, :], in_=QTD[kind][:, :, bass.ds(off(e, 16), 2048)]), reads=[b_KV], writes=[b_ROT])
            P.dma(lambda e: e.dma_start(out=PAO[kind].rearrange("p r t -> (p r) t"), in_=PA[kind].rearrange("p r t -> (p r) t")[:, bass.ds(off(e, 16), 2050)]),
                  reads=[b_PA], writes=[b_PAO])
            P.dma(lambda e: e.dma_start(out=PALO[kind][:, :], in_=PAL[kind][:, bass.ds(off(e, 16), 2050)]), reads=[b_PA], writes=[b_PAO])
            P.barrier()

        def own_y_phase(kind):
            def off(e, extra):
                return (core_off(e) + extra) if kind == 0 else extra
            P.dma(lambda e: e.dma_start(out=YDO[kind].rearrange("p r t -> (p r) t"), in_=YD[kind].rearrange("p r t -> (p r) t")[:, bass.ds(off(e, 16), 2048)]),
                  reads=[b_YD], writes=[b_YDO])
            P.barrier()

        def attn_job(kind, p):
            T = TK[kind]
            NKT = T // 128
            hb = p // 2
            m = 2.0 ** (-8.0 * (hb + 1) / 4)
            tg = "t%d_%d" % (kind, p)
            with ExitStack() as s2:
                KT = sb(s2, tg + "KT", [66, T], BF16); b_KT = Buf()
                VP = sb(s2, tg + "VP", [128, NKT, 129], BF16); b_VP = Buf()
                QT = sb(s2, tg + "QT", [64, 2048], BF16); b_QT = Buf()
                ksf = sb(s2, tg + "ksf", [66, T], F32) if False else None
                kvf = sb(s2, tg + "kvf", [128, 129], F32)
                ksr = sb(s2, tg + "ksr", [66, 2048], F32)
                bc = sb(s2, tg + "bc", [128, 4, 129], F32); b_bc = Buf()
                dmk = sb(s2, tg + "dmk", [128, 5, 512], F32)
                qbf = sb(s2, tg + "qbf", [66, 2, 512], F32)
                Qp = sb(s2, tg + "Qp", [66, 512], BF16)
                Qn = sb(s2, tg + "Qn", [66, 512], BF16)
                b_Q = Buf()
                PT = [sb(s2, tg + "PT%d" % i, [128, 512], BF16) for i in range(3)]
                b_PT = [Buf() for _ in range(3)]
                dtmp = [sb(s2, tg + "dt%d" % i, [128, 512], F32) for i in range(2)]
                b_dt = [Buf() for _ in range(2)]
                rinv = sb(s2, tg + "rinv", [128, 4], F32); b_ri = Buf()
                On = sb(s2, tg + "On", [128, 4, 128], F32); b_On = Buf()
                OT = [sb(s2, tg + "OT%d" % i, [128, 512], BF16) for i in range(2)]
                b_OT = [Buf() for _ in range(2)]

                def dyn(e, base_mult):
                    if kind == 0:
                        return core_off(e)
                    return 0
                P.dma(lambda e: e.dma_start(out=KT[0:64, :], in_=KTR[kind][p, :, :]), reads=[b_ROT], writes=[b_KT])
                P.dma(lambda e: e.dma_start(out=VP[:, :, 0:128], in_=VR[kind][p - p % 2, :, :].rearrange("(s q) e -> q s e", q=128)),
                      reads=[b_ROT], writes=[b_VP])
                P.dma(lambda e: e.dma_start(out=QT[:, :], in_=QO[kind][p, :, :]), reads=[b_ROT], writes=[b_QT])
                P.dma(lambda e: e.dma_start(out=kvf[:], in_=kval_d[kind]), writes=[b_VP])
                P.op("dve", lambda e: e.tensor_copy(out=VP[:, :, 128:129], in_=kvf[:, 0:NKT].unsqueeze(2)), reads=[b_VP], writes=[b_VP])
                for c0 in range(0, T, 2048):
                    cw = min(2048, T - c0)
                    P.dma(lambda e, c0=c0, cw=cw: e.dma_start(out=ksr[64:66, 0:cw], in_=ks_d[kind, :, c0:c0 + cw]), writes=[b_bc])
                    P.op("dve", lambda e, c0=c0, cw=cw: e.tensor_copy(out=KT[64:66, c0:c0 + cw], in_=ksr[64:66, 0:cw]), reads=[b_bc], writes=[b_KT])
                P.dma(lambda e: e.dma_start(out=bc[:], in_=bc_d[kind, hb]), writes=[b_bc])
                P.dma(lambda e: e.dma_start(out=dmk[:], in_=dm_d[hb]), writes=[b_bc])
                P.dma(lambda e: e.dma_start(out=qbf[64:66, :, :], in_=qb_d[hb].rearrange("s r t -> r s t")), writes=[b_bc])
                P.op("dve", lambda e: e.tensor_copy(out=Qp[64:66, :], in_=qbf[64:66, 0, :]), reads=[b_bc], writes=[b_Q])
                P.op("dve", lambda e: e.tensor_copy(out=Qn[64:66, :], in_=qbf[64:66, 1, :]), reads=[b_bc], writes=[b_Q])
                st_ = {"blk": 0, "pend": None}

                def do_q(jl):
                    P.op("pool", lambda e: e.tensor_copy(out=Qp[0:64, :], in_=QT[:, jl * 512:(jl + 1) * 512]), reads=[b_QT], writes=[b_Q])
                    P.op("pool", lambda e: e.tensor_copy(out=Qn[0:64, :], in_=QT[:, jl * 512:(jl + 1) * 512]), reads=[b_QT], writes=[b_Q])
                    for bk in (2, 3):
                        P.op("pe", lambda e, bk=bk: e.matmul(out=PB[bk][:, :], lhsT=zb[0:1, 0:128], rhs=zb[0:1, 0:512], start=True, stop=True,
                                                             skip_group_check=True), reads=[b_cst], writes=[bPB[bk]])
                    for r in range(NKT):
                        dl = r - 4 * jl
                        diag = (r <= 16) and (0 <= dl <= 4)
                        if r <= 16:
                            if dl < 0:
                                dist = (16 + 512 * jl) - (128 * r + 127)
                            elif dl > 4:
                                dist = 128 * r - (16 + 512 * jl + 511)
                            else:
                                dist = 0
                        else:
                            d_right = 128 * r - (16 + 512 * jl + 511)
                            d_left = 512 * jl - 128 * r + 16401
                            dist = min(d_right, d_left) if kind == 0 else d_right
                        if m * dist > 150.0:
                            continue
                        sbk = st_["blk"] % 2
                        pb = st_["blk"] % 3
                        st_["blk"] += 1
                        if diag:
                            P.op("pe", lambda e, r=r, sbk=sbk: e.matmul(out=PB[sbk][:, :], lhsT=KT[0:64, r * 128:(r + 1) * 128], rhs=Qp[0:64, :],
                                                                        start=True, stop=True), reads=[b_KT, b_Q], writes=[bPB[sbk]])
                            P.op("dve", lambda e, sbk=sbk, dl=dl: e.tensor_tensor(out=dtmp[sbk][:], in0=PB[sbk][:, :], in1=dmk[:, dl, :], op=ALU.add),
                                 reads=[bPB[sbk], b_bc], writes=[b_dt[sbk]])
                            P.op("act", lambda e, sbk=sbk, pb=pb: e.activation(out=PT[pb][:], in_=dtmp[sbk][:], func=AF.Exp),
                                 reads=[b_dt[sbk]], writes=[b_PT[pb]])
                        else:
                            Qx = Qn if (r <= 16 and dl > 4) else Qp
                            P.op("pe", lambda e, r=r, sbk=sbk, Qx=Qx: e.matmul(out=PB[sbk][:, :], lhsT=KT[0:66, r * 128:(r + 1) * 128], rhs=Qx[0:66, :],
                                                                               start=True, stop=True), reads=[b_KT, b_Q], writes=[bPB[sbk]])
                            P.op("act", lambda e, r=r, sbk=sbk, pb=pb: e.activation(out=PT[pb][:], in_=PB[sbk][:, :], func=AF.Exp, bias=bc[:, jl, r:r + 1]),
                                 reads=[bPB[sbk], b_bc], writes=[b_PT[pb]])
                        def pv(pb=pb, r=r):
                            for s in range(4):
                                bk, off = (2, s * 129) if s < 3 else (3, 0)
                                P.op("pe", lambda e, pb=pb, s=s, bk=bk, off=off, r=r: e.matmul(out=PB[bk][:, off:off + 129], lhsT=PT[pb][:, s * 128:(s + 1) * 128],
                                                                                               rhs=VP[:, r, :], start=False, stop=False, skip_group_check=True),
                                     reads=[b_PT[pb], b_VP], writes=[bPB[bk]])
                        if st_["pend"] is not None:
                            st_["pend"]()
                        st_["pend"] = pv
                    if st_["pend"] is not None:
                        st_["pend"]()
                        st_["pend"] = None
                    ob = jl % 2
                    for s in range(4):
                        bk, off = (2, s * 129) if s < 3 else (3, 0)
                        P.op("dve", lambda e, s=s, bk=bk, off=off: e.reciprocal(out=rinv[:, s:s + 1], in_=PB[bk][:, off + 128:off + 129]),
                             reads=[bPB[bk]], writes=[b_ri])
                        P.op("dve", lambda e, s=s, bk=bk, off=off: e.tensor_scalar(out=On[:, s, :], in0=PB[bk][:, off:off + 128], scalar1=rinv[:, s:s + 1],
                                                                                  scalar2=None, op0=ALU.mult), reads=[bPB[bk], b_ri], writes=[b_On])
                    for s in range(4):
                        P.op("pe", lambda e, s=s: e.transpose(out=PB[4][:, s * 128:(s + 1) * 128], in_=On[:, s, :], identity=ident),
                             reads=[b_On, b_cst], writes=[bPB[4]])
                    P.op("act", lambda e: e.copy(out=OT[ob][:], in_=PB[4][:, :]), reads=[bPB[4]], writes=[b_OT[ob]])
                    P.dma(lambda e: e.dma_start(out=XR[kind][p, :, jl * 512:(jl + 1) * 512], in_=OT[ob][:]), reads=[b_OT[ob]], writes=[b_XR[kind]])

                for jl in range(4):
                    do_q(jl)
                P.barrier()

        def run_streams(gens):
            gens = list(gens)
            skew = dbg.get("skew", 25)
            for gi, g in enumerate(gens[:-1]):
                try:
                    for _ in range(skew * (len(gens) - 1 - gi)):
                        next(g)
                except StopIteration:
                    pass
            while gens:
                for g in list(gens):
                    try:
                        next(g)
                    except StopIteration:
                        gens.remove(g)

        class RwkvCtx:
            def __init__(self, st, kind, pi, slot, full):
                self.kind, self.pi, self.slot = kind, pi, slot
                self.T = TK[kind]
                self.paT = PA[kind][pi] if full else PAO[kind][pi]
                self.paL = PAL[kind] if full else PALO[kind]
                self.b_src = b_PA if full else b_PAO
                tg = "r%d_%d_%d" % (kind, pi, 1 if full else 0)
                self.bb = 4 * slot

                def F(name, shape, dt=F32):
                    return sb(st, tg + name, shape, dt)
                self.F = F
                self.pp = F("pp", [128, NPP]); self.b_pp = Buf()
                self.lw = F("lw", [128, 64]); self.la = F("la", [128, 64]); self.lg = F("lg", [128, 64])
                for (t_, d_) in ((self.pp, pp_d), (self.lw, lw_d), (self.la, la_d), (self.lg, lg_d)):
                    P.dma(lambda e, t_=t_, d_=d_: e.dma_start(out=t_[:], in_=d_[pi]), writes=[self.b_pp])
                WW = 512
                self.raw = {n: F("raw_" + n, [128, WW + 2]) for n in ("r", "k", "v", "wd", "ad")}
                self.b_raw = Buf()
                self.t1 = F("t1", [128, WW]); self.b_t1 = Buf()
                self.X = {n: F("X_" + n, [128, WW]) for n in ("r", "k", "v", "wd", "ad")}
                self.b_X = Buf()
                self.b_Xn = {n: Buf() for n in ("r", "k", "v", "wd", "ad")}
                names = ("a", "kd", "tmp") + (("sg", "cs", "x", "d1", "d2", "ginc", "ginv", "kapa") if full else ())
                self.E = {n: F("E_" + n, [128, WW]) for n in names}
                self.b_E = {n: Buf() for n in self.E}
                if full:
                    for n, r in (("kk", "k"), ("rn", "r"), ("kap", "v")):
                        self.E[n] = self.raw[r]
                        self.b_E[n] = self.b_raw

            def PBk(self, i):
                return PB[self.bb + i]

            def bPBk(self, i):
                return bPB[self.bb + i]

            def load_raw(self, W, f0, b0):
                for gi, n in enumerate(("r", "k", "v", "wd", "ad")):
                    srcT, r0 = (self.paT, 64 * gi) if gi < 3 else (self.paL, 64 * (gi - 3))
                    P.dma(lambda e, n=n, r0=r0, srcT=srcT: e.dma_start(out=self.raw[n][0:64, 0:W + 2], in_=srcT[r0:r0 + 64, f0:f0 + W + 2]), reads=[self.b_src], writes=[self.b_raw])
                    P.dma(lambda e, n=n, r0=r0, srcT=srcT: e.dma_start(out=self.raw[n][64:128, 0:W + 2], in_=srcT[r0:r0 + 64, b0:b0 + W + 2]), reads=[self.b_src], writes=[self.b_raw])

            def prep_common(self, W):
                raw, t1, X, E, pp, la = self.raw, self.t1, self.X, self.E, self.pp, self.la
                b_raw, b_t1, b_X, b_E, b_pp = self.b_raw, self.b_t1, self.b_X, self.b_E, self.b_pp
                for gi, n in enumerate(("r", "k", "v", "wd", "ad")):
                    c0 = 3 * gi
                    bx = self.b_Xn[n]
                    P.op("act", lambda e, n=n, c0=c0: e.activation(out=X[n][:, 0:W], in_=raw[n][:, 1:W + 1], func=AF.Identity, scale=pp[:, c0:c0 + 1]),
                         reads=[b_raw, b_pp], writes=[bx], after_readers=[b_X])
                    P.op("dve", lambda e, n=n, c0=c0: e.scalar_tensor_tensor(out=X[n][:, 0:W], in0=raw[n][:, 0:W], scalar=pp[:, c0 + 1:c0 + 2], op0=ALU.mult,
                                                                              in1=X[n][:, 0:W], op1=ALU.add), reads=[b_raw, b_pp, bx], writes=[bx])
                    P.op("dve", lambda e, n=n, c0=c0: e.scalar_tensor_tensor(out=X[n][:, 0:W], in0=raw[n][:, 2:W + 2], scalar=pp[:, c0 + 2:c0 + 3], op0=ALU.mult,
                                                                              in1=X[n][:, 0:W], op1=ALU.add), reads=[b_raw, b_pp, bx], writes=[bx, b_X])
                pb0, bpb0 = self.PBk(0), self.bPBk(0)
                for h in (0, 64):
                    P.op("pe", lambda e, h=h: e.matmul(out=pb0[h:h + 64, 0:W], lhsT=la[h:h + 64, :], rhs=X["ad"][h:h + 64, 0:W], start=True, stop=True),
                         reads=[b_pp, b_X], writes=[bpb0])
                P.op("act", lambda e: e.activation(out=E["a"][:, 0:W], in_=pb0[:, 0:W], func=AF.Sigmoid, bias=pp[:, 16:17]),
                     reads=[bpb0, b_pp], writes=[b_E["a"]])
                P.op("dve", lambda e: e.tensor_scalar(out=E["tmp"][:, 0:W], in0=E["a"][:, 0:W], scalar1=-1.0, scalar2=pp[:, 18:19], op0=ALU.add, op1=ALU.mult),
                     reads=[b_E["a"], b_pp], writes=[b_E["tmp"]])
                P.op("dve", lambda e: e.scalar_tensor_tensor(out=E["kd"][:, 0:W], in0=E["tmp"][:, 0:W], scalar=1.0, op0=ALU.add, in1=X["k"][:, 0:W], op1=ALU.mult),
                     reads=[b_E["tmp"], b_X], writes=[b_E["kd"]])

        def rwkv_steps(st, kind, pi, slot):
            C = RwkvCtx(st, kind, pi, slot, True)
            T, F, pp, lw = C.T, C.F, C.pp, C.lw
            raw, t1, X, E = C.raw, C.t1, C.X, C.E
            b_raw, b_t1, b_X, b_E, b_pp = C.b_raw, C.b_t1, C.b_X, C.b_E, C.b_pp
            PBk, bPBk = C.PBk, C.bPBk
            NCH = T // 64
            steps = []
            c = 0
            while c < NCH:
                nj = min(8, NCH - c)
                steps.append((c, nj))
                c += nj
            ST = F("ST", [128, 64]); b_ST = Buf()
            RST = F("RST", [128, 512])
            P.op("dve", lambda e: e.memset(ST[:], 0.0), writes=[b_ST])
            P.op("dve", lambda e: e.memset(RST[:], 1.0), writes=[b_pp])
            P.op("dve", lambda e: e.memset(RST[:].rearrange("p (j t) -> p j t", t=64)[:, :, 0:1], 0.0), writes=[b_pp])
            GE = F("GE", [128, 8]); b_GE = Buf()
            BK = F("BK", [128, 8, 2, 64]); b_BK = Buf()
            AR = F("AR", [128, 8, 2, 64]); b_AR = Buf()
            BKH = F("BKH", [128, 8, 2, 64]); b_BKH = Buf()
            VV = F("VV", [128, 8, 2, 64]); b_VV = Buf()
            MX = F("MX", [128, 16, 128]); b_MX = Buf()
            Nm = [F("Nm%d" % i, [128, 8, 64], BF16) for i in range(2)]; b_Nm = [Buf(), Buf()]
            Mm = [F("Mm%d" % i, [128, 8, 64], BF16) for i in range(2)]; b_Mm = [Buf(), Buf()]
            Rm = F("Rm", [128, 8, 64]); b_Rm = Buf()
            Rb = F("Rb", [128, 8, 64], BF16); b_Rb = Buf()
            BKHs = F("BKHs", [128, 16, 64]); b_BKHs = Buf()
            V2s = F("V2s", [128, 16, 64]); b_V2s = Buf()
            BVs = F("BVs", [128, 8, 64]); b_BVs = Buf()
            YVs = F("YVs", [128, 8, 64]); b_YVs = Buf()
            KVs = F("KVs", [128, 8, 64]); b_KVs = Buf()
            Ws = F("Ws", [128, 64]); b_Ws = Buf()
            Us = F("Us", [128, 64]); b_Us = Buf()
            ST2 = F("ST2", [128, 64]); b_ST2 = Buf()
            Yst = [F("Yst%d" % i, [128, 512]) for i in range(1)] * 2; b_Yst = [Buf()] * 2
            E["tw"], b_E["tw"] = t1, b_t1
            E["gexc"], b_E["gexc"] = E["d1"], b_E["d1"]
            E["gtail"], b_E["gtail"] = E["d2"], b_E["d2"]
            E["kk2"], b_E["kk2"] = E["tmp"], b_E["tmp"]
            pending = []
            yield

            def do_step(g):
                cf, nj = steps[g]
                W = nj * 64
                f0 = cf * 64
                b0 = T - f0 - W
                C.load_raw(W, f0, b0)
                while pending:
                    pending.pop(0)()
                C.prep_common(W)
                yield
                P.op("act", lambda e: e.activation(out=E["tw"][:, 0:W], in_=X["wd"][:, 0:W], func=AF.Tanh), reads=[b_X, b_t1], writes=[b_E["tw"]])
                for h in (0, 64):
                    P.op("pe", lambda e, h=h: e.matmul(out=PBk(1)[h:h + 64, 0:W], lhsT=lw[h:h + 64, :], rhs=E["tw"][h:h + 64, 0:W], start=True, stop=True),
                         reads=[b_pp, b_E["tw"]], writes=[bPBk(1)])
                P.op("act", lambda e: e.activation(out=E["sg"][:, 0:W], in_=PBk(1)[:, 0:W], func=AF.Sigmoid, bias=pp[:, 15:16]),
                     reads=[bPBk(1), b_pp], writes=[b_E["sg"]])
                P.op("dve", lambda e: e.tensor_tensor_scan(out=E["cs"][:, 0:W], data0=RST[:, 0:W], data1=E["sg"][:, 0:W], initial=0.0, op0=ALU.mult, op1=ALU.add),
                     reads=[b_E["sg"], b_pp], writes=[b_E["cs"]])
                cs3 = E["cs"][:, 0:W].rearrange("p (j t) -> p j t", t=64)
                ceb = cs3[:, :, 63:64].to_broadcast([128, nj, 64])
                x3 = E["x"][:, 0:W].rearrange("p (j t) -> p j t", t=64)
                P.op("dve", lambda e: e.tensor_copy(out=E["x"][0:64, 0:W], in_=E["cs"][0:64, 0:W]), reads=[b_E["cs"]], writes=[b_E["x"]])
                P.op("dve", lambda e: e.tensor_tensor(out=x3[64:128], in0=ceb[64:128], in1=cs3[64:128], op=ALU.subtract), reads=[b_E["cs"]], writes=[b_E["x"]])
                P.op("dve", lambda e: e.tensor_tensor(out=E["x"][64:128, 0:W], in0=E["x"][64:128, 0:W], in1=E["sg"][64:128, 0:W], op=ALU.add),
                     reads=[b_E["x"], b_E["sg"]], writes=[b_E["x"]])
                yield
                P.op("dve", lambda e: e.tensor_tensor(out=E["d1"][:, 0:W], in0=E["x"][:, 0:W], in1=E["sg"][:, 0:W], op=ALU.subtract),
                     reads=[b_E["x"], b_E["sg"]], writes=[b_E["d1"]])
                d23 = E["d2"][:, 0:W].rearrange("p (j t) -> p j t", t=64)
                P.op("dve", lambda e: e.tensor_tensor(out=d23, in0=ceb, in1=x3, op=ALU.subtract), reads=[b_E["cs"], b_E["x"]], writes=[b_E["d2"]])
                P.op("act", lambda e: e.activation(out=E["ginc"][:, 0:W], in_=E["x"][:, 0:W], func=AF.Exp, scale=-K0), reads=[b_E["x"]], writes=[b_E["ginc"]])
                P.op("act", lambda e: e.activation(out=E["ginv"][:, 0:W], in_=E["x"][:, 0:W], func=AF.Exp, scale=K0), reads=[b_E["x"]], writes=[b_E["ginv"]])
                P.op("act", lambda e: e.activation(out=E["gexc"][:, 0:W], in_=E["d1"][:, 0:W], func=AF.Exp, scale=-K0), reads=[b_E["d1"]], writes=[b_E["gexc"]])
                P.op("act", lambda e: e.activation(out=E["gtail"][:, 0:W], in_=E["d2"][:, 0:W], func=AF.Exp, scale=-K0), reads=[b_E["d2"]], writes=[b_E["gtail"]])
                P.op("act", lambda e: e.activation(out=GE[:, 0:nj], in_=cs3[:, :, 63], func=AF.Exp, scale=-K0), reads=[b_E["cs"]], writes=[b_GE])
                yield
                P.op("dve", lambda e: e.tensor_scalar(out=E["kk"][:, 0:W], in0=X["k"][:, 0:W], scalar1=pp[:, 17:18], scalar2=None, op0=ALU.mult),
                     reads=[b_X, b_pp], writes=[b_E["kk"]])
                P.op("act", lambda e: e.activation(out=E["kk2"][:, 0:W], in_=E["kk"][:, 0:W], func=AF.Square), reads=[b_E["kk"], b_E["tmp"]], writes=[b_E["kk2"]])
                P.op("pe", lambda e: e.matmul(out=PBk(2)[:, 0:W], lhsT=BLK, rhs=E["kk2"][:, 0:W], start=True, stop=True), reads=[b_cst, b_E["kk2"]], writes=[bPBk(2)])
                P.op("dve", lambda e: e.tensor_scalar(out=E["rn"][:, 0:W], in0=PBk(2)[:, 0:W], scalar1=1e-24, scalar2=None, op0=ALU.max),
                     reads=[bPBk(2)], writes=[b_E["rn"]])
                P.op("act", lambda e: e.activation(out=E["rn"][:, 0:W], in_=E["rn"][:, 0:W], func=AF.Sqrt), reads=[b_E["rn"]], writes=[b_E["rn"]])
                P.op("dve", lambda e: e.reciprocal(out=E["rn"][:, 0:W], in_=E["rn"][:, 0:W]), reads=[b_E["rn"]], writes=[b_E["rn"]])
                P.op("dve", lambda e: e.tensor_tensor(out=E["kap"][:, 0:W], in0=E["kk"][:, 0:W], in1=E["rn"][:, 0:W], op=ALU.mult),
                     reads=[b_E["kk"], b_E["rn"]], writes=[b_E["kap"]])
                P.op("dve", lambda e: e.tensor_tensor(out=E["kapa"][:, 0:W], in0=E["kap"][:, 0:W], in1=E["a"][:, 0:W], op=ALU.mult),
                     reads=[b_E["kap"], b_E["a"]], writes=[b_E["kapa"]])
                yield

                def v4(tns, slot_):
                    return tns[:, 0:nj, slot_, :]

                def e3(n):
                    return E[n][:, 0:W].rearrange("p (j t) -> p j t", t=64)
                x3r = X["r"][:, 0:W].rearrange("p (j t) -> p j t", t=64)
                x3v = X["v"][:, 0:W].rearrange("p (j t) -> p j t", t=64)
                P.op("dve", lambda e: e.scalar_tensor_tensor(out=v4(AR, 0), in0=e3("kap"), scalar=-1.0, op0=ALU.mult, in1=e3("gexc"), op1=ALU.mult),
                     reads=[b_E["kap"], b_E["gexc"]], writes=[b_AR])
                P.op("dve", lambda e: e.tensor_tensor(out=v4(AR, 1), in0=x3r, in1=e3("ginc"), op=ALU.mult), reads=[b_X, b_E["ginc"]], writes=[b_AR])

                def halves(tns, n_beta, n_k, g_, b_dst, eng):
                    for (h, sb_, sk_) in ((0, 0, 1), (64, 1, 0)):
                        P.op(eng, lambda e, h=h, sb_=sb_: e.tensor_tensor(out=tns[h:h + 64, 0:nj, sb_, :], in0=e3(n_beta)[h:h + 64], in1=e3(g_)[h:h + 64], op=ALU.mult),
                             reads=[b_E[n_beta], b_E[g_]], writes=[b_dst])
                        P.op(eng, lambda e, h=h, sk_=sk_: e.tensor_tensor(out=tns[h:h + 64, 0:nj, sk_, :], in0=e3(n_k)[h:h + 64], in1=e3(g_)[h:h + 64], op=ALU.mult),
                             reads=[b_E[n_k], b_E[g_]], writes=[b_dst])
                halves(BK, "kapa", "kd", "ginv", b_BK, "dve")
                halves(BKH, "kapa", "kd", "gtail", b_BKH, "pool")
                P.op("pool", lambda e: e.tensor_copy(out=v4(VV, 0), in_=x3v), reads=[b_X], writes=[b_VV])
                P.op("pool", lambda e: e.tensor_copy(out=v4(VV, 1), in_=x3v), reads=[b_X], writes=[b_VV])
                yield

                def uidx(d, j):
                    return d * 8 + j
                SBm = {0: 0, 1: 1}
                for d in (0, 1):
                    h = 64 * d
                    msk = (MF if d == 0 else MB)
                    for j0 in range(0, nj, 4):
                        n_in = min(4, nj - j0)
                        bk = 2 * d + (j0 // 4) % 2
                        for jo in range(n_in):
                            j = j0 + jo
                            off = jo * 128
                            lhs = BK[h:h + 64, j, :, :].rearrange("p a t -> p (a t)")
                            rhs = AR[h:h + 64, j, :, :].rearrange("p a t -> p (a t)")
                            P.op("pe", lambda e, bk=bk, off=off, lhs=lhs, rhs=rhs: e.matmul(out=PBk(bk)[:, off:off + 128], lhsT=lhs, rhs=rhs, start=True, stop=True),
                                 reads=[b_BK, b_AR], writes=[bPBk(bk)])
                        u0 = uidx(d, j0)
                        P.op("dve", lambda e, bk=bk, n_in=n_in, u0=u0, msk=msk: e.tensor_tensor(
                            out=MX[:, u0:u0 + n_in, :], in0=PBk(bk)[:, 0:n_in * 128].rearrange("p (u t) -> p u t", t=128),
                            in1=msk.unsqueeze(1).to_broadcast([128, n_in, 128]), op=ALU.mult), reads=[bPBk(bk), b_cst], writes=[b_MX])
                    yield
                for d in (0, 1):
                    h = 64 * d
                    for j in range(nj):
                        lhs = AR[h:h + 64, j, 0, :]
                        rhs = BK[h:h + 64, j, SBm[d], :]
                        P.op("pe", lambda e, j=j, h=h, lhs=lhs, rhs=rhs: e.matmul(out=PBk(0)[h:h + 64, j * 64:(j + 1) * 64], lhsT=lhs, rhs=rhs, start=True, stop=True),
                             reads=[b_AR, b_BK], writes=[bPBk(0)])
                P.op("dve", lambda e: e.tensor_tensor(
                    out=Nm[0][:, 0:nj, :], in0=PBk(0)[:, 0:nj * 64].rearrange("p (u t) -> p u t", t=64),
                    in1=cst[:, 384:448].unsqueeze(1).to_broadcast([128, nj, 64]), op=ALU.mult), reads=[bPBk(0), b_cst], writes=[b_Nm[0]])
                for d in (0, 1):
                    h = 64 * d
                    P.op("dve", lambda e, d=d, h=h: e.tensor_tensor(out=Rm[h:h + 64, 0:nj, :], in0=MX[h:h + 64, d * 8:d * 8 + nj, 0:64],
                                                                    in1=cst[h:h + 64, h:h + 64].unsqueeze(1).to_broadcast([64, nj, 64]), op=ALU.add),
                         reads=[b_MX, b_cst], writes=[b_Rm])
                    P.op("pool", lambda e, d=d, h=h: e.tensor_copy(out=Mm[0][h:h + 64, 0:nj, :], in_=MX[h:h + 64, d * 8:d * 8 + nj, 0:64]), reads=[b_MX], writes=[b_Mm[0]])
                    P.op("act", lambda e, h=h: e.copy(out=Rb[h:h + 64, 0:nj, :], in_=Rm[h:h + 64, 0:nj, :]), reads=[b_Rm], writes=[b_Rb])
                yield

                def prod(bank0, lhs_fn, rhs_fn, reads, evac, rev=False):
                    bk = bank0 // 2
                    for d in (0, 1):
                        h = 64 * d
                        for j in range(nj):
                            lhs = lhs_fn(d, h, j)
                            rhs = rhs_fn(d, h, j)
                            jp = (nj - 1 - j) if (rev and d == 1) else j
                            P.op("pe", lambda e, jp=jp, h=h, lhs=lhs, rhs=rhs: e.matmul(out=PBk(bk)[h:h + 64, jp * 64:(jp + 1) * 64], lhsT=lhs, rhs=rhs, start=True, stop=True),
                                 reads=reads, writes=[bPBk(bk)])
                    evac(bk, PBk(bk)[:, 0:nj * 64].rearrange("p (u t) -> p u t", t=64))

                def ev_copy(dst_t, b_dst, eng):
                    def f(bk, pv):
                        dst = dst_t[:, 0:nj, :]
                        if eng == "act":
                            P.op("act", lambda e: e.copy(out=dst, in_=pv), reads=[bPBk(bk)], writes=[b_dst])
                        else:
                            P.op("dve", lambda e: e.tensor_copy(out=dst, in_=pv), reads=[bPBk(bk)], writes=[b_dst])
                    return f

                def ev_acc(dst_t, b_dst):
                    def f(bk, pv):
                        dst = dst_t[:, 0:nj, :]
                        P.op("dve", lambda e: e.tensor_tensor(out=dst, in0=dst, in1=pv, op=ALU.add), reads=[bPBk(bk), b_dst], writes=[b_dst])
                    return f

                cur = 0
                for k in range(1, 6):
                    nxt = 1 - cur
                    Nc, Mc, Nn, Mn = Nm[cur], Mm[cur], Nm[nxt], Mm[nxt]
                    if k <= 4:
                        prod(0, lambda d, h, j, Nc=Nc: Nc[h:h + 64, j, :], lambda d, h, j, Mc=Mc: Mc[h:h + 64, j, :], [b_Nm[cur], b_Mm[cur]], ev_copy(Mn, b_Mm[nxt], "act"))
                    prod(2, lambda d, h, j, Mc=Mc: Mc[h:h + 64, j, :], lambda d, h, j, Nc=Nc: Nc[h:h + 64, j, :], [b_Mm[cur], b_Nm[cur]], ev_copy(Nn, b_Nm[nxt], "dve"))
                    yield
                    prod(4, lambda d, h, j, Nn=Nn: Nn[h:h + 64, j, :], lambda d, h, j: Rb[h:h + 64, j, :], [b_Nm[nxt], b_Rb], ev_acc(Rm, b_Rm))
                    if k < 5:
                        P.op("act", lambda e: e.copy(out=Rb[:, 0:nj, :], in_=Rm[:, 0:nj, :]), reads=[b_Rm], writes=[b_Rb])
                    yield
                    cur = nxt
                for (srct, b_src, dst, b_dst, bks) in ((BKH, b_BKH, BKHs, b_BKHs, (0, 1)), (VV, b_VV, V2s, b_V2s, (2, 3))):
                    for d in (0, 1):
                        h = 64 * d
                        bk = bks[d]
                        for j in range(nj):
                            P.op("pe", lambda e, h=h, j=j, bk=bk, srct=srct: e.transpose(out=PBk(bk)[:, j * 64:(j + 1) * 64], in_=srct[h:h + 64, j, :, :].rearrange("p a t -> p (a t)"),
                                                                                        identity=cst[h:h + 64, h:h + 64]),
                                 reads=[b_src, b_cst], writes=[bPBk(bk)])
                        P.op("act", lambda e, d=d, bk=bk, dst=dst: e.copy(out=dst[:, d * 8:d * 8 + nj, :], in_=PBk(bk)[:, 0:nj * 64].rearrange("p (u t) -> p u t", t=64)),
                             reads=[bPBk(bk)], writes=[b_dst])
                    yield

                def hv_of(h):
                    return 64 - h
                prod(0, lambda d, h, j: MX[hv_of(h):hv_of(h) + 64, uidx(d, j), 0:64], lambda d, h, j: V2s[hv_of(h):hv_of(h) + 64, uidx(d, j), :],
                     [b_MX, b_V2s], ev_copy(BVs, b_BVs, "act"), rev=True)
                prod(2, lambda d, h, j: V2s[hv_of(h):hv_of(h) + 64, uidx(d, j), :], lambda d, h, j: MX[hv_of(h):hv_of(h) + 64, uidx(d, j), 64:128],
                     [b_MX, b_V2s], ev_copy(YVs, b_YVs, "dve"))
                yield
                prod(4, lambda d, h, j: BKHs[hv_of(h):hv_of(h) + 64, uidx(d, j), :], lambda d, h, j: V2s[hv_of(h):hv_of(h) + 64, uidx(d, j), :],
                     [b_BKHs, b_V2s], ev_copy(KVs, b_KVs, "act"), rev=True)
                yield
                for i in range(nj):
                    jj = {0: i, 1: nj - 1 - i}
                    for d in (0, 1):
                        h = 64 * d
                        j = jj[d]
                        P.op("pe", lambda e, h=h, j=j: e.matmul(out=PBk(0)[h:h + 64, 0:64], lhsT=AR[h:h + 64, j, 0, :], rhs=ST[h:h + 64, :], start=True, stop=True),
                             reads=[b_AR, b_ST], writes=[bPBk(0)])
                    P.op("dve", lambda e, i=i: e.tensor_tensor(out=Ws[:, :], in0=PBk(0)[:, 0:64], in1=BVs[:, i, :], op=ALU.add),
                         reads=[bPBk(0), b_BVs], writes=[b_Ws])
                    for d in (0, 1):
                        h = 64 * d
                        j = jj[d]
                        P.op("dve", lambda e, h=h, j=j, i=i: e.scalar_tensor_tensor(out=ST2[h:h + 64, :], in0=ST[h:h + 64, :], scalar=GE[h:h + 64, j:j + 1], op0=ALU.mult,
                                                                                  in1=KVs[h:h + 64, i, :], op1=ALU.add), reads=[b_ST, b_GE, b_KVs], writes=[b_ST2])
                    yield
                    for d in (0, 1):
                        h = 64 * d
                        j = jj[d]
                        P.op("pe", lambda e, h=h, j=j: e.matmul(out=PBk(0)[h:h + 64, 64:128], lhsT=Rm[h:h + 64, j, :], rhs=Ws[h:h + 64, :], start=True, stop=True),
                             reads=[b_Rm, b_Ws], writes=[bPBk(0)])
                    P.op("act", lambda e: e.copy(out=Us[:, :], in_=PBk(0)[:, 64:128]), reads=[bPBk(0)], writes=[b_Us])
                    yield
                    for d in (0, 1):
                        h = 64 * d
                        j = jj[d]
                        u = uidx(d, j)
                        P.op("pe", lambda e, h=h, u=u: e.matmul(out=PBk(0)[h:h + 64, 128:192], lhsT=BKHs[h:h + 64, u, :], rhs=Us[h:h + 64, :], start=True, stop=True),
                             reads=[b_BKHs, b_Us], writes=[bPBk(0)])
                        P.op("pe", lambda e, h=h, j=j, d=d: e.matmul(out=PBk(2 + d)[h:h + 64, j * 64:(j + 1) * 64], lhsT=ST[h:h + 64, :], rhs=AR[h:h + 64, j, 1, :], start=True, stop=False),
                             reads=[b_ST, b_AR], writes=[bPBk(2 + d)])
                        P.op("pe", lambda e, h=h, j=j, u=u, d=d: e.matmul(out=PBk(2 + d)[h:h + 64, j * 64:(j + 1) * 64], lhsT=Us[h:h + 64, :], rhs=MX[h:h + 64, u, 64:128], start=False, stop=True),
                             reads=[b_Us, b_MX], writes=[bPBk(2 + d)])
                    P.op("dve", lambda e: e.tensor_tensor(out=ST[:, :], in0=ST2[:, :], in1=PBk(0)[:, 128:192], op=ALU.add),
                         reads=[b_ST2, bPBk(0)], writes=[b_ST])
                    yield
                ys = g % 2
                for d in (0, 1):
                    h = 64 * d
                    P.op("dve", lambda e, h=h, d=d: e.tensor_tensor(out=Yst[ys][h:h + 64, 0:W].rearrange("p (j t) -> p j t", t=64),
                                                                    in0=PBk(2 + d)[h:h + 64, 0:W].rearrange("p (j t) -> p j t", t=64),
                                                                    in1=YVs[h:h + 64, 0:nj, :], op=ALU.add), reads=[bPBk(2 + d), b_YVs], writes=[b_Yst[ys]])
                def flush(ys=ys, f0=f0, b0=b0, W=W):
                    P.dma(lambda e: e.dma_start(out=YD[kind][pi, 0:64, f0:f0 + W], in_=Yst[ys][0:64, 0:W]), reads=[b_Yst[ys]], writes=[b_YD])
                    P.dma(lambda e: e.dma_start(out=YD[kind][pi, 64:128, b0:b0 + W], in_=Yst[ys][64:128, 0:W]), reads=[b_Yst[ys]], writes=[b_YD])
                pending.append(flush)
                yield

            for g in range(len(steps) if dbg.get("rwkv_steps") is None else dbg["rwkv_steps"]):
                yield from do_step(g)
            while pending:
                pending.pop(0)()

        def rwkv_stage_e(st, kind, pi, slot):
            C = RwkvCtx(st, kind, pi, slot, False)
            T, F, pp, lg = C.T, C.F, C.pp, C.lg
            X, E, t1 = C.X, C.E, C.t1
            b_X, b_E, b_pp, b_t1 = C.b_X, C.b_E, C.b_pp, C.b_t1
            PBk, bPBk = C.PBk, C.bPBk
            WW = 512
            rawg = F("rawg", [128, WW + 2]); b_rawg = Buf()
            Xg = F("Xg", [128, WW]); b_Xg = Buf()
            Yt = F("Yt", [128, WW]); b_Yt = Buf()
            Ys = F("Ysum", [64, WW]); b_Ys = Buf()
            Dd = F("Dd", [64, WW]); b_Dd = Buf()
            D2 = F("D2", [64, WW]); b_D2 = Buf()
            Rs = F("Rs", [64, WW]); b_Rs = Buf()
            Oa = [F("Oa%d" % i, [64, WW], BF16) for i in range(2)]; b_Oa = [Buf(), Buf()]
            tiles = [(t0, 512) for t0 in range(0, 2048, 512)]
            pending = []
            yield

            def do_tile(ti, t0, W):
                C.load_raw(W, t0, t0)
                P.dma(lambda e: e.dma_start(out=rawg[:, 0:W + 2], in_=C.paL[128:256, t0:t0 + W + 2]), reads=[b_PAO], writes=[b_rawg])
                P.dma(lambda e: e.dma_start(out=Yt[:, 0:W], in_=YDO[kind][pi, :, t0:t0 + W]), reads=[b_YDO], writes=[b_Yt])
                while pending:
                    pending.pop(0)()
                C.prep_common(W)
                yield
                P.op("act", lambda e: e.activation(out=t1[:, 0:W], in_=rawg[:, 1:W + 1], func=AF.Identity, scale=pp[:, 22:23]), reads=[b_rawg, b_pp], writes=[b_t1])
                P.op("dve", lambda e: e.scalar_tensor_tensor(out=t1[:, 0:W], in0=rawg[:, 0:W], scalar=pp[:, 23:24], op0=ALU.mult, in1=t1[:, 0:W], op1=ALU.add),
                     reads=[b_rawg, b_pp, b_t1], writes=[b_t1])
                P.op("dve", lambda e: e.scalar_tensor_tensor(out=Xg[:, 0:W], in0=rawg[:, 2:W + 2], scalar=pp[:, 24:25], op0=ALU.mult, in1=t1[:, 0:W], op1=ALU.add),
                     reads=[b_rawg, b_pp, b_t1], writes=[b_Xg])
                P.op("act", lambda e: e.activation(out=Xg[:, 0:W], in_=Xg[:, 0:W], func=AF.Sigmoid), reads=[b_Xg], writes=[b_Xg])
                P.op("dve", lambda e: e.scalar_tensor_tensor(out=E["tmp"][:, 0:W], in0=X["r"][:, 0:W], scalar=pp[:, 19:20], op0=ALU.mult, in1=E["kd"][:, 0:W], op1=ALU.mult),
                     reads=[b_X, b_pp, b_E["kd"]], writes=[b_E["tmp"]])
                P.op("pe", lambda e: e.matmul(out=PBk(1)[0:64, 0:W], lhsT=ONES[:, 0:64], rhs=E["tmp"][:, 0:W], start=True, stop=True), reads=[b_cst, b_E["tmp"]], writes=[bPBk(1)])
                P.op("pe", lambda e: e.matmul(out=PBk(2)[0:64, 0:W], lhsT=SEL, rhs=Yt[:, 0:W], start=True, stop=True), reads=[b_cst, b_Yt], writes=[bPBk(2)])
                P.op("act", lambda e: e.copy(out=Ys[:, 0:W], in_=PBk(2)[0:64, 0:W]), reads=[bPBk(2)], writes=[b_Ys])
                yield
                P.op("pe", lambda e: e.matmul(out=PBk(3)[0:64, 0:W], lhsT=O64, rhs=Ys[:, 0:W], start=True, stop=True), reads=[b_cst, b_Ys], writes=[bPBk(3)])
                P.op("dve", lambda e: e.tensor_tensor(out=Dd[:, 0:W], in0=Ys[:, 0:W], in1=PBk(3)[0:64, 0:W], op=ALU.subtract), reads=[b_Ys, bPBk(3)], writes=[b_Dd])
                P.op("act", lambda e: e.activation(out=D2[:, 0:W], in_=Dd[:, 0:W], func=AF.Square), reads=[b_Dd], writes=[b_D2])
                P.op("pe", lambda e: e.matmul(out=PBk(3)[0:64, 0:W], lhsT=O64, rhs=D2[:, 0:W], start=True, stop=True), reads=[b_cst, b_D2], writes=[bPBk(3)])
                P.op("act", lambda e: e.activation(out=Rs[:, 0:W], in_=PBk(3)[0:64, 0:W], func=AF.Sqrt, bias=epsb[0:64, 2:3]), reads=[bPBk(3), b_cst], writes=[b_Rs])
                yield
                P.op("dve", lambda e: e.reciprocal(out=Rs[:, 0:W], in_=Rs[:, 0:W]), reads=[b_Rs], writes=[b_Rs])
                P.op("dve", lambda e: e.tensor_tensor(out=Dd[:, 0:W], in0=Dd[:, 0:W], in1=Rs[:, 0:W], op=ALU.mult), reads=[b_Dd, b_Rs], writes=[b_Dd])
                P.op("dve", lambda e: e.tensor_scalar(out=Dd[:, 0:W], in0=Dd[:, 0:W], scalar1=pp[0:64, 20:21], scalar2=pp[0:64, 21:22], op0=ALU.mult, op1=ALU.add),
                     reads=[b_Dd, b_pp], writes=[b_Dd])
                P.op("dve", lambda e: e.tensor_tensor(out=D2[:, 0:W], in0=PBk(1)[0:64, 0:W], in1=X["v"][0:64, 0:W], op=ALU.mult), reads=[bPBk(1), b_X], writes=[b_D2])
                P.op("dve", lambda e: e.tensor_tensor(out=Dd[:, 0:W], in0=Dd[:, 0:W], in1=D2[:, 0:W], op=ALU.add), reads=[b_Dd, b_D2], writes=[b_Dd])
                P.op("pe", lambda e: e.matmul(out=PBk(0)[0:64, 0:W], lhsT=lg[:, :], rhs=Xg[:, 0:W], start=True, stop=True), reads=[b_pp, b_Xg], writes=[bPBk(0)])
                ob = ti % 2
                P.op("dve", lambda e: e.tensor_tensor(out=Oa[ob][:, 0:W], in0=Dd[:, 0:W], in1=PBk(0)[0:64, 0:W], op=ALU.mult), reads=[b_Dd, bPBk(0)], writes=[b_Oa[ob]])
                def flush(ob=ob, t0=t0, W=W):
                    P.dma(lambda e: e.dma_start(out=OAO[kind][pi, :, t0:t0 + W], in_=Oa[ob][:, 0:W]), reads=[b_Oa[ob]], writes=[b_OAO])
                pending.append(flush)
                yield

            if not dbg.get("skip_stage_e"):
                for ti, (t0, W) in enumerate(tiles):
                    yield from do_tile(ti, t0, W)
                while pending:
                    pending.pop(0)()

        def rwkv_group(kind, plist, stage):
            with ExitStack() as st:
                if stage == 0:
                    run_streams([rwkv_steps(st, kind, p, i) for i, p in enumerate(plist)])
                else:
                    run_streams([rwkv_stage_e(st, kind, p, i) for i, p in enumerate(plist)])
                P.barrier()

        kinds = dbg.get("kinds", [0, 1])
        for kind in kinds:
            if do_p1:
                p1_phase(kind)
                rotate_phase(kind)
            if do_attn:
                for p in pieces_run:
                    attn_job(kind, p)
            if do_rwkv:
                ns = dbg.get("streams", 2)
                for i in range(0, len(pieces_run), ns):
                    rwkv_group(kind, pieces_run[i:i + ns], 0)
                P.barrier()
                own_y_phase(kind)
                for i in range(0, len(pieces_run), ns):
                    rwkv_group(kind, pieces_run[i:i + ns], 1)
            P.barrier()

        def post_phase(kind, xsrc, x_row0, yout, tg):
            with ExitStack() as st:
                def F(name, shape, dt=F32):
                    return sb(st, tg + name, shape, dt)
                wg = F("wg", [128, 8, 2048], BF16); wua = F("wua", [128, 4, D], BF16); wub = F("wub", [128, 4, D], BF16)
                wo = [F("wo%d" % i, [128, 8, 128], BF16) for i in range(2)]; b_wo = [Buf(), Buf()]
                b_w = Buf()
                P.dma(lambda e: e.dma_start(out=wg[:], in_=wg_b.rearrange("(k p) m -> p k m", p=128)), reads=[b_wsc], writes=[b_w])
                P.dma(lambda e: e.dma_start(out=wua[:], in_=wua_b.rearrange("(k p) m -> p k m", p=128)), reads=[b_wsc], writes=[b_w])
                P.dma(lambda e: e.dma_start(out=wub[:], in_=wub_b.rearrange("(k p) m -> p k m", p=128)), reads=[b_wsc], writes=[b_w])
                w1 = [F("w1_%d" % i, [128, 8, 256], BF16) for i in range(2)]; b_w1 = [Buf(), Buf()]
                w2 = [F("w2_%d" % i, [128, 32, 128], BF16) for i in range(2)]; b_w2 = [Buf(), Buf()]
                gfin = F("gfin", [128, D]); slg = F("slg", [128, 1]); lamt = F("lamt", [128, 4, 64]); lamv = F("lamv", [128, 8])
                b_c = Buf()
                P.dma(lambda e: e.dma_start(out=gfin[:], in_=gfin_d), writes=[b_c])
                P.dma(lambda e: e.dma_start(out=slg[:], in_=slg_d), writes=[b_c])
                P.dma(lambda e: e.dma_start(out=lamt[:], in_=lam_d), writes=[b_c])
                P.op("dve", lambda e: e.tensor_tensor(out=lamt[:, 0, :], in0=lamt[:, 0, :], in1=lamt[:, 1, :], op=ALU.mult), reads=[b_c], writes=[b_c])
                P.op("dve", lambda e: e.tensor_tensor(out=lamt[:, 2, :], in0=lamt[:, 2, :], in1=lamt[:, 3, :], op=ALU.mult), reads=[b_c], writes=[b_c])
                P.op("dve", lambda e: e.tensor_reduce(out=lamv[:, 0:1], in_=lamt[:, 0, :], op=ALU.add, axis=mybir.AxisListType.X), reads=[b_c], writes=[b_c])
                P.op("dve", lambda e: e.tensor_reduce(out=lamv[:, 1:2], in_=lamt[:, 2, :], op=ALU.add, axis=mybir.AxisListType.X), reads=[b_c], writes=[b_c])
                P.op("act", lambda e: e.activation(out=lamv[:, 2:4], in_=lamv[:, 0:2], func=AF.Exp), reads=[b_c], writes=[b_c])
                P.op("dve", lambda e: e.tensor_tensor(out=lamv[:, 4:5], in0=lamv[:, 3:4], in1=lamv[:, 2:3], op=ALU.subtract), reads=[b_c], writes=[b_c])
                P.op("dve", lambda e: e.tensor_scalar(out=lamv[:, 4:5], in0=lamv[:, 4:5], scalar1=-LAMBDA_INIT, scalar2=None, op0=ALU.add), reads=[b_c], writes=[b_c])
                P.op("dve", lambda e: e.tensor_scalar(out=slg[:], in0=slg[:], scalar1=1.0 - LAMBDA_INIT, scalar2=None, op0=ALU.mult), reads=[b_c], writes=[b_c])
                xres = F("xres", [128, 4, D]); b_xres = Buf()
                h2 = F("h2", [128, 4, D]); b_h2 = Buf()
                sq_sh = F("sqsh", [128, D])
                tmp = [(sq_sh, F("ss%d" % i, [128, 4]), F("xn%d" % i, [128, D], BF16), Buf()) for i in range(2)]
                nT = F("nT", [128, 8, 512], BF16); b_nT = Buf()
                oaT = F("oaT", [128, 4, 512], BF16); b_oaT = Buf()
                AB = F("AB", [128, 2, 512], BF16); b_AB = Buf()
                Dh = F("Dh", [128, 512]); b_Dh = Buf()
                D2h = F("D2h", [128, 512]); b_D2h = Buf()
                rsh = F("rsh", [128, 512]); b_rsh = Buf()
                obT = F("obT", [128, 4, 512], BF16); b_obT = Buf()
                sg = [F("sg%d" % i, [128, 512]) for i in range(2)]; b_sg = [Buf(), Buf()]
                ma = F("ma", [128, 512]); b_ma = Buf()
                mT = F("mT", [128, 8, 512], BF16); b_mT = Buf()
                ao = [F("ao%d" % i, [128, 512]) for i in range(2)]; b_ao = [Buf(), Buf()]
                hT = F("hT", [128, 32, 512], BF16); b_hT = Buf()
                hx = [F("hx%d" % i, [128, 512]) for i in range(2)]; b_hx = [Buf(), Buf()]
                yo = xres; b_yo = b_xres
                sqf = tmp[0][0]; ssf = F("ssf", [128, 4]); b_f = tmp[0][3]
                wcnt = {"w1": 0, "w2": 0}

                def do_tt(tt):
                    tk0 = tt * 512
                    for s in range(4):
                        P.dma(lambda e, s=s: e.dma_start(out=xres[:, s, :], in_=xsrc[x_row0 + tk0 + s * 128:x_row0 + tk0 + (s + 1) * 128, :]), writes=[b_xres])
                    for q in range(4):
                        for hh in range(2):
                            P.dma(lambda e, q=q, hh=hh: e.dma_start(out=oaT[hh * 64:(hh + 1) * 64, q, :], in_=OAO[kind][2 * q + hh, :, tk0:tk0 + 512]),
                                  reads=[b_OAO], writes=[b_oaT])
                    for s in range(4):
                        norm_T2(tmp[s % 2], xres[:, s, :], b_xres, nT[:, :, s * 128:(s + 1) * 128], b_nT, s % 2)
                    for hh in range(4):
                        for mm in range(2):
                            P.dma(lambda e, hh=hh, mm=mm: e.dma_start(out=AB[:, mm, :], in_=XR[kind][2 * hh + mm, :, tk0:tk0 + 512]), reads=[b_XR[kind]], writes=[b_AB])
                        P.op("dve", lambda e, hh=hh: e.scalar_tensor_tensor(out=Dh[:], in0=AB[:, 1, :], scalar=lamv[:, 4:5], op0=ALU.mult, in1=AB[:, 0, :], op1=ALU.add),
                             reads=[b_AB, b_c], writes=[b_Dh])
                        P.op("act", lambda e: e.activation(out=D2h[:], in_=Dh[:], func=AF.Square), reads=[b_Dh], writes=[b_D2h])
                        P.op("pe", lambda e: e.matmul(out=PB[2][:, :], lhsT=O128, rhs=D2h[:], start=True, stop=True), reads=[b_cst, b_D2h], writes=[bPB[2]])
                        P.op("act", lambda e: e.activation(out=rsh[:], in_=PB[2][:, :], func=AF.Sqrt, bias=epsb[:, 1:2]), reads=[bPB[2], b_cst], writes=[b_rsh])
                        P.op("dve", lambda e: e.reciprocal(out=rsh[:], in_=rsh[:]), reads=[b_rsh], writes=[b_rsh])
                        P.op("dve", lambda e, hh=hh: e.scalar_tensor_tensor(out=obT[:, hh, :], in0=Dh[:], scalar=slg[:, 0:1], op0=ALU.mult, in1=rsh[:], op1=ALU.mult),
                             reads=[b_Dh, b_rsh, b_c], writes=[b_obT])
                    for mo in range(8):
                        for br, (wu, src, b_src) in enumerate(((wua, oaT, b_oaT), (wub, obT, b_obT))):
                            gb = 3 + br
                            ub = 5 + br
                            for k in range(8):
                                P.op("pe", lambda e, k=k, mo=mo, br=br, gb=gb: e.matmul(out=PB[gb][:, :], lhsT=wg[:, k, br * 1024 + mo * 128:br * 1024 + (mo + 1) * 128], rhs=nT[:, k, :],
                                                                                        start=(k == 0), stop=(k == 7)), reads=[b_w, b_nT], writes=[bPB[gb]])
                            P.op("act", lambda e, br=br, gb=gb: e.activation(out=sg[br][:], in_=PB[gb][:, :], func=AF.Sigmoid), reads=[bPB[gb]], writes=[b_sg[br]])
                            for k in range(4):
                                P.op("pe", lambda e, k=k, mo=mo, wu=wu, src=src, ub=ub: e.matmul(out=PB[ub][:, :], lhsT=wu[:, k, mo * 128:(mo + 1) * 128], rhs=src[:, k, :],
                                                                                                 start=(k == 0), stop=(k == 3)), reads=[b_w, b_src], writes=[bPB[ub]])
                        P.op("dve", lambda e: e.tensor_tensor(out=ma[:], in0=sg[0][:], in1=PB[5][:, :], op=ALU.mult), reads=[b_sg[0], bPB[5]], writes=[b_ma])
                        P.op("dve", lambda e: e.tensor_tensor(out=sg[1][:], in0=sg[1][:], in1=PB[6][:, :], op=ALU.mult), reads=[b_sg[1], bPB[6]], writes=[b_sg[1]])
                        P.op("pool", lambda e, mo=mo: e.tensor_tensor(out=mT[:, mo, :], in0=ma[:], in1=sg[1][:], op=ALU.add), reads=[b_ma, b_sg[1]], writes=[b_mT])
                    for mo in range(8):
                        ab = mo % 2
                        P.dma(lambda e, ab=ab, mo=mo: e.dma_start(out=wo[ab][:], in_=wout_b[:, mo * 128:(mo + 1) * 128].rearrange("(k p) m -> p k m", p=128)),
                              reads=[b_wsc], writes=[b_wo[ab]])
                        for k in range(8):
                            P.op("pe", lambda e, k=k, ab=ab: e.matmul(out=PB[3][:, :], lhsT=wo[ab][:, k, :], rhs=mT[:, k, :], start=(k == 0), stop=(k == 7)),
                                 reads=[b_wo[ab], b_mT], writes=[bPB[3]])
                        P.op("act", lambda e, ab=ab: e.copy(out=ao[ab][:], in_=PB[3][:, :]), reads=[bPB[3]], writes=[b_ao[ab]])
                        for s in range(4):
                            P.op("pe", lambda e, s=s, ab=ab: e.transpose(out=PB[4][:, s * 128:(s + 1) * 128], in_=ao[ab][:, s * 128:(s + 1) * 128], identity=ident),
                                 reads=[b_ao[ab], b_cst], writes=[bPB[4]])
                        P.op("dve", lambda e, mo=mo: e.tensor_tensor(out=h2[:, :, mo * 128:(mo + 1) * 128], in0=xres[:, :, mo * 128:(mo + 1) * 128],
                                                                     in1=PB[4][:, :].rearrange("p (s f) -> p s f", s=4), op=ALU.add), reads=[b_xres, bPB[4]], writes=[b_h2])
                    for s in range(4):
                        norm_T2(tmp[s % 2], h2[:, s, :], b_h2, nT[:, :, s * 128:(s + 1) * 128], b_nT, s % 2)
                    for mg in range(16):
                        wb = wcnt["w1"] % 2
                        wcnt["w1"] += 1
                        P.dma(lambda e, wb=wb, mg=mg: e.dma_start(out=w1[wb][:], in_=wf1_b[:, mg * 256:(mg + 1) * 256].rearrange("(k p) m -> p k m", p=128)),
                              reads=[b_wsc], writes=[b_w1[wb]])
                        for mc in range(2):
                            hb_ = mc % 2
                            pbk = 3 + (mc % 2)
                            for k in range(8):
                                P.op("pe", lambda e, k=k, mc=mc, wb=wb, pbk=pbk: e.matmul(out=PB[pbk][:, :], lhsT=w1[wb][:, k, mc * 128:(mc + 1) * 128], rhs=nT[:, k, :],
                                                                                          start=(k == 0), stop=(k == 7)), reads=[b_w1[wb], b_nT], writes=[bPB[pbk]])
                            P.op("act", lambda e, hb_=hb_, pbk=pbk: e.copy(out=hx[hb_][:], in_=PB[pbk][:, :]), reads=[bPB[pbk]], writes=[b_hx[hb_]])
                            P.op("dve", lambda e, hb_=hb_, mg=mg, mc=mc: e.scalar_tensor_tensor(out=hT[:, mg * 2 + mc, :], in0=hx[hb_][:], scalar=0.0, op0=ALU.max, in1=hx[hb_][:], op1=ALU.mult),
                                 reads=[b_hx[hb_]], writes=[b_hT])
                    for mg in range(8):
                        wb = wcnt["w2"] % 2
                        wcnt["w2"] += 1
                        P.dma(lambda e, wb=wb, mg=mg: e.dma_start(out=w2[wb][:], in_=wf2_b[:, mg * 128:(mg + 1) * 128].rearrange("(k p) m -> p k m", p=128)),
                              reads=[b_wsc], writes=[b_w2[wb]])
                        for mc in range(1):
                            mo = mg
                            ab = mo % 2
                            pbk = 5 + (mg % 2)
                            for k in range(32):
                                P.op("pe", lambda e, k=k, mc=mc, wb=wb, pbk=pbk: e.matmul(out=PB[pbk][:, :], lhsT=w2[wb][:, k, mc * 128:(mc + 1) * 128], rhs=hT[:, k, :],
                                                                                          start=(k == 0), stop=(k == 31)), reads=[b_w2[wb], b_hT], writes=[bPB[pbk]])
                            P.op("act", lambda e, ab=ab, pbk=pbk: e.copy(out=ao[ab][:], in_=PB[pbk][:, :]), reads=[bPB[pbk]], writes=[b_ao[ab]])
                            for s in range(4):
                                P.op("pe", lambda e, s=s, ab=ab: e.transpose(out=PB[7][:, s * 128:(s + 1) * 128], in_=ao[ab][:, s * 128:(s + 1) * 128], identity=ident),
                                     reads=[b_ao[ab], b_cst], writes=[bPB[7]])
                            P.op("dve", lambda e, mo=mo: e.tensor_tensor(out=yo[:, :, mo * 128:(mo + 1) * 128], in0=h2[:, :, mo * 128:(mo + 1) * 128],
                                                                         in1=PB[7][:, :].rearrange("p (s f) -> p s f", s=4), op=ALU.add), reads=[b_h2, bPB[7]], writes=[b_yo])
                    for s in range(4):
                        P.op("act", lambda e, s=s: e.activation(out=sqf[:], in_=yo[:, s, :], func=AF.Square, accum_out=ssf[:, 0:1]), reads=[b_yo], writes=[b_f])
                        P.op("act", lambda e: e.activation(out=ssf[:, 1:2], in_=ssf[:, 0:1], func=AF.Sqrt, scale=1.0 / D, bias=epsb[:, 0:1]), reads=[b_f, b_cst], writes=[b_f])
                        P.op("dve", lambda e: e.reciprocal(out=ssf[:, 2:3], in_=ssf[:, 1:2]), reads=[b_f], writes=[b_f])
                        P.op("dve", lambda e, s=s: e.scalar_tensor_tensor(out=yo[:, s, :], in0=yo[:, s, :], scalar=ssf[:, 2:3], op0=ALU.mult, in1=gfin[:], op1=ALU.mult),
                             reads=[b_yo, b_f, b_c], writes=[b_yo])
                        P.dma(lambda e, s=s: e.dma_start(out=yout[tk0 + s * 128:tk0 + (s + 1) * 128, :], in_=yo[:, s, :]), reads=[b_yo])

                for tt in range(dbg.get("post_tiles", 4)):
                    do_tt(tt)
                P.barrier()

        if do_post:
            if 0 in kinds:
                post_phase(0, xpp, 0, yp, "pp")
            if 1 in kinds:
                post_phase(1, hs, 16, ys, "ps")

        P.finish()
        P.emit(top)
    return nc


def prepare_inputs(inputs):
    f = lambda a: np.ascontiguousarray(np.asarray(a, dtype=np.float32))
    x_prompt, x_sample, meta = f(inputs["x_prompt"]), f(inputs["x_sample"]), f(inputs["meta_tokens"])
    w_in = f(inputs["w_in"])[0]
    mu_p, mu_n = f(inputs["mu_prev"])[0], f(inputs["mu_next"])[0]
    w0, w_up, a0, a_up, g_up = f(inputs["w0"])[0], f(inputs["w_up"])[0], f(inputs["a0"])[0], f(inputs["a_up"])[0], f(inputs["g_up"])[0]
    k_k, k_a, r_k = f(inputs["k_k"])[0], f(inputs["k_a"])[0], f(inputs["r_k"])[0].reshape(-1)
    ln_w, ln_b = f(inputs["ln_x_w"])[0], f(inputs["ln_x_b"])[0]
    hp = np.zeros((T_P, D), np.float32)
    hp[0:16] = meta
    hp[16:16 + 16384] = x_prompt[0]
    cst = _consts()
    gm = np.ascontiguousarray(f(inputs["g_mix"])[0].reshape(8, 128).T)
    gfc = np.ascontiguousarray(f(inputs["g_ffn"])[0].reshape(8, 128).T)
    gfin = np.ascontiguousarray(np.broadcast_to(f(inputs["g_final"])[None, :], (128, D)))
    slg = np.ascontiguousarray(f(inputs["subln_g"])[0].reshape(128, 1))
    lam = np.stack([f(inputs["lam_q1"])[0], f(inputs["lam_k1"])[0], f(inputs["lam_q2"])[0], f(inputs["lam_k2"])[0]])
    lam = np.ascontiguousarray(np.broadcast_to(lam[None], (128, 4, 64)))
    wg = np.ascontiguousarray(w_in[:, 3328:5376])

    def piece(ha, hb, m):
        cols = np.concatenate([np.arange(ha * 64, ha * 64 + 64), 512 + np.arange(ha * 64, ha * 64 + 64), 1024 + np.arange(ha * 64, ha * 64 + 64),
                               np.arange(1536, 1792),
                               1792 + hb * 128 + m * 64 + np.arange(64), 1792 + 512 + hb * 128 + m * 64 + np.arange(64),
                               1792 + 1024 + hb * 128 + np.arange(128)])
        W = w_in[:, cols]
        pp = np.zeros((128, NPP), np.float32)
        rc = cols[0:448]
        for gi in range(5):
            cc = rc[gi * 64:(gi + 1) * 64]
            for half in (0, 64):
                pp[half:half + 64, 3 * gi] = 1.0 - mu_p[cc] - mu_n[cc]
                pp[half:half + 64, 3 * gi + 1] = mu_p[cc]
                pp[half:half + 64, 3 * gi + 2] = mu_n[cc]
        cg = rc[320:448]
        pp[:, 22] = 1.0 - mu_p[cg] - mu_n[cg]
        pp[:, 23] = mu_p[cg]
        pp[:, 24] = mu_n[cg]
        ch = np.arange(ha * 64, ha * 64 + 64)
        for d in (0, 1):
            pp[d * 64:(d + 1) * 64, 15] = w0[d][ch]
            pp[d * 64:(d + 1) * 64, 16] = a0[d][ch]
            pp[d * 64:(d + 1) * 64, 17] = k_k[ch]
            pp[d * 64:(d + 1) * 64, 18] = k_a[ch]
            pp[d * 64:(d + 1) * 64, 19] = r_k[ch]
            pp[d * 64:(d + 1) * 64, 20] = ln_w[ch]
            pp[d * 64:(d + 1) * 64, 21] = ln_b[ch]
        lw = np.concatenate([w_up[0][:, ch], w_up[1][:, ch]], axis=0)
        la = np.concatenate([a_up[0][:, ch], a_up[1][:, ch]], axis=0)
        lg = g_up[:, ch]
        return W, pp, lw, la, lg

    spieces = [piece(p, p // 2, p % 2) for p in range(8)]
    stat = [_alibi_static(hb) for hb in range(4)]
    shared = dict(hp=hp, wg=wg, wua=f(inputs["w_up_a"])[0], wub=f(inputs["w_up_b"])[0], wout=f(inputs["w_out"])[0],
                  wf1=f(inputs["w_ff1"])[0], wf2=f(inputs["w_ff2"])[0], gm=gm, gf=gfc, gfin=gfin, slg=slg, lam=lam, cst=cst,
                  wpc=np.stack([p[0] for p in spieces]), pp=np.stack([p[1] for p in spieces]), lw=np.stack([p[2] for p in spieces]),
                  la=np.stack([p[3] for p in spieces]), lg=np.stack([p[4] for p in spieces]),
                  dm=np.stack([s[0] for s in stat]), qb=np.stack([s[1] for s in stat]))
    in_maps = []
    for c in range(NCORE):
        hs = np.zeros((T_S, D), np.float32)
        hs[0:16] = meta
        hs[16:16 + 2048] = x_sample[c]
        bc, ks, kval = _alibi_core(c)
        m = dict(shared)
        m["hs"] = hs
        m["xpp"] = np.ascontiguousarray(x_prompt[0, c * 2048:(c + 1) * 2048])
        m["bc"] = bc
        m["ks"] = ks
        m["kval"] = kval
        in_maps.append(m)
    return in_maps


_NC_CACHE = {}


def kernel(**inputs):
    in_maps = prepare_inputs(inputs)
    if "nc" not in _NC_CACHE:
        _NC_CACHE["nc"] = build_program()
    nc = _NC_CACHE["nc"]
    res = run_bass_kernel_spmd(nc, in_maps, core_ids=list(range(NCORE)))
    y_prompt = np.concatenate([np.asarray(r["yp"], dtype=np.float32) for r in res.results], axis=0)[None]
    y_sample = np.stack([np.asarray(r["ys"], dtype=np.float32) for r in res.results], axis=0)
    return (y_prompt, y_sample)
```

```python
import math
import numpy as np
from contextlib import ExitStack
import concourse.bass as bass
import concourse.mybir as mybir
from concourse.bass_utils import run_bass_kernel_spmd

F32 = mybir.dt.float32
BF16 = mybir.dt.bfloat16
AF = mybir.ActivationFunctionType
ALU = mybir.AluOpType
ENGS = ("pe", "act", "dve", "pool", "sp")
NDMA = 8

D = 1024
NCORE = 8
T_P, T_S = 16512, 2176
PC = 704
K0 = math.exp(-0.5)
LAMBDA_INIT = 0.2
NPP = 32


class Buf:
    __slots__ = ("w", "r", "name")

    def __init__(self, name=""):
        self.w = None
        self.r = []
        self.name = name


class Prog:
    def __init__(self, nc):
        self.nc = nc
        self.q = {e: [] for e in ENGS}
        self.cnt = {e: 0 for e in ENGS}
        self.seen = {e: {} for e in ENGS}
        self.dman = {e: 0 for e in ENGS}
        self.dma_last = {}

    def _wait(self, eng, key, val):
        if val <= 0:
            return
        s = self.seen[eng]
        if s.get(key, 0) >= val:
            return
        s[key] = val
        self.q[eng].append(("wait", key, val))

    def _deps(self, eng, reads, writes):
        for b in reads:
            if b.w is not None:
                k, v = b.w
                if k == eng and eng == "pe":
                    continue
                self._wait(eng, k, v)
        for b in writes:
            if b.w is not None:
                k, v = b.w
                if not (k == eng and eng == "pe"):
                    self._wait(eng, k, v)
            for (k, v) in b.r:
                if not (k == eng and eng == "pe"):
                    self._wait(eng, k, v)

    def _mark(self, ticket, reads, writes):
        k = ticket[0]
        for b in reads:
            b.r = [t for t in b.r if t[0] != k]
            b.r.append(ticket)
        for b in writes:
            b.w = ticket
            b.r = []

    def op(self, eng, fn, reads=(), writes=(), after_readers=()):
        self._deps(eng, reads, writes)
        for b in after_readers:
            for (k, v) in b.r:
                if not (k == eng and eng == "pe"):
                    self._wait(eng, k, v)
        self.cnt[eng] += 1
        t = (eng, self.cnt[eng])
        self.q[eng].append(("op", fn, eng, 1))
        self._mark(t, reads, writes)
        return t

    def dma(self, fn, reads=(), writes=(), eng="sp"):
        self._deps(eng, reads, writes)
        j = self.dman[eng]
        self.dman[eng] += 1
        slot = j % NDMA
        key = ("d", eng, slot)
        self._wait(eng, key, 16 * (j // NDMA))
        val = 16 * (j // NDMA + 1)
        self.q[eng].append(("op", fn, key, 16))
        self.dma_last[key] = val
        t = (key, val)
        self._mark(t, reads, writes)
        return t

    def barrier(self):
        keys = [(e, self.cnt[e]) for e in ENGS] + list(self.dma_last.items())
        for e in ENGS:
            for (k, v) in keys:
                if k != e:
                    self._wait(e, k, v)

    def finish(self):
        for (k, v) in list(self.dma_last.items()):
            self._wait("sp", k, v)
        for e in ENGS:
            if e != "sp":
                self._wait("sp", e, self.cnt[e])

    def emit(self, stack):
        nc = self.nc
        sems = {}
        for e in ENGS:
            sems[e] = stack.enter_context(nc.semaphore("ps_" + e))
        for k in self.dma_last:
            sems[k] = stack.enter_context(nc.semaphore("ds_%s_%d" % (k[1], k[2])))
        handles = {"pe": "tensor", "act": "scalar", "dve": "vector", "pool": "gpsimd", "sp": "sync"}
        block = stack.enter_context(nc.Block())

        def replay(engname):
            def body(engh):
                for it in self.q[engname]:
                    if it[0] == "wait":
                        engh.wait_ge(sems[it[1]], it[2])
                    else:
                        ins = it[1](engh)
                        ins.then_inc(sems[it[2]], it[3])
            return body

        for e in ENGS:
            getattr(block, handles[e])(replay(e))


def _consts():
    c = np.zeros((128, 1024), np.float32)
    c[:, 0:128] = np.eye(128)
    s = np.arange(64)
    mf = np.zeros((128, 128), np.float32)
    mb = np.zeros((128, 128), np.float32)
    for a in range(2):
        for b in range(2):
            if b == 0:
                mf[a * 64:(a + 1) * 64, 0:64] = (s[:, None] < s[None, :])
                mb[a * 64:(a + 1) * 64, 0:64] = (s[:, None] > s[None, :])
            else:
                mf[a * 64:(a + 1) * 64, 64:128] = (s[:, None] <= s[None, :])
                mb[a * 64:(a + 1) * 64, 64:128] = (s[:, None] >= s[None, :])
    c[:, 128:256] = mf
    c[:, 256:384] = mb
    c[0:64, 384:448] = (s[None, :] < s[:, None])
    c[64:128, 384:448] = (s[None, :] > s[:, None])
    c[0:64, 512:576] = 1.0
    c[64:128, 576:640] = 1.0
    c[0:64, 640:704] = np.eye(64)
    c[64:128, 640:704] = np.eye(64)
    c[:, 704:832] = 1.0
    c[0:64, 832:896] = 1.0 / 64
    c[:, 896:1024] = 1.0 / 128
    return c


def _alibi_static(hb):
    m = 2.0 ** (-8.0 * (hb + 1) / 4)
    sl = np.arange(128, dtype=np.float64)
    tq = np.arange(512, dtype=np.float64)
    DM = np.zeros((128, 5, 512))
    for dl in range(5):
        DM[:, dl, :] = -m * np.abs(16.0 + tq[None, :] - 128.0 * dl - sl[:, None])
    hi = -m * 16.0 * np.floor(tq / 16)
    lo = -m * (tq % 16)
    qb = np.stack([np.stack([hi, lo]), np.stack([-hi, -lo])])
    return DM.astype(np.float32), qb.astype(np.float32)


def _alibi_core(c):
    bc = np.zeros((2, 4, 128, 4, 129), np.float32)
    ks = np.ones((2, 2, T_P), np.float32)
    kval = np.zeros((2, 128, 129), np.float32)
    sl = np.arange(128, dtype=np.float64)
    for kind, (T, nreal, rot, own0) in enumerate(((T_P, 16400, 16 * c, 16 + 2048 * c), (T_S, 2064, 0, 16))):
        nkt = T // 128
        for r in range(nkt):
            i = (r + rot) % nkt
            s = 128.0 * i + sl
            kval[kind, :, r] = (s < nreal)
            if r > 16:
                left = (128 * i + 127) < own0
                ks[kind, :, r * 128:(r + 1) * 128] = 1.0 if left else -1.0
            for hb in range(4):
                m = 2.0 ** (-8.0 * (hb + 1) / 4)
                for jl in range(4):
                    t0 = own0 + 512 * jl
                    if 128 * i + 127 < t0:
                        bc[kind, hb, :, jl, r] = -m * (t0 - s)
                    elif 128 * i >= t0 + 512:
                        bc[kind, hb, :, jl, r] = -m * (s - t0)
    return bc, ks, kval


def build_program(dbg=None):
    dbg = dbg or {}
    pieces_run = dbg.get("pieces", list(range(8)))
    do_cast = dbg.get("cast", True)
    do_p1 = dbg.get("p1", True)
    do_attn = dbg.get("attn", True)
    do_rwkv = dbg.get("rwkv", True)
    do_xchg = dbg.get("xchg", True)
    do_post = dbg.get("post", True)
    dbg_out = dbg.get("dbg_out", False)
    skind = "ExternalOutput" if dbg_out else "Internal"

    nc = bass.Bass("TRN2", target_bir_lowering=False)

    def din(name, shape):
        return nc.dram_tensor(name, list(shape), F32, kind="ExternalInput").ap()

    def dscr(name, shape, dt):
        if name in dbg.get("outs", ()):
            return nc.dram_tensor(name, list(shape), dt, kind="ExternalOutput").ap()
        return nc.dram_tensor(name, list(shape), dt).ap()

    hp = din("hp", [T_P, D])
    hs = din("hs", [T_S, D])
    xpp = din("xpp", [2048, D])
    wpc = din("wpc", [8, D, PC])
    pp_d = din("pp", [8, 128, NPP])
    lw_d = din("lw", [8, 128, 64])
    la_d = din("la", [8, 128, 64])
    lg_d = din("lg", [8, 128, 64])
    bc_d = din("bc", [2, 4, 128, 4, 129])
    dm_d = din("dm", [4, 128, 5, 512])
    qb_d = din("qb", [4, 2, 2, 512])
    ks_d = din("ks", [2, 2, T_P])
    wg_d = din("wg", [D, 2048])
    wua_d = din("wua", [512, D])
    wub_d = din("wub", [512, D])
    wout_d = din("wout", [D, D])
    wf1_d = din("wf1", [D, 4096])
    wf2_d = din("wf2", [4096, D])
    gm_d = din("gm", [128, 8])
    gf_d = din("gf", [128, 8])
    gfin_d = din("gfin", [128, D])
    slg_d = din("slg", [128, 1])
    lam_d = din("lam", [128, 4, 64])
    cst_d = din("cst", [128, 1024])
    kval_d = din("kval", [2, 128, 129])

    yp = nc.dram_tensor("yp", [2048, D], F32, kind="ExternalOutput").ap()
    ys = nc.dram_tensor("ys", [2048, D], F32, kind="ExternalOutput").ap()

    wpc_b = dscr("wpc_b", [8, D, PC], BF16)
    wg_b = dscr("wg_b", [D, 2048], BF16)
    wua_b = dscr("wua_b", [512, D], BF16)
    wub_b = dscr("wub_b", [512, D], BF16)
    wout_b = dscr("wout_b", [D, D], BF16)
    wf1_b = dscr("wf1_b", [D, 4096], BF16)
    wf2_b = dscr("wf2_b", [4096, D], BF16)
    TK = (T_P, T_S)
    PA = [dscr("PA%d" % k, [8, 192, TK[k] + 2], F32) for k in range(2)]
    PAL = [dscr("PAL%d" % k, [256, TK[k] + 2], F32) for k in range(2)]
    KTD = [dscr("KTD%d" % k, [8, 64, 2 * TK[k]], BF16) for k in range(2)]
    VD = [dscr("VD%d" % k, [8, 2 * TK[k], 128], BF16) for k in range(2)]
    QTD = [dscr("QTD%d" % k, [8, 64, TK[k]], BF16) for k in range(2)]
    OA = [dscr("OA%d" % k, [8, 64, TK[k]], BF16) for k in range(2)]
    XR = [dscr("XR%d" % k, [8, 128, 2048], BF16) for k in range(2)]
    KTR = [dscr("KTR%d" % k, [8, 64, TK[k]], BF16) for k in range(2)]
    VR = [dscr("VR%d" % k, [8, TK[k], 128], BF16) for k in range(2)]
    QO = [dscr("QO%d" % k, [8, 64, 2048], BF16) for k in range(2)]
    OAO = [dscr("OAO%d" % k, [8, 64, 2048], BF16) for k in range(2)]
    YD = [dscr("YD%d" % k, [8, 128, TK[k]], F32) for k in range(2)]
    PAO = [dscr("PAO%d" % k, [8, 192, 2050], F32) for k in range(2)]
    PALO = [dscr("PALO%d" % k, [256, 2050], F32) for k in range(2)]
    YDO = [dscr("YDO%d" % k, [8, 128, 2048], F32) for k in range(2)]
    b_PA, b_KV, b_wsc, b_ROT, b_OAO, b_YD, b_PAO, b_YDO = Buf(), Buf(), Buf(), Buf(), Buf(), Buf(), Buf(), Buf()
    b_XR = [Buf(), Buf()]
    b_OA = [Buf(), Buf()]

    P = Prog(nc)
    with ExitStack() as top:
        _uid = [0]

        def sb(st, name, shape, dt):
            _uid[0] += 1
            return st.enter_context(nc.sbuf_tensor("s%d_%s" % (_uid[0], name), list(shape), dt))

        PB = [top.enter_context(nc.psum_tensor("pb%d" % i, [128, 512], F32)) for i in range(8)]
        bPB = [Buf("pb%d" % i) for i in range(8)]

        cst = sb(top, "cst", [128, 1024], F32); b_cst = Buf()
        cstb = sb(top, "cstb", [128, 128], BF16)
        gm = sb(top, "gm", [128, 8], F32)
        gf = sb(top, "gf", [128, 8], F32)
        zb = sb(top, "zb", [128, 512], BF16)
        zf = sb(top, "zf", [128, 8], F32)
        P.dma(lambda e: e.dma_start(out=cst[:], in_=cst_d), writes=[b_cst])
        P.dma(lambda e: e.dma_start(out=gm[:], in_=gm_d), writes=[b_cst])
        P.dma(lambda e: e.dma_start(out=gf[:], in_=gf_d), writes=[b_cst])
        P.op("dve", lambda e: e.tensor_copy(out=cstb[:], in_=cst[:, 0:128]), reads=[b_cst], writes=[b_cst])
        P.op("dve", lambda e: e.memset(zb[:], 0.0), writes=[b_cst])
        P.op("dve", lambda e: e.memset(zf[:], 0.0), writes=[b_cst])
        P.barrier()
        ident = cst[:, 0:128]
        MF, MB = cst[:, 128:256], cst[:, 256:384]
        MAF, MAB = cst[0:64, 384:448], cst[0:64, 448:512]
        BLK = cst[:, 512:640]
        SEL = cst[:, 640:704]
        ONES = cst[:, 704:832]
        O64 = cst[0:64, 832:896]
        O128 = cst[:, 896:1024]

        def castw(src, dst, rows, cols, scale_cols=None):
            with ExitStack() as st:
                nb = 3
                fin = [sb(st, "cw_f%d" % i, [128, 2048], F32) for i in range(nb)]
                fout = [sb(st, "cw_b%d" % i, [128, 2048], BF16) for i in range(nb)]
                bi = [Buf() for _ in range(nb)]
                bo = [Buf() for _ in range(nb)]
                it = 0
                for kc in range(rows // 128):
                    for c0 in range(0, cols, 2048):
                        cw = min(2048, cols - c0)
                        s = it % nb
                        P.dma(lambda e, s=s, kc=kc, c0=c0, cw=cw: e.dma_start(out=fin[s][:, 0:cw], in_=src[kc * 128:(kc + 1) * 128, c0:c0 + cw]),
                              writes=[bi[s]])
                        eng = "dve" if it % 2 == 0 else "pool"
                        if scale_cols is not None:
                            P.op(eng, lambda e, s=s, kc=kc, cw=cw: e.tensor_scalar(out=fout[s][:, 0:cw], in0=fin[s][:, 0:cw],
                                                                                  scalar1=scale_cols[:, kc % 8:kc % 8 + 1], scalar2=None, op0=ALU.mult),
                                 reads=[bi[s]], writes=[bo[s]])
                        else:
                            P.op(eng, lambda e, s=s, cw=cw: e.tensor_copy(out=fout[s][:, 0:cw], in_=fin[s][:, 0:cw]),
                                 reads=[bi[s]], writes=[bo[s]])
                        P.dma(lambda e, s=s, kc=kc, c0=c0, cw=cw: e.dma_start(out=dst[kc * 128:(kc + 1) * 128, c0:c0 + cw], in_=fout[s][:, 0:cw]),
                              reads=[bo[s]], writes=[b_wsc], eng="act")
                        it += 1
                P.barrier()

        if do_cast:
            for p in range(8):
                castw(wpc[p], wpc_b[p], D, PC, gm)
            if do_post:
                castw(wg_d, wg_b, D, 2048, gm)
                castw(wua_d, wua_b, 512, D)
                castw(wub_d, wub_b, 512, D)
                castw(wout_d, wout_b, D, D)
                castw(wf1_d, wf1_b, D, 4096, gf)
                castw(wf2_d, wf2_b, 4096, D)

        def norm_T(st_tmp, src_ap, b_src, dst_ap, b_dst, pbank, tag):
            sq, ss, xn, bt = st_tmp
            P.op("act", lambda e: e.activation(out=sq[:], in_=src_ap, func=AF.Square, accum_out=ss[:, 0:1]),
                 reads=[b_src], writes=[bt])
            P.op("act", lambda e: e.activation(out=ss[:, 1:2], in_=ss[:, 0:1], func=AF.Sqrt, scale=1.0 / D, bias=zf[:, 0:1]),
                 reads=[bt], writes=[bt])
            P.op("dve", lambda e: e.tensor_scalar(out=ss[:, 1:2], in0=ss[:, 1:2], scalar1=1e-12, scalar2=None, op0=ALU.max),
                 reads=[bt], writes=[bt])
            P.op("dve", lambda e: e.reciprocal(out=ss[:, 2:3], in_=ss[:, 1:2]), reads=[bt], writes=[bt])
            P.op("dve", lambda e: e.tensor_scalar(out=xn[:], in0=src_ap, scalar1=ss[:, 2:3], scalar2=None, op0=ALU.mult),
                 reads=[b_src, bt], writes=[bt])
            pt = PB[pbank].bitcast(BF16)
            for k in range(8):
                P.op("pe", lambda e, k=k: e.transpose(out=pt[:, k * 128:(k + 1) * 128], in_=xn[:, k * 128:(k + 1) * 128], identity=cstb[:]),
                     reads=[bt], writes=[bPB[pbank]])
            P.op("act", lambda e: e.copy(out=dst_ap, in_=pt[:].rearrange("p (k t) -> p k t", k=8)), reads=[bPB[pbank]], writes=[b_dst])

        epsb = sb(top, "epsb", [128, 4], F32)
        P.op("dve", lambda e: e.memset(epsb[:, 0:1], 1e-6), writes=[b_cst])
        P.op("dve", lambda e: e.memset(epsb[:, 1:2], 1e-5), writes=[b_cst])
        P.op("dve", lambda e: e.memset(epsb[:, 2:3], 64e-5), writes=[b_cst])
        P.barrier()

        def norm_T2(st_tmp, src_ap, b_src, dst_ap, b_dst, pbank):
            sq, ss, xn, bt = st_tmp
            P.op("act", lambda e: e.activation(out=sq[:], in_=src_ap, func=AF.Square, accum_out=ss[:, 0:1]),
                 reads=[b_src], writes=[bt])
            P.op("act", lambda e: e.activation(out=ss[:, 1:2], in_=ss[:, 0:1], func=AF.Sqrt, scale=1.0 / D, bias=epsb[:, 0:1]),
                 reads=[bt], writes=[bt])
            P.op("dve", lambda e: e.reciprocal(out=ss[:, 2:3], in_=ss[:, 1:2]), reads=[bt], writes=[bt])
            P.op("dve", lambda e: e.tensor_scalar(out=xn[:], in0=src_ap, scalar1=ss[:, 2:3], scalar2=None, op0=ALU.mult),
                 reads=[b_src, bt], writes=[bt])
            pt = PB[pbank].bitcast(BF16)
            for k in range(8):
                P.op("pe", lambda e, k=k: e.transpose(out=pt[:, k * 128:(k + 1) * 128], in_=xn[:, k * 128:(k + 1) * 128], identity=cstb[:]),
                     reads=[bt], writes=[bPB[pbank]])
            P.op("act", lambda e: e.copy(out=dst_ap, in_=pt[:].rearrange("p (k t) -> p k t", k=8)), reads=[bPB[pbank]], writes=[b_dst])

        _pid = {}

        def core_off(e):
            k = id(e)
            if k not in _pid:
                _pid[k] = e.snap(e.partition_id() * 2048) if hasattr(e, "snap") else e.partition_id() * 2048
            return _pid[k]

        def p1_phase(kind):
            T = TK[kind]
            hsrc = hp if kind == 0 else hs
            tiles = [(t0, min(512, T - t0)) for t0 in range(0, T, 512)]
            tg = "a%d" % kind
            with ExitStack() as s1:
                wp = sb(s1, tg + "wp", [128, 8, 8, PC], BF16); b_wp = Buf()
                for p in range(8):
                    P.dma(lambda e, p=p: e.dma_start(out=wp[:, p, :, :], in_=wpc_b[p].rearrange("(k p) m -> p k m", p=128)), reads=[b_wsc], writes=[b_wp])
                    for (r0, nr) in ((0, 128), (128, 64)):
                        for col in (0, T + 1):
                            P.dma(lambda e, p=p, r0=r0, nr=nr, col=col: e.dma_start(out=PA[kind][p, r0:r0 + nr, col:col + 1], in_=zf[0:nr, 0:1], allow_slow_non_contiguous=True), writes=[b_PA])
                for r0 in (0, 128):
                    for col in (0, T + 1):
                        P.dma(lambda e, r0=r0, col=col: e.dma_start(out=PAL[kind][r0:r0 + 128, col:col + 1], in_=zf[:, 0:1], allow_slow_non_contiguous=True), writes=[b_PA])
                NXB = 3
                xt = [sb(s1, tg + "xt%d" % i, [128, D], F32) for i in range(NXB)]
                b_xt = [Buf() for _ in range(NXB)]
                sqs = sb(s1, tg + "sq", [128, D], F32)
                tmp = [(sqs, sb(s1, tg + "ss%d" % i, [128, 4], F32), sb(s1, tg + "xn%d" % i, [128, D], BF16), Buf()) for i in range(2)]
                nT = [sb(s1, tg + "nT%d" % i, [128, 8, 512], BF16) for i in range(2)]
                b_nT = [Buf() for _ in range(2)]
                stg = [sb(s1, tg + "stg%d" % i, [128, 4, 512], F32) for i in range(2)]
                b_stg = [Buf() for _ in range(2)]
                qk = [sb(s1, tg + "qk%d" % i, [64, 2, 512], BF16) for i in range(2)]
                b_qk = [Buf() for _ in range(2)]
                vst = [sb(s1, tg + "vst%d" % i, [128, 4, 128], BF16) for i in range(2)]
                b_vst = [Buf() for _ in range(2)]
                cnt = {"sub": 0, "it": 0}

                def do_tile(ti, t0, w):
                    nb = ti % 2
                    nsub = w // 128
                    for s in range(nsub):
                        xb = cnt["sub"] % NXB
                        P.dma(lambda e, xb=xb, s=s: e.dma_start(out=xt[xb][:], in_=hsrc[t0 + s * 128:t0 + (s + 1) * 128, :]), writes=[b_xt[xb]])
                        norm_T2(tmp[cnt["sub"] % 2], xt[xb][:], b_xt[xb], nT[nb][:, :, s * 128:(s + 1) * 128], b_nT[nb], cnt["sub"] % 2)
                        cnt["sub"] += 1
                    sbl = cnt["it"] % 2
                    cnt["it"] += 1
                    for ci, c0 in enumerate((192, 320)):
                        bk = 2 + (ci % 2)
                        for k in range(8):
                            P.op("pe", lambda e, k=k, c0=c0, bk=bk: e.matmul(out=PB[bk][:, 0:w], lhsT=wp[:, 0, k, c0:c0 + 128], rhs=nT[nb][:, k, 0:w], start=(k == 0), stop=(k == 7)),
                                 reads=[b_wp, b_nT[nb]], writes=[bPB[bk]])
                        if ci == 0:
                            P.op("act", lambda e, bk=bk, ci=ci: e.copy(out=stg[sbl][:, ci, 0:w], in_=PB[bk][:, 0:w]), reads=[bPB[bk]], writes=[b_stg[sbl]])
                        else:
                            P.op("dve", lambda e, bk=bk, ci=ci: e.tensor_copy(out=stg[sbl][:, ci, 0:w], in_=PB[bk][:, 0:w]), reads=[bPB[bk]], writes=[b_stg[sbl]])
                    P.dma(lambda e: e.dma_start(out=PAL[kind][:, 1 + t0:1 + t0 + w].rearrange("(c p) t -> p c t", p=128), in_=stg[sbl][:, 0:2, 0:w]),
                          reads=[b_stg[sbl]], writes=[b_PA])
                    for p in range(8):
                        sbi = cnt["it"] % 2
                        cnt["it"] += 1
                        for ci, (c0, cw) in enumerate([(0, 128), (128, 64)]):
                            bk = 2 + (ci % 2)
                            for k in range(8):
                                P.op("pe", lambda e, k=k, c0=c0, cw=cw, bk=bk, p=p: e.matmul(
                                    out=PB[bk][0:cw, 0:w], lhsT=wp[:, p, k, c0:c0 + cw], rhs=nT[nb][:, k, 0:w], start=(k == 0), stop=(k == 7)),
                                    reads=[b_wp, b_nT[nb]], writes=[bPB[bk]])
                            if ci % 2 == 0:
                                P.op("act", lambda e, cw=cw, bk=bk, ci=ci, sbi=sbi: e.copy(out=stg[sbi][0:cw, ci, 0:w], in_=PB[bk][0:cw, 0:w]),
                                     reads=[bPB[bk]], writes=[b_stg[sbi]])
                            else:
                                P.op("dve", lambda e, cw=cw, bk=bk, ci=ci, sbi=sbi: e.tensor_copy(out=stg[sbi][0:cw, ci, 0:w], in_=PB[bk][0:cw, 0:w]),
                                     reads=[bPB[bk]], writes=[b_stg[sbi]])
                        P.dma(lambda e, p=p, sbi=sbi: e.dma_start(out=PA[kind][p, 0:128, 1 + t0:1 + t0 + w], in_=stg[sbi][:, 0, 0:w]),
                              reads=[b_stg[sbi]], writes=[b_PA])
                        P.dma(lambda e, p=p, sbi=sbi: e.dma_start(out=PA[kind][p, 128:192, 1 + t0:1 + t0 + w], in_=stg[sbi][0:64, 1, 0:w]),
                              reads=[b_stg[sbi]], writes=[b_PA])
                        for k in range(8):
                            P.op("pe", lambda e, k=k, p=p: e.matmul(out=PB[4][0:64, 0:w], lhsT=wp[:, p, k, 448:512], rhs=nT[nb][:, k, 0:w], start=(k == 0), stop=(k == 7)),
                                 reads=[b_wp, b_nT[nb]], writes=[bPB[4]])
                        P.op("act", lambda e, sbi=sbi: e.activation(out=qk[sbi][:, 0, 0:w], in_=PB[4][0:64, 0:w], func=AF.Copy, scale=0.125),
                             reads=[bPB[4]], writes=[b_qk[sbi]])
                        for k in range(8):
                            P.op("pe", lambda e, k=k, p=p: e.matmul(out=PB[5][0:64, 0:w], lhsT=wp[:, p, k, 512:576], rhs=nT[nb][:, k, 0:w], start=(k == 0), stop=(k == 7)),
                                 reads=[b_wp, b_nT[nb]], writes=[bPB[5]])
                        P.op("dve", lambda e, sbi=sbi: e.tensor_copy(out=qk[sbi][:, 1, 0:w], in_=PB[5][0:64, 0:w]), reads=[bPB[5]], writes=[b_qk[sbi]])
                        P.dma(lambda e, p=p, sbi=sbi: e.dma_start(out=QTD[kind][p, :, t0:t0 + w], in_=qk[sbi][:, 0, 0:w]), reads=[b_qk[sbi]], writes=[b_KV])
                        for rep_ in range(2):
                            P.dma(lambda e, p=p, sbi=sbi, rep_=rep_: e.dma_start(out=KTD[kind][p, :, rep_ * T + t0:rep_ * T + t0 + w], in_=qk[sbi][:, 1, 0:w]),
                                  reads=[b_qk[sbi]], writes=[b_KV])
                        if p % 2 == 1:
                            continue
                        for s in range(nsub):
                            for k in range(8):
                                P.op("pe", lambda e, k=k, s=s, p=p: e.matmul(out=PB[6][:, s * 128:(s + 1) * 128], lhsT=nT[nb][:, k, s * 128:(s + 1) * 128],
                                                                             rhs=wp[:, p, k, 576:704], start=(k == 0), stop=(k == 7)),
                                     reads=[b_wp, b_nT[nb]], writes=[bPB[6]])
                        P.op("act", lambda e, sbi=sbi: e.copy(out=vst[sbi][:, 0:nsub, :], in_=PB[6][:, 0:nsub * 128].rearrange("p (s e) -> p s e", s=nsub)),
                             reads=[bPB[6]], writes=[b_vst[sbi]])
                        for rep_ in range(2):
                            P.dma(lambda e, p=p, sbi=sbi, rep_=rep_: e.dma_start(
                                out=VD[kind][p, rep_ * T + t0:rep_ * T + t0 + w, :].rearrange("(s q) e -> q s e", q=128), in_=vst[sbi][:, 0:nsub, :]),
                                reads=[b_vst[sbi]], writes=[b_KV])

                for ti, (t0, w) in enumerate(tiles[:dbg.get("p1_tiles", 10 ** 9)]):
                    do_tile(ti, t0, w)
                P.barrier()

        def rotate_phase(kind):
            T = TK[kind]

            def off(e, extra):
                return (core_off(e) + extra) if kind == 0 else extra
            P.dma(lambda e: e.dma_start(out=KTR[kind][:, :, :], in_=KTD[kind][:, :, bass.ds(off(e, 0), T)]), reads=[b_KV], writes=[b_ROT])
            P.dma(lambda e: e.dma_start(out=VR[kind].rearrange("p t e -> p (t e)"),
                                        in_=VD[kind].rearrange("p t e -> p (t e)")[:, bass.ds(off(e, 0) * 128, T * 128)]), reads=[b_KV], writes=[b_ROT])
            P.dma(lambda e: e.dma_start(out=QO[kind][:, :, :], in_=QTD[kind][:, :, bass.ds(off(e, 16), 2048)]), reads=[b_KV], writes=[b_ROT])
            P.dma(lambda e: e.dma_start(out=PAO[kind].rearrange("p r t -> (p r) t"), in_=PA[kind].rearrange("p r t -> (p r) t")[:, bass.ds(off(e, 16), 2050)]),
                  reads=[b_PA], writes=[b_PAO])
            P.dma(lambda e: e.dma_start(out=PALO[kind][:, :], in_=PAL[kind][:, bass.ds(off(e, 16), 2050)]), reads=[b_PA], writes=[b_PAO])
            P.barrier()

        def own_y_phase(kind):
            def off(e, extra):
                return (core_off(e) + extra) if kind == 0 else extra
            P.dma(lambda e: e.dma_start(out=YDO[kind].rearrange("p r t -> (p r) t"), in_=YD[kind].rearrange("p r t -> (p r) t")[:, bass.ds(off(e, 16), 2048)]),
                  reads=[b_YD], writes=[b_YDO])
            P.barrier()

        def attn_job(kind, p):
            T = TK[kind]
            NKT = T // 128
            hb = p // 2
            m = 2.0 ** (-8.0 * (hb + 1) / 4)
            tg = "t%d_%d" % (kind, p)
            with ExitStack() as s2:
                KT = sb(s2, tg + "KT", [66, T], BF16); b_KT = Buf()
                VP = sb(s2, tg + "VP", [128, NKT, 129], BF16); b_VP = Buf()
                QT = sb(s2, tg + "QT", [64, 2048], BF16); b_QT = Buf()
                ksf = sb(s2, tg + "ksf", [66, T], F32) if False else None
                kvf = sb(s2, tg + "kvf", [128, 129], F32)
                ksr = sb(s2, tg + "ksr", [66, 2048], F32)
                bc = sb(s2, tg + "bc", [128, 4, 129], F32); b_bc = Buf()
                dmk = sb(s2, tg + "dmk", [128, 5, 512], F32)
                qbf = sb(s2, tg + "qbf", [66, 2, 512], F32)
                Qp = sb(s2, tg + "Qp", [66, 512], BF16)
                Qn = sb(s2, tg + "Qn", [66, 512], BF16)
                b_Q = Buf()
                PT = [sb(s2, tg + "PT%d" % i, [128, 512], BF16) for i in range(3)]
                b_PT = [Buf() for _ in range(3)]
                dtmp = [sb(s2, tg + "dt%d" % i, [128, 512], F32) for i in range(2)]
                b_dt = [Buf() for _ in range(2)]
                rinv = sb(s2, tg + "rinv", [128, 4], F32); b_ri = Buf()
                On = sb(s2, tg + "On", [128, 4, 128], F32); b_On = Buf()
                OT = [sb(s2, tg + "OT%d" % i, [128, 512], BF16) for i in range(2)]
                b_OT = [Buf() for _ in range(2)]

                def dyn(e, base_mult):
                    if kind == 0:
                        return core_off(e)
                    return 0
                P.dma(lambda e: e.dma_start(out=KT[0:64, :], in_=KTR[kind][p, :, :]), reads=[b_ROT], writes=[b_KT])
                P.dma(lambda e: e.dma_start(out=VP[:, :, 0:128], in_=VR[kind][p - p % 2, :, :].rearrange("(s q) e -> q s e", q=128)),
                      reads=[b_ROT], writes=[b_VP])
                P.dma(lambda e: e.dma_start(out=QT[:, :], in_=QO[kind][p, :, :]), reads=[b_ROT], writes=[b_QT])
                P.dma(lambda e: e.dma_start(out=kvf[:], in_=kval_d[kind]), writes=[b_VP])
                P.op("dve", lambda e: e.tensor_copy(out=VP[:, :, 128:129], in_=kvf[:, 0:NKT].unsqueeze(2)), reads=[b_VP], writes=[b_VP])
                for c0 in range(0, T, 2048):
                    cw = min(2048, T - c0)
                    P.dma(lambda e, c0=c0, cw=cw: e.dma_start(out=ksr[64:66, 0:cw], in_=ks_d[kind, :, c0:c0 + cw]), writes=[b_bc])
                    P.op("dve", lambda e, c0=c0, cw=cw: e.tensor_copy(out=KT[64:66, c0:c0 + cw], in_=ksr[64:66, 0:cw]), reads=[b_bc], writes=[b_KT])
                P.dma(lambda e: e.dma_start(out=bc[:], in_=bc_d[kind, hb]), writes=[b_bc])
                P.dma(lambda e: e.dma_start(out=dmk[:], in_=dm_d[hb]), writes=[b_bc])
                P.dma(lambda e: e.dma_start(out=qbf[64:66, :, :], in_=qb_d[hb].rearrange("s r t -> r s t")), writes=[b_bc])
                P.op("dve", lambda e: e.tensor_copy(out=Qp[64:66, :], in_=qbf[64:66, 0, :]), reads=[b_bc], writes=[b_Q])
                P.op("dve", lambda e: e.tensor_copy(out=Qn[64:66, :], in_=qbf[64:66, 1, :]), reads=[b_bc], writes=[b_Q])
                st_ = {"blk": 0, "pend": None}

                def do_q(jl):
                    P.op("pool", lambda e: e.tensor_copy(out=Qp[0:64, :], in_=QT[:, jl * 512:(jl + 1) * 512]), reads=[b_QT], writes=[b_Q])
                    P.op("pool", lambda e: e.tensor_copy(out=Qn[0:64, :], in_=QT[:, jl * 512:(jl + 1) * 512]), reads=[b_QT], writes=[b_Q])
                    for bk in (2, 3):
                        P.op("pe", lambda e, bk=bk: e.matmul(out=PB[bk][:, :], lhsT=zb[0:1, 0:128], rhs=zb[0:1, 0:512], start=True, stop=True,
                                                             skip_group_check=True), reads=[b_cst], writes=[bPB[bk]])
                    for r in range(NKT):
                        dl = r - 4 * jl
                        diag = (r <= 16) and (0 <= dl <= 4)
                        if r <= 16:
                            if dl < 0:
                                dist = (16 + 512 * jl) - (128 * r + 127)
                            elif dl > 4:
                                dist = 128 * r - (16 + 512 * jl + 511)
                            else:
                                dist = 0
                        else:
                            d_right = 128 * r - (16 + 512 * jl + 511)
                            d_left = 512 * jl - 128 * r + 16401
                            dist = min(d_right, d_left) if kind == 0 else d_right
                        if m * dist > 150.0:
                            continue
                        sbk = st_["blk"] % 2
                        pb = st_["blk"] % 3
                        st_["blk"] += 1
                        if diag:
                            P.op("pe", lambda e, r=r, sbk=sbk: e.matmul(out=PB[sbk][:, :], lhsT=KT[0:64, r * 128:(r + 1) * 128], rhs=Qp[0:64, :],
                                                                        start=True, stop=True), reads=[b_KT, b_Q], writes=[bPB[sbk]])
                            P.op("dve", lambda e, sbk=sbk, dl=dl: e.tensor_tensor(out=dtmp[sbk][:], in0=PB[sbk][:, :], in1=dmk[:, dl, :], op=ALU.add),
                                 reads=[bPB[sbk], b_bc], writes=[b_dt[sbk]])
                            P.op("act", lambda e, sbk=sbk, pb=pb: e.activation(out=PT[pb][:], in_=dtmp[sbk][:], func=AF.Exp),
                                 reads=[b_dt[sbk]], writes=[b_PT[pb]])
                        else:
                            Qx = Qn if (r <= 16 and dl > 4) else Qp
                            P.op("pe", lambda e, r=r, sbk=sbk, Qx=Qx: e.matmul(out=PB[sbk][:, :], lhsT=KT[0:66, r * 128:(r + 1) * 128], rhs=Qx[0:66, :],
                                                                               start=True, stop=True), reads=[b_KT, b_Q], writes=[bPB[sbk]])
                            P.op("act", lambda e, r=r, sbk=sbk, pb=pb: e.activation(out=PT[pb][:], in_=PB[sbk][:, :], func=AF.Exp, bias=bc[:, jl, r:r + 1]),
                                 reads=[bPB[sbk], b_bc], writes=[b_PT[pb]])
                        def pv(pb=pb, r=r):
                            for s in range(4):
                                bk, off = (2, s * 129) if s < 3 else (3, 0)
                                P.op("pe", lambda e, pb=pb, s=s, bk=bk, off=off, r=r: e.matmul(out=PB[bk][:, off:off + 129], lhsT=PT[pb][:, s * 128:(s + 1) * 128],
                                                                                               rhs=VP[:, r, :], start=False, stop=False, skip_group_check=True),
                                     reads=[b_PT[pb], b_VP], writes=[bPB[bk]])
                        if st_["pend"] is not None:
                            st_["pend"]()
                        st_["pend"] = pv
                    if st_["pend"] is not None:
                        st_["pend"]()
                        st_["pend"] = None
                    ob = jl % 2
                    for s in range(4):
                        bk, off = (2, s * 129) if s < 3 else (3, 0)
                        P.op("dve", lambda e, s=s, bk=bk, off=off: e.reciprocal(out=rinv[:, s:s + 1], in_=PB[bk][:, off + 128:off + 129]),
                             reads=[bPB[bk]], writes=[b_ri])
                        P.op("dve", lambda e, s=s, bk=bk, off=off: e.tensor_scalar(out=On[:, s, :], in0=PB[bk][:, off:off + 128], scalar1=rinv[:, s:s + 1],
                                                                                  scalar2=None, op0=ALU.mult), reads=[bPB[bk], b_ri], writes=[b_On])
                    for s in range(4):
                        P.op("pe", lambda e, s=s: e.transpose(out=PB[4][:, s * 128:(s + 1) * 128], in_=On[:, s, :], identity=ident),
                             reads=[b_On, b_cst], writes=[bPB[4]])
                    P.op("act", lambda e: e.copy(out=OT[ob][:], in_=PB[4][:, :]), reads=[bPB[4]], writes=[b_OT[ob]])
                    P.dma(lambda e: e.dma_start(out=XR[kind][p, :, jl * 512:(jl + 1) * 512], in_=OT[ob][:]), reads=[b_OT[ob]], writes=[b_XR[kind]])

                for jl in range(4):
                    do_q(jl)
                P.barrier()

        def run_streams(gens):
            gens = list(gens)
            skew = dbg.get("skew", 0)
            for gi, g in enumerate(gens[:-1]):
                try:
                    for _ in range(skew * (len(gens) - 1 - gi)):
                        next(g)
                except StopIteration:
                    pass
            while gens:
                for g in list(gens):
                    try:
                        next(g)
                    except StopIteration:
                        gens.remove(g)

        class RwkvCtx:
            def __init__(self, st, kind, pi, slot, full):
                self.kind, self.pi, self.slot = kind, pi, slot
                self.T = TK[kind]
                self.paT = PA[kind][pi] if full else PAO[kind][pi]
                self.paL = PAL[kind] if full else PALO[kind]
                self.b_src = b_PA if full else b_PAO
                tg = "r%d_%d_%d" % (kind, pi, 1 if full else 0)
                self.bb = 4 * slot

                def F(name, shape, dt=F32):
                    return sb(st, tg + name, shape, dt)
                self.F = F
                self.pp = F("pp", [128, NPP]); self.b_pp = Buf()
                self.lw = F("lw", [128, 64]); self.la = F("la", [128, 64]); self.lg = F("lg", [128, 64])
                for (t_, d_) in ((self.pp, pp_d), (self.lw, lw_d), (self.la, la_d), (self.lg, lg_d)):
                    P.dma(lambda e, t_=t_, d_=d_: e.dma_start(out=t_[:], in_=d_[pi]), writes=[self.b_pp])
                WW = 512
                self.raw = {n: F("raw_" + n, [128, WW + 2]) for n in ("r", "k", "v", "wd", "ad")}
                self.b_raw = Buf()
                self.t1 = F("t1", [128, WW]); self.b_t1 = Buf()
                self.X = {n: F("X_" + n, [128, WW]) for n in ("r", "k", "v", "wd", "ad")}
                self.b_X = Buf()
                self.b_Xn = {n: Buf() for n in ("r", "k", "v", "wd", "ad")}
                names = ("a", "kd", "tmp") + (("sg", "cs", "x", "d1", "d2", "ginc", "ginv", "kapa") if full else ())
                self.E = {n: F("E_" + n, [128, WW]) for n in names}
                self.b_E = {n: Buf() for n in self.E}
                if full:
                    for n, r in (("kk", "k"), ("rn", "r"), ("kap", "v")):
                        self.E[n] = self.raw[r]
                        self.b_E[n] = self.b_raw

            def PBk(self, i):
                return PB[self.bb + i]

            def bPBk(self, i):
                return bPB[self.bb + i]

            def load_raw(self, W, f0, b0):
                for gi, n in enumerate(("r", "k", "v", "wd", "ad")):
                    srcT, r0 = (self.paT, 64 * gi) if gi < 3 else (self.paL, 64 * (gi - 3))
                    P.dma(lambda e, n=n, r0=r0, srcT=srcT: e.dma_start(out=self.raw[n][0:64, 0:W + 2], in_=srcT[r0:r0 + 64, f0:f0 + W + 2]), reads=[self.b_src], writes=[self.b_raw])
                    P.dma(lambda e, n=n, r0=r0, srcT=srcT: e.dma_start(out=self.raw[n][64:128, 0:W + 2], in_=srcT[r0:r0 + 64, b0:b0 + W + 2]), reads=[self.b_src], writes=[self.b_raw])

            def prep_common(self, W):
                raw, t1, X, E, pp, la = self.raw, self.t1, self.X, self.E, self.pp, self.la
                b_raw, b_t1, b_X, b_E, b_pp = self.b_raw, self.b_t1, self.b_X, self.b_E, self.b_pp
                for gi, n in enumerate(("r", "k", "v", "wd", "ad")):
                    c0 = 3 * gi
                    bx = self.b_Xn[n]
                    P.op("act", lambda e, n=n, c0=c0: e.activation(out=X[n][:, 0:W], in_=raw[n][:, 1:W + 1], func=AF.Identity, scale=pp[:, c0:c0 + 1]),
                         reads=[b_raw, b_pp], writes=[bx], after_readers=[b_X])
                    P.op("dve", lambda e, n=n, c0=c0: e.scalar_tensor_tensor(out=X[n][:, 0:W], in0=raw[n][:, 0:W], scalar=pp[:, c0 + 1:c0 + 2], op0=ALU.mult,
                                                                              in1=X[n][:, 0:W], op1=ALU.add), reads=[b_raw, b_pp, bx], writes=[bx])
                    P.op("dve", lambda e, n=n, c0=c0: e.scalar_tensor_tensor(out=X[n][:, 0:W], in0=raw[n][:, 2:W + 2], scalar=pp[:, c0 + 2:c0 + 3], op0=ALU.mult,
                                                                              in1=X[n][:, 0:W], op1=ALU.add), reads=[b_raw, b_pp, bx], writes=[bx, b_X])
                pb0, bpb0 = self.PBk(0), self.bPBk(0)
                for h in (0, 64):
                    P.op("pe", lambda e, h=h: e.matmul(out=pb0[h:h + 64, 0:W], lhsT=la[h:h + 64, :], rhs=X["ad"][h:h + 64, 0:W], start=True, stop=True),
                         reads=[b_pp, b_X], writes=[bpb0])
                P.op("act", lambda e: e.activation(out=E["a"][:, 0:W], in_=pb0[:, 0:W], func=AF.Sigmoid, bias=pp[:, 16:17]),
                     reads=[bpb0, b_pp], writes=[b_E["a"]])
                P.op("dve", lambda e: e.tensor_scalar(out=E["tmp"][:, 0:W], in0=E["a"][:, 0:W], scalar1=-1.0, scalar2=pp[:, 18:19], op0=ALU.add, op1=ALU.mult),
                     reads=[b_E["a"], b_pp], writes=[b_E["tmp"]])
                P.op("dve", lambda e: e.scalar_tensor_tensor(out=E["kd"][:, 0:W], in0=E["tmp"][:, 0:W], scalar=1.0, op0=ALU.add, in1=X["k"][:, 0:W], op1=ALU.mult),
                     reads=[b_E["tmp"], b_X], writes=[b_E["kd"]])

        def rwkv_steps(st, kind, pi, slot):
            C = RwkvCtx(st, kind, pi, slot, True)
            T, F, pp, lw = C.T, C.F, C.pp, C.lw
            raw, t1, X, E = C.raw, C.t1, C.X, C.E
            b_raw, b_t1, b_X, b_E, b_pp = C.b_raw, C.b_t1, C.b_X, C.b_E, C.b_pp
            PBk, bPBk = C.PBk, C.bPBk
            NCH = T // 64
            steps = []
            c = 0
            while c < NCH:
                nj = min(8, NCH - c)
                steps.append((c, nj))
                c += nj
            ST = F("ST", [128, 64]); b_ST = Buf()
            RST = F("RST", [128, 512])
            P.op("dve", lambda e: e.memset(ST[:], 0.0), writes=[b_ST])
            P.op("dve", lambda e: e.memset(RST[:], 1.0), writes=[b_pp])
            P.op("dve", lambda e: e.memset(RST[:].rearrange("p (j t) -> p j t", t=64)[:, :, 0:1], 0.0), writes=[b_pp])
            GE = F("GE", [128, 8]); b_GE = Buf()
            BK = F("BK", [128, 8, 2, 64]); b_BK = Buf()
            AR = F("AR", [128, 8, 2, 64]); b_AR = Buf()
            BKH = F("BKH", [128, 8, 2, 64]); b_BKH = Buf()
            VV = F("VV", [128, 8, 2, 64]); b_VV = Buf()
            MX = F("MX", [128, 16, 128]); b_MX = Buf()
            Nm = [F("Nm%d" % i, [128, 8, 64], BF16) for i in range(2)]; b_Nm = [Buf(), Buf()]
            Mm = [F("Mm%d" % i, [128, 8, 64], BF16) for i in range(2)]; b_Mm = [Buf(), Buf()]
            Rm = F("Rm", [128, 8, 64]); b_Rm = Buf()
            Rb = F("Rb", [128, 8, 64], BF16); b_Rb = Buf()
            BKHs = F("BKHs", [128, 16, 64]); b_BKHs = Buf()
            V2s = F("V2s", [128, 16, 64]); b_V2s = Buf()
            BVs = F("BVs", [128, 8, 64]); b_BVs = Buf()
            YVs = F("YVs", [128, 8, 64]); b_YVs = Buf()
            KVs = F("KVs", [128, 8, 64]); b_KVs = Buf()
            Ws = F("Ws", [128, 64]); b_Ws = Buf()
            Us = F("Us", [128, 64]); b_Us = Buf()
            ST2 = F("ST2", [128, 64]); b_ST2 = Buf()
            Yst = [F("Yst%d" % i, [128, 512]) for i in range(1)] * 2; b_Yst = [Buf()] * 2
            E["tw"], b_E["tw"] = t1, b_t1
            E["gexc"], b_E["gexc"] = E["d1"], b_E["d1"]
            E["gtail"], b_E["gtail"] = E["d2"], b_E["d2"]
            E["kk2"], b_E["kk2"] = E["tmp"], b_E["tmp"]
            pending = []
            nsteps_run = len(steps) if dbg.get("rwkv_steps") is None else dbg["rwkv_steps"]
            yield

            def do_step(g):
                cf, nj = steps[g]
                W = nj * 64
                f0 = cf * 64
                b0 = T - f0 - W
                if g == 0:
                    C.load_raw(W, f0, b0)
                while pending:
                    pending.pop(0)()
                C.prep_common(W)
                yield
                P.op("act", lambda e: e.activation(out=E["tw"][:, 0:W], in_=X["wd"][:, 0:W], func=AF.Tanh), reads=[b_X, b_t1], writes=[b_E["tw"]])
                for h in (0, 64):
                    P.op("pe", lambda e, h=h: e.matmul(out=PBk(1)[h:h + 64, 0:W], lhsT=lw[h:h + 64, :], rhs=E["tw"][h:h + 64, 0:W], start=True, stop=True),
                         reads=[b_pp, b_E["tw"]], writes=[bPBk(1)])
                P.op("act", lambda e: e.activation(out=E["sg"][:, 0:W], in_=PBk(1)[:, 0:W], func=AF.Sigmoid, bias=pp[:, 15:16]),
                     reads=[bPBk(1), b_pp], writes=[b_E["sg"]])
                P.op("dve", lambda e: e.tensor_tensor_scan(out=E["cs"][:, 0:W], data0=RST[:, 0:W], data1=E["sg"][:, 0:W], initial=0.0, op0=ALU.mult, op1=ALU.add),
                     reads=[b_E["sg"], b_pp], writes=[b_E["cs"]])
                cs3 = E["cs"][:, 0:W].rearrange("p (j t) -> p j t", t=64)
                ceb = cs3[:, :, 63:64].to_broadcast([128, nj, 64])
                x3 = E["x"][:, 0:W].rearrange("p (j t) -> p j t", t=64)
                P.op("dve", lambda e: e.tensor_copy(out=E["x"][0:64, 0:W], in_=E["cs"][0:64, 0:W]), reads=[b_E["cs"]], writes=[b_E["x"]])
                P.op("dve", lambda e: e.tensor_tensor(out=x3[64:128], in0=ceb[64:128], in1=cs3[64:128], op=ALU.subtract), reads=[b_E["cs"]], writes=[b_E["x"]])
                P.op("dve", lambda e: e.tensor_tensor(out=E["x"][64:128, 0:W], in0=E["x"][64:128, 0:W], in1=E["sg"][64:128, 0:W], op=ALU.add),
                     reads=[b_E["x"], b_E["sg"]], writes=[b_E["x"]])
                yield
                P.op("dve", lambda e: e.tensor_tensor(out=E["d1"][:, 0:W], in0=E["x"][:, 0:W], in1=E["sg"][:, 0:W], op=ALU.subtract),
                     reads=[b_E["x"], b_E["sg"]], writes=[b_E["d1"]])
                d23 = E["d2"][:, 0:W].rearrange("p (j t) -> p j t", t=64)
                P.op("dve", lambda e: e.tensor_tensor(out=d23, in0=ceb, in1=x3, op=ALU.subtract), reads=[b_E["cs"], b_E["x"]], writes=[b_E["d2"]])
                P.op("act", lambda e: e.activation(out=E["ginc"][:, 0:W], in_=E["x"][:, 0:W], func=AF.Exp, scale=-K0), reads=[b_E["x"]], writes=[b_E["ginc"]])
                P.op("act", lambda e: e.activation(out=E["ginv"][:, 0:W], in_=E["x"][:, 0:W], func=AF.Exp, scale=K0), reads=[b_E["x"]], writes=[b_E["ginv"]])
                P.op("act", lambda e: e.activation(out=E["gexc"][:, 0:W], in_=E["d1"][:, 0:W], func=AF.Exp, scale=-K0), reads=[b_E["d1"]], writes=[b_E["gexc"]])
                P.op("act", lambda e: e.activation(out=E["gtail"][:, 0:W], in_=E["d2"][:, 0:W], func=AF.Exp, scale=-K0), reads=[b_E["d2"]], writes=[b_E["gtail"]])
                P.op("act", lambda e: e.activation(out=GE[:, 0:nj], in_=cs3[:, :, 63], func=AF.Exp, scale=-K0), reads=[b_E["cs"]], writes=[b_GE])
                yield
                P.op("dve", lambda e: e.tensor_scalar(out=E["kk"][:, 0:W], in0=X["k"][:, 0:W], scalar1=pp[:, 17:18], scalar2=None, op0=ALU.mult),
                     reads=[b_X, b_pp], writes=[b_E["kk"]])
                P.op("act", lambda e: e.activation(out=E["kk2"][:, 0:W], in_=E["kk"][:, 0:W], func=AF.Square), reads=[b_E["kk"], b_E["tmp"]], writes=[b_E["kk2"]])
                P.op("pe", lambda e: e.matmul(out=PBk(2)[:, 0:W], lhsT=BLK, rhs=E["kk2"][:, 0:W], start=True, stop=True), reads=[b_cst, b_E["kk2"]], writes=[bPBk(2)])
                P.op("dve", lambda e: e.tensor_scalar(out=E["rn"][:, 0:W], in0=PBk(2)[:, 0:W], scalar1=1e-24, scalar2=None, op0=ALU.max),
                     reads=[bPBk(2)], writes=[b_E["rn"]])
                P.op("act", lambda e: e.activation(out=E["rn"][:, 0:W], in_=E["rn"][:, 0:W], func=AF.Sqrt), reads=[b_E["rn"]], writes=[b_E["rn"]])
                P.op("dve", lambda e: e.reciprocal(out=E["rn"][:, 0:W], in_=E["rn"][:, 0:W]), reads=[b_E["rn"]], writes=[b_E["rn"]])
                P.op("dve", lambda e: e.tensor_tensor(out=E["kap"][:, 0:W], in0=E["kk"][:, 0:W], in1=E["rn"][:, 0:W], op=ALU.mult),
                     reads=[b_E["kk"], b_E["rn"]], writes=[b_E["kap"]])
                P.op("dve", lambda e: e.tensor_tensor(out=E["kapa"][:, 0:W], in0=E["kap"][:, 0:W], in1=E["a"][:, 0:W], op=ALU.mult),
                     reads=[b_E["kap"], b_E["a"]], writes=[b_E["kapa"]])
                yield

                def v4(tns, slot_):
                    return tns[:, 0:nj, slot_, :]

                def e3(n):
                    return E[n][:, 0:W].rearrange("p (j t) -> p j t", t=64)
                x3r = X["r"][:, 0:W].rearrange("p (j t) -> p j t", t=64)
                x3v = X["v"][:, 0:W].rearrange("p (j t) -> p j t", t=64)
                P.op("dve", lambda e: e.scalar_tensor_tensor(out=v4(AR, 0), in0=e3("kap"), scalar=-1.0, op0=ALU.mult, in1=e3("gexc"), op1=ALU.mult),
                     reads=[b_E["kap"], b_E["gexc"]], writes=[b_AR])
                P.op("dve", lambda e: e.tensor_tensor(out=v4(AR, 1), in0=x3r, in1=e3("ginc"), op=ALU.mult), reads=[b_X, b_E["ginc"]], writes=[b_AR])

                def halves(tns, n_beta, n_k, g_, b_dst, eng):
                    for (h, sb_, sk_) in ((0, 0, 1), (64, 1, 0)):
                        P.op(eng, lambda e, h=h, sb_=sb_: e.tensor_tensor(out=tns[h:h + 64, 0:nj, sb_, :], in0=e3(n_beta)[h:h + 64], in1=e3(g_)[h:h + 64], op=ALU.mult),
                             reads=[b_E[n_beta], b_E[g_]], writes=[b_dst])
                        P.op(eng, lambda e, h=h, sk_=sk_: e.tensor_tensor(out=tns[h:h + 64, 0:nj, sk_, :], in0=e3(n_k)[h:h + 64], in1=e3(g_)[h:h + 64], op=ALU.mult),
                             reads=[b_E[n_k], b_E[g_]], writes=[b_dst])
                halves(BK, "kapa", "kd", "ginv", b_BK, "dve")
                halves(BKH, "kapa", "kd", "gtail", b_BKH, "pool")
                P.op("pool", lambda e: e.tensor_copy(out=v4(VV, 0), in_=x3v), reads=[b_X], writes=[b_VV])
                P.op("pool", lambda e: e.tensor_copy(out=v4(VV, 1), in_=x3v), reads=[b_X], writes=[b_VV])
                if g + 1 < nsteps_run:
                    cf2, nj2 = steps[g + 1]
                    C.load_raw(nj2 * 64, cf2 * 64, T - cf2 * 64 - nj2 * 64)
                yield

                def uidx(d, j):
                    return d * 8 + j
                SBm = {0: 0, 1: 1}
                for d in (0, 1):
                    h = 64 * d
                    msk = (MF if d == 0 else MB)
                    for j0 in range(0, nj, 4):
                        n_in = min(4, nj - j0)
                        bk = 2 * d + (j0 // 4) % 2
                        for jo in range(n_in):
                            j = j0 + jo
                            off = jo * 128
                            lhs = BK[h:h + 64, j, :, :].rearrange("p a t -> p (a t)")
                            rhs = AR[h:h + 64, j, :, :].rearrange("p a t -> p (a t)")
                            P.op("pe", lambda e, bk=bk, off=off, lhs=lhs, rhs=rhs: e.matmul(out=PBk(bk)[:, off:off + 128], lhsT=lhs, rhs=rhs, start=True, stop=True),
                                 reads=[b_BK, b_AR], writes=[bPBk(bk)])
                        u0 = uidx(d, j0)
                        P.op("dve", lambda e, bk=bk, n_in=n_in, u0=u0, msk=msk: e.tensor_tensor(
                            out=MX[:, u0:u0 + n_in, :], in0=PBk(bk)[:, 0:n_in * 128].rearrange("p (u t) -> p u t", t=128),
                            in1=msk.unsqueeze(1).to_broadcast([128, n_in, 128]), op=ALU.mult), reads=[bPBk(bk), b_cst], writes=[b_MX])
                    yield
                for d in (0, 1):
                    h = 64 * d
                    for j in range(nj):
                        lhs = AR[h:h + 64, j, 0, :]
                        rhs = BK[h:h + 64, j, SBm[d], :]
                        P.op("pe", lambda e, j=j, h=h, lhs=lhs, rhs=rhs: e.matmul(out=PBk(0)[h:h + 64, j * 64:(j + 1) * 64], lhsT=lhs, rhs=rhs, start=True, stop=True),
                             reads=[b_AR, b_BK], writes=[bPBk(0)])
                P.op("dve", lambda e: e.tensor_tensor(
                    out=Nm[0][:, 0:nj, :], in0=PBk(0)[:, 0:nj * 64].rearrange("p (u t) -> p u t", t=64),
                    in1=cst[:, 384:448].unsqueeze(1).to_broadcast([128, nj, 64]), op=ALU.mult), reads=[bPBk(0), b_cst], writes=[b_Nm[0]])
                for d in (0, 1):
                    h = 64 * d
                    P.op("dve", lambda e, d=d, h=h: e.tensor_tensor(out=Rm[h:h + 64, 0:nj, :], in0=MX[h:h + 64, d * 8:d * 8 + nj, 0:64],
                                                                    in1=cst[h:h + 64, h:h + 64].unsqueeze(1).to_broadcast([64, nj, 64]), op=ALU.add),
                         reads=[b_MX, b_cst], writes=[b_Rm])
                    P.op("pool", lambda e, d=d, h=h: e.tensor_copy(out=Mm[0][h:h + 64, 0:nj, :], in_=MX[h:h + 64, d * 8:d * 8 + nj, 0:64]), reads=[b_MX], writes=[b_Mm[0]])
                    P.op("act", lambda e, h=h: e.copy(out=Rb[h:h + 64, 0:nj, :], in_=Rm[h:h + 64, 0:nj, :]), reads=[b_Rm], writes=[b_Rb])
                yield

                def prod(bank0, lhs_fn, rhs_fn, reads, evac, rev=False):
                    bk = bank0 // 2
                    for d in (0, 1):
                        h = 64 * d
                        for j in range(nj):
                            lhs = lhs_fn(d, h, j)
                            rhs = rhs_fn(d, h, j)
                            jp = (nj - 1 - j) if (rev and d == 1) else j
                            P.op("pe", lambda e, jp=jp, h=h, lhs=lhs, rhs=rhs: e.matmul(out=PBk(bk)[h:h + 64, jp * 64:(jp + 1) * 64], lhsT=lhs, rhs=rhs, start=True, stop=True),
                                 reads=reads, writes=[bPBk(bk)])
                    evac(bk, PBk(bk)[:, 0:nj * 64].rearrange("p (u t) -> p u t", t=64))

                def ev_copy(dst_t, b_dst, eng):
                    def f(bk, pv):
                        dst = dst_t[:, 0:nj, :]
                        if eng == "act":
                            P.op("act", lambda e: e.copy(out=dst, in_=pv), reads=[bPBk(bk)], writes=[b_dst])
                        else:
                            P.op("dve", lambda e: e.tensor_copy(out=dst, in_=pv), reads=[bPBk(bk)], writes=[b_dst])
                    return f

                def ev_acc(dst_t, b_dst):
                    def f(bk, pv):
                        dst = dst_t[:, 0:nj, :]
                        P.op("dve", lambda e: e.tensor_tensor(out=dst, in0=dst, in1=pv, op=ALU.add), reads=[bPBk(bk), b_dst], writes=[b_dst])
                    return f

                cur = 0
                for k in range(1, 6):
                    nxt = 1 - cur
                    Nc, Mc, Nn, Mn = Nm[cur], Mm[cur], Nm[nxt], Mm[nxt]
                    if k <= 4:
                        prod(0, lambda d, h, j, Nc=Nc: Nc[h:h + 64, j, :], lambda d, h, j, Mc=Mc: Mc[h:h + 64, j, :], [b_Nm[cur], b_Mm[cur]], ev_copy(Mn, b_Mm[nxt], "act"))
                    prod(2, lambda d, h, j, Mc=Mc: Mc[h:h + 64, j, :], lambda d, h, j, Nc=Nc: Nc[h:h + 64, j, :], [b_Mm[cur], b_Nm[cur]], ev_copy(Nn, b_Nm[nxt], "dve"))
                    yield
                    prod(4, lambda d, h, j, Nn=Nn: Nn[h:h + 64, j, :], lambda d, h, j: Rb[h:h + 64, j, :], [b_Nm[nxt], b_Rb], ev_acc(Rm, b_Rm))
                    if k < 5:
                        P.op("act", lambda e: e.copy(out=Rb[:, 0:nj, :], in_=Rm[:, 0:nj, :]), reads=[b_Rm], writes=[b_Rb])
                    yield
                    cur = nxt
                for (srct, b_src, dst, b_dst, bks) in ((BKH, b_BKH, BKHs, b_BKHs, (0, 1)), (VV, b_VV, V2s, b_V2s, (2, 3))):
                    for d in (0, 1):
                        h = 64 * d
                        bk = bks[d]
                        for j in range(nj):
                            P.op("pe", lambda e, h=h, j=j, bk=bk, srct=srct: e.transpose(out=PBk(bk)[:, j * 64:(j + 1) * 64], in_=srct[h:h + 64, j, :, :].rearrange("p a t -> p (a t)"),
                                                                                        identity=cst[h:h + 64, h:h + 64]),
                                 reads=[b_src, b_cst], writes=[bPBk(bk)])
                        P.op("act", lambda e, d=d, bk=bk, dst=dst: e.copy(out=dst[:, d * 8:d * 8 + nj, :], in_=PBk(bk)[:, 0:nj * 64].rearrange("p (u t) -> p u t", t=64)),
                             reads=[bPBk(bk)], writes=[b_dst])
                    yield

                def hv_of(h):
                    return 64 - h
                prod(0, lambda d, h, j: MX[hv_of(h):hv_of(h) + 64, uidx(d, j), 0:64], lambda d, h, j: V2s[hv_of(h):hv_of(h) + 64, uidx(d, j), :],
                     [b_MX, b_V2s], ev_copy(BVs, b_BVs, "act"), rev=True)
                prod(2, lambda d, h, j: V2s[hv_of(h):hv_of(h) + 64, uidx(d, j), :], lambda d, h, j: MX[hv_of(h):hv_of(h) + 64, uidx(d, j), 64:128],
                     [b_MX, b_V2s], ev_copy(YVs, b_YVs, "dve"))
                yield
                prod(4, lambda d, h, j: BKHs[hv_of(h):hv_of(h) + 64, uidx(d, j), :], lambda d, h, j: V2s[hv_of(h):hv_of(h) + 64, uidx(d, j), :],
                     [b_BKHs, b_V2s], ev_copy(KVs, b_KVs, "act"), rev=True)
                yield
                for i in range(nj):
                    jj = {0: i, 1: nj - 1 - i}
                    for d in (0, 1):
                        h = 64 * d
                        j = jj[d]
                        P.op("pe", lambda e, h=h, j=j: e.matmul(out=PBk(0)[h:h + 64, 0:64], lhsT=AR[h:h + 64, j, 0, :], rhs=ST[h:h + 64, :], start=True, stop=True),
                             reads=[b_AR, b_ST], writes=[bPBk(0)])
                    P.op("dve", lambda e, i=i: e.tensor_tensor(out=Ws[:, :], in0=PBk(0)[:, 0:64], in1=BVs[:, i, :], op=ALU.add),
                         reads=[bPBk(0), b_BVs], writes=[b_Ws])
                    for d in (0, 1):
                        h = 64 * d
                        j = jj[d]
                        P.op("dve", lambda e, h=h, j=j, i=i: e.scalar_tensor_tensor(out=ST2[h:h + 64, :], in0=ST[h:h + 64, :], scalar=GE[h:h + 64, j:j + 1], op0=ALU.mult,
                                                                                  in1=KVs[h:h + 64, i, :], op1=ALU.add), reads=[b_ST, b_GE, b_KVs], writes=[b_ST2])
                    yield
                    for d in (0, 1):
                        h = 64 * d
                        j = jj[d]
                        P.op("pe", lambda e, h=h, j=j: e.matmul(out=PBk(0)[h:h + 64, 64:128], lhsT=Rm[h:h + 64, j, :], rhs=Ws[h:h + 64, :], start=True, stop=True),
                             reads=[b_Rm, b_Ws], writes=[bPBk(0)])
                    P.op("act", lambda e: e.copy(out=Us[:, :], in_=PBk(0)[:, 64:128]), reads=[bPBk(0)], writes=[b_Us])
                    yield
                    for d in (0, 1):
                        h = 64 * d
                        j = jj[d]
                        u = uidx(d, j)
                        P.op("pe", lambda e, h=h, u=u: e.matmul(out=PBk(0)[h:h + 64, 128:192], lhsT=BKHs[h:h + 64, u, :], rhs=Us[h:h + 64, :], start=True, stop=True),
                             reads=[b_BKHs, b_Us], writes=[bPBk(0)])
                        P.op("pe", lambda e, h=h, j=j, d=d: e.matmul(out=PBk(2 + d)[h:h + 64, j * 64:(j + 1) * 64], lhsT=ST[h:h + 64, :], rhs=AR[h:h + 64, j, 1, :], start=True, stop=False),
                             reads=[b_ST, b_AR], writes=[bPBk(2 + d)])
                        P.op("pe", lambda e, h=h, j=j, u=u, d=d: e.matmul(out=PBk(2 + d)[h:h + 64, j * 64:(j + 1) * 64], lhsT=Us[h:h + 64, :], rhs=MX[h:h + 64, u, 64:128], start=False, stop=True),
                             reads=[b_Us, b_MX], writes=[bPBk(2 + d)])
                    P.op("dve", lambda e: e.tensor_tensor(out=ST[:, :], in0=ST2[:, :], in1=PBk(0)[:, 128:192], op=ALU.add),
                         reads=[b_ST2, bPBk(0)], writes=[b_ST])
                    yield
                ys = g % 2
                for d in (0, 1):
                    h = 64 * d
                    P.op("dve", lambda e, h=h, d=d: e.tensor_tensor(out=Yst[ys][h:h + 64, 0:W].rearrange("p (j t) -> p j t", t=64),
                                                                    in0=PBk(2 + d)[h:h + 64, 0:W].rearrange("p (j t) -> p j t", t=64),
                                                                    in1=YVs[h:h + 64, 0:nj, :], op=ALU.add), reads=[bPBk(2 + d), b_YVs], writes=[b_Yst[ys]])
                def flush(ys=ys, f0=f0, b0=b0, W=W):
                    P.dma(lambda e: e.dma_start(out=YD[kind][pi, 0:64, f0:f0 + W], in_=Yst[ys][0:64, 0:W]), reads=[b_Yst[ys]], writes=[b_YD])
                    P.dma(lambda e: e.dma_start(out=YD[kind][pi, 64:128, b0:b0 + W], in_=Yst[ys][64:128, 0:W]), reads=[b_Yst[ys]], writes=[b_YD])
                pending.append(flush)
                yield

            for g in range(nsteps_run):
                yield from do_step(g)
            while pending:
                pending.pop(0)()

        def rwkv_stage_e(st, kind, pi, slot):
            C = RwkvCtx(st, kind, pi, slot, False)
            T, F, pp, lg = C.T, C.F, C.pp, C.lg
            X, E, t1 = C.X, C.E, C.t1
            b_X, b_E, b_pp, b_t1 = C.b_X, C.b_E, C.b_pp, C.b_t1
            PBk, bPBk = C.PBk, C.bPBk
            WW = 512
            rawg = F("rawg", [128, WW + 2]); b_rawg = Buf()
            Xg = F("Xg", [128, WW]); b_Xg = Buf()
            Yt = F("Yt", [128, WW]); b_Yt = Buf()
            Ys = F("Ysum", [64, WW]); b_Ys = Buf()
            Dd = F("Dd", [64, WW]); b_Dd = Buf()
            D2 = F("D2", [64, WW]); b_D2 = Buf()
            Rs = F("Rs", [64, WW]); b_Rs = Buf()
            Oa = [F("Oa%d" % i, [64, WW], BF16) for i in range(2)]; b_Oa = [Buf(), Buf()]
            tiles = [(t0, 512) for t0 in range(0, 2048, 512)]
            pending = []
            yield

            def do_tile(ti, t0, W):
                C.load_raw(W, t0, t0)
                P.dma(lambda e: e.dma_start(out=rawg[:, 0:W + 2], in_=C.paL[128:256, t0:t0 + W + 2]), reads=[b_PAO], writes=[b_rawg])
                P.dma(lambda e: e.dma_start(out=Yt[:, 0:W], in_=YDO[kind][pi, :, t0:t0 + W]), reads=[b_YDO], writes=[b_Yt])
                while pending:
                    pending.pop(0)()
                C.prep_common(W)
                yield
                P.op("act", lambda e: e.activation(out=t1[:, 0:W], in_=rawg[:, 1:W + 1], func=AF.Identity, scale=pp[:, 22:23]), reads=[b_rawg, b_pp], writes=[b_t1])
                P.op("dve", lambda e: e.scalar_tensor_tensor(out=t1[:, 0:W], in0=rawg[:, 0:W], scalar=pp[:, 23:24], op0=ALU.mult, in1=t1[:, 0:W], op1=ALU.add),
                     reads=[b_rawg, b_pp, b_t1], writes=[b_t1])
                P.op("dve", lambda e: e.scalar_tensor_tensor(out=Xg[:, 0:W], in0=rawg[:, 2:W + 2], scalar=pp[:, 24:25], op0=ALU.mult, in1=t1[:, 0:W], op1=ALU.add),
                     reads=[b_rawg, b_pp, b_t1], writes=[b_Xg])
                P.op("act", lambda e: e.activation(out=Xg[:, 0:W], in_=Xg[:, 0:W], func=AF.Sigmoid), reads=[b_Xg], writes=[b_Xg])
                P.op("dve", lambda e: e.scalar_tensor_tensor(out=E["tmp"][:, 0:W], in0=X["r"][:, 0:W], scalar=pp[:, 19:20], op0=ALU.mult, in1=E["kd"][:, 0:W], op1=ALU.mult),
                     reads=[b_X, b_pp, b_E["kd"]], writes=[b_E["tmp"]])
                P.op("pe", lambda e: e.matmul(out=PBk(1)[0:64, 0:W], lhsT=ONES[:, 0:64], rhs=E["tmp"][:, 0:W], start=True, stop=True), reads=[b_cst, b_E["tmp"]], writes=[bPBk(1)])
                P.op("pe", lambda e: e.matmul(out=PBk(2)[0:64, 0:W], lhsT=SEL, rhs=Yt[:, 0:W], start=True, stop=True), reads=[b_cst, b_Yt], writes=[bPBk(2)])
                P.op("act", lambda e: e.copy(out=Ys[:, 0:W], in_=PBk(2)[0:64, 0:W]), reads=[bPBk(2)], writes=[b_Ys])
                yield
                P.op("pe", lambda e: e.matmul(out=PBk(3)[0:64, 0:W], lhsT=O64, rhs=Ys[:, 0:W], start=True, stop=True), reads=[b_cst, b_Ys], writes=[bPBk(3)])
                P.op("dve", lambda e: e.tensor_tensor(out=Dd[:, 0:W], in0=Ys[:, 0:W], in1=PBk(3)[0:64, 0:W], op=ALU.subtract), reads=[b_Ys, bPBk(3)], writes=[b_Dd])
                P.op("act", lambda e: e.activation(out=D2[:, 0:W], in_=Dd[:, 0:W], func=AF.Square), reads=[b_Dd], writes=[b_D2])
                P.op("pe", lambda e: e.matmul(out=PBk(3)[0:64, 0:W], lhsT=O64, rhs=D2[:, 0:W], start=True, stop=True), reads=[b_cst, b_D2], writes=[bPBk(3)])
                P.op("act", lambda e: e.activation(out=Rs[:, 0:W], in_=PBk(3)[0:64, 0:W], func=AF.Sqrt, bias=epsb[0:64, 2:3]), reads=[bPBk(3), b_cst], writes=[b_Rs])
                yield
                P.op("dve", lambda e: e.reciprocal(out=Rs[:, 0:W], in_=Rs[:, 0:W]), reads=[b_Rs], writes=[b_Rs])
                P.op("dve", lambda e: e.tensor_tensor(out=Dd[:, 0:W], in0=Dd[:, 0:W], in1=Rs[:, 0:W], op=ALU.mult), reads=[b_Dd, b_Rs], writes=[b_Dd])
                P.op("dve", lambda e: e.tensor_scalar(out=Dd[:, 0:W], in0=Dd[:, 0:W], scalar1=pp[0:64, 20:21], scalar2=pp[0:64, 21:22], op0=ALU.mult, op1=ALU.add),
                     reads=[b_Dd, b_pp], writes=[b_Dd])
                P.op("dve", lambda e: e.tensor_tensor(out=D2[:, 0:W], in0=PBk(1)[0:64, 0:W], in1=X["v"][0:64, 0:W], op=ALU.mult), reads=[bPBk(1), b_X], writes=[b_D2])
                P.op("dve", lambda e: e.tensor_tensor(out=Dd[:, 0:W], in0=Dd[:, 0:W], in1=D2[:, 0:W], op=ALU.add), reads=[b_Dd, b_D2], writes=[b_Dd])
                P.op("pe", lambda e: e.matmul(out=PBk(0)[0:64, 0:W], lhsT=lg[:, :], rhs=Xg[:, 0:W], start=True, stop=True), reads=[b_pp, b_Xg], writes=[bPBk(0)])
                ob = ti % 2
                P.op("dve", lambda e: e.tensor_tensor(out=Oa[ob][:, 0:W], in0=Dd[:, 0:W], in1=PBk(0)[0:64, 0:W], op=ALU.mult), reads=[b_Dd, bPBk(0)], writes=[b_Oa[ob]])
                def flush(ob=ob, t0=t0, W=W):
                    P.dma(lambda e: e.dma_start(out=OAO[kind][pi, :, t0:t0 + W], in_=Oa[ob][:, 0:W]), reads=[b_Oa[ob]], writes=[b_OAO])
                pending.append(flush)
                yield

            if not dbg.get("skip_stage_e"):
                for ti, (t0, W) in enumerate(tiles):
                    yield from do_tile(ti, t0, W)
                while pending:
                    pending.pop(0)()

        def rwkv_group(kind, plist, stage):
            with ExitStack() as st:
                if stage == 0:
                    run_streams([rwkv_steps(st, kind, p, i) for i, p in enumerate(plist)])
                else:
                    run_streams([rwkv_stage_e(st, kind, p, i) for i, p in enumerate(plist)])
                P.barrier()

        kinds = dbg.get("kinds", [0, 1])
        for kind in kinds:
            if do_p1:
                p1_phase(kind)
                rotate_phase(kind)
            if do_attn:
                for p in pieces_run:
                    attn_job(kind, p)
            if do_rwkv:
                ns = dbg.get("streams", 2)
                for i in range(0, len(pieces_run), ns):
                    rwkv_group(kind, pieces_run[i:i + ns], 0)
                P.barrier()
                own_y_phase(kind)
                for i in range(0, len(pieces_run), ns):
                    rwkv_group(kind, pieces_run[i:i + ns], 1)
            P.barrier()

        def post_phase(kind, xsrc, x_row0, yout, tg):
            with ExitStack() as st:
                def F(name, shape, dt=F32):
                    return sb(st, tg + name, shape, dt)
                wg = F("wg", [128, 8, 2048], BF16); wua = F("wua", [128, 4, D], BF16); wub = F("wub", [128, 4, D], BF16)
                wo = [F("wo%d" % i, [128, 8, 128], BF16) for i in range(2)]; b_wo = [Buf(), Buf()]
                b_w = Buf()
                P.dma(lambda e: e.dma_start(out=wg[:], in_=wg_b.rearrange("(k p) m -> p k m", p=128)), reads=[b_wsc], writes=[b_w])
                P.dma(lambda e: e.dma_start(out=wua[:], in_=wua_b.rearrange("(k p) m -> p k m", p=128)), reads=[b_wsc], writes=[b_w])
                P.dma(lambda e: e.dma_start(out=wub[:], in_=wub_b.rearrange("(k p) m -> p k m", p=128)), reads=[b_wsc], writes=[b_w])
                w1 = [F("w1_%d" % i, [128, 8, 256], BF16) for i in range(2)]; b_w1 = [Buf(), Buf()]
                w2 = [F("w2_%d" % i, [128, 32, 128], BF16) for i in range(2)]; b_w2 = [Buf(), Buf()]
                gfin = F("gfin", [128, D]); slg = F("slg", [128, 1]); lamt = F("lamt", [128, 4, 64]); lamv = F("lamv", [128, 8])
                b_c = Buf()
                P.dma(lambda e: e.dma_start(out=gfin[:], in_=gfin_d), writes=[b_c])
                P.dma(lambda e: e.dma_start(out=slg[:], in_=slg_d), writes=[b_c])
                P.dma(lambda e: e.dma_start(out=lamt[:], in_=lam_d), writes=[b_c])
                P.op("dve", lambda e: e.tensor_tensor(out=lamt[:, 0, :], in0=lamt[:, 0, :], in1=lamt[:, 1, :], op=ALU.mult), reads=[b_c], writes=[b_c])
                P.op("dve", lambda e: e.tensor_tensor(out=lamt[:, 2, :], in0=lamt[:, 2, :], in1=lamt[:, 3, :], op=ALU.mult), reads=[b_c], writes=[b_c])
                P.op("dve", lambda e: e.tensor_reduce(out=lamv[:, 0:1], in_=lamt[:, 0, :], op=ALU.add, axis=mybir.AxisListType.X), reads=[b_c], writes=[b_c])
                P.op("dve", lambda e: e.tensor_reduce(out=lamv[:, 1:2], in_=lamt[:, 2, :], op=ALU.add, axis=mybir.AxisListType.X), reads=[b_c], writes=[b_c])
                P.op("act", lambda e: e.activation(out=lamv[:, 2:4], in_=lamv[:, 0:2], func=AF.Exp), reads=[b_c], writes=[b_c])
                P.op("dve", lambda e: e.tensor_tensor(out=lamv[:, 4:5], in0=lamv[:, 3:4], in1=lamv[:, 2:3], op=ALU.subtract), reads=[b_c], writes=[b_c])
                P.op("dve", lambda e: e.tensor_scalar(out=lamv[:, 4:5], in0=lamv[:, 4:5], scalar1=-LAMBDA_INIT, scalar2=None, op0=ALU.add), reads=[b_c], writes=[b_c])
                P.op("dve", lambda e: e.tensor_scalar(out=slg[:], in0=slg[:], scalar1=1.0 - LAMBDA_INIT, scalar2=None, op0=ALU.mult), reads=[b_c], writes=[b_c])
                xres = F("xres", [128, 4, D]); b_xres = Buf()
                h2 = F("h2", [128, 4, D]); b_h2 = Buf()
                sq_sh = F("sqsh", [128, D])
                tmp = [(sq_sh, F("ss%d" % i, [128, 4]), F("xn%d" % i, [128, D], BF16), Buf()) for i in range(2)]
                nT = F("nT", [128, 8, 512], BF16); b_nT = Buf()
                oaT = F("oaT", [128, 4, 512], BF16); b_oaT = Buf()
                AB = F("AB", [128, 2, 512], BF16); b_AB = Buf()
                Dh = F("Dh", [128, 512]); b_Dh = Buf()
                D2h = F("D2h", [128, 512]); b_D2h = Buf()
                rsh = F("rsh", [128, 512]); b_rsh = Buf()
                obT = F("obT", [128, 4, 512], BF16); b_obT = Buf()
                sg = [F("sg%d" % i, [128, 512]) for i in range(2)]; b_sg = [Buf(), Buf()]
                ma = F("ma", [128, 512]); b_ma = Buf()
                mT = F("mT", [128, 8, 512], BF16); b_mT = Buf()
                ao = [F("ao%d" % i, [128, 512]) for i in range(2)]; b_ao = [Buf(), Buf()]
                hT = F("hT", [128, 32, 512], BF16); b_hT = Buf()
                hx = [F("hx%d" % i, [128, 512]) for i in range(2)]; b_hx = [Buf(), Buf()]
                yo = xres; b_yo = b_xres
                sqf = tmp[0][0]; ssf = F("ssf", [128, 4]); b_f = tmp[0][3]
                wcnt = {"w1": 0, "w2": 0}

                def do_tt(tt):
                    tk0 = tt * 512
                    for s in range(4):
                        P.dma(lambda e, s=s: e.dma_start(out=xres[:, s, :], in_=xsrc[x_row0 + tk0 + s * 128:x_row0 + tk0 + (s + 1) * 128, :]), writes=[b_xres])
                    for q in range(4):
                        for hh in range(2):
                            P.dma(lambda e, q=q, hh=hh: e.dma_start(out=oaT[hh * 64:(hh + 1) * 64, q, :], in_=OAO[kind][2 * q + hh, :, tk0:tk0 + 512]),
                                  reads=[b_OAO], writes=[b_oaT])
                    for s in range(4):
                        norm_T2(tmp[s % 2], xres[:, s, :], b_xres, nT[:, :, s * 128:(s + 1) * 128], b_nT, s % 2)
                    for hh in range(4):
                        for mm in range(2):
                            P.dma(lambda e, hh=hh, mm=mm: e.dma_start(out=AB[:, mm, :], in_=XR[kind][2 * hh + mm, :, tk0:tk0 + 512]), reads=[b_XR[kind]], writes=[b_AB])
                        P.op("dve", lambda e, hh=hh: e.scalar_tensor_tensor(out=Dh[:], in0=AB[:, 1, :], scalar=lamv[:, 4:5], op0=ALU.mult, in1=AB[:, 0, :], op1=ALU.add),
                             reads=[b_AB, b_c], writes=[b_Dh])
                        P.op("act", lambda e: e.activation(out=D2h[:], in_=Dh[:], func=AF.Square), reads=[b_Dh], writes=[b_D2h])
                        P.op("pe", lambda e: e.matmul(out=PB[2][:, :], lhsT=O128, rhs=D2h[:], start=True, stop=True), reads=[b_cst, b_D2h], writes=[bPB[2]])
                        P.op("act", lambda e: e.activation(out=rsh[:], in_=PB[2][:, :], func=AF.Sqrt, bias=epsb[:, 1:2]), reads=[bPB[2], b_cst], writes=[b_rsh])
                        P.op("dve", lambda e: e.reciprocal(out=rsh[:], in_=rsh[:]), reads=[b_rsh], writes=[b_rsh])
                        P.op("dve", lambda e, hh=hh: e.scalar_tensor_tensor(out=obT[:, hh, :], in0=Dh[:], scalar=slg[:, 0:1], op0=ALU.mult, in1=rsh[:], op1=ALU.mult),
                             reads=[b_Dh, b_rsh, b_c], writes=[b_obT])
                    for mo in range(8):
                        for br, (wu, src, b_src) in enumerate(((wua, oaT, b_oaT), (wub, obT, b_obT))):
                            gb = 3 + br
                            ub = 5 + br
                            for k in range(8):
                                P.op("pe", lambda e, k=k, mo=mo, br=br, gb=gb: e.matmul(out=PB[gb][:, :], lhsT=wg[:, k, br * 1024 + mo * 128:br * 1024 + (mo + 1) * 128], rhs=nT[:, k, :],
                                                                                        start=(k == 0), stop=(k == 7)), reads=[b_w, b_nT], writes=[bPB[gb]])
                            P.op("act", lambda e, br=br, gb=gb: e.activation(out=sg[br][:], in_=PB[gb][:, :], func=AF.Sigmoid), reads=[bPB[gb]], writes=[b_sg[br]])
                            for k in range(4):
                                P.op("pe", lambda e, k=k, mo=mo, wu=wu, src=src, ub=ub: e.matmul(out=PB[ub][:, :], lhsT=wu[:, k, mo * 128:(mo + 1) * 128], rhs=src[:, k, :],
                                                                                                 start=(k == 0), stop=(k == 3)), reads=[b_w, b_src], writes=[bPB[ub]])
                        P.op("dve", lambda e: e.tensor_tensor(out=ma[:], in0=sg[0][:], in1=PB[5][:, :], op=ALU.mult), reads=[b_sg[0], bPB[5]], writes=[b_ma])
                        P.op("dve", lambda e: e.tensor_tensor(out=sg[1][:], in0=sg[1][:], in1=PB[6][:, :], op=ALU.mult), reads=[b_sg[1], bPB[6]], writes=[b_sg[1]])
                        P.op("pool", lambda e, mo=mo: e.tensor_tensor(out=mT[:, mo, :], in0=ma[:], in1=sg[1][:], op=ALU.add), reads=[b_ma, b_sg[1]], writes=[b_mT])
                    for mo in range(8):
                        ab = mo % 2
                        P.dma(lambda e, ab=ab, mo=mo: e.dma_start(out=wo[ab][:], in_=wout_b[:, mo * 128:(mo + 1) * 128].rearrange("(k p) m -> p k m", p=128)),
                              reads=[b_wsc], writes=[b_wo[ab]])
                        for k in range(8):
                            P.op("pe", lambda e, k=k, ab=ab: e.matmul(out=PB[3][:, :], lhsT=wo[ab][:, k, :], rhs=mT[:, k, :], start=(k == 0), stop=(k == 7)),
                                 reads=[b_wo[ab], b_mT], writes=[bPB[3]])
                        P.op("act", lambda e, ab=ab: e.copy(out=ao[ab][:], in_=PB[3][:, :]), reads=[bPB[3]], writes=[b_ao[ab]])
                        for s in range(4):
                            P.op("pe", lambda e, s=s, ab=ab: e.transpose(out=PB[4][:, s * 128:(s + 1) * 128], in_=ao[ab][:, s * 128:(s + 1) * 128], identity=ident),
                                 reads=[b_ao[ab], b_cst], writes=[bPB[4]])
                        P.op("dve", lambda e, mo=mo: e.tensor_tensor(out=h2[:, :, mo * 128:(mo + 1) * 128], in0=xres[:, :, mo * 128:(mo + 1) * 128],
                                                                     in1=PB[4][:, :].rearrange("p (s f) -> p s f", s=4), op=ALU.add), reads=[b_xres, bPB[4]], writes=[b_h2])
                    for s in range(4):
                        norm_T2(tmp[s % 2], h2[:, s, :], b_h2, nT[:, :, s * 128:(s + 1) * 128], b_nT, s % 2)
                    for mg in range(16):
                        wb = wcnt["w1"] % 2
                        wcnt["w1"] += 1
                        P.dma(lambda e, wb=wb, mg=mg: e.dma_start(out=w1[wb][:], in_=wf1_b[:, mg * 256:(mg + 1) * 256].rearrange("(k p) m -> p k m", p=128)),
                              reads=[b_wsc], writes=[b_w1[wb]])
                        for mc in range(2):
                            hb_ = mc % 2
                            pbk = 3 + (mc % 2)
                            for k in range(8):
                                P.op("pe", lambda e, k=k, mc=mc, wb=wb, pbk=pbk: e.matmul(out=PB[pbk][:, :], lhsT=w1[wb][:, k, mc * 128:(mc + 1) * 128], rhs=nT[:, k, :],
                                                                                          start=(k == 0), stop=(k == 7)), reads=[b_w1[wb], b_nT], writes=[bPB[pbk]])
                            P.op("act", lambda e, hb_=hb_, pbk=pbk: e.copy(out=hx[hb_][:], in_=PB[pbk][:, :]), reads=[bPB[pbk]], writes=[b_hx[hb_]])
                            P.op("dve", lambda e, hb_=hb_, mg=mg, mc=mc: e.scalar_tensor_tensor(out=hT[:, mg * 2 + mc, :], in0=hx[hb_][:], scalar=0.0, op0=ALU.max, in1=hx[hb_][:], op1=ALU.mult),
                                 reads=[b_hx[hb_]], writes=[b_hT])
                    for mg in range(8):
                        wb = wcnt["w2"] % 2
                        wcnt["w2"] += 1
                        P.dma(lambda e, wb=wb, mg=mg: e.dma_start(out=w2[wb][:], in_=wf2_b[:, mg * 128:(mg + 1) * 128].rearrange("(k p) m -> p k m", p=128)),
                              reads=[b_wsc], writes=[b_w2[wb]])
                        for mc in range(1):
                            mo = mg
                            ab = mo % 2
                            pbk = 5 + (mg % 2)
                            for k in range(32):
                                P.op("pe", lambda e, k=k, mc=mc, wb=wb, pbk=pbk: e.matmul(out=PB[pbk][:, :], lhsT=w2[wb][:, k, mc * 128:(mc + 1) * 128], rhs=hT[:, k, :],
                                                                                          start=(k == 0), stop=(k == 31)), reads=[b_w2[wb], b_hT], writes=[bPB[pbk]])
                            P.op("act", lambda e, ab=ab, pbk=pbk: e.copy(out=ao[ab][:], in_=PB[pbk][:, :]), reads=[bPB[pbk]], writes=[b_ao[ab]])
                            for s in range(4):
                                P.op("pe", lambda e, s=s, ab=ab: e.transpose(out=PB[7][:, s * 128:(s + 1) * 128], in_=ao[ab][:, s * 128:(s + 1) * 128], identity=ident),
                                     reads=[b_ao[ab], b_cst], writes=[bPB[7]])
                            P.op("dve", lambda e, mo=mo: e.tensor_tensor(out=yo[:, :, mo * 128:(mo + 1) * 128], in0=h2[:, :, mo * 128:(mo + 1) * 128],
                                                                         in1=PB[7][:, :].rearrange("p (s f) -> p s f", s=4), op=ALU.add), reads=[b_h2, bPB[7]], writes=[b_yo])
                    for s in range(4):
                        P.op("act", lambda e, s=s: e.activation(out=sqf[:], in_=yo[:, s, :], func=AF.Square, accum_out=ssf[:, 0:1]), reads=[b_yo], writes=[b_f])
                        P.op("act", lambda e: e.activation(out=ssf[:, 1:2], in_=ssf[:, 0:1], func=AF.Sqrt, scale=1.0 / D, bias=epsb[:, 0:1]), reads=[b_f, b_cst], writes=[b_f])
                        P.op("dve", lambda e: e.reciprocal(out=ssf[:, 2:3], in_=ssf[:, 1:2]), reads=[b_f], writes=[b_f])
                        P.op("dve", lambda e, s=s: e.scalar_tensor_tensor(out=yo[:, s, :], in0=yo[:, s, :], scalar=ssf[:, 2:3], op0=ALU.mult, in1=gfin[:], op1=ALU.mult),
                             reads=[b_yo, b_f, b_c], writes=[b_yo])
                        P.dma(lambda e, s=s: e.dma_start(out=yout[tk0 + s * 128:tk0 + (s + 1) * 128, :], in_=yo[:, s, :]), reads=[b_yo])

                for tt in range(dbg.get("post_tiles", 4)):
                    do_tt(tt)
                P.barrier()

        if do_post:
            if 0 in kinds:
                post_phase(0, xpp, 0, yp, "pp")
            if 1 in kinds:
                post_phase(1, hs, 16, ys, "ps")

        P.finish()
        P.emit(top)
    return nc


def prepare_inputs(inputs):
    f = lambda a: np.ascontiguousarray(np.asarray(a, dtype=np.float32))
    x_prompt, x_sample, meta = f(inputs["x_prompt"]), f(inputs["x_sample"]), f(inputs["meta_tokens"])
    w_in = f(inputs["w_in"])[0]
    mu_p, mu_n = f(inputs["mu_prev"])[0], f(inputs["mu_next"])[0]
    w0, w_up, a0, a_up, g_up = f(inputs["w0"])[0], f(inputs["w_up"])[0], f(inputs["a0"])[0], f(inputs["a_up"])[0], f(inputs["g_up"])[0]
    k_k, k_a, r_k = f(inputs["k_k"])[0], f(inputs["k_a"])[0], f(inputs["r_k"])[0].reshape(-1)
    ln_w, ln_b = f(inputs["ln_x_w"])[0], f(inputs["ln_x_b"])[0]
    hp = np.zeros((T_P, D), np.float32)
    hp[0:16] = meta
    hp[16:16 + 16384] = x_prompt[0]
    cst = _consts()
    gm = np.ascontiguousarray(f(inputs["g_mix"])[0].reshape(8, 128).T)
    gfc = np.ascontiguousarray(f(inputs["g_ffn"])[0].reshape(8, 128).T)
    gfin = np.ascontiguousarray(np.broadcast_to(f(inputs["g_final"])[None, :], (128, D)))
    slg = np.ascontiguousarray(f(inputs["subln_g"])[0].reshape(128, 1))
    lam = np.stack([f(inputs["lam_q1"])[0], f(inputs["lam_k1"])[0], f(inputs["lam_q2"])[0], f(inputs["lam_k2"])[0]])
    lam = np.ascontiguousarray(np.broadcast_to(lam[None], (128, 4, 64)))
    wg = np.ascontiguousarray(w_in[:, 3328:5376])

    def piece(ha, hb, m):
        cols = np.concatenate([np.arange(ha * 64, ha * 64 + 64), 512 + np.arange(ha * 64, ha * 64 + 64), 1024 + np.arange(ha * 64, ha * 64 + 64),
                               np.arange(1536, 1792),
                               1792 + hb * 128 + m * 64 + np.arange(64), 1792 + 512 + hb * 128 + m * 64 + np.arange(64),
                               1792 + 1024 + hb * 128 + np.arange(128)])
        W = w_in[:, cols]
        pp = np.zeros((128, NPP), np.float32)
        rc = cols[0:448]
        for gi in range(5):
            cc = rc[gi * 64:(gi + 1) * 64]
            for half in (0, 64):
                pp[half:half + 64, 3 * gi] = 1.0 - mu_p[cc] - mu_n[cc]
                pp[half:half + 64, 3 * gi + 1] = mu_p[cc]
                pp[half:half + 64, 3 * gi + 2] = mu_n[cc]
        cg = rc[320:448]
        pp[:, 22] = 1.0 - mu_p[cg] - mu_n[cg]
        pp[:, 23] = mu_p[cg]
        pp[:, 24] = mu_n[cg]
        ch = np.arange(ha * 64, ha * 64 + 64)
        for d in (0, 1):
            pp[d * 64:(d + 1) * 64, 15] = w0[d][ch]
            pp[d * 64:(d + 1) * 64, 16] = a0[d][ch]
            pp[d * 64:(d + 1) * 64, 17] = k_k[ch]
            pp[d * 64:(d + 1) * 64, 18] = k_a[ch]
            pp[d * 64:(d + 1) * 64, 19] = r_k[ch]
            pp[d * 64:(d + 1) * 64, 20] = ln_w[ch]
            pp[d * 64:(d + 1) * 64, 21] = ln_b[ch]
        lw = np.concatenate([w_up[0][:, ch], w_up[1][:, ch]], axis=0)
        la = np.concatenate([a_up[0][:, ch], a_up[1][:, ch]], axis=0)
        lg = g_up[:, ch]
        return W, pp, lw, la, lg

    spieces = [piece(p, p // 2, p % 2) for p in range(8)]
    stat = [_alibi_static(hb) for hb in range(4)]
    shared = dict(hp=hp, wg=wg, wua=f(inputs["w_up_a"])[0], wub=f(inputs["w_up_b"])[0], wout=f(inputs["w_out"])[0],
                  wf1=f(inputs["w_ff1"])[0], wf2=f(inputs["w_ff2"])[0], gm=gm, gf=gfc, gfin=gfin, slg=slg, lam=lam, cst=cst,
                  wpc=np.stack([p[0] for p in spieces]), pp=np.stack([p[1] for p in spieces]), lw=np.stack([p[2] for p in spieces]),
                  la=np.stack([p[3] for p in spieces]), lg=np.stack([p[4] for p in spieces]),
                  dm=np.stack([s[0] for s in stat]), qb=np.stack([s[1] for s in stat]))
    in_maps = []
    for c in range(NCORE):
        hs = np.zeros((T_S, D), np.float32)
        hs[0:16] = meta
        hs[16:16 + 2048] = x_sample[c]
        bc, ks, kval = _alibi_core(c)
        m = dict(shared)
        m["hs"] = hs
        m["xpp"] = np.ascontiguousarray(x_prompt[0, c * 2048:(c + 1) * 2048])
        m["bc"] = bc
        m["ks"] = ks
        m["kval"] = kval
        in_maps.append(m)
    return in_maps


_NC_CACHE = {}


def kernel(**inputs):
    in_maps = prepare_inputs(inputs)
    if "nc" not in _NC_CACHE:
        _NC_CACHE["nc"] = build_program()
    nc = _NC_CACHE["nc"]
    res = run_bass_kernel_spmd(nc, in_maps, core_ids=list(range(NCORE)))
    y_prompt = np.concatenate([np.asarray(r["yp"], dtype=np.float32) for r in res.results], axis=0)[None]
    y_sample = np.stack([np.asarray(r["ys"], dtype=np.float32) for r in res.results], axis=0)
    return (y_prompt, y_sample)
```

```python
import math
import numpy as np
from contextlib import ExitStack
import concourse.bass as bass
import concourse.mybir as mybir
from concourse.bass_utils import run_bass_kernel_spmd

F32 = mybir.dt.float32
BF16 = mybir.dt.bfloat16
AF = mybir.ActivationFunctionType
ALU = mybir.AluOpType
ENGS = ("pe", "act", "dve", "pool", "sp")
NDMA = 8

D = 1024
NCORE = 8
T_P, T_S = 16512, 2176
PC = 704
K0 = math.exp(-0.5)
LAMBDA_INIT = 0.2
NPP = 32


class Buf:
    __slots__ = ("w", "r", "name")

    def __init__(self, name=""):
        self.w = None
        self.r = []
        self.name = name


class Prog:
    def __init__(self, nc):
        self.nc = nc
        self.q = {e: [] for e in ENGS}
        self.cnt = {e: 0 for e in ENGS}
        self.seen = {e: {} for e in ENGS}
        self.dman = {e: 0 for e in ENGS}
        self.dma_last = {}

    def _wait(self, eng, key, val):
        if val <= 0:
            return
        s = self.seen[eng]
        if s.get(key, 0) >= val:
            return
        s[key] = val
        self.q[eng].append(("wait", key, val))

    def _deps(self, eng, reads, writes):
        for b in reads:
            if b.w is not None:
                k, v = b.w
                if k == eng and eng == "pe":
                    continue
                self._wait(eng, k, v)
        for b in writes:
            if b.w is not None:
                k, v = b.w
                if not (k == eng and eng == "pe"):
                    self._wait(eng, k, v)
            for (k, v) in b.r:
                if not (k == eng and eng == "pe"):
                    self._wait(eng, k, v)

    def _mark(self, ticket, reads, writes):
        k = ticket[0]
        for b in reads:
            b.r = [t for t in b.r if t[0] != k]
            b.r.append(ticket)
        for b in writes:
            b.w = ticket
            b.r = []

    def op(self, eng, fn, reads=(), writes=(), after_readers=()):
        self._deps(eng, reads, writes)
        for b in after_readers:
            for (k, v) in b.r:
                if not (k == eng and eng == "pe"):
                    self._wait(eng, k, v)
        self.cnt[eng] += 1
        t = (eng, self.cnt[eng])
        self.q[eng].append(("op", fn, eng, 1))
        self._mark(t, reads, writes)
        return t

    def dma(self, fn, reads=(), writes=(), eng="sp"):
        self._deps(eng, reads, writes)
        j = self.dman[eng]
        self.dman[eng] += 1
        slot = j % NDMA
        key = ("d", eng, slot)
        self._wait(eng, key, 16 * (j // NDMA))
        val = 16 * (j // NDMA + 1)
        self.q[eng].append(("op", fn, key, 16))
        self.dma_last[key] = val
        t = (key, val)
        self._mark(t, reads, writes)
        return t

    def barrier(self):
        keys = [(e, self.cnt[e]) for e in ENGS] + list(self.dma_last.items())
        for e in ENGS:
            for (k, v) in keys:
                if k != e:
                    self._wait(e, k, v)

    def finish(self):
        for (k, v) in list(self.dma_last.items()):
            self._wait("sp", k, v)
        for e in ENGS:
            if e != "sp":
                self._wait("sp", e, self.cnt[e])

    def emit(self, stack):
        nc = self.nc
        sems = {}
        for e in ENGS:
            sems[e] = stack.enter_context(nc.semaphore("ps_" + e))
        for k in self.dma_last:
            sems[k] = stack.enter_context(nc.semaphore("ds_%s_%d" % (k[1], k[2])))
        handles = {"pe": "tensor", "act": "scalar", "dve": "vector", "pool": "gpsimd", "sp": "sync"}
        block = stack.enter_context(nc.Block())

        def replay(engname):
            def body(engh):
                for it in self.q[engname]:
                    if it[0] == "wait":
                        engh.wait_ge(sems[it[1]], it[2])
                    else:
                        ins = it[1](engh)
                        ins.then_inc(sems[it[2]], it[3])
            return body

        for e in ENGS:
            getattr(block, handles[e])(replay(e))


def _consts():
    c = np.zeros((128, 1024), np.float32)
    c[:, 0:128] = np.eye(128)
    s = np.arange(64)
    mf = np.zeros((128, 128), np.float32)
    mb = np.zeros((128, 128), np.float32)
    for a in range(2):
        for b in range(2):
            if b == 0:
                mf[a * 64:(a + 1) * 64, 0:64] = (s[:, None] < s[None, :])
                mb[a * 64:(a + 1) * 64, 0:64] = (s[:, None] > s[None, :])
            else:
                mf[a * 64:(a + 1) * 64, 64:128] = (s[:, None] <= s[None, :])
                mb[a * 64:(a + 1) * 64, 64:128] = (s[:, None] >= s[None, :])
    c[:, 128:256] = mf
    c[:, 256:384] = mb
    c[0:64, 384:448] = (s[None, :] < s[:, None])
    c[64:128, 384:448] = (s[None, :] > s[:, None])
    c[0:64, 512:576] = 1.0
    c[64:128, 576:640] = 1.0
    c[0:64, 640:704] = np.eye(64)
    c[64:128, 640:704] = np.eye(64)
    c[:, 704:832] = 1.0
    c[0:64, 832:896] = 1.0 / 64
    c[:, 896:1024] = 1.0 / 128
    return c


def _alibi_static(hb):
    m = 2.0 ** (-8.0 * (hb + 1) / 4)
    sl = np.arange(128, dtype=np.float64)
    tq = np.arange(512, dtype=np.float64)
    DM = np.zeros((128, 5, 512))
    for dl in range(5):
        DM[:, dl, :] = -m * np.abs(16.0 + tq[None, :] - 128.0 * dl - sl[:, None])
    hi = -m * 16.0 * np.floor(tq / 16)
    lo = -m * (tq % 16)
    qb = np.stack([np.stack([hi, lo]), np.stack([-hi, -lo])])
    return DM.astype(np.float32), qb.astype(np.float32)


def _alibi_core(c):
    bc = np.zeros((2, 4, 128, 4, 129), np.float32)
    ks = np.ones((2, 2, T_P), np.float32)
    kval = np.zeros((2, 128, 129), np.float32)
    sl = np.arange(128, dtype=np.float64)
    for kind, (T, nreal, rot, own0) in enumerate(((T_P, 16400, 16 * c, 16 + 2048 * c), (T_S, 2064, 0, 16))):
        nkt = T // 128
        for r in range(nkt):
            i = (r + rot) % nkt
            s = 128.0 * i + sl
            kval[kind, :, r] = (s < nreal)
            if r > 16:
                left = (128 * i + 127) < own0
                ks[kind, :, r * 128:(r + 1) * 128] = 1.0 if left else -1.0
            for hb in range(4):
                m = 2.0 ** (-8.0 * (hb + 1) / 4)
                for jl in range(4):
                    t0 = own0 + 512 * jl
                    if 128 * i + 127 < t0:
                        bc[kind, hb, :, jl, r] = -m * (t0 - s)
                    elif 128 * i >= t0 + 512:
                        bc[kind, hb, :, jl, r] = -m * (s - t0)
    return bc, ks, kval


def build_program(dbg=None):
    dbg = dbg or {}
    pieces_run = dbg.get("pieces", list(range(8)))
    do_cast = dbg.get("cast", True)
    do_p1 = dbg.get("p1", True)
    do_attn = dbg.get("attn", True)
    do_rwkv = dbg.get("rwkv", True)
    do_xchg = dbg.get("xchg", True)
    do_post = dbg.get("post", True)
    dbg_out = dbg.get("dbg_out", False)
    skind = "ExternalOutput" if dbg_out else "Internal"

    nc = bass.Bass("TRN2", target_bir_lowering=False)

    def din(name, shape):
        return nc.dram_tensor(name, list(shape), F32, kind="ExternalInput").ap()

    def dscr(name, shape, dt):
        if name in dbg.get("outs", ()):
            return nc.dram_tensor(name, list(shape), dt, kind="ExternalOutput").ap()
        return nc.dram_tensor(name, list(shape), dt).ap()

    hp = din("hp", [T_P, D])
    hs = din("hs", [T_S, D])
    xpp = din("xpp", [2048, D])
    wpc = din("wpc", [8, D, PC])
    pp_d = din("pp", [8, 128, NPP])
    lw_d = din("lw", [8, 128, 64])
    la_d = din("la", [8, 128, 64])
    lg_d = din("lg", [8, 128, 64])
    bc_d = din("bc", [2, 4, 128, 4, 129])
    dm_d = din("dm", [4, 128, 5, 512])
    qb_d = din("qb", [4, 2, 2, 512])
    ks_d = din("ks", [2, 2, T_P])
    wg_d = din("wg", [D, 2048])
    wua_d = din("wua", [512, D])
    wub_d = din("wub", [512, D])
    wout_d = din("wout", [D, D])
    wf1_d = din("wf1", [D, 4096])
    wf2_d = din("wf2", [4096, D])
    gm_d = din("gm", [128, 8])
    gf_d = din("gf", [128, 8])
    gfin_d = din("gfin", [128, D])
    slg_d = din("slg", [128, 1])
    lam_d = din("lam", [128, 4, 64])
    cst_d = din("cst", [128, 1024])
    kval_d = din("kval", [2, 128, 129])

    yp = nc.dram_tensor("yp", [2048, D], F32, kind="ExternalOutput").ap()
    ys = nc.dram_tensor("ys", [2048, D], F32, kind="ExternalOutput").ap()

    wpc_b = dscr("wpc_b", [8, D, PC], BF16)
    wg_b = dscr("wg_b", [D, 2048], BF16)
    wua_b = dscr("wua_b", [512, D], BF16)
    wub_b = dscr("wub_b", [512, D], BF16)
    wout_b = dscr("wout_b", [D, D], BF16)
    wf1_b = dscr("wf1_b", [D, 4096], BF16)
    wf2_b = dscr("wf2_b", [4096, D], BF16)
    TK = (T_P, T_S)
    PA = [dscr("PA%d" % k, [8, 192, TK[k] + 2], F32) for k in range(2)]
    PAL = [dscr("PAL%d" % k, [256, TK[k] + 2], F32) for k in range(2)]
    KTD = [dscr("KTD%d" % k, [8, 64, 2 * TK[k]], BF16) for k in range(2)]
    VD = [dscr("VD%d" % k, [8, 2 * TK[k], 128], BF16) for k in range(2)]
    QTD = [dscr("QTD%d" % k, [8, 64, TK[k]], BF16) for k in range(2)]
    OA = [dscr("OA%d" % k, [8, 64, TK[k]], BF16) for k in range(2)]
    XR = [dscr("XR%d" % k, [8, 128, 2048], BF16) for k in range(2)]
    KTR = [dscr("KTR%d" % k, [8, 64, TK[k]], BF16) for k in range(2)]
    VR = [dscr("VR%d" % k, [8, TK[k], 128], BF16) for k in range(2)]
    QO = [dscr("QO%d" % k, [8, 64, 2048], BF16) for k in range(2)]
    OAO = [dscr("OAO%d" % k, [8, 64, 2048], BF16) for k in range(2)]
    YD = [dscr("YD%d" % k, [8, 128, TK[k]], F32) for k in range(2)]
    PAO = [dscr("PAO%d" % k, [8, 192, 2050], F32) for k in range(2)]
    PALO = [dscr("PALO%d" % k, [256, 2050], F32) for k in range(2)]
    YDO = [dscr("YDO%d" % k, [8, 128, 2048], F32) for k in range(2)]
    b_PA, b_KV, b_wsc, b_ROT, b_OAO, b_YD, b_PAO, b_YDO = Buf(), Buf(), Buf(), Buf(), Buf(), Buf(), Buf(), Buf()
    b_XR = [Buf(), Buf()]
    b_OA = [Buf(), Buf()]

    P = Prog(nc)
    with ExitStack() as top:
        _uid = [0]

        def sb(st, name, shape, dt):
            _uid[0] += 1
            return st.enter_context(nc.sbuf_tensor("s%d_%s" % (_uid[0], name), list(shape), dt))

        PB = [top.enter_context(nc.psum_tensor("pb%d" % i, [128, 512], F32)) for i in range(8)]
        bPB = [Buf("pb%d" % i) for i in range(8)]

        cst = sb(top, "cst", [128, 1024], F32); b_cst = Buf()
        cstb = sb(top, "cstb", [128, 128], BF16)
        gm = sb(top, "gm", [128, 8], F32)
        gf = sb(top, "gf", [128, 8], F32)
        zb = sb(top, "zb", [128, 512], BF16)
        zf = sb(top, "zf", [128, 8], F32)
        P.dma(lambda e: e.dma_start(out=cst[:], in_=cst_d), writes=[b_cst])
        P.dma(lambda e: e.dma_start(out=gm[:], in_=gm_d), writes=[b_cst])
        P.dma(lambda e: e.dma_start(out=gf[:], in_=gf_d), writes=[b_cst])
        P.op("dve", lambda e: e.tensor_copy(out=cstb[:], in_=cst[:, 0:128]), reads=[b_cst], writes=[b_cst])
        P.op("dve", lambda e: e.memset(zb[:], 0.0), writes=[b_cst])
        P.op("dve", lambda e: e.memset(zf[:], 0.0), writes=[b_cst])
        P.barrier()
        ident = cst[:, 0:128]
        MF, MB = cst[:, 128:256], cst[:, 256:384]
        MAF, MAB = cst[0:64, 384:448], cst[0:64, 448:512]
        BLK = cst[:, 512:640]
        SEL = cst[:, 640:704]
        ONES = cst[:, 704:832]
        O64 = cst[0:64, 832:896]
        O128 = cst[:, 896:1024]

        def castw(src, dst, rows, cols, scale_cols=None):
            with ExitStack() as st:
                nb = 3
                fin = [sb(st, "cw_f%d" % i, [128, 2048], F32) for i in range(nb)]
                fout = [sb(st, "cw_b%d" % i, [128, 2048], BF16) for i in range(nb)]
                bi = [Buf() for _ in range(nb)]
                bo = [Buf() for _ in range(nb)]
                it = 0
                for kc in range(rows // 128):
                    for c0 in range(0, cols, 2048):
                        cw = min(2048, cols - c0)
                        s = it % nb
                        P.dma(lambda e, s=s, kc=kc, c0=c0, cw=cw: e.dma_start(out=fin[s][:, 0:cw], in_=src[kc * 128:(kc + 1) * 128, c0:c0 + cw]),
                              writes=[bi[s]])
                        eng = "dve" if it % 2 == 0 else "pool"
                        if scale_cols is not None:
                            P.op(eng, lambda e, s=s, kc=kc, cw=cw: e.tensor_scalar(out=fout[s][:, 0:cw], in0=fin[s][:, 0:cw],
                                                                                  scalar1=scale_cols[:, kc % 8:kc % 8 + 1], scalar2=None, op0=ALU.mult),
                                 reads=[bi[s]], writes=[bo[s]])
                        else:
                            P.op(eng, lambda e, s=s, cw=cw: e.tensor_copy(out=fout[s][:, 0:cw], in_=fin[s][:, 0:cw]),
                                 reads=[bi[s]], writes=[bo[s]])
                        P.dma(lambda e, s=s, kc=kc, c0=c0, cw=cw: e.dma_start(out=dst[kc * 128:(kc + 1) * 128, c0:c0 + cw], in_=fout[s][:, 0:cw]),
                              reads=[bo[s]], writes=[b_wsc], eng="act")
                        it += 1
                P.barrier()

        if do_cast:
            for p in range(8):
                castw(wpc[p], wpc_b[p], D, PC, gm)
            if do_post:
                castw(wg_d, wg_b, D, 2048, gm)
                castw(wua_d, wua_b, 512, D)
                castw(wub_d, wub_b, 512, D)
                castw(wout_d, wout_b, D, D)
                castw(wf1_d, wf1_b, D, 4096, gf)
                castw(wf2_d, wf2_b, 4096, D)

        def norm_T(st_tmp, src_ap, b_src, dst_ap, b_dst, pbank, tag):
            sq, ss, xn, bt = st_tmp
            P.op("act", lambda e: e.activation(out=sq[:], in_=src_ap, func=AF.Square, accum_out=ss[:, 0:1]),
                 reads=[b_src], writes=[bt])
            P.op("act", lambda e: e.activation(out=ss[:, 1:2], in_=ss[:, 0:1], func=AF.Sqrt, scale=1.0 / D, bias=zf[:, 0:1]),
                 reads=[bt], writes=[bt])
            P.op("dve", lambda e: e.tensor_scalar(out=ss[:, 1:2], in0=ss[:, 1:2], scalar1=1e-12, scalar2=None, op0=ALU.max),
                 reads=[bt], writes=[bt])
            P.op("dve", lambda e: e.reciprocal(out=ss[:, 2:3], in_=ss[:, 1:2]), reads=[bt], writes=[bt])
            P.op("dve", lambda e: e.tensor_scalar(out=xn[:], in0=src_ap, scalar1=ss[:, 2:3], scalar2=None, op0=ALU.mult),
                 reads=[b_src, bt], writes=[bt])
            pt = PB[pbank].bitcast(BF16)
            for k in range(8):
                P.op("pe", lambda e, k=k: e.transpose(out=pt[:, k * 128:(k + 1) * 128], in_=xn[:, k * 128:(k + 1) * 128], identity=cstb[:]),
                     reads=[bt], writes=[bPB[pbank]])
            P.op("act", lambda e: e.copy(out=dst_ap, in_=pt[:].rearrange("p (k t) -> p k t", k=8)), reads=[bPB[pbank]], writes=[b_dst])

        epsb = sb(top, "epsb", [128, 4], F32)
        P.op("dve", lambda e: e.memset(epsb[:, 0:1], 1e-6), writes=[b_cst])
        P.op("dve", lambda e: e.memset(epsb[:, 1:2], 1e-5), writes=[b_cst])
        P.op("dve", lambda e: e.memset(epsb[:, 2:3], 64e-5), writes=[b_cst])
        P.barrier()

        def norm_T2(st_tmp, src_ap, b_src, dst_ap, b_dst, pbank):
            sq, ss, xn, bt = st_tmp
            P.op("act", lambda e: e.activation(out=sq[:], in_=src_ap, func=AF.Square, accum_out=ss[:, 0:1]),
                 reads=[b_src], writes=[bt])
            P.op("act", lambda e: e.activation(out=ss[:, 1:2], in_=ss[:, 0:1], func=AF.Sqrt, scale=1.0 / D, bias=epsb[:, 0:1]),
                 reads=[bt], writes=[bt])
            P.op("dve", lambda e: e.reciprocal(out=ss[:, 2:3], in_=ss[:, 1:2]), reads=[bt], writes=[bt])
            P.op("dve", lambda e: e.tensor_scalar(out=xn[:], in0=src_ap, scalar1=ss[:, 2:3], scalar2=None, op0=ALU.mult),
                 reads=[b_src, bt], writes=[bt])
            pt = PB[pbank].bitcast(BF16)
            for k in range(8):
                P.op("pe", lambda e, k=k: e.transpose(out=pt[:, k * 128:(k + 1) * 128], in_=xn[:, k * 128:(k + 1) * 128], identity=cstb[:]),
                     reads=[bt], writes=[bPB[pbank]])
            P.op("act", lambda e: e.copy(out=dst_ap, in_=pt[:].rearrange("p (k t) -> p k t", k=8)), reads=[bPB[pbank]], writes=[b_dst])

        _pid = {}

        def core_off(e):
            k = id(e)
            if k not in _pid:
                _pid[k] = e.snap(e.partition_id() * 2048) if hasattr(e, "snap") else e.partition_id() * 2048
            return _pid[k]

        def p1_phase(kind):
            T = TK[kind]
            hsrc = hp if kind == 0 else hs
            tiles = [(t0, min(512, T - t0)) for t0 in range(0, T, 512)]
            tg = "a%d" % kind
            with ExitStack() as s1:
                wp = sb(s1, tg + "wp", [128, 8, 8, PC], BF16); b_wp = Buf()
                for p in range(8):
                    P.dma(lambda e, p=p: e.dma_start(out=wp[:, p, :, :], in_=wpc_b[p].rearrange("(k p) m -> p k m", p=128)), reads=[b_wsc], writes=[b_wp])
                    for (r0, nr) in ((0, 128), (128, 64)):
                        for col in (0, T + 1):
                            P.dma(lambda e, p=p, r0=r0, nr=nr, col=col: e.dma_start(out=PA[kind][p, r0:r0 + nr, col:col + 1], in_=zf[0:nr, 0:1], allow_slow_non_contiguous=True), writes=[b_PA])
                for r0 in (0, 128):
                    for col in (0, T + 1):
                        P.dma(lambda e, r0=r0, col=col: e.dma_start(out=PAL[kind][r0:r0 + 128, col:col + 1], in_=zf[:, 0:1], allow_slow_non_contiguous=True), writes=[b_PA])
                NXB = 3
                xt = [sb(s1, tg + "xt%d" % i, [128, D], F32) for i in range(NXB)]
                b_xt = [Buf() for _ in range(NXB)]
                sqs = sb(s1, tg + "sq", [128, D], F32)
                tmp = [(sqs, sb(s1, tg + "ss%d" % i, [128, 4], F32), sb(s1, tg + "xn%d" % i, [128, D], BF16), Buf()) for i in range(2)]
                nT = [sb(s1, tg + "nT%d" % i, [128, 8, 512], BF16) for i in range(2)]
                b_nT = [Buf() for _ in range(2)]
                stg = [sb(s1, tg + "stg%d" % i, [128, 4, 512], F32) for i in range(2)]
                b_stg = [Buf() for _ in range(2)]
                qk = [sb(s1, tg + "qk%d" % i, [64, 2, 512], BF16) for i in range(2)]
                b_qk = [Buf() for _ in range(2)]
                vst = [sb(s1, tg + "vst%d" % i, [128, 4, 128], BF16) for i in range(2)]
                b_vst = [Buf() for _ in range(2)]
                cnt = {"sub": 0, "it": 0}

                def do_tile(ti, t0, w):
                    nb = ti % 2
                    nsub = w // 128
                    for s in range(nsub):
                        xb = cnt["sub"] % NXB
                        P.dma(lambda e, xb=xb, s=s: e.dma_start(out=xt[xb][:], in_=hsrc[t0 + s * 128:t0 + (s + 1) * 128, :]), writes=[b_xt[xb]])
                        norm_T2(tmp[cnt["sub"] % 2], xt[xb][:], b_xt[xb], nT[nb][:, :, s * 128:(s + 1) * 128], b_nT[nb], cnt["sub"] % 2)
                        cnt["sub"] += 1
                    sbl = cnt["it"] % 2
                    cnt["it"] += 1
                    for ci, c0 in enumerate((192, 320)):
                        bk = 2 + (ci % 2)
                        for k in range(8):
                            P.op("pe", lambda e, k=k, c0=c0, bk=bk: e.matmul(out=PB[bk][:, 0:w], lhsT=wp[:, 0, k, c0:c0 + 128], rhs=nT[nb][:, k, 0:w], start=(k == 0), stop=(k == 7)),
                                 reads=[b_wp, b_nT[nb]], writes=[bPB[bk]])
                        if ci == 0:
                            P.op("act", lambda e, bk=bk, ci=ci: e.copy(out=stg[sbl][:, ci, 0:w], in_=PB[bk][:, 0:w]), reads=[bPB[bk]], writes=[b_stg[sbl]])
                        else:
                            P.op("dve", lambda e, bk=bk, ci=ci: e.tensor_copy(out=stg[sbl][:, ci, 0:w], in_=PB[bk][:, 0:w]), reads=[bPB[bk]], writes=[b_stg[sbl]])
                    P.dma(lambda e: e.dma_start(out=PAL[kind][:, 1 + t0:1 + t0 + w].rearrange("(c p) t -> p c t", p=128), in_=stg[sbl][:, 0:2, 0:w]),
                          reads=[b_stg[sbl]], writes=[b_PA])
                    for p in range(8):
                        sbi = cnt["it"] % 2
                        cnt["it"] += 1
                        for ci, (c0, cw) in enumerate([(0, 128), (128, 64)]):
                            bk = 2 + (ci % 2)
                            for k in range(8):
                                P.op("pe", lambda e, k=k, c0=c0, cw=cw, bk=bk, p=p: e.matmul(
                                    out=PB[bk][0:cw, 0:w], lhsT=wp[:, p, k, c0:c0 + cw], rhs=nT[nb][:, k, 0:w], start=(k == 0), stop=(k == 7)),
                                    reads=[b_wp, b_nT[nb]], writes=[bPB[bk]])
                            if ci % 2 == 0:
                                P.op("act", lambda e, cw=cw, bk=bk, ci=ci, sbi=sbi: e.copy(out=stg[sbi][0:cw, ci, 0:w], in_=PB[bk][0:cw, 0:w]),
                                     reads=[bPB[bk]], writes=[b_stg[sbi]])
                            else:
                                P.op("dve", lambda e, cw=cw, bk=bk, ci=ci, sbi=sbi: e.tensor_copy(out=stg[sbi][0:cw, ci, 0:w], in_=PB[bk][0:cw, 0:w]),
                                     reads=[bPB[bk]], writes=[b_stg[sbi]])
                        P.dma(lambda e, p=p, sbi=sbi: e.dma_start(out=PA[kind][p, 0:128, 1 + t0:1 + t0 + w], in_=stg[sbi][:, 0, 0:w]),
                              reads=[b_stg[sbi]], writes=[b_PA])
                        P.dma(lambda e, p=p, sbi=sbi: e.dma_start(out=PA[kind][p, 128:192, 1 + t0:1 + t0 + w], in_=stg[sbi][0:64, 1, 0:w]),
                              reads=[b_stg[sbi]], writes=[b_PA])
                        for k in range(8):
                            P.op("pe", lambda e, k=k, p=p: e.matmul(out=PB[4][0:64, 0:w], lhsT=wp[:, p, k, 448:512], rhs=nT[nb][:, k, 0:w], start=(k == 0), stop=(k == 7)),
                                 reads=[b_wp, b_nT[nb]], writes=[bPB[4]])
                        P.op("act", lambda e, sbi=sbi: e.activation(out=qk[sbi][:, 0, 0:w], in_=PB[4][0:64, 0:w], func=AF.Copy, scale=0.125),
                             reads=[bPB[4]], writes=[b_qk[sbi]])
                        for k in range(8):
                            P.op("pe", lambda e, k=k, p=p: e.matmul(out=PB[5][0:64, 0:w], lhsT=wp[:, p, k, 512:576], rhs=nT[nb][:, k, 0:w], start=(k == 0), stop=(k == 7)),
                                 reads=[b_wp, b_nT[nb]], writes=[bPB[5]])
                        P.op("dve", lambda e, sbi=sbi: e.tensor_copy(out=qk[sbi][:, 1, 0:w], in_=PB[5][0:64, 0:w]), reads=[bPB[5]], writes=[b_qk[sbi]])
                        P.dma(lambda e, p=p, sbi=sbi: e.dma_start(out=QTD[kind][p, :, t0:t0 + w], in_=qk[sbi][:, 0, 0:w]), reads=[b_qk[sbi]], writes=[b_KV])
                        for rep_ in range(2):
                            P.dma(lambda e, p=p, sbi=sbi, rep_=rep_: e.dma_start(out=KTD[kind][p, :, rep_ * T + t0:rep_ * T + t0 + w], in_=qk[sbi][:, 1, 0:w]),
                                  reads=[b_qk[sbi]], writes=[b_KV])
                        if p % 2 == 1:
                            continue
                        for s in range(nsub):
                            for k in range(8):
                                P.op("pe", lambda e, k=k, s=s, p=p: e.matmul(out=PB[6][:, s * 128:(s + 1) * 128], lhsT=nT[nb][:, k, s * 128:(s + 1) * 128],
                                                                             rhs=wp[:, p, k, 576:704], start=(k == 0), stop=(k == 7)),
                                     reads=[b_wp, b_nT[nb]], writes=[bPB[6]])
                        P.op("act", lambda e, sbi=sbi: e.copy(out=vst[sbi][:, 0:nsub, :], in_=PB[6][:, 0:nsub * 128].rearrange("p (s e) -> p s e", s=nsub)),
                             reads=[bPB[6]], writes=[b_vst[sbi]])
                        for rep_ in range(2):
                            P.dma(lambda e, p=p, sbi=sbi, rep_=rep_: e.dma_start(
                                out=VD[kind][p, rep_ * T + t0:rep_ * T + t0 + w, :].rearrange("(s q) e -> q s e", q=128), in_=vst[sbi][:, 0:nsub, :]),
                                reads=[b_vst[sbi]], writes=[b_KV])

                for ti, (t0, w) in enumerate(tiles[:dbg.get("p1_tiles", 10 ** 9)]):
                    do_tile(ti, t0, w)
                P.barrier()

        def rotate_phase(kind):
            T = TK[kind]

            def off(e, extra):
                return (core_off(e) + extra) if kind == 0 else extra
            P.dma(lambda e: e.dma_start(out=KTR[kind][:, :, :], in_=KTD[kind][:, :, bass.ds(off(e, 0), T)]), reads=[b_KV], writes=[b_ROT])
            P.dma(lambda e: e.dma_start(out=VR[kind].rearrange("p t e -> p (t e)"),
                                        in_=VD[kind].rearrange("p t e -> p (t e)")[:, bass.ds(off(e, 0) * 128, T * 128)]), reads=[b_KV], writes=[b_ROT])
            P.dma(lambda e: e.dma_start(out=QO[kind][:, :, :], in_=QTD[kind][:, :, bass.ds(off(e, 16), 2048)]), reads=[b_KV], writes=[b_ROT])
            P.dma(lambda e: e.dma_start(out=PAO[kind].rearrange("p r t -> (p r) t"), in_=PA[kind].rearrange("p r t -> (p r) t")[:, bass.ds(off(e, 16), 2050)]),
                  reads=[b_PA], writes=[b_PAO])
            P.dma(lambda e: e.dma_start(out=PALO[kind][:, :], in_=PAL[kind][:, bass.ds(off(e, 16), 2050)]), reads=[b_PA], writes=[b_PAO])
            P.barrier()

        def own_y_phase(kind):
            def off(e, extra):
                return (core_off(e) + extra) if kind == 0 else extra
            P.dma(lambda e: e.dma_start(out=YDO[kind].rearrange("p r t -> (p r) t"), in_=YD[kind].rearrange("p r t -> (p r) t")[:, bass.ds(off(e, 16), 2048)]),
                  reads=[b_YD], writes=[b_YDO])
            P.barrier()

        def attn_alloc_set(st, kind, tg):
            T = TK[kind]
            NKT = T // 128
            S = dict(
                KT=sb(st, tg + "KT", [66, T], BF16), VP=sb(st, tg + "VP", [128, NKT, 129], BF16), QT=sb(st, tg + "QT", [64, 2048], BF16),
                kvf=sb(st, tg + "kvf", [128, 129], F32), ksr=sb(st, tg + "ksr", [66, 2048], F32), bc=sb(st, tg + "bc", [128, 4, 129], F32),
                dmk=sb(st, tg + "dmk", [128, 5, 512], F32), qbf=sb(st, tg + "qbf", [66, 2, 512], F32),
                Qp=sb(st, tg + "Qp", [66, 512], BF16), Qn=sb(st, tg + "Qn", [66, 512], BF16),
                b_KT=Buf(), b_VP=Buf(), b_QT=Buf(), b_bc=Buf(), b_Q=Buf())
            return S

        def attn_alloc_shared(st, tg):
            return dict(
                PT=[sb(st, tg + "PT%d" % i, [128, 512], BF16) for i in range(3)], b_PT=[Buf() for _ in range(3)],
                dtmp=[sb(st, tg + "dt%d" % i, [128, 512], F32) for i in range(2)], b_dt=[Buf() for _ in range(2)],
                rinv=sb(st, tg + "rinv", [128, 4], F32), b_ri=Buf(), On=sb(st, tg + "On", [128, 4, 128], F32), b_On=Buf(),
                OT=[sb(st, tg + "OT%d" % i, [128, 512], BF16) for i in range(2)], b_OT=[Buf() for _ in range(2)])

        def attn_load(kind, p, S):
            T = TK[kind]
            NKT = T // 128
            hb = p // 2
            KT, VP, QT, kvf, ksr, bc, dmk, qbf, Qp, Qn = (S[k] for k in ("KT", "VP", "QT", "kvf", "ksr", "bc", "dmk", "qbf", "Qp", "Qn"))
            b_KT, b_VP, b_QT, b_bc, b_Q = (S[k] for k in ("b_KT", "b_VP", "b_QT", "b_bc", "b_Q"))
            P.dma(lambda e: e.dma_start(out=KT[0:64, :], in_=KTR[kind][p, :, :]), reads=[b_ROT], writes=[b_KT])
            P.dma(lambda e: e.dma_start(out=VP[:, :, 0:128], in_=VR[kind][p - p % 2, :, :].rearrange("(s q) e -> q s e", q=128)),
                  reads=[b_ROT], writes=[b_VP])
            P.dma(lambda e: e.dma_start(out=QT[:, :], in_=QO[kind][p, :, :]), reads=[b_ROT], writes=[b_QT])
            P.dma(lambda e: e.dma_start(out=kvf[:], in_=kval_d[kind]), writes=[b_VP])
            P.op("dve", lambda e: e.tensor_copy(out=VP[:, :, 128:129], in_=kvf[:, 0:NKT].unsqueeze(2)), reads=[b_VP], writes=[b_VP])
            for c0 in range(0, T, 2048):
                cw = min(2048, T - c0)
                P.dma(lambda e, c0=c0, cw=cw: e.dma_start(out=ksr[64:66, 0:cw], in_=ks_d[kind, :, c0:c0 + cw]), writes=[b_bc])
                P.op("dve", lambda e, c0=c0, cw=cw: e.tensor_copy(out=KT[64:66, c0:c0 + cw], in_=ksr[64:66, 0:cw]), reads=[b_bc], writes=[b_KT])
            P.dma(lambda e: e.dma_start(out=bc[:], in_=bc_d[kind, hb]), writes=[b_bc])
            P.dma(lambda e: e.dma_start(out=dmk[:], in_=dm_d[hb]), writes=[b_bc])
            P.dma(lambda e: e.dma_start(out=qbf[64:66, :, :], in_=qb_d[hb].rearrange("s r t -> r s t")), writes=[b_bc])
            P.op("dve", lambda e: e.tensor_copy(out=Qp[64:66, :], in_=qbf[64:66, 0, :]), reads=[b_bc], writes=[b_Q])
            P.op("dve", lambda e: e.tensor_copy(out=Qn[64:66, :], in_=qbf[64:66, 1, :]), reads=[b_bc], writes=[b_Q])

        def attn_compute(kind, p, S, SH):
            T = TK[kind]
            NKT = T // 128
            hb = p // 2
            m = 2.0 ** (-8.0 * (hb + 1) / 4)
            KT, VP, QT, bc, dmk, Qp, Qn = (S[k] for k in ("KT", "VP", "QT", "bc", "dmk", "Qp", "Qn"))
            b_KT, b_VP, b_QT, b_bc, b_Q = (S[k] for k in ("b_KT", "b_VP", "b_QT", "b_bc", "b_Q"))
            PT, b_PT, dtmp, b_dt, rinv, b_ri, On, b_On, OT, b_OT = (SH[k] for k in ("PT", "b_PT", "dtmp", "b_dt", "rinv", "b_ri", "On", "b_On", "OT", "b_OT"))
            st_ = {"blk": 0, "pend": None}

            def do_q(jl):
                P.op("pool", lambda e: e.tensor_copy(out=Qp[0:64, :], in_=QT[:, jl * 512:(jl + 1) * 512]), reads=[b_QT], writes=[b_Q])
                P.op("pool", lambda e: e.tensor_copy(out=Qn[0:64, :], in_=QT[:, jl * 512:(jl + 1) * 512]), reads=[b_QT], writes=[b_Q])
                for bk in (2, 3):
                    P.op("pe", lambda e, bk=bk: e.matmul(out=PB[bk][:, :], lhsT=zb[0:1, 0:128], rhs=zb[0:1, 0:512], start=True, stop=True,
                                                         skip_group_check=True), reads=[b_cst], writes=[bPB[bk]])
                for r in range(NKT):
                    dl = r - 4 * jl
                    diag = (r <= 16) and (0 <= dl <= 4)
                    if r <= 16:
                        if dl < 0:
                            dist = (16 + 512 * jl) - (128 * r + 127)
                        elif dl > 4:
                            dist = 128 * r - (16 + 512 * jl + 511)
                        else:
                            dist = 0
                    else:
                        d_right = 128 * r - (16 + 512 * jl + 511)
                        d_left = 512 * jl - 128 * r + 16401
                        dist = min(d_right, d_left) if kind == 0 else d_right
                    if m * dist > 150.0:
                        continue
                    sbk = st_["blk"] % 2
                    pb = st_["blk"] % 3
                    st_["blk"] += 1
                    if diag:
                        P.op("pe", lambda e, r=r, sbk=sbk: e.matmul(out=PB[sbk][:, :], lhsT=KT[0:64, r * 128:(r + 1) * 128], rhs=Qp[0:64, :],
                                                                    start=True, stop=True), reads=[b_KT, b_Q], writes=[bPB[sbk]])
                        P.op("dve", lambda e, sbk=sbk, dl=dl: e.tensor_tensor(out=dtmp[sbk][:], in0=PB[sbk][:, :], in1=dmk[:, dl, :], op=ALU.add),
                             reads=[bPB[sbk], b_bc], writes=[b_dt[sbk]])
                        P.op("act", lambda e, sbk=sbk, pb=pb: e.activation(out=PT[pb][:], in_=dtmp[sbk][:], func=AF.Exp),
                             reads=[b_dt[sbk]], writes=[b_PT[pb]])
                    else:
                        Qx = Qn if (r <= 16 and dl > 4) else Qp
                        P.op("pe", lambda e, r=r, sbk=sbk, Qx=Qx: e.matmul(out=PB[sbk][:, :], lhsT=KT[0:66, r * 128:(r + 1) * 128], rhs=Qx[0:66, :],
                                                                           start=True, stop=True), reads=[b_KT, b_Q], writes=[bPB[sbk]])
                        P.op("act", lambda e, r=r, sbk=sbk, pb=pb: e.activation(out=PT[pb][:], in_=PB[sbk][:, :], func=AF.Exp, bias=bc[:, jl, r:r + 1]),
                             reads=[bPB[sbk], b_bc], writes=[b_PT[pb]])
                    def pv(pb=pb, r=r):
                        for s in range(4):
                            bk, off = (2, s * 129) if s < 3 else (3, 0)
                            P.op("pe", lambda e, pb=pb, s=s, bk=bk, off=off, r=r: e.matmul(out=PB[bk][:, off:off + 129], lhsT=PT[pb][:, s * 128:(s + 1) * 128],
                                                                                           rhs=VP[:, r, :], start=False, stop=False, skip_group_check=True),
                                 reads=[b_PT[pb], b_VP], writes=[bPB[bk]])
                    if st_["pend"] is not None:
                        st_["pend"]()
                    st_["pend"] = pv
                if st_["pend"] is not None:
                    st_["pend"]()
                    st_["pend"] = None
                ob = jl % 2
                for s in range(4):
                    bk, off = (2, s * 129) if s < 3 else (3, 0)
                    P.op("dve", lambda e, s=s, bk=bk, off=off: e.reciprocal(out=rinv[:, s:s + 1], in_=PB[bk][:, off + 128:off + 129]),
                         reads=[bPB[bk]], writes=[b_ri])
                    P.op("dve", lambda e, s=s, bk=bk, off=off: e.tensor_scalar(out=On[:, s, :], in0=PB[bk][:, off:off + 128], scalar1=rinv[:, s:s + 1],
                                                                              scalar2=None, op0=ALU.mult), reads=[bPB[bk], b_ri], writes=[b_On])
                for s in range(4):
                    P.op("pe", lambda e, s=s: e.transpose(out=PB[4][:, s * 128:(s + 1) * 128], in_=On[:, s, :], identity=ident),
                         reads=[b_On, b_cst], writes=[bPB[4]])
                P.op("act", lambda e: e.copy(out=OT[ob][:], in_=PB[4][:, :]), reads=[bPB[4]], writes=[b_OT[ob]])
                P.dma(lambda e: e.dma_start(out=XR[kind][p, :, jl * 512:(jl + 1) * 512], in_=OT[ob][:]), reads=[b_OT[ob]], writes=[b_XR[kind]])

            for jl in range(4):
                do_q(jl)

        def attn_phase(kind, pieces):
            if not pieces:
                return
            with ExitStack() as s2:
                sets = [attn_alloc_set(s2, kind, "t%d_s%d" % (kind, i)) for i in range(2)]
                SH = attn_alloc_shared(s2, "t%d_sh" % kind)
                attn_load(kind, pieces[0], sets[0])
                for idx, p in enumerate(pieces):
                    if idx + 1 < len(pieces):
                        attn_load(kind, pieces[idx + 1], sets[(idx + 1) % 2])
                    attn_compute(kind, p, sets[idx % 2], SH)
                P.barrier()

        def run_streams(gens):
            gens = list(gens)
            skew = dbg.get("skew", 0)
            for gi, g in enumerate(gens[:-1]):
                try:
                    for _ in range(skew * (len(gens) - 1 - gi)):
                        next(g)
                except StopIteration:
                    pass
            while gens:
                for g in list(gens):
                    try:
                        next(g)
                    except StopIteration:
                        gens.remove(g)

        class RwkvCtx:
            def __init__(self, st, kind, pi, slot, full):
                self.kind, self.pi, self.slot = kind, pi, slot
                self.T = TK[kind]
                self.paT = PA[kind][pi] if full else PAO[kind][pi]
                self.paL = PAL[kind] if full else PALO[kind]
                self.b_src = b_PA if full else b_PAO
                tg = "r%d_%d_%d" % (kind, pi, 1 if full else 0)
                self.bb = 4 * slot

                def F(name, shape, dt=F32):
                    return sb(st, tg + name, shape, dt)
                self.F = F
                self.pp = F("pp", [128, NPP]); self.b_pp = Buf()
                self.lw = F("lw", [128, 64]); self.la = F("la", [128, 64]); self.lg = F("lg", [128, 64])
                for (t_, d_) in ((self.pp, pp_d), (self.lw, lw_d), (self.la, la_d), (self.lg, lg_d)):
                    P.dma(lambda e, t_=t_, d_=d_: e.dma_start(out=t_[:], in_=d_[pi]), writes=[self.b_pp])
                WW = 512
                self.raw = {n: F("raw_" + n, [128, WW + 2]) for n in ("r", "k", "v", "wd", "ad")}
                self.b_raw = Buf()
                self.t1 = F("t1", [128, WW]); self.b_t1 = Buf()
                self.X = {n: F("X_" + n, [128, WW]) for n in ("r", "k", "v", "wd", "ad")}
                self.b_X = Buf()
                self.b_Xn = {n: Buf() for n in ("r", "k", "v", "wd", "ad")}
                names = ("a", "kd", "tmp") + (("sg", "cs", "x", "d1", "d2", "ginc", "ginv", "kapa") if full else ())
                self.E = {n: F("E_" + n, [128, WW]) for n in names}
                self.b_E = {n: Buf() for n in self.E}
                if full:
                    for n, r in (("kk", "k"), ("rn", "r"), ("kap", "v")):
                        self.E[n] = self.raw[r]
                        self.b_E[n] = self.b_raw

            def PBk(self, i):
                return PB[self.bb + i]

            def bPBk(self, i):
                return bPB[self.bb + i]

            def load_raw(self, W, f0, b0):
                for gi, n in enumerate(("r", "k", "v", "wd", "ad")):
                    srcT, r0 = (self.paT, 64 * gi) if gi < 3 else (self.paL, 64 * (gi - 3))
                    P.dma(lambda e, n=n, r0=r0, srcT=srcT: e.dma_start(out=self.raw[n][0:64, 0:W + 2], in_=srcT[r0:r0 + 64, f0:f0 + W + 2]), reads=[self.b_src], writes=[self.b_raw])
                    P.dma(lambda e, n=n, r0=r0, srcT=srcT: e.dma_start(out=self.raw[n][64:128, 0:W + 2], in_=srcT[r0:r0 + 64, b0:b0 + W + 2]), reads=[self.b_src], writes=[self.b_raw])

            def prep_common(self, W):
                raw, t1, X, E, pp, la = self.raw, self.t1, self.X, self.E, self.pp, self.la
                b_raw, b_t1, b_X, b_E, b_pp = self.b_raw, self.b_t1, self.b_X, self.b_E, self.b_pp
                for gi, n in enumerate(("r", "k", "v", "wd", "ad")):
                    c0 = 3 * gi
                    bx = self.b_Xn[n]
                    P.op("act", lambda e, n=n, c0=c0: e.activation(out=X[n][:, 0:W], in_=raw[n][:, 1:W + 1], func=AF.Identity, scale=pp[:, c0:c0 + 1]),
                         reads=[b_raw, b_pp], writes=[bx], after_readers=[b_X])
                    P.op("dve", lambda e, n=n, c0=c0: e.scalar_tensor_tensor(out=X[n][:, 0:W], in0=raw[n][:, 0:W], scalar=pp[:, c0 + 1:c0 + 2], op0=ALU.mult,
                                                                              in1=X[n][:, 0:W], op1=ALU.add), reads=[b_raw, b_pp, bx], writes=[bx])
                    P.op("dve", lambda e, n=n, c0=c0: e.scalar_tensor_tensor(out=X[n][:, 0:W], in0=raw[n][:, 2:W + 2], scalar=pp[:, c0 + 2:c0 + 3], op0=ALU.mult,
                                                                              in1=X[n][:, 0:W], op1=ALU.add), reads=[b_raw, b_pp, bx], writes=[bx, b_X])
                pb0, bpb0 = self.PBk(0), self.bPBk(0)
                for h in (0, 64):
                    P.op("pe", lambda e, h=h: e.matmul(out=pb0[h:h + 64, 0:W], lhsT=la[h:h + 64, :], rhs=X["ad"][h:h + 64, 0:W], start=True, stop=True),
                         reads=[b_pp, b_X], writes=[bpb0])
                P.op("act", lambda e: e.activation(out=E["a"][:, 0:W], in_=pb0[:, 0:W], func=AF.Sigmoid, bias=pp[:, 16:17]),
                     reads=[bpb0, b_pp], writes=[b_E["a"]])
                P.op("dve", lambda e: e.tensor_scalar(out=E["tmp"][:, 0:W], in0=E["a"][:, 0:W], scalar1=-1.0, scalar2=pp[:, 18:19], op0=ALU.add, op1=ALU.mult),
                     reads=[b_E["a"], b_pp], writes=[b_E["tmp"]])
                P.op("dve", lambda e: e.scalar_tensor_tensor(out=E["kd"][:, 0:W], in0=E["tmp"][:, 0:W], scalar=1.0, op0=ALU.add, in1=X["k"][:, 0:W], op1=ALU.mult),
                     reads=[b_E["tmp"], b_X], writes=[b_E["kd"]])

        def rwkv_steps(st, kind, pi, slot):
            C = RwkvCtx(st, kind, pi, slot, True)
            T, F, pp, lw = C.T, C.F, C.pp, C.lw
            raw, t1, X, E = C.raw, C.t1, C.X, C.E
            b_raw, b_t1, b_X, b_E, b_pp = C.b_raw, C.b_t1, C.b_X, C.b_E, C.b_pp
            PBk, bPBk = C.PBk, C.bPBk
            NCH = T // 64
            steps = []
            c = 0
            while c < NCH:
                nj = min(8, NCH - c)
                steps.append((c, nj))
                c += nj
            ST = F("ST", [128, 64]); b_ST = Buf()
            RST = F("RST", [128, 512])
            P.op("dve", lambda e: e.memset(ST[:], 0.0), writes=[b_ST])
            P.op("dve", lambda e: e.memset(RST[:], 1.0), writes=[b_pp])
            P.op("dve", lambda e: e.memset(RST[:].rearrange("p (j t) -> p j t", t=64)[:, :, 0:1], 0.0), writes=[b_pp])
            GE = F("GE", [128, 8]); b_GE = Buf()
            BK = F("BK", [128, 8, 2, 64]); b_BK = Buf()
            AR = F("AR", [128, 8, 2, 64]); b_AR = Buf()
            BKH = F("BKH", [128, 8, 2, 64]); b_BKH = Buf()
            VV = F("VV", [128, 8, 2, 64]); b_VV = Buf()
            MX = F("MX", [128, 16, 128]); b_MX = Buf()
            Nm = [F("Nm%d" % i, [128, 8, 64], BF16) for i in range(2)]; b_Nm = [Buf(), Buf()]
            Mm = [F("Mm%d" % i, [128, 8, 64], BF16) for i in range(2)]; b_Mm = [Buf(), Buf()]
            Rm = F("Rm", [128, 8, 64]); b_Rm = Buf()
            Rb = F("Rb", [128, 8, 64], BF16); b_Rb = Buf()
            BKHs = F("BKHs", [128, 16, 64]); b_BKHs = Buf()
            V2s = F("V2s", [128, 16, 64]); b_V2s = Buf()
            BVs = F("BVs", [128, 8, 64]); b_BVs = Buf()
            YVs = F("YVs", [128, 8, 64]); b_YVs = Buf()
            KVs = F("KVs", [128, 8, 64]); b_KVs = Buf()
            Ws = F("Ws", [128, 64]); b_Ws = Buf()
            Us = F("Us", [128, 64]); b_Us = Buf()
            ST2 = F("ST2", [128, 64]); b_ST2 = Buf()
            Yst = [F("Yst%d" % i, [128, 512]) for i in range(1)] * 2; b_Yst = [Buf()] * 2
            E["tw"], b_E["tw"] = t1, b_t1
            E["gexc"], b_E["gexc"] = E["d1"], b_E["d1"]
            E["gtail"], b_E["gtail"] = E["d2"], b_E["d2"]
            E["kk2"], b_E["kk2"] = E["tmp"], b_E["tmp"]
            pending = []
            nsteps_run = len(steps) if dbg.get("rwkv_steps") is None else dbg["rwkv_steps"]
            yield

            def do_step(g):
                cf, nj = steps[g]
                W = nj * 64
                f0 = cf * 64
                b0 = T - f0 - W
                if g == 0:
                    C.load_raw(W, f0, b0)
                while pending:
                    pending.pop(0)()
                C.prep_common(W)
                yield
                P.op("act", lambda e: e.activation(out=E["tw"][:, 0:W], in_=X["wd"][:, 0:W], func=AF.Tanh), reads=[b_X, b_t1], writes=[b_E["tw"]])
                for h in (0, 64):
                    P.op("pe", lambda e, h=h: e.matmul(out=PBk(1)[h:h + 64, 0:W], lhsT=lw[h:h + 64, :], rhs=E["tw"][h:h + 64, 0:W], start=True, stop=True),
                         reads=[b_pp, b_E["tw"]], writes=[bPBk(1)])
                P.op("act", lambda e: e.activation(out=E["sg"][:, 0:W], in_=PBk(1)[:, 0:W], func=AF.Sigmoid, bias=pp[:, 15:16]),
                     reads=[bPBk(1), b_pp], writes=[b_E["sg"]])
                P.op("dve", lambda e: e.tensor_tensor_scan(out=E["cs"][:, 0:W], data0=RST[:, 0:W], data1=E["sg"][:, 0:W], initial=0.0, op0=ALU.mult, op1=ALU.add),
                     reads=[b_E["sg"], b_pp], writes=[b_E["cs"]])
                cs3 = E["cs"][:, 0:W].rearrange("p (j t) -> p j t", t=64)
                ceb = cs3[:, :, 63:64].to_broadcast([128, nj, 64])
                x3 = E["x"][:, 0:W].rearrange("p (j t) -> p j t", t=64)
                P.op("dve", lambda e: e.tensor_copy(out=E["x"][0:64, 0:W], in_=E["cs"][0:64, 0:W]), reads=[b_E["cs"]], writes=[b_E["x"]])
                P.op("dve", lambda e: e.tensor_tensor(out=x3[64:128], in0=ceb[64:128], in1=cs3[64:128], op=ALU.subtract), reads=[b_E["cs"]], writes=[b_E["x"]])
                P.op("dve", lambda e: e.tensor_tensor(out=E["x"][64:128, 0:W], in0=E["x"][64:128, 0:W], in1=E["sg"][64:128, 0:W], op=ALU.add),
                     reads=[b_E["x"], b_E["sg"]], writes=[b_E["x"]])
                yield
                P.op("dve", lambda e: e.tensor_tensor(out=E["d1"][:, 0:W], in0=E["x"][:, 0:W], in1=E["sg"][:, 0:W], op=ALU.subtract),
                     reads=[b_E["x"], b_E["sg"]], writes=[b_E["d1"]])
                d23 = E["d2"][:, 0:W].rearrange("p (j t) -> p j t", t=64)
                P.op("dve", lambda e: e.tensor_tensor(out=d23, in0=ceb, in1=x3, op=ALU.subtract), reads=[b_E["cs"], b_E["x"]], writes=[b_E["d2"]])
                P.op("act", lambda e: e.activation(out=E["ginc"][:, 0:W], in_=E["x"][:, 0:W], func=AF.Exp, scale=-K0), reads=[b_E["x"]], writes=[b_E["ginc"]])
                P.op("act", lambda e: e.activation(out=E["ginv"][:, 0:W], in_=E["x"][:, 0:W], func=AF.Exp, scale=K0), reads=[b_E["x"]], writes=[b_E["ginv"]])
                P.op("act", lambda e: e.activation(out=E["gexc"][:, 0:W], in_=E["d1"][:, 0:W], func=AF.Exp, scale=-K0), reads=[b_E["d1"]], writes=[b_E["gexc"]])
                P.op("act", lambda e: e.activation(out=E["gtail"][:, 0:W], in_=E["d2"][:, 0:W], func=AF.Exp, scale=-K0), reads=[b_E["d2"]], writes=[b_E["gtail"]])
                P.op("act", lambda e: e.activation(out=GE[:, 0:nj], in_=cs3[:, :, 63], func=AF.Exp, scale=-K0), reads=[b_E["cs"]], writes=[b_GE])
                yield
                P.op("dve", lambda e: e.tensor_scalar(out=E["kk"][:, 0:W], in0=X["k"][:, 0:W], scalar1=pp[:, 17:18], scalar2=None, op0=ALU.mult),
                     reads=[b_X, b_pp], writes=[b_E["kk"]])
                P.op("act", lambda e: e.activation(out=E["kk2"][:, 0:W], in_=E["kk"][:, 0:W], func=AF.Square), reads=[b_E["kk"], b_E["tmp"]], writes=[b_E["kk2"]])
                P.op("pe", lambda e: e.matmul(out=PBk(2)[:, 0:W], lhsT=BLK, rhs=E["kk2"][:, 0:W], start=True, stop=True), reads=[b_cst, b_E["kk2"]], writes=[bPBk(2)])
                P.op("dve", lambda e: e.tensor_scalar(out=E["rn"][:, 0:W], in0=PBk(2)[:, 0:W], scalar1=1e-24, scalar2=None, op0=ALU.max),
                     reads=[bPBk(2)], writes=[b_E["rn"]])
                P.op("act", lambda e: e.activation(out=E["rn"][:, 0:W], in_=E["rn"][:, 0:W], func=AF.Sqrt), reads=[b_E["rn"]], writes=[b_E["rn"]])
                P.op("dve", lambda e: e.reciprocal(out=E["rn"][:, 0:W], in_=E["rn"][:, 0:W]), reads=[b_E["rn"]], writes=[b_E["rn"]])
                P.op("dve", lambda e: e.tensor_tensor(out=E["kap"][:, 0:W], in0=E["kk"][:, 0:W], in1=E["rn"][:, 0:W], op=ALU.mult),
                     reads=[b_E["kk"], b_E["rn"]], writes=[b_E["kap"]])
                P.op("dve", lambda e: e.tensor_tensor(out=E["kapa"][:, 0:W], in0=E["kap"][:, 0:W], in1=E["a"][:, 0:W], op=ALU.mult),
                     reads=[b_E["kap"], b_E["a"]], writes=[b_E["kapa"]])
                yield

                def v4(tns, slot_):
                    return tns[:, 0:nj, slot_, :]

                def e3(n):
                    return E[n][:, 0:W].rearrange("p (j t) -> p j t", t=64)
                x3r = X["r"][:, 0:W].rearrange("p (j t) -> p j t", t=64)
                x3v = X["v"][:, 0:W].rearrange("p (j t) -> p j t", t=64)
                P.op("dve", lambda e: e.scalar_tensor_tensor(out=v4(AR, 0), in0=e3("kap"), scalar=-1.0, op0=ALU.mult, in1=e3("gexc"), op1=ALU.mult),
                     reads=[b_E["kap"], b_E["gexc"]], writes=[b_AR])
                P.op("dve", lambda e: e.tensor_tensor(out=v4(AR, 1), in0=x3r, in1=e3("ginc"), op=ALU.mult), reads=[b_X, b_E["ginc"]], writes=[b_AR])

                def halves(tns, n_beta, n_k, g_, b_dst, eng):
                    for (h, sb_, sk_) in ((0, 0, 1), (64, 1, 0)):
                        P.op(eng, lambda e, h=h, sb_=sb_: e.tensor_tensor(out=tns[h:h + 64, 0:nj, sb_, :], in0=e3(n_beta)[h:h + 64], in1=e3(g_)[h:h + 64], op=ALU.mult),
                             reads=[b_E[n_beta], b_E[g_]], writes=[b_dst])
                        P.op(eng, lambda e, h=h, sk_=sk_: e.tensor_tensor(out=tns[h:h + 64, 0:nj, sk_, :], in0=e3(n_k)[h:h + 64], in1=e3(g_)[h:h + 64], op=ALU.mult),
                             reads=[b_E[n_k], b_E[g_]], writes=[b_dst])
                halves(BK, "kapa", "kd", "ginv", b_BK, "dve")
                halves(BKH, "kapa", "kd", "gtail", b_BKH, "pool")
                P.op("pool", lambda e: e.tensor_copy(out=v4(VV, 0), in_=x3v), reads=[b_X], writes=[b_VV])
                P.op("pool", lambda e: e.tensor_copy(out=v4(VV, 1), in_=x3v), reads=[b_X], writes=[b_VV])
                if g + 1 < nsteps_run:
                    cf2, nj2 = steps[g + 1]
                    C.load_raw(nj2 * 64, cf2 * 64, T - cf2 * 64 - nj2 * 64)
                yield

                def uidx(d, j):
                    return d * 8 + j
                SBm = {0: 0, 1: 1}
                for d in (0, 1):
                    h = 64 * d
                    msk = (MF if d == 0 else MB)
                    for j0 in range(0, nj, 4):
                        n_in = min(4, nj - j0)
                        bk = 2 * d + (j0 // 4) % 2
                        for jo in range(n_in):
                            j = j0 + jo
                            off = jo * 128
                            lhs = BK[h:h + 64, j, :, :].rearrange("p a t -> p (a t)")
                            rhs = AR[h:h + 64, j, :, :].rearrange("p a t -> p (a t)")
                            P.op("pe", lambda e, bk=bk, off=off, lhs=lhs, rhs=rhs: e.matmul(out=PBk(bk)[:, off:off + 128], lhsT=lhs, rhs=rhs, start=True, stop=True),
                                 reads=[b_BK, b_AR], writes=[bPBk(bk)])
                        u0 = uidx(d, j0)
                        P.op("dve", lambda e, bk=bk, n_in=n_in, u0=u0, msk=msk: e.tensor_tensor(
                            out=MX[:, u0:u0 + n_in, :], in0=PBk(bk)[:, 0:n_in * 128].rearrange("p (u t) -> p u t", t=128),
                            in1=msk.unsqueeze(1).to_broadcast([128, n_in, 128]), op=ALU.mult), reads=[bPBk(bk), b_cst], writes=[b_MX])
                    yield
                for d in (0, 1):
                    h = 64 * d
                    for j in range(nj):
                        lhs = AR[h:h + 64, j, 0, :]
                        rhs = BK[h:h + 64, j, SBm[d], :]
                        P.op("pe", lambda e, j=j, h=h, lhs=lhs, rhs=rhs: e.matmul(out=PBk(0)[h:h + 64, j * 64:(j + 1) * 64], lhsT=lhs, rhs=rhs, start=True, stop=True),
                             reads=[b_AR, b_BK], writes=[bPBk(0)])
                P.op("dve", lambda e: e.tensor_tensor(
                    out=Nm[0][:, 0:nj, :], in0=PBk(0)[:, 0:nj * 64].rearrange("p (u t) -> p u t", t=64),
                    in1=cst[:, 384:448].unsqueeze(1).to_broadcast([128, nj, 64]), op=ALU.mult), reads=[bPBk(0), b_cst], writes=[b_Nm[0]])
                for d in (0, 1):
                    h = 64 * d
                    P.op("dve", lambda e, d=d, h=h: e.tensor_tensor(out=Rm[h:h + 64, 0:nj, :], in0=MX[h:h + 64, d * 8:d * 8 + nj, 0:64],
                                                                    in1=cst[h:h + 64, h:h + 64].unsqueeze(1).to_broadcast([64, nj, 64]), op=ALU.add),
                         reads=[b_MX, b_cst], writes=[b_Rm])
                    P.op("pool", lambda e, d=d, h=h: e.tensor_copy(out=Mm[0][h:h + 64, 0:nj, :], in_=MX[h:h + 64, d * 8:d * 8 + nj, 0:64]), reads=[b_MX], writes=[b_Mm[0]])
                    P.op("act", lambda e, h=h: e.copy(out=Rb[h:h + 64, 0:nj, :], in_=Rm[h:h + 64, 0:nj, :]), reads=[b_Rm], writes=[b_Rb])
                yield

                def prod(bank0, lhs_fn, rhs_fn, reads, evac, rev=False):
                    bk = bank0 // 2
                    for d in (0, 1):
                        h = 64 * d
                        for j in range(nj):
                            lhs = lhs_fn(d, h, j)
                            rhs = rhs_fn(d, h, j)
                            jp = (nj - 1 - j) if (rev and d == 1) else j
                            P.op("pe", lambda e, jp=jp, h=h, lhs=lhs, rhs=rhs: e.matmul(out=PBk(bk)[h:h + 64, jp * 64:(jp + 1) * 64], lhsT=lhs, rhs=rhs, start=True, stop=True),
                                 reads=reads, writes=[bPBk(bk)])
                    evac(bk, PBk(bk)[:, 0:nj * 64].rearrange("p (u t) -> p u t", t=64))

                def ev_copy(dst_t, b_dst, eng):
                    def f(bk, pv):
                        dst = dst_t[:, 0:nj, :]
                        if eng == "act":
                            P.op("act", lambda e: e.copy(out=dst, in_=pv), reads=[bPBk(bk)], writes=[b_dst])
                        else:
                            P.op("dve", lambda e: e.tensor_copy(out=dst, in_=pv), reads=[bPBk(bk)], writes=[b_dst])
                    return f

                def ev_acc(dst_t, b_dst):
                    def f(bk, pv):
                        dst = dst_t[:, 0:nj, :]
                        P.op("dve", lambda e: e.tensor_tensor(out=dst, in0=dst, in1=pv, op=ALU.add), reads=[bPBk(bk), b_dst], writes=[b_dst])
                    return f

                cur = 0
                for k in range(1, 6):
                    nxt = 1 - cur
                    Nc, Mc, Nn, Mn = Nm[cur], Mm[cur], Nm[nxt], Mm[nxt]
                    if k <= 4:
                        prod(0, lambda d, h, j, Nc=Nc: Nc[h:h + 64, j, :], lambda d, h, j, Mc=Mc: Mc[h:h + 64, j, :], [b_Nm[cur], b_Mm[cur]], ev_copy(Mn, b_Mm[nxt], "act"))
                    prod(2, lambda d, h, j, Mc=Mc: Mc[h:h + 64, j, :], lambda d, h, j, Nc=Nc: Nc[h:h + 64, j, :], [b_Mm[cur], b_Nm[cur]], ev_copy(Nn, b_Nm[nxt], "dve"))
                    yield
                    prod(4, lambda d, h, j, Nn=Nn: Nn[h:h + 64, j, :], lambda d, h, j: Rb[h:h + 64, j, :], [b_Nm[nxt], b_Rb], ev_acc(Rm, b_Rm))
                    if k < 5:
                        P.op("act", lambda e: e.copy(out=Rb[:, 0:nj, :], in_=Rm[:, 0:nj, :]), reads=[b_Rm], writes=[b_Rb])
                    yield
                    cur = nxt
                for (srct, b_src, dst, b_dst, bks) in ((BKH, b_BKH, BKHs, b_BKHs, (0, 1)), (VV, b_VV, V2s, b_V2s, (2, 3))):
                    for d in (0, 1):
                        h = 64 * d
                        bk = bks[d]
                        for j in range(nj):
                            P.op("pe", lambda e, h=h, j=j, bk=bk, srct=srct: e.transpose(out=PBk(bk)[:, j * 64:(j + 1) * 64], in_=srct[h:h + 64, j, :, :].rearrange("p a t -> p (a t)"),
                                                                                        identity=cst[h:h + 64, h:h + 64]),
                                 reads=[b_src, b_cst], writes=[bPBk(bk)])
                        P.op("act", lambda e, d=d, bk=bk, dst=dst: e.copy(out=dst[:, d * 8:d * 8 + nj, :], in_=PBk(bk)[:, 0:nj * 64].rearrange("p (u t) -> p u t", t=64)),
                             reads=[bPBk(bk)], writes=[b_dst])
                    yield

                def hv_of(h):
                    return 64 - h
                prod(0, lambda d, h, j: MX[hv_of(h):hv_of(h) + 64, uidx(d, j), 0:64], lambda d, h, j: V2s[hv_of(h):hv_of(h) + 64, uidx(d, j), :],
                     [b_MX, b_V2s], ev_copy(BVs, b_BVs, "act"), rev=True)
                prod(2, lambda d, h, j: V2s[hv_of(h):hv_of(h) + 64, uidx(d, j), :], lambda d, h, j: MX[hv_of(h):hv_of(h) + 64, uidx(d, j), 64:128],
                     [b_MX, b_V2s], ev_copy(YVs, b_YVs, "dve"))
                yield
                prod(4, lambda d, h, j: BKHs[hv_of(h):hv_of(h) + 64, uidx(d, j), :], lambda d, h, j: V2s[hv_of(h):hv_of(h) + 64, uidx(d, j), :],
                     [b_BKHs, b_V2s], ev_copy(KVs, b_KVs, "act"), rev=True)
                yield
                for i in range(nj):
                    jj = {0: i, 1: nj - 1 - i}
                    for d in (0, 1):
                        h = 64 * d
                        j = jj[d]
                        P.op("pe", lambda e, h=h, j=j: e.matmul(out=PBk(0)[h:h + 64, 0:64], lhsT=AR[h:h + 64, j, 0, :], rhs=ST[h:h + 64, :], start=True, stop=True),
                             reads=[b_AR, b_ST], writes=[bPBk(0)])
                    P.op("dve", lambda e, i=i: e.tensor_tensor(out=Ws[:, :], in0=PBk(0)[:, 0:64], in1=BVs[:, i, :], op=ALU.add),
                         reads=[bPBk(0), b_BVs], writes=[b_Ws])
                    for d in (0, 1):
                        h = 64 * d
                        j = jj[d]
                        P.op("dve", lambda e, h=h, j=j, i=i: e.scalar_tensor_tensor(out=ST2[h:h + 64, :], in0=ST[h:h + 64, :], scalar=GE[h:h + 64, j:j + 1], op0=ALU.mult,
                                                                                  in1=KVs[h:h + 64, i, :], op1=ALU.add), reads=[b_ST, b_GE, b_KVs], writes=[b_ST2])
                    yield
                    for d in (0, 1):
                        h = 64 * d
                        j = jj[d]
                        P.op("pe", lambda e, h=h, j=j: e.matmul(out=PBk(0)[h:h + 64, 64:128], lhsT=Rm[h:h + 64, j, :], rhs=Ws[h:h + 64, :], start=True, stop=True),
                             reads=[b_Rm, b_Ws], writes=[bPBk(0)])
                    P.op("act", lambda e: e.copy(out=Us[:, :], in_=PBk(0)[:, 64:128]), reads=[bPBk(0)], writes=[b_Us])
                    yield
                    for d in (0, 1):
                        h = 64 * d
                        j = jj[d]
                        u = uidx(d, j)
                        P.op("pe", lambda e, h=h, u=u: e.matmul(out=PBk(0)[h:h + 64, 128:192], lhsT=BKHs[h:h + 64, u, :], rhs=Us[h:h + 64, :], start=True, stop=True),
                             reads=[b_BKHs, b_Us], writes=[bPBk(0)])
                        P.op("pe", lambda e, h=h, j=j, d=d: e.matmul(out=PBk(2 + d)[h:h + 64, j * 64:(j + 1) * 64], lhsT=ST[h:h + 64, :], rhs=AR[h:h + 64, j, 1, :], start=True, stop=False),
                             reads=[b_ST, b_AR], writes=[bPBk(2 + d)])
                        P.op("pe", lambda e, h=h, j=j, u=u, d=d: e.matmul(out=PBk(2 + d)[h:h + 64, j * 64:(j + 1) * 64], lhsT=Us[h:h + 64, :], rhs=MX[h:h + 64, u, 64:128], start=False, stop=True),
                             reads=[b_Us, b_MX], writes=[bPBk(2 + d)])
                    P.op("dve", lambda e: e.tensor_tensor(out=ST[:, :], in0=ST2[:, :], in1=PBk(0)[:, 128:192], op=ALU.add),
                         reads=[b_ST2, bPBk(0)], writes=[b_ST])
                    yield
                ys = g % 2
                for d in (0, 1):
                    h = 64 * d
                    P.op("dve", lambda e, h=h, d=d: e.tensor_tensor(out=Yst[ys][h:h + 64, 0:W].rearrange("p (j t) -> p j t", t=64),
                                                                    in0=PBk(2 + d)[h:h + 64, 0:W].rearrange("p (j t) -> p j t", t=64),
                                                                    in1=YVs[h:h + 64, 0:nj, :], op=ALU.add), reads=[bPBk(2 + d), b_YVs], writes=[b_Yst[ys]])
                def flush(ys=ys, f0=f0, b0=b0, W=W):
                    P.dma(lambda e: e.dma_start(out=YD[kind][pi, 0:64, f0:f0 + W], in_=Yst[ys][0:64, 0:W]), reads=[b_Yst[ys]], writes=[b_YD])
                    P.dma(lambda e: e.dma_start(out=YD[kind][pi, 64:128, b0:b0 + W], in_=Yst[ys][64:128, 0:W]), reads=[b_Yst[ys]], writes=[b_YD])
                pending.append(flush)
                yield

            for g in range(nsteps_run):
                yield from do_step(g)
            while pending:
                pending.pop(0)()

        def rwkv_stage_e(st, kind, pi, slot):
            C = RwkvCtx(st, kind, pi, slot, False)
            T, F, pp, lg = C.T, C.F, C.pp, C.lg
            X, E, t1 = C.X, C.E, C.t1
            b_X, b_E, b_pp, b_t1 = C.b_X, C.b_E, C.b_pp, C.b_t1
            PBk, bPBk = C.PBk, C.bPBk
            WW = 512
            rawg = F("rawg", [128, WW + 2]); b_rawg = Buf()
            Xg = F("Xg", [128, WW]); b_Xg = Buf()
            Yt = F("Yt", [128, WW]); b_Yt = Buf()
            Ys = F("Ysum", [64, WW]); b_Ys = Buf()
            Dd = F("Dd", [64, WW]); b_Dd = Buf()
            D2 = F("D2", [64, WW]); b_D2 = Buf()
            Rs = F("Rs", [64, WW]); b_Rs = Buf()
            Oa = [F("Oa%d" % i, [64, WW], BF16) for i in range(2)]; b_Oa = [Buf(), Buf()]
            tiles = [(t0, 512) for t0 in range(0, 2048, 512)]
            pending = []
            yield

            def do_tile(ti, t0, W):
                C.load_raw(W, t0, t0)
                P.dma(lambda e: e.dma_start(out=rawg[:, 0:W + 2], in_=C.paL[128:256, t0:t0 + W + 2]), reads=[b_PAO], writes=[b_rawg])
                P.dma(lambda e: e.dma_start(out=Yt[:, 0:W], in_=YDO[kind][pi, :, t0:t0 + W]), reads=[b_YDO], writes=[b_Yt])
                while pending:
                    pending.pop(0)()
                C.prep_common(W)
                yield
                P.op("act", lambda e: e.activation(out=t1[:, 0:W], in_=rawg[:, 1:W + 1], func=AF.Identity, scale=pp[:, 22:23]), reads=[b_rawg, b_pp], writes=[b_t1])
                P.op("dve", lambda e: e.scalar_tensor_tensor(out=t1[:, 0:W], in0=rawg[:, 0:W], scalar=pp[:, 23:24], op0=ALU.mult, in1=t1[:, 0:W], op1=ALU.add),
                     reads=[b_rawg, b_pp, b_t1], writes=[b_t1])
                P.op("dve", lambda e: e.scalar_tensor_tensor(out=Xg[:, 0:W], in0=rawg[:, 2:W + 2], scalar=pp[:, 24:25], op0=ALU.mult, in1=t1[:, 0:W], op1=ALU.add),
                     reads=[b_rawg, b_pp, b_t1], writes=[b_Xg])
                P.op("act", lambda e: e.activation(out=Xg[:, 0:W], in_=Xg[:, 0:W], func=AF.Sigmoid), reads=[b_Xg], writes=[b_Xg])
                P.op("dve", lambda e: e.scalar_tensor_tensor(out=E["tmp"][:, 0:W], in0=X["r"][:, 0:W], scalar=pp[:, 19:20], op0=ALU.mult, in1=E["kd"][:, 0:W], op1=ALU.mult),
                     reads=[b_X, b_pp, b_E["kd"]], writes=[b_E["tmp"]])
                P.op("pe", lambda e: e.matmul(out=PBk(1)[0:64, 0:W], lhsT=ONES[:, 0:64], rhs=E["tmp"][:, 0:W], start=True, stop=True), reads=[b_cst, b_E["tmp"]], writes=[bPBk(1)])
                P.op("pe", lambda e: e.matmul(out=PBk(2)[0:64, 0:W], lhsT=SEL, rhs=Yt[:, 0:W], start=True, stop=True), reads=[b_cst, b_Yt], writes=[bPBk(2)])
                P.op("act", lambda e: e.copy(out=Ys[:, 0:W], in_=PBk(2)[0:64, 0:W]), reads=[bPBk(2)], writes=[b_Ys])
                yield
                P.op("pe", lambda e: e.matmul(out=PBk(3)[0:64, 0:W], lhsT=O64, rhs=Ys[:, 0:W], start=True, stop=True), reads=[b_cst, b_Ys], writes=[bPBk(3)])
                P.op("dve", lambda e: e.tensor_tensor(out=Dd[:, 0:W], in0=Ys[:, 0:W], in1=PBk(3)[0:64, 0:W], op=ALU.subtract), reads=[b_Ys, bPBk(3)], writes=[b_Dd])
                P.op("act", lambda e: e.activation(out=D2[:, 0:W], in_=Dd[:, 0:W], func=AF.Square), reads=[b_Dd], writes=[b_D2])
                P.op("pe", lambda e: e.matmul(out=PBk(3)[0:64, 0:W], lhsT=O64, rhs=D2[:, 0:W], start=True, stop=True), reads=[b_cst, b_D2], writes=[bPBk(3)])
                P.op("act", lambda e: e.activation(out=Rs[:, 0:W], in_=PBk(3)[0:64, 0:W], func=AF.Sqrt, bias=epsb[0:64, 2:3]), reads=[bPBk(3), b_cst], writes=[b_Rs])
                yield
                P.op("dve", lambda e: e.reciprocal(out=Rs[:, 0:W], in_=Rs[:, 0:W]), reads=[b_Rs], writes=[b_Rs])
                P.op("dve", lambda e: e.tensor_tensor(out=Dd[:, 0:W], in0=Dd[:, 0:W], in1=Rs[:, 0:W], op=ALU.mult), reads=[b_Dd, b_Rs], writes=[b_Dd])
                P.op("dve", lambda e: e.tensor_scalar(out=Dd[:, 0:W], in0=Dd[:, 0:W], scalar1=pp[0:64, 20:21], scalar2=pp[0:64, 21:22], op0=ALU.mult, op1=ALU.add),
                     reads=[b_Dd, b_pp], writes=[b_Dd])
                P.op("dve", lambda e: e.tensor_tensor(out=D2[:, 0:W], in0=PBk(1)[0:64, 0:W], in1=X["v"][0:64, 0:W], op=ALU.mult), reads=[bPBk(1), b_X], writes=[b_D2])
                P.op("dve", lambda e: e.tensor_tensor(out=Dd[:, 0:W], in0=Dd[:, 0:W], in1=D2[:, 0:W], op=ALU.add), reads=[b_Dd, b_D2], writes=[b_Dd])
                P.op("pe", lambda e: e.matmul(out=PBk(0)[0:64, 0:W], lhsT=lg[:, :], rhs=Xg[:, 0:W], start=True, stop=True), reads=[b_pp, b_Xg], writes=[bPBk(0)])
                ob = ti % 2
                P.op("dve", lambda e: e.tensor_tensor(out=Oa[ob][:, 0:W], in0=Dd[:, 0:W], in1=PBk(0)[0:64, 0:W], op=ALU.mult), reads=[b_Dd, bPBk(0)], writes=[b_Oa[ob]])
                def flush(ob=ob, t0=t0, W=W):
                    P.dma(lambda e: e.dma_start(out=OAO[kind][pi, :, t0:t0 + W], in_=Oa[ob][:, 0:W]), reads=[b_Oa[ob]], writes=[b_OAO])
                pending.append(flush)
                yield

            if not dbg.get("skip_stage_e"):
                for ti, (t0, W) in enumerate(tiles):
                    yield from do_tile(ti, t0, W)
                while pending:
                    pending.pop(0)()

        def rwkv_group(kind, plist, stage):
            with ExitStack() as st:
                if stage == 0:
                    run_streams([rwkv_steps(st, kind, p, i) for i, p in enumerate(plist)])
                else:
                    run_streams([rwkv_stage_e(st, kind, p, i) for i, p in enumerate(plist)])
                P.barrier()

        kinds = dbg.get("kinds", [0, 1])
        for kind in kinds:
            if do_p1:
                p1_phase(kind)
                rotate_phase(kind)
            if do_attn:
                attn_phase(kind, pieces_run)
            if do_rwkv:
                ns = dbg.get("streams", 2)
                for i in range(0, len(pieces_run), ns):
                    rwkv_group(kind, pieces_run[i:i + ns], 0)
                P.barrier()
                own_y_phase(kind)
                for i in range(0, len(pieces_run), ns):
                    rwkv_group(kind, pieces_run[i:i + ns], 1)
            P.barrier()

        def post_phase(kind, xsrc, x_row0, yout, tg):
            with ExitStack() as st:
                def F(name, shape, dt=F32):
                    return sb(st, tg + name, shape, dt)
                wg = F("wg", [128, 8, 2048], BF16); wua = F("wua", [128, 4, D], BF16); wub = F("wub", [128, 4, D], BF16)
                wo = [F("wo%d" % i, [128, 8, 128], BF16) for i in range(2)]; b_wo = [Buf(), Buf()]
                b_w = Buf()
                P.dma(lambda e: e.dma_start(out=wg[:], in_=wg_b.rearrange("(k p) m -> p k m", p=128)), reads=[b_wsc], writes=[b_w])
                P.dma(lambda e: e.dma_start(out=wua[:], in_=wua_b.rearrange("(k p) m -> p k m", p=128)), reads=[b_wsc], writes=[b_w])
                P.dma(lambda e: e.dma_start(out=wub[:], in_=wub_b.rearrange("(k p) m -> p k m", p=128)), reads=[b_wsc], writes=[b_w])
                w1 = [F("w1_%d" % i, [128, 8, 256], BF16) for i in range(2)]; b_w1 = [Buf(), Buf()]
                w2 = [F("w2_%d" % i, [128, 32, 128], BF16) for i in range(2)]; b_w2 = [Buf(), Buf()]
                gfin = F("gfin", [128, D]); slg = F("slg", [128, 1]); lamt = F("lamt", [128, 4, 64]); lamv = F("lamv", [128, 8])
                b_c = Buf()
                P.dma(lambda e: e.dma_start(out=gfin[:], in_=gfin_d), writes=[b_c])
                P.dma(lambda e: e.dma_start(out=slg[:], in_=slg_d), writes=[b_c])
                P.dma(lambda e: e.dma_start(out=lamt[:], in_=lam_d), writes=[b_c])
                P.op("dve", lambda e: e.tensor_tensor(out=lamt[:, 0, :], in0=lamt[:, 0, :], in1=lamt[:, 1, :], op=ALU.mult), reads=[b_c], writes=[b_c])
                P.op("dve", lambda e: e.tensor_tensor(out=lamt[:, 2, :], in0=lamt[:, 2, :], in1=lamt[:, 3, :], op=ALU.mult), reads=[b_c], writes=[b_c])
                P.op("dve", lambda e: e.tensor_reduce(out=lamv[:, 0:1], in_=lamt[:, 0, :], op=ALU.add, axis=mybir.AxisListType.X), reads=[b_c], writes=[b_c])
                P.op("dve", lambda e: e.tensor_reduce(out=lamv[:, 1:2], in_=lamt[:, 2, :], op=ALU.add, axis=mybir.AxisListType.X), reads=[b_c], writes=[b_c])
                P.op("act", lambda e: e.activation(out=lamv[:, 2:4], in_=lamv[:, 0:2], func=AF.Exp), reads=[b_c], writes=[b_c])
                P.op("dve", lambda e: e.tensor_tensor(out=lamv[:, 4:5], in0=lamv[:, 3:4], in1=lamv[:, 2:3], op=ALU.subtract), reads=[b_c], writes=[b_c])
                P.op("dve", lambda e: e.tensor_scalar(out=lamv[:, 4:5], in0=lamv[:, 4:5], scalar1=-LAMBDA_INIT, scalar2=None, op0=ALU.add), reads=[b_c], writes=[b_c])
                P.op("dve", lambda e: e.tensor_scalar(out=slg[:], in0=slg[:], scalar1=1.0 - LAMBDA_INIT, scalar2=None, op0=ALU.mult), reads=[b_c], writes=[b_c])
                xres = F("xres", [128, 4, D]); b_xres = Buf()
                h2 = F("h2", [128, 4, D]); b_h2 = Buf()
                sq_sh = F("sqsh", [128, D])
                tmp = [(sq_sh, F("ss%d" % i, [128, 4]), F("xn%d" % i, [128, D], BF16), Buf()) for i in range(2)]
                nT = F("nT", [128, 8, 512], BF16); b_nT = Buf()
                oaT = F("oaT", [128, 4, 512], BF16); b_oaT = Buf()
                AB = F("AB", [128, 2, 512], BF16); b_AB = Buf()
                Dh = F("Dh", [128, 512]); b_Dh = Buf()
                D2h = F("D2h", [128, 512]); b_D2h = Buf()
                rsh = F("rsh", [128, 512]); b_rsh = Buf()
                obT = F("obT", [128, 4, 512], BF16); b_obT = Buf()
                sg = [F("sg%d" % i, [128, 512]) for i in range(2)]; b_sg = [Buf(), Buf()]
                ma = F("ma", [128, 512]); b_ma = Buf()
                mT = F("mT", [128, 8, 512], BF16); b_mT = Buf()
                ao = [F("ao%d" % i, [128, 512]) for i in range(2)]; b_ao = [Buf(), Buf()]
                hT = F("hT", [128, 32, 512], BF16); b_hT = Buf()
                hx = [F("hx%d" % i, [128, 512]) for i in range(2)]; b_hx = [Buf(), Buf()]
                yo = xres; b_yo = b_xres
                sqf = tmp[0][0]; ssf = F("ssf", [128, 4]); b_f = tmp[0][3]
                wcnt = {"w1": 0, "w2": 0}

                def do_tt(tt):
                    tk0 = tt * 512
                    for s in range(4):
                        P.dma(lambda e, s=s: e.dma_start(out=xres[:, s, :], in_=xsrc[x_row0 + tk0 + s * 128:x_row0 + tk0 + (s + 1) * 128, :]), writes=[b_xres])
                    for q in range(4):
                        for hh in range(2):
                            P.dma(lambda e, q=q, hh=hh: e.dma_start(out=oaT[hh * 64:(hh + 1) * 64, q, :], in_=OAO[kind][2 * q + hh, :, tk0:tk0 + 512]),
                                  reads=[b_OAO], writes=[b_oaT])
                    for s in range(4):
                        norm_T2(tmp[s % 2], xres[:, s, :], b_xres, nT[:, :, s * 128:(s + 1) * 128], b_nT, s % 2)
                    for hh in range(4):
                        for mm in range(2):
                            P.dma(lambda e, hh=hh, mm=mm: e.dma_start(out=AB[:, mm, :], in_=XR[kind][2 * hh + mm, :, tk0:tk0 + 512]), reads=[b_XR[kind]], writes=[b_AB])
                        P.op("dve", lambda e, hh=hh: e.scalar_tensor_tensor(out=Dh[:], in0=AB[:, 1, :], scalar=lamv[:, 4:5], op0=ALU.mult, in1=AB[:, 0, :], op1=ALU.add),
                             reads=[b_AB, b_c], writes=[b_Dh])
                        P.op("act", lambda e: e.activation(out=D2h[:], in_=Dh[:], func=AF.Square), reads=[b_Dh], writes=[b_D2h])
                        P.op("pe", lambda e: e.matmul(out=PB[2][:, :], lhsT=O128, rhs=D2h[:], start=True, stop=True), reads=[b_cst, b_D2h], writes=[bPB[2]])
                        P.op("act", lambda e: e.activation(out=rsh[:], in_=PB[2][:, :], func=AF.Sqrt, bias=epsb[:, 1:2]), reads=[bPB[2], b_cst], writes=[b_rsh])
                        P.op("dve", lambda e: e.reciprocal(out=rsh[:], in_=rsh[:]), reads=[b_rsh], writes=[b_rsh])
                        P.op("dve", lambda e, hh=hh: e.scalar_tensor_tensor(out=obT[:, hh, :], in0=Dh[:], scalar=slg[:, 0:1], op0=ALU.mult, in1=rsh[:], op1=ALU.mult),
                             reads=[b_Dh, b_rsh, b_c], writes=[b_obT])
                    for mo in range(8):
                        for br, (wu, src, b_src) in enumerate(((wua, oaT, b_oaT), (wub, obT, b_obT))):
                            gb = 3 + br
                            ub = 5 + br
                            for k in range(8):
                                P.op("pe", lambda e, k=k, mo=mo, br=br, gb=gb: e.matmul(out=PB[gb][:, :], lhsT=wg[:, k, br * 1024 + mo * 128:br * 1024 + (mo + 1) * 128], rhs=nT[:, k, :],
                                                                                        start=(k == 0), stop=(k == 7)), reads=[b_w, b_nT], writes=[bPB[gb]])
                            P.op("act", lambda e, br=br, gb=gb: e.activation(out=sg[br][:], in_=PB[gb][:, :], func=AF.Sigmoid), reads=[bPB[gb]], writes=[b_sg[br]])
                            for k in range(4):
                                P.op("pe", lambda e, k=k, mo=mo, wu=wu, src=src, ub=ub: e.matmul(out=PB[ub][:, :], lhsT=wu[:, k, mo * 128:(mo + 1) * 128], rhs=src[:, k, :],
                                                                                                 start=(k == 0), stop=(k == 3)), reads=[b_w, b_src], writes=[bPB[ub]])
                        P.op("dve", lambda e: e.tensor_tensor(out=ma[:], in0=sg[0][:], in1=PB[5][:, :], op=ALU.mult), reads=[b_sg[0], bPB[5]], writes=[b_ma])
                        P.op("dve", lambda e: e.tensor_tensor(out=sg[1][:], in0=sg[1][:], in1=PB[6][:, :], op=ALU.mult), reads=[b_sg[1], bPB[6]], writes=[b_sg[1]])
                        P.op("pool", lambda e, mo=mo: e.tensor_tensor(out=mT[:, mo, :], in0=ma[:], in1=sg[1][:], op=ALU.add), reads=[b_ma, b_sg[1]], writes=[b_mT])
                    for mo in range(8):
                        ab = mo % 2
                        P.dma(lambda e, ab=ab, mo=mo: e.dma_start(out=wo[ab][:], in_=wout_b[:, mo * 128:(mo + 1) * 128].rearrange("(k p) m -> p k m", p=128)),
                              reads=[b_wsc], writes=[b_wo[ab]])
                        for k in range(8):
                            P.op("pe", lambda e, k=k, ab=ab: e.matmul(out=PB[3][:, :], lhsT=wo[ab][:, k, :], rhs=mT[:, k, :], start=(k == 0), stop=(k == 7)),
                                 reads=[b_wo[ab], b_mT], writes=[bPB[3]])
                        P.op("act", lambda e, ab=ab: e.copy(out=ao[ab][:], in_=PB[3][:, :]), reads=[bPB[3]], writes=[b_ao[ab]])
                        for s in range(4):
                            P.op("pe", lambda e, s=s, ab=ab: e.transpose(out=PB[4][:, s * 128:(s + 1) * 128], in_=ao[ab][:, s * 128:(s + 1) * 128], identity=ident),
                                 reads=[b_ao[ab], b_cst], writes=[bPB[4]])
                        P.op("dve", lambda e, mo=mo: e.tensor_tensor(out=h2[:, :, mo * 128:(mo + 1) * 128], in0=xres[:, :, mo * 128:(mo + 1) * 128],
                                                                     in1=PB[4][:, :].rearrange("p (s f) -> p s f", s=4), op=ALU.add), reads=[b_xres, bPB[4]], writes=[b_h2])
                    for s in range(4):
                        norm_T2(tmp[s % 2], h2[:, s, :], b_h2, nT[:, :, s * 128:(s + 1) * 128], b_nT, s % 2)
                    for mg in range(16):
                        wb = wcnt["w1"] % 2
                        wcnt["w1"] += 1
                        P.dma(lambda e, wb=wb, mg=mg: e.dma_start(out=w1[wb][:], in_=wf1_b[:, mg * 256:(mg + 1) * 256].rearrange("(k p) m -> p k m", p=128)),
                              reads=[b_wsc], writes=[b_w1[wb]])
                        for mc in range(2):
                            hb_ = mc % 2
                            pbk = 3 + (mc % 2)
                            for k in range(8):
                                P.op("pe", lambda e, k=k, mc=mc, wb=wb, pbk=pbk: e.matmul(out=PB[pbk][:, :], lhsT=w1[wb][:, k, mc * 128:(mc + 1) * 128], rhs=nT[:, k, :],
                                                                                          start=(k == 0), stop=(k == 7)), reads=[b_w1[wb], b_nT], writes=[bPB[pbk]])
                            P.op("act", lambda e, hb_=hb_, pbk=pbk: e.copy(out=hx[hb_][:], in_=PB[pbk][:, :]), reads=[bPB[pbk]], writes=[b_hx[hb_]])
                            P.op("dve", lambda e, hb_=hb_, mg=mg, mc=mc: e.scalar_tensor_tensor(out=hT[:, mg * 2 + mc, :], in0=hx[hb_][:], scalar=0.0, op0=ALU.max, in1=hx[hb_][:], op1=ALU.mult),
                                 reads=[b_hx[hb_]], writes=[b_hT])
                    for mg in range(8):
                        wb = wcnt["w2"] % 2
                        wcnt["w2"] += 1
                        P.dma(lambda e, wb=wb, mg=mg: e.dma_start(out=w2[wb][:], in_=wf2_b[:, mg * 128:(mg + 1) * 128].rearrange("(k p) m -> p k m", p=128)),
                              reads=[b_wsc], writes=[b_w2[wb]])
                        for mc in range(1):
                            mo = mg
                            ab = mo % 2
                            pbk = 5 + (mg % 2)
                            for k in range(32):
                                P.op("pe", lambda e, k=k, mc=mc, wb=wb, pbk=pbk: e.matmul(out=PB[pbk][:, :], lhsT=w2[wb][:, k, mc * 128:(mc + 1) * 128], rhs=hT[:, k, :],
                                                                                          start=(k == 0), stop=(k == 31)), reads=[b_w2[wb], b_hT], writes=[bPB[pbk]])
                            P.op("act", lambda e, ab=ab, pbk=pbk: e.copy(out=ao[ab][:], in_=PB[pbk][:, :]), reads=[bPB[pbk]], writes=[b_ao[ab]])
                            for s in range(4):
                                P.op("pe", lambda e, s=s, ab=ab: e.transpose(out=PB[7][:, s * 128:(s + 1) * 128], in_=ao[ab][:, s * 128:(s + 1) * 128], identity=ident),
                                     reads=[b_ao[ab], b_cst], writes=[bPB[7]])
                            P.op("dve", lambda e, mo=mo: e.tensor_tensor(out=yo[:, :, mo * 128:(mo + 1) * 128], in0=h2[:, :, mo * 128:(mo + 1) * 128],
                                                                         in1=PB[7][:, :].rearrange("p (s f) -> p s f", s=4), op=ALU.add), reads=[b_h2, bPB[7]], writes=[b_yo])
                    for s in range(4):
                        P.op("act", lambda e, s=s: e.activation(out=sqf[:], in_=yo[:, s, :], func=AF.Square, accum_out=ssf[:, 0:1]), reads=[b_yo], writes=[b_f])
                        P.op("act", lambda e: e.activation(out=ssf[:, 1:2], in_=ssf[:, 0:1], func=AF.Sqrt, scale=1.0 / D, bias=epsb[:, 0:1]), reads=[b_f, b_cst], writes=[b_f])
                        P.op("dve", lambda e: e.reciprocal(out=ssf[:, 2:3], in_=ssf[:, 1:2]), reads=[b_f], writes=[b_f])
                        P.op("dve", lambda e, s=s: e.scalar_tensor_tensor(out=yo[:, s, :], in0=yo[:, s, :], scalar=ssf[:, 2:3], op0=ALU.mult, in1=gfin[:], op1=ALU.mult),
                             reads=[b_yo, b_f, b_c], writes=[b_yo])
                        P.dma(lambda e, s=s: e.dma_start(out=yout[tk0 + s * 128:tk0 + (s + 1) * 128, :], in_=yo[:, s, :]), reads=[b_yo])

                for tt in range(dbg.get("post_tiles", 4)):
                    do_tt(tt)
                P.barrier()

        if do_post:
            if 0 in kinds:
                post_phase(0, xpp, 0, yp, "pp")
            if 1 in kinds:
                post_phase(1, hs, 16, ys, "ps")

        P.finish()
        P.emit(top)
    return nc


def prepare_inputs(inputs):
    f = lambda a: np.ascontiguousarray(np.asarray(a, dtype=np.float32))
    x_prompt, x_sample, meta = f(inputs["x_prompt"]), f(inputs["x_sample"]), f(inputs["meta_tokens"])
    w_in = f(inputs["w_in"])[0]
    mu_p, mu_n = f(inputs["mu_prev"])[0], f(inputs["mu_next"])[0]
    w0, w_up, a0, a_up, g_up = f(inputs["w0"])[0], f(inputs["w_up"])[0], f(inputs["a0"])[0], f(inputs["a_up"])[0], f(inputs["g_up"])[0]
    k_k, k_a, r_k = f(inputs["k_k"])[0], f(inputs["k_a"])[0], f(inputs["r_k"])[0].reshape(-1)
    ln_w, ln_b = f(inputs["ln_x_w"])[0], f(inputs["ln_x_b"])[0]
    hp = np.zeros((T_P, D), np.float32)
    hp[0:16] = meta
    hp[16:16 + 16384] = x_prompt[0]
    cst = _consts()
    gm = np.ascontiguousarray(f(inputs["g_mix"])[0].reshape(8, 128).T)
    gfc = np.ascontiguousarray(f(inputs["g_ffn"])[0].reshape(8, 128).T)
    gfin = np.ascontiguousarray(np.broadcast_to(f(inputs["g_final"])[None, :], (128, D)))
    slg = np.ascontiguousarray(f(inputs["subln_g"])[0].reshape(128, 1))
    lam = np.stack([f(inputs["lam_q1"])[0], f(inputs["lam_k1"])[0], f(inputs["lam_q2"])[0], f(inputs["lam_k2"])[0]])
    lam = np.ascontiguousarray(np.broadcast_to(lam[None], (128, 4, 64)))
    wg = np.ascontiguousarray(w_in[:, 3328:5376])

    def piece(ha, hb, m):
        cols = np.concatenate([np.arange(ha * 64, ha * 64 + 64), 512 + np.arange(ha * 64, ha * 64 + 64), 1024 + np.arange(ha * 64, ha * 64 + 64),
                               np.arange(1536, 1792),
                               1792 + hb * 128 + m * 64 + np.arange(64), 1792 + 512 + hb * 128 + m * 64 + np.arange(64),
                               1792 + 1024 + hb * 128 + np.arange(128)])
        W = w_in[:, cols]
        pp = np.zeros((128, NPP), np.float32)
        rc = cols[0:448]
        for gi in range(5):
            cc = rc[gi * 64:(gi + 1) * 64]
            for half in (0, 64):
                pp[half:half + 64, 3 * gi] = 1.0 - mu_p[cc] - mu_n[cc]
                pp[half:half + 64, 3 * gi + 1] = mu_p[cc]
                pp[half:half + 64, 3 * gi + 2] = mu_n[cc]
        cg = rc[320:448]
        pp[:, 22] = 1.0 - mu_p[cg] - mu_n[cg]
        pp[:, 23] = mu_p[cg]
        pp[:, 24] = mu_n[cg]
        ch = np.arange(ha * 64, ha * 64 + 64)
        for d in (0, 1):
            pp[d * 64:(d + 1) * 64, 15] = w0[d][ch]
            pp[d * 64:(d + 1) * 64, 16] = a0[d][ch]
            pp[d * 64:(d + 1) * 64, 17] = k_k[ch]
            pp[d * 64:(d + 1) * 64, 18] = k_a[ch]
            pp[d * 64:(d + 1) * 64, 19] = r_k[ch]
            pp[d * 64:(d + 1) * 64, 20] = ln_w[ch]
            pp[d * 64:(d + 1) * 64, 21] = ln_b[ch]
        lw = np.concatenate([w_up[0][:, ch], w_up[1][:, ch]], axis=0)
        la = np.concatenate([a_up[0][:, ch], a_up[1][:, ch]], axis=0)
        lg = g_up[:, ch]
        return W, pp, lw, la, lg

    spieces = [piece(p, p // 2, p % 2) for p in range(8)]
    stat = [_alibi_static(hb) for hb in range(4)]
    shared = dict(hp=hp, wg=wg, wua=f(inputs["w_up_a"])[0], wub=f(inputs["w_up_b"])[0], wout=f(inputs["w_out"])[0],
                  wf1=f(inputs["w_ff1"])[0], wf2=f(inputs["w_ff2"])[0], gm=gm, gf=gfc, gfin=gfin, slg=slg, lam=lam, cst=cst,
                  wpc=np.stack([p[0] for p in spieces]), pp=np.stack([p[1] for p in spieces]), lw=np.stack([p[2] for p in spieces]),
                  la=np.stack([p[3] for p in spieces]), lg=np.stack([p[4] for p in spieces]),
                  dm=np.stack([s[0] for s in stat]), qb=np.stack([s[1] for s in stat]))
    in_maps = []
    for c in range(NCORE):
        hs = np.zeros((T_S, D), np.float32)
        hs[0:16] = meta
        hs[16:16 + 2048] = x_sample[c]
        bc, ks, kval = _alibi_core(c)
        m = dict(shared)
        m["hs"] = hs
        m["xpp"] = np.ascontiguousarray(x_prompt[0, c * 2048:(c + 1) * 2048])
        m["bc"] = bc
        m["ks"] = ks
        m["kval"] = kval
        in_maps.append(m)
    return in_maps


_NC_CACHE = {}


def kernel(**inputs):
    in_maps = prepare_inputs(inputs)
    if "nc" not in _NC_CACHE:
        _NC_CACHE["nc"] = build_program()
    nc = _NC_CACHE["nc"]
    res = run_bass_kernel_spmd(nc, in_maps, core_ids=list(range(NCORE)))
    y_prompt = np.concatenate([np.asarray(r["yp"], dtype=np.float32) for r in res.results], axis=0)[None]
    y_sample = np.stack([np.asarray(r["ys"], dtype=np.float32) for r in res.results], axis=0)
    return (y_prompt, y_sample)
```

```python
import math
import numpy as np
from contextlib import ExitStack
import concourse.bass as bass
import concourse.mybir as mybir
from concourse.bass_utils import run_bass_kernel_spmd

F32 = mybir.dt.float32
BF16 = mybir.dt.bfloat16
AF = mybir.ActivationFunctionType
ALU = mybir.AluOpType
ENGS = ("pe", "act", "dve", "pool", "sp")
NDMA = 8

D = 1024
NCORE = 8
T_P, T_S = 16512, 2176
PC = 704
K0 = math.exp(-0.5)
LAMBDA_INIT = 0.2
NPP = 32


class Buf:
    __slots__ = ("w", "r", "name")

    def __init__(self, name=""):
        self.w = None
        self.r = []
        self.name = name


class Prog:
    def __init__(self, nc):
        self.nc = nc
        self.q = {e: [] for e in ENGS}
        self.cnt = {e: 0 for e in ENGS}
        self.seen = {e: {} for e in ENGS}
        self.dman = {e: 0 for e in ENGS}
        self.dma_last = {}

    def _wait(self, eng, key, val):
        if val <= 0:
            return
        s = self.seen[eng]
        if s.get(key, 0) >= val:
            return
        s[key] = val
        self.q[eng].append(("wait", key, val))

    def _deps(self, eng, reads, writes):
        for b in reads:
            if b.w is not None:
                k, v = b.w
                if k == eng and eng == "pe":
                    continue
                self._wait(eng, k, v)
        for b in writes:
            if b.w is not None:
                k, v = b.w
                if not (k == eng and eng == "pe"):
                    self._wait(eng, k, v)
            for (k, v) in b.r:
                if not (k == eng and eng == "pe"):
                    self._wait(eng, k, v)

    def _mark(self, ticket, reads, writes):
        k = ticket[0]
        for b in reads:
            b.r = [t for t in b.r if t[0] != k]
            b.r.append(ticket)
        for b in writes:
            b.w = ticket
            b.r = []

    def op(self, eng, fn, reads=(), writes=(), after_readers=()):
        self._deps(eng, reads, writes)
        for b in after_readers:
            for (k, v) in b.r:
                if not (k == eng and eng == "pe"):
                    self._wait(eng, k, v)
        self.cnt[eng] += 1
        t = (eng, self.cnt[eng])
        self.q[eng].append(("op", fn, eng, 1))
        self._mark(t, reads, writes)
        return t

    def dma(self, fn, reads=(), writes=(), eng="sp"):
        self._deps(eng, reads, writes)
        j = self.dman[eng]
        self.dman[eng] += 1
        slot = j % NDMA
        key = ("d", eng, slot)
        self._wait(eng, key, 16 * (j // NDMA))
        val = 16 * (j // NDMA + 1)
        self.q[eng].append(("op", fn, key, 16))
        self.dma_last[key] = val
        t = (key, val)
        self._mark(t, reads, writes)
        return t

    def barrier(self):
        keys = [(e, self.cnt[e]) for e in ENGS] + list(self.dma_last.items())
        for e in ENGS:
            for (k, v) in keys:
                if k != e:
                    self._wait(e, k, v)

    def finish(self):
        for (k, v) in list(self.dma_last.items()):
            self._wait("sp", k, v)
        for e in ENGS:
            if e != "sp":
                self._wait("sp", e, self.cnt[e])

    def emit(self, stack):
        nc = self.nc
        sems = {}
        for e in ENGS:
            sems[e] = stack.enter_context(nc.semaphore("ps_" + e))
        for k in self.dma_last:
            sems[k] = stack.enter_context(nc.semaphore("ds_%s_%d" % (k[1], k[2])))
        handles = {"pe": "tensor", "act": "scalar", "dve": "vector", "pool": "gpsimd", "sp": "sync"}
        block = stack.enter_context(nc.Block())

        def replay(engname):
            def body(engh):
                for it in self.q[engname]:
                    if it[0] == "wait":
                        engh.wait_ge(sems[it[1]], it[2])
                    else:
                        ins = it[1](engh)
                        ins.then_inc(sems[it[2]], it[3])
            return body

        for e in ENGS:
            getattr(block, handles[e])(replay(e))


def _consts():
    c = np.zeros((128, 1024), np.float32)
    c[:, 0:128] = np.eye(128)
    s = np.arange(64)
    mf = np.zeros((128, 128), np.float32)
    mb = np.zeros((128, 128), np.float32)
    for a in range(2):
        for b in range(2):
            if b == 0:
                mf[a * 64:(a + 1) * 64, 0:64] = (s[:, None] < s[None, :])
                mb[a * 64:(a + 1) * 64, 0:64] = (s[:, None] > s[None, :])
            else:
                mf[a * 64:(a + 1) * 64, 64:128] = (s[:, None] <= s[None, :])
                mb[a * 64:(a + 1) * 64, 64:128] = (s[:, None] >= s[None, :])
    c[:, 128:256] = mf
    c[:, 256:384] = mb
    c[0:64, 384:448] = (s[None, :] < s[:, None])
    c[64:128, 384:448] = (s[None, :] > s[:, None])
    c[0:64, 512:576] = 1.0
    c[64:128, 576:640] = 1.0
    c[0:64, 640:704] = np.eye(64)
    c[64:128, 640:704] = np.eye(64)
    c[:, 704:832] = 1.0
    c[0:64, 832:896] = 1.0 / 64
    c[:, 896:1024] = 1.0 / 128
    return c


def _alibi_static(hb):
    m = 2.0 ** (-8.0 * (hb + 1) / 4)
    sl = np.arange(128, dtype=np.float64)
    tq = np.arange(512, dtype=np.float64)
    DM = np.zeros((128, 5, 512))
    for dl in range(5):
        DM[:, dl, :] = -m * np.abs(16.0 + tq[None, :] - 128.0 * dl - sl[:, None])
    hi = -m * 16.0 * np.floor(tq / 16)
    lo = -m * (tq % 16)
    qb = np.stack([np.stack([hi, lo]), np.stack([-hi, -lo])])
    return DM.astype(np.float32), qb.astype(np.float32)


def _alibi_core(c):
    bc = np.zeros((2, 4, 128, 4, 129), np.float32)
    ks = np.ones((2, 2, T_P), np.float32)
    kval = np.zeros((2, 128, 129), np.float32)
    sl = np.arange(128, dtype=np.float64)
    for kind, (T, nreal, rot, own0) in enumerate(((T_P, 16400, 16 * c, 16 + 2048 * c), (T_S, 2064, 0, 16))):
        nkt = T // 128
        for r in range(nkt):
            i = (r + rot) % nkt
            s = 128.0 * i + sl
            kval[kind, :, r] = (s < nreal)
            if r > 16:
                left = (128 * i + 127) < own0
                ks[kind, :, r * 128:(r + 1) * 128] = 1.0 if left else -1.0
            for hb in range(4):
                m = 2.0 ** (-8.0 * (hb + 1) / 4)
                for jl in range(4):
                    t0 = own0 + 512 * jl
                    if 128 * i + 127 < t0:
                        bc[kind, hb, :, jl, r] = -m * (t0 - s)
                    elif 128 * i >= t0 + 512:
                        bc[kind, hb, :, jl, r] = -m * (s - t0)
    return bc, ks, kval


def build_program(dbg=None):
    dbg = dbg or {}
    pieces_run = dbg.get("pieces", list(range(8)))
    do_cast = dbg.get("cast", True)
    do_p1 = dbg.get("p1", True)
    do_attn = dbg.get("attn", True)
    do_rwkv = dbg.get("rwkv", True)
    do_xchg = dbg.get("xchg", True)
    do_post = dbg.get("post", True)
    dbg_out = dbg.get("dbg_out", False)
    skind = "ExternalOutput" if dbg_out else "Internal"

    nc = bass.Bass("TRN2", target_bir_lowering=False)

    def din(name, shape):
        return nc.dram_tensor(name, list(shape), F32, kind="ExternalInput").ap()

    def dscr(name, shape, dt):
        if name in dbg.get("outs", ()):
            return nc.dram_tensor(name, list(shape), dt, kind="ExternalOutput").ap()
        return nc.dram_tensor(name, list(shape), dt).ap()

    hp = din("hp", [T_P, D])
    hs = din("hs", [T_S, D])
    xpp = din("xpp", [2048, D])
    wpc = din("wpc", [8, D, PC])
    pp_d = din("pp", [8, 128, NPP])
    lw_d = din("lw", [8, 128, 64])
    la_d = din("la", [8, 128, 64])
    lg_d = din("lg", [8, 128, 64])
    bc_d = din("bc", [2, 4, 128, 4, 129])
    dm_d = din("dm", [4, 128, 5, 512])
    qb_d = din("qb", [4, 2, 2, 512])
    ks_d = din("ks", [2, 2, T_P])
    wg_d = din("wg", [D, 2048])
    wua_d = din("wua", [512, D])
    wub_d = din("wub", [512, D])
    wout_d = din("wout", [D, D])
    wf1_d = din("wf1", [D, 4096])
    wf2_d = din("wf2", [4096, D])
    gm_d = din("gm", [128, 8])
    gf_d = din("gf", [128, 8])
    gfin_d = din("gfin", [128, D])
    slg_d = din("slg", [128, 1])
    lam_d = din("lam", [128, 4, 64])
    cst_d = din("cst", [128, 1024])
    kval_d = din("kval", [2, 128, 129])

    yp = nc.dram_tensor("yp", [2048, D], F32, kind="ExternalOutput").ap()
    ys = nc.dram_tensor("ys", [2048, D], F32, kind="ExternalOutput").ap()

    wpc_b = dscr("wpc_b", [8, D, PC], BF16)
    wg_b = dscr("wg_b", [D, 2048], BF16)
    wua_b = dscr("wua_b", [512, D], BF16)
    wub_b = dscr("wub_b", [512, D], BF16)
    wout_b = dscr("wout_b", [D, D], BF16)
    wf1_b = dscr("wf1_b", [D, 4096], BF16)
    wf2_b = dscr("wf2_b", [4096, D], BF16)
    TK = (T_P, T_S)
    PA = [dscr("PA%d" % k, [8, 192, TK[k] + 2], F32) for k in range(2)]
    PAL = [dscr("PAL%d" % k, [256, TK[k] + 2], F32) for k in range(2)]
    KTD = [dscr("KTD%d" % k, [8, 64, 2 * TK[k]], BF16) for k in range(2)]
    VD = [dscr("VD%d" % k, [8, 2 * TK[k], 128], BF16) for k in range(2)]
    QTD = [dscr("QTD%d" % k, [8, 64, TK[k]], BF16) for k in range(2)]
    OA = [dscr("OA%d" % k, [8, 64, TK[k]], BF16) for k in range(2)]
    XR = [dscr("XR%d" % k, [8, 128, 2048], BF16) for k in range(2)]
    KTR = [dscr("KTR%d" % k, [8, 64, TK[k]], BF16) for k in range(2)]
    VR = [dscr("VR%d" % k, [8, TK[k], 128], BF16) for k in range(2)]
    QO = [dscr("QO%d" % k, [8, 64, 2048], BF16) for k in range(2)]
    OAO = [dscr("OAO%d" % k, [8, 64, 2048], BF16) for k in range(2)]
    YD = [dscr("YD%d" % k, [8, 128, TK[k]], F32) for k in range(2)]
    PAO = [dscr("PAO%d" % k, [8, 192, 2050], F32) for k in range(2)]
    PALO = [dscr("PALO%d" % k, [256, 2050], F32) for k in range(2)]
    YDO = [dscr("YDO%d" % k, [8, 128, 2048], F32) for k in range(2)]
    b_PA, b_KV, b_wsc, b_ROT, b_OAO, b_YD, b_PAO, b_YDO = Buf(), Buf(), Buf(), Buf(), Buf(), Buf(), Buf(), Buf()
    b_XR = [Buf(), Buf()]
    b_OA = [Buf(), Buf()]

    P = Prog(nc)
    with ExitStack() as top:
        _uid = [0]

        def sb(st, name, shape, dt):
            _uid[0] += 1
            return st.enter_context(nc.sbuf_tensor("s%d_%s" % (_uid[0], name), list(shape), dt))

        PB = [top.enter_context(nc.psum_tensor("pb%d" % i, [128, 512], F32)) for i in range(8)]
        bPB = [Buf("pb%d" % i) for i in range(8)]

        cst = sb(top, "cst", [128, 1024], F32); b_cst = Buf()
        cstb = sb(top, "cstb", [128, 128], BF16)
        gm = sb(top, "gm", [128, 8], F32)
        gf = sb(top, "gf", [128, 8], F32)
        zb = sb(top, "zb", [128, 512], BF16)
        zf = sb(top, "zf", [128, 8], F32)
        P.dma(lambda e: e.dma_start(out=cst[:], in_=cst_d), writes=[b_cst])
        P.dma(lambda e: e.dma_start(out=gm[:], in_=gm_d), writes=[b_cst])
        P.dma(lambda e: e.dma_start(out=gf[:], in_=gf_d), writes=[b_cst])
        P.op("dve", lambda e: e.tensor_copy(out=cstb[:], in_=cst[:, 0:128]), reads=[b_cst], writes=[b_cst])
        P.op("dve", lambda e: e.memset(zb[:], 0.0), writes=[b_cst])
        P.op("dve", lambda e: e.memset(zf[:], 0.0), writes=[b_cst])
        P.barrier()
        ident = cst[:, 0:128]
        MF, MB = cst[:, 128:256], cst[:, 256:384]
        MAF, MAB = cst[0:64, 384:448], cst[0:64, 448:512]
        BLK = cst[:, 512:640]
        SEL = cst[:, 640:704]
        ONES = cst[:, 704:832]
        O64 = cst[0:64, 832:896]
        O128 = cst[:, 896:1024]

        def castw(src, dst, rows, cols, scale_cols=None):
            with ExitStack() as st:
                nb = 3
                fin = [sb(st, "cw_f%d" % i, [128, 2048], F32) for i in range(nb)]
                fout = [sb(st, "cw_b%d" % i, [128, 2048], BF16) for i in range(nb)]
                bi = [Buf() for _ in range(nb)]
                bo = [Buf() for _ in range(nb)]
                it = 0
                for kc in range(rows // 128):
                    for c0 in range(0, cols, 2048):
                        cw = min(2048, cols - c0)
                        s = it % nb
                        P.dma(lambda e, s=s, kc=kc, c0=c0, cw=cw: e.dma_start(out=fin[s][:, 0:cw], in_=src[kc * 128:(kc + 1) * 128, c0:c0 + cw]),
                              writes=[bi[s]])
                        eng = "dve" if it % 2 == 0 else "pool"
                        if scale_cols is not None:
                            P.op(eng, lambda e, s=s, kc=kc, cw=cw: e.tensor_scalar(out=fout[s][:, 0:cw], in0=fin[s][:, 0:cw],
                                                                                  scalar1=scale_cols[:, kc % 8:kc % 8 + 1], scalar2=None, op0=ALU.mult),
                                 reads=[bi[s]], writes=[bo[s]])
                        else:
                            P.op(eng, lambda e, s=s, cw=cw: e.tensor_copy(out=fout[s][:, 0:cw], in_=fin[s][:, 0:cw]),
                                 reads=[bi[s]], writes=[bo[s]])
                        P.dma(lambda e, s=s, kc=kc, c0=c0, cw=cw: e.dma_start(out=dst[kc * 128:(kc + 1) * 128, c0:c0 + cw], in_=fout[s][:, 0:cw]),
                              reads=[bo[s]], writes=[b_wsc], eng="act")
                        it += 1
                P.barrier()

        if do_cast:
            for p in range(8):
                castw(wpc[p], wpc_b[p], D, PC, gm)
            if do_post and not (do_rwkv and dbg.get("cast_overlap", True)):
                castw(wg_d, wg_b, D, 2048, gm)
                castw(wua_d, wua_b, 512, D)
                castw(wub_d, wub_b, 512, D)
                castw(wout_d, wout_b, D, D)
                castw(wf1_d, wf1_b, D, 4096, gf)
                castw(wf2_d, wf2_b, 4096, D)

        def norm_T(st_tmp, src_ap, b_src, dst_ap, b_dst, pbank, tag):
            sq, ss, xn, bt = st_tmp
            P.op("act", lambda e: e.activation(out=sq[:], in_=src_ap, func=AF.Square, accum_out=ss[:, 0:1]),
                 reads=[b_src], writes=[bt])
            P.op("act", lambda e: e.activation(out=ss[:, 1:2], in_=ss[:, 0:1], func=AF.Sqrt, scale=1.0 / D, bias=zf[:, 0:1]),
                 reads=[bt], writes=[bt])
            P.op("dve", lambda e: e.tensor_scalar(out=ss[:, 1:2], in0=ss[:, 1:2], scalar1=1e-12, scalar2=None, op0=ALU.max),
                 reads=[bt], writes=[bt])
            P.op("dve", lambda e: e.reciprocal(out=ss[:, 2:3], in_=ss[:, 1:2]), reads=[bt], writes=[bt])
            P.op("dve", lambda e: e.tensor_scalar(out=xn[:], in0=src_ap, scalar1=ss[:, 2:3], scalar2=None, op0=ALU.mult),
                 reads=[b_src, bt], writes=[bt])
            pt = PB[pbank].bitcast(BF16)
            for k in range(8):
                P.op("pe", lambda e, k=k: e.transpose(out=pt[:, k * 128:(k + 1) * 128], in_=xn[:, k * 128:(k + 1) * 128], identity=cstb[:]),
                     reads=[bt], writes=[bPB[pbank]])
            P.op("act", lambda e: e.copy(out=dst_ap, in_=pt[:].rearrange("p (k t) -> p k t", k=8)), reads=[bPB[pbank]], writes=[b_dst])

        epsb = sb(top, "epsb", [128, 4], F32)
        P.op("dve", lambda e: e.memset(epsb[:, 0:1], 1e-6), writes=[b_cst])
        P.op("dve", lambda e: e.memset(epsb[:, 1:2], 1e-5), writes=[b_cst])
        P.op("dve", lambda e: e.memset(epsb[:, 2:3], 64e-5), writes=[b_cst])
        P.barrier()

        def norm_T2(st_tmp, src_ap, b_src, dst_ap, b_dst, pbank):
            sq, ss, xn, bt = st_tmp
            P.op("act", lambda e: e.activation(out=sq[:], in_=src_ap, func=AF.Square, accum_out=ss[:, 0:1]),
                 reads=[b_src], writes=[bt])
            P.op("act", lambda e: e.activation(out=ss[:, 1:2], in_=ss[:, 0:1], func=AF.Sqrt, scale=1.0 / D, bias=epsb[:, 0:1]),
                 reads=[bt], writes=[bt])
            P.op("dve", lambda e: e.reciprocal(out=ss[:, 2:3], in_=ss[:, 1:2]), reads=[bt], writes=[bt])
            P.op("dve", lambda e: e.tensor_scalar(out=xn[:], in0=src_ap, scalar1=ss[:, 2:3], scalar2=None, op0=ALU.mult),
                 reads=[b_src, bt], writes=[bt])
            pt = PB[pbank].bitcast(BF16)
            for k in range(8):
                P.op("pe", lambda e, k=k: e.transpose(out=pt[:, k * 128:(k + 1) * 128], in_=xn[:, k * 128:(k + 1) * 128], identity=cstb[:]),
                     reads=[bt], writes=[bPB[pbank]])
            P.op("act", lambda e: e.copy(out=dst_ap, in_=pt[:].rearrange("p (k t) -> p k t", k=8)), reads=[bPB[pbank]], writes=[b_dst])

        _pid = {}

        def core_off(e):
            k = id(e)
            if k not in _pid:
                _pid[k] = e.snap(e.partition_id() * 2048) if hasattr(e, "snap") else e.partition_id() * 2048
            return _pid[k]

        def p1_phase(kind):
            T = TK[kind]
            hsrc = hp if kind == 0 else hs
            tiles = [(t0, min(512, T - t0)) for t0 in range(0, T, 512)]
            tg = "a%d" % kind
            with ExitStack() as s1:
                wp = sb(s1, tg + "wp", [128, 8, 8, PC], BF16); b_wp = Buf()
                for p in range(8):
                    P.dma(lambda e, p=p: e.dma_start(out=wp[:, p, :, :], in_=wpc_b[p].rearrange("(k p) m -> p k m", p=128)), reads=[b_wsc], writes=[b_wp])
                    for (r0, nr) in ((0, 128), (128, 64)):
                        for col in (0, T + 1):
                            P.dma(lambda e, p=p, r0=r0, nr=nr, col=col: e.dma_start(out=PA[kind][p, r0:r0 + nr, col:col + 1], in_=zf[0:nr, 0:1], allow_slow_non_contiguous=True), writes=[b_PA])
                for r0 in (0, 128):
                    for col in (0, T + 1):
                        P.dma(lambda e, r0=r0, col=col: e.dma_start(out=PAL[kind][r0:r0 + 128, col:col + 1], in_=zf[:, 0:1], allow_slow_non_contiguous=True), writes=[b_PA])
                NXB = 3
                xt = [sb(s1, tg + "xt%d" % i, [128, D], F32) for i in range(NXB)]
                b_xt = [Buf() for _ in range(NXB)]
                sqs = sb(s1, tg + "sq", [128, D], F32)
                tmp = [(sqs, sb(s1, tg + "ss%d" % i, [128, 4], F32), sb(s1, tg + "xn%d" % i, [128, D], BF16), Buf()) for i in range(2)]
                nT = [sb(s1, tg + "nT%d" % i, [128, 8, 512], BF16) for i in range(2)]
                b_nT = [Buf() for _ in range(2)]
                stg = [sb(s1, tg + "stg%d" % i, [128, 4, 512], F32) for i in range(2)]
                b_stg = [Buf() for _ in range(2)]
                qk = [sb(s1, tg + "qk%d" % i, [128, 512], BF16) for i in range(2)]
                b_qk = [Buf() for _ in range(2)]
                vst = [sb(s1, tg + "vst%d" % i, [128, 4, 128], BF16) for i in range(2)]
                b_vst = [Buf() for _ in range(2)]
                cnt = {"sub": 0, "it": 0}

                def do_tile(ti, t0, w):
                    nb = ti % 2
                    nsub = w // 128
                    for s in range(nsub):
                        xb = cnt["sub"] % NXB
                        P.dma(lambda e, xb=xb, s=s: e.dma_start(out=xt[xb][:], in_=hsrc[t0 + s * 128:t0 + (s + 1) * 128, :]), writes=[b_xt[xb]])
                        norm_T2(tmp[cnt["sub"] % 2], xt[xb][:], b_xt[xb], nT[nb][:, :, s * 128:(s + 1) * 128], b_nT[nb], cnt["sub"] % 2)
                        cnt["sub"] += 1
                    sbl = cnt["it"] % 2
                    cnt["it"] += 1
                    for ci, c0 in enumerate((192, 320)):
                        bk = 2 + (ci % 2)
                        for k in range(8):
                            P.op("pe", lambda e, k=k, c0=c0, bk=bk: e.matmul(out=PB[bk][:, 0:w], lhsT=wp[:, 0, k, c0:c0 + 128], rhs=nT[nb][:, k, 0:w], start=(k == 0), stop=(k == 7)),
                                 reads=[b_wp, b_nT[nb]], writes=[bPB[bk]])
                        if ci == 0:
                            P.op("act", lambda e, bk=bk, ci=ci: e.copy(out=stg[sbl][:, ci, 0:w], in_=PB[bk][:, 0:w]), reads=[bPB[bk]], writes=[b_stg[sbl]])
                        else:
                            P.op("dve", lambda e, bk=bk, ci=ci: e.tensor_copy(out=stg[sbl][:, ci, 0:w], in_=PB[bk][:, 0:w]), reads=[bPB[bk]], writes=[b_stg[sbl]])
                    P.dma(lambda e: e.dma_start(out=PAL[kind][:, 1 + t0:1 + t0 + w].rearrange("(c p) t -> p c t", p=128), in_=stg[sbl][:, 0:2, 0:w]),
                          reads=[b_stg[sbl]], writes=[b_PA])
                    for p in range(8):
                        sbi = cnt["it"] % 2
                        cnt["it"] += 1
                        for ci, (c0, cw) in enumerate([(0, 128), (128, 64)]):
                            bk = 2 + (ci % 2)
                            for k in range(8):
                                P.op("pe", lambda e, k=k, c0=c0, cw=cw, bk=bk, p=p: e.matmul(
                                    out=PB[bk][0:cw, 0:w], lhsT=wp[:, p, k, c0:c0 + cw], rhs=nT[nb][:, k, 0:w], start=(k == 0), stop=(k == 7)),
                                    reads=[b_wp, b_nT[nb]], writes=[bPB[bk]])
                            if ci % 2 == 0:
                                P.op("act", lambda e, cw=cw, bk=bk, ci=ci, sbi=sbi: e.copy(out=stg[sbi][0:cw, ci, 0:w], in_=PB[bk][0:cw, 0:w]),
                                     reads=[bPB[bk]], writes=[b_stg[sbi]])
                            else:
                                P.op("dve", lambda e, cw=cw, bk=bk, ci=ci, sbi=sbi: e.tensor_copy(out=stg[sbi][0:cw, ci, 0:w], in_=PB[bk][0:cw, 0:w]),
                                     reads=[bPB[bk]], writes=[b_stg[sbi]])
                        P.dma(lambda e, p=p, sbi=sbi: e.dma_start(out=PA[kind][p, 0:128, 1 + t0:1 + t0 + w], in_=stg[sbi][:, 0, 0:w]),
                              reads=[b_stg[sbi]], writes=[b_PA])
                        P.dma(lambda e, p=p, sbi=sbi: e.dma_start(out=PA[kind][p, 128:192, 1 + t0:1 + t0 + w], in_=stg[sbi][0:64, 1, 0:w]),
                              reads=[b_stg[sbi]], writes=[b_PA])
                        qb_ = 4 + (p % 2)
                        for k in range(8):
                            P.op("pe", lambda e, k=k, p=p, qb_=qb_: e.matmul(out=PB[qb_][:, 0:w], lhsT=wp[:, p, k, 448:576], rhs=nT[nb][:, k, 0:w], start=(k == 0), stop=(k == 7)),
                                 reads=[b_wp, b_nT[nb]], writes=[bPB[qb_]])
                        P.op("act", lambda e, sbi=sbi, qb_=qb_: e.activation(out=qk[sbi][0:64, 0:w], in_=PB[qb_][0:64, 0:w], func=AF.Copy, scale=0.125),
                             reads=[bPB[qb_]], writes=[b_qk[sbi]])
                        P.op("dve", lambda e, sbi=sbi, qb_=qb_: e.tensor_copy(out=qk[sbi][64:128, 0:w], in_=PB[qb_][64:128, 0:w]), reads=[bPB[qb_]], writes=[b_qk[sbi]])
                        P.dma(lambda e, p=p, sbi=sbi: e.dma_start(out=QTD[kind][p, :, t0:t0 + w], in_=qk[sbi][0:64, 0:w]), reads=[b_qk[sbi]], writes=[b_KV])
                        for rep_ in range(2):
                            P.dma(lambda e, p=p, sbi=sbi, rep_=rep_: e.dma_start(out=KTD[kind][p, :, rep_ * T + t0:rep_ * T + t0 + w], in_=qk[sbi][64:128, 0:w]),
                                  reads=[b_qk[sbi]], writes=[b_KV])
                        if p % 2 == 1:
                            continue
                        for s in range(nsub):
                            for k in range(8):
                                P.op("pe", lambda e, k=k, s=s, p=p: e.matmul(out=PB[6][:, s * 128:(s + 1) * 128], lhsT=nT[nb][:, k, s * 128:(s + 1) * 128],
                                                                             rhs=wp[:, p, k, 576:704], start=(k == 0), stop=(k == 7)),
                                     reads=[b_wp, b_nT[nb]], writes=[bPB[6]])
                        P.op("act", lambda e, sbi=sbi: e.copy(out=vst[sbi][:, 0:nsub, :], in_=PB[6][:, 0:nsub * 128].rearrange("p (s e) -> p s e", s=nsub)),
                             reads=[bPB[6]], writes=[b_vst[sbi]])
                        for rep_ in range(2):
                            P.dma(lambda e, p=p, sbi=sbi, rep_=rep_: e.dma_start(
                                out=VD[kind][p, rep_ * T + t0:rep_ * T + t0 + w, :].rearrange("(s q) e -> q s e", q=128), in_=vst[sbi][:, 0:nsub, :]),
                                reads=[b_vst[sbi]], writes=[b_KV])

                for ti, (t0, w) in enumerate(tiles[:dbg.get("p1_tiles", 10 ** 9)]):
                    do_tile(ti, t0, w)
                P.barrier()

        def rotate_phase(kind):
            T = TK[kind]

            def off(e, extra):
                return (core_off(e) + extra) if kind == 0 else extra
            P.dma(lambda e: e.dma_start(out=KTR[kind][:, :, :], in_=KTD[kind][:, :, bass.ds(off(e, 0), T)]), reads=[b_KV], writes=[b_ROT])
            P.dma(lambda e: e.dma_start(out=VR[kind].rearrange("p t e -> p (t e)"),
                                        in_=VD[kind].rearrange("p t e -> p (t e)")[:, bass.ds(off(e, 0) * 128, T * 128)]), reads=[b_KV], writes=[b_ROT])
            P.dma(lambda e: e.dma_start(out=QO[kind][:, :, :], in_=QTD[kind][:, :, bass.ds(off(e, 16), 2048)]), reads=[b_KV], writes=[b_ROT])
            P.dma(lambda e: e.dma_start(out=PAO[kind].rearrange("p r t -> (p r) t"), in_=PA[kind].rearrange("p r t -> (p r) t")[:, bass.ds(off(e, 16), 2050)]),
                  reads=[b_PA], writes=[b_PAO])
            P.dma(lambda e: e.dma_start(out=PALO[kind][:, :], in_=PAL[kind][:, bass.ds(off(e, 16), 2050)]), reads=[b_PA], writes=[b_PAO])
            P.barrier()

        def own_y_phase(kind):
            def off(e, extra):
                return (core_off(e) + extra) if kind == 0 else extra
            P.dma(lambda e: e.dma_start(out=YDO[kind].rearrange("p r t -> (p r) t"), in_=YD[kind].rearrange("p r t -> (p r) t")[:, bass.ds(off(e, 16), 2048)]),
                  reads=[b_YD], writes=[b_YDO])
            P.barrier()

        def attn_alloc_set(st, kind, tg):
            T = TK[kind]
            NKT = T // 128
            S = dict(
                KT=sb(st, tg + "KT", [66, T], BF16), VP=sb(st, tg + "VP", [128, NKT, 129], BF16), QT=sb(st, tg + "QT", [64, 2048], BF16),
                kvf=sb(st, tg + "kvf", [128, 129], F32), ksr=sb(st, tg + "ksr", [66, 2048], F32), bc=sb(st, tg + "bc", [128, 4, 129], F32),
                dmk=sb(st, tg + "dmk", [128, 5, 512], F32), qbf=sb(st, tg + "qbf", [66, 2, 512], F32),
                Qp=sb(st, tg + "Qp", [66, 512], BF16), Qn=sb(st, tg + "Qn", [66, 512], BF16),
                b_KT=Buf(), b_VP=Buf(), b_QT=Buf(), b_bc=Buf(), b_Q=Buf())
            return S

        def attn_alloc_shared(st, tg):
            return dict(
                PT=[sb(st, tg + "PT%d" % i, [128, 512], BF16) for i in range(3)], b_PT=[Buf() for _ in range(3)],
                dtmp=[sb(st, tg + "dt%d" % i, [128, 512], F32) for i in range(2)], b_dt=[Buf() for _ in range(2)],
                rinv=sb(st, tg + "rinv", [128, 4], F32), b_ri=Buf(), On=sb(st, tg + "On", [128, 4, 128], F32), b_On=Buf(),
                OT=[sb(st, tg + "OT%d" % i, [128, 512], BF16) for i in range(2)], b_OT=[Buf() for _ in range(2)])

        def attn_load(kind, p, S):
            T = TK[kind]
            NKT = T // 128
            hb = p // 2
            KT, VP, QT, kvf, ksr, bc, dmk, qbf, Qp, Qn = (S[k] for k in ("KT", "VP", "QT", "kvf", "ksr", "bc", "dmk", "qbf", "Qp", "Qn"))
            b_KT, b_VP, b_QT, b_bc, b_Q = (S[k] for k in ("b_KT", "b_VP", "b_QT", "b_bc", "b_Q"))
            P.dma(lambda e: e.dma_start(out=KT[0:64, :], in_=KTR[kind][p, :, :]), reads=[b_ROT], writes=[b_KT])
            P.dma(lambda e: e.dma_start(out=VP[:, :, 0:128], in_=VR[kind][p - p % 2, :, :].rearrange("(s q) e -> q s e", q=128)),
                  reads=[b_ROT], writes=[b_VP])
            P.dma(lambda e: e.dma_start(out=QT[:, :], in_=QO[kind][p, :, :]), reads=[b_ROT], writes=[b_QT])
            P.dma(lambda e: e.dma_start(out=kvf[:], in_=kval_d[kind]), writes=[b_VP])
            P.op("dve", lambda e: e.tensor_copy(out=VP[:, :, 128:129], in_=kvf[:, 0:NKT].unsqueeze(2)), reads=[b_VP], writes=[b_VP])
            for c0 in range(0, T, 2048):
                cw = min(2048, T - c0)
                P.dma(lambda e, c0=c0, cw=cw: e.dma_start(out=ksr[64:66, 0:cw], in_=ks_d[kind, :, c0:c0 + cw]), writes=[b_bc])
                P.op("dve", lambda e, c0=c0, cw=cw: e.tensor_copy(out=KT[64:66, c0:c0 + cw], in_=ksr[64:66, 0:cw]), reads=[b_bc], writes=[b_KT])
            P.dma(lambda e: e.dma_start(out=bc[:], in_=bc_d[kind, hb]), writes=[b_bc])
            P.dma(lambda e: e.dma_start(out=dmk[:], in_=dm_d[hb]), writes=[b_bc])
            P.dma(lambda e: e.dma_start(out=qbf[64:66, :, :], in_=qb_d[hb].rearrange("s r t -> r s t")), writes=[b_bc])
            P.op("dve", lambda e: e.tensor_copy(out=Qp[64:66, :], in_=qbf[64:66, 0, :]), reads=[b_bc], writes=[b_Q])
            P.op("dve", lambda e: e.tensor_copy(out=Qn[64:66, :], in_=qbf[64:66, 1, :]), reads=[b_bc], writes=[b_Q])

        def attn_compute(kind, p, S, SH):
            T = TK[kind]
            NKT = T // 128
            hb = p // 2
            m = 2.0 ** (-8.0 * (hb + 1) / 4)
            KT, VP, QT, bc, dmk, Qp, Qn = (S[k] for k in ("KT", "VP", "QT", "bc", "dmk", "Qp", "Qn"))
            b_KT, b_VP, b_QT, b_bc, b_Q = (S[k] for k in ("b_KT", "b_VP", "b_QT", "b_bc", "b_Q"))
            PT, b_PT, dtmp, b_dt, rinv, b_ri, On, b_On, OT, b_OT = (SH[k] for k in ("PT", "b_PT", "dtmp", "b_dt", "rinv", "b_ri", "On", "b_On", "OT", "b_OT"))
            st_ = {"blk": 0, "pend": None}

            def do_q(jl):
                P.op("pool", lambda e: e.tensor_copy(out=Qp[0:64, :], in_=QT[:, jl * 512:(jl + 1) * 512]), reads=[b_QT], writes=[b_Q])
                P.op("pool", lambda e: e.tensor_copy(out=Qn[0:64, :], in_=QT[:, jl * 512:(jl + 1) * 512]), reads=[b_QT], writes=[b_Q])
                for bk in (2, 3):
                    P.op("pe", lambda e, bk=bk: e.matmul(out=PB[bk][:, :], lhsT=zb[0:1, 0:128], rhs=zb[0:1, 0:512], start=True, stop=True,
                                                         skip_group_check=True), reads=[b_cst], writes=[bPB[bk]])
                for r in range(NKT):
                    dl = r - 4 * jl
                    diag = (r <= 16) and (0 <= dl <= 4)
                    if r <= 16:
                        if dl < 0:
                            dist = (16 + 512 * jl) - (128 * r + 127)
                        elif dl > 4:
                            dist = 128 * r - (16 + 512 * jl + 511)
                        else:
                            dist = 0
                    else:
                        d_right = 128 * r - (16 + 512 * jl + 511)
                        d_left = 512 * jl - 128 * r + 16401
                        dist = min(d_right, d_left) if kind == 0 else d_right
                    if m * dist > 150.0:
                        continue
                    sbk = st_["blk"] % 2
                    pb = st_["blk"] % 3
                    st_["blk"] += 1
                    if diag:
                        P.op("pe", lambda e, r=r, sbk=sbk: e.matmul(out=PB[sbk][:, :], lhsT=KT[0:64, r * 128:(r + 1) * 128], rhs=Qp[0:64, :],
                                                                    start=True, stop=True), reads=[b_KT, b_Q], writes=[bPB[sbk]])
                        P.op("dve", lambda e, sbk=sbk, dl=dl: e.tensor_tensor(out=dtmp[sbk][:], in0=PB[sbk][:, :], in1=dmk[:, dl, :], op=ALU.add),
                             reads=[bPB[sbk], b_bc], writes=[b_dt[sbk]])
                        P.op("act", lambda e, sbk=sbk, pb=pb: e.activation(out=PT[pb][:], in_=dtmp[sbk][:], func=AF.Exp),
                             reads=[b_dt[sbk]], writes=[b_PT[pb]])
                    else:
                        Qx = Qn if (r <= 16 and dl > 4) else Qp
                        P.op("pe", lambda e, r=r, sbk=sbk, Qx=Qx: e.matmul(out=PB[sbk][:, :], lhsT=KT[0:66, r * 128:(r + 1) * 128], rhs=Qx[0:66, :],
                                                                           start=True, stop=True), reads=[b_KT, b_Q], writes=[bPB[sbk]])
                        P.op("act", lambda e, r=r, sbk=sbk, pb=pb: e.activation(out=PT[pb][:], in_=PB[sbk][:, :], func=AF.Exp, bias=bc[:, jl, r:r + 1]),
                             reads=[bPB[sbk], b_bc], writes=[b_PT[pb]])
                    def pv(pb=pb, r=r):
                        for s in range(4):
                            bk, off = (2, s * 129) if s < 3 else (3, 0)
                            P.op("pe", lambda e, pb=pb, s=s, bk=bk, off=off, r=r: e.matmul(out=PB[bk][:, off:off + 129], lhsT=PT[pb][:, s * 128:(s + 1) * 128],
                                                                                           rhs=VP[:, r, :], start=False, stop=False, skip_group_check=True),
                                 reads=[b_PT[pb], b_VP], writes=[bPB[bk]])
                    if st_["pend"] is not None:
                        st_["pend"]()
                    st_["pend"] = pv
                if st_["pend"] is not None:
                    st_["pend"]()
                    st_["pend"] = None
                ob = jl % 2
                for s in range(4):
                    bk, off = (2, s * 129) if s < 3 else (3, 0)
                    P.op("dve", lambda e, s=s, bk=bk, off=off: e.reciprocal(out=rinv[:, s:s + 1], in_=PB[bk][:, off + 128:off + 129]),
                         reads=[bPB[bk]], writes=[b_ri])
                    P.op("dve", lambda e, s=s, bk=bk, off=off: e.tensor_scalar(out=On[:, s, :], in0=PB[bk][:, off:off + 128], scalar1=rinv[:, s:s + 1],
                                                                              scalar2=None, op0=ALU.mult), reads=[bPB[bk], b_ri], writes=[b_On])
                for s in range(4):
                    P.op("pe", lambda e, s=s: e.transpose(out=PB[4][:, s * 128:(s + 1) * 128], in_=On[:, s, :], identity=ident),
                         reads=[b_On, b_cst], writes=[bPB[4]])
                P.op("act", lambda e: e.copy(out=OT[ob][:], in_=PB[4][:, :]), reads=[bPB[4]], writes=[b_OT[ob]])
                P.dma(lambda e: e.dma_start(out=XR[kind][p, :, jl * 512:(jl + 1) * 512], in_=OT[ob][:]), reads=[b_OT[ob]], writes=[b_XR[kind]])

            for jl in range(4):
                do_q(jl)

        def attn_phase(kind, pieces):
            if not pieces:
                return
            with ExitStack() as s2:
                sets = [attn_alloc_set(s2, kind, "t%d_s%d" % (kind, i)) for i in range(2)]
                SH = attn_alloc_shared(s2, "t%d_sh" % kind)
                attn_load(kind, pieces[0], sets[0])
                for idx, p in enumerate(pieces):
                    if idx + 1 < len(pieces):
                        attn_load(kind, pieces[idx + 1], sets[(idx + 1) % 2])
                    attn_compute(kind, p, sets[idx % 2], SH)
                P.barrier()

        def cast_stream(st):
            fin = [sb(st, "cs_f%d" % i, [128, 1024], F32) for i in range(2)]
            fout = [sb(st, "cs_b%d" % i, [128, 1024], BF16) for i in range(2)]
            bi = [Buf(), Buf()]
            bo = [Buf(), Buf()]
            it = 0
            for (src_, dst_, rows, cols, sc) in ((wg_d, wg_b, D, 2048, gm), (wua_d, wua_b, 512, D, None), (wub_d, wub_b, 512, D, None),
                                                 (wout_d, wout_b, D, D, None), (wf1_d, wf1_b, D, 4096, gf), (wf2_d, wf2_b, 4096, D, None)):
                for kc in range(rows // 128):
                    for c0 in range(0, cols, 1024):
                        s = it % 2
                        it += 1
                        P.dma(lambda e, s=s, kc=kc, c0=c0, src_=src_: e.dma_start(out=fin[s][:, :], in_=src_[kc * 128:(kc + 1) * 128, c0:c0 + 1024]), writes=[bi[s]])
                        if sc is not None:
                            P.op("pool", lambda e, s=s, kc=kc, sc=sc: e.tensor_scalar(out=fout[s][:, :], in0=fin[s][:, :], scalar1=sc[:, kc % 8:kc % 8 + 1], scalar2=None, op0=ALU.mult),
                                 reads=[bi[s]], writes=[bo[s]])
                        else:
                            P.op("pool", lambda e, s=s: e.tensor_copy(out=fout[s][:, :], in_=fin[s][:, :]), reads=[bi[s]], writes=[bo[s]])
                        P.dma(lambda e, s=s, kc=kc, c0=c0, dst_=dst_: e.dma_start(out=dst_[kc * 128:(kc + 1) * 128, c0:c0 + 1024], in_=fout[s][:, :]),
                              reads=[bo[s]], writes=[b_wsc])
                        yield
                        yield
                        yield

        def run_streams(gens):
            gens = list(gens)
            skew = dbg.get("skew", 0)
            for gi, g in enumerate(gens[:-1]):
                try:
                    for _ in range(skew * (len(gens) - 1 - gi)):
                        next(g)
                except StopIteration:
                    pass
            while gens:
                for g in list(gens):
                    try:
                        next(g)
                    except StopIteration:
                        gens.remove(g)

        class RwkvCtx:
            def __init__(self, st, kind, pi, slot, full):
                self.kind, self.pi, self.slot = kind, pi, slot
                self.T = TK[kind]
                self.paT = PA[kind][pi] if full else PAO[kind][pi]
                self.paL = PAL[kind] if full else PALO[kind]
                self.b_src = b_PA if full else b_PAO
                tg = "r%d_%d_%d" % (kind, pi, 1 if full else 0)
                self.bb = 4 * slot

                def F(name, shape, dt=F32):
                    return sb(st, tg + name, shape, dt)
                self.F = F
                self.pp = F("pp", [128, NPP]); self.b_pp = Buf()
                self.lw = F("lw", [128, 64]); self.la = F("la", [128, 64]); self.lg = F("lg", [128, 64])
                for (t_, d_) in ((self.pp, pp_d), (self.lw, lw_d), (self.la, la_d), (self.lg, lg_d)):
                    P.dma(lambda e, t_=t_, d_=d_: e.dma_start(out=t_[:], in_=d_[pi]), writes=[self.b_pp])
                WW = 512
                self.raw = {n: F("raw_" + n, [128, WW + 2]) for n in ("r", "k", "v", "wd", "ad")}
                self.b_raw = Buf()
                self.t1 = F("t1", [128, WW]); self.b_t1 = Buf()
                self.X = {n: F("X_" + n, [128, WW]) for n in ("r", "k", "v", "wd", "ad")}
                self.b_X = Buf()
                self.b_Xn = {n: Buf() for n in ("r", "k", "v", "wd", "ad")}
                names = ("a", "kd", "tmp") + (("sg", "cs", "x", "d1", "d2", "ginc", "ginv", "kapa") if full else ())
                self.E = {n: F("E_" + n, [128, WW]) for n in names}
                self.b_E = {n: Buf() for n in self.E}
                if full:
                    for n, r in (("kk", "k"), ("rn", "r"), ("kap", "v")):
                        self.E[n] = self.raw[r]
                        self.b_E[n] = self.b_raw

            def PBk(self, i):
                return PB[self.bb + i]

            def bPBk(self, i):
                return bPB[self.bb + i]

            def load_raw(self, W, f0, b0):
                for gi, n in enumerate(("r", "k", "v", "wd", "ad")):
                    srcT, r0 = (self.paT, 64 * gi) if gi < 3 else (self.paL, 64 * (gi - 3))
                    P.dma(lambda e, n=n, r0=r0, srcT=srcT: e.dma_start(out=self.raw[n][0:64, 0:W + 2], in_=srcT[r0:r0 + 64, f0:f0 + W + 2]), reads=[self.b_src], writes=[self.b_raw])
                    P.dma(lambda e, n=n, r0=r0, srcT=srcT: e.dma_start(out=self.raw[n][64:128, 0:W + 2], in_=srcT[r0:r0 + 64, b0:b0 + W + 2]), reads=[self.b_src], writes=[self.b_raw])

            def prep_common(self, W):
                raw, t1, X, E, pp, la = self.raw, self.t1, self.X, self.E, self.pp, self.la
                b_raw, b_t1, b_X, b_E, b_pp = self.b_raw, self.b_t1, self.b_X, self.b_E, self.b_pp
                for gi, n in enumerate(("r", "k", "v", "wd", "ad")):
                    c0 = 3 * gi
                    bx = self.b_Xn[n]
                    P.op("act", lambda e, n=n, c0=c0: e.activation(out=X[n][:, 0:W], in_=raw[n][:, 1:W + 1], func=AF.Identity, scale=pp[:, c0:c0 + 1]),
                         reads=[b_raw, b_pp], writes=[bx], after_readers=[b_X])
                    P.op("dve", lambda e, n=n, c0=c0: e.scalar_tensor_tensor(out=X[n][:, 0:W], in0=raw[n][:, 0:W], scalar=pp[:, c0 + 1:c0 + 2], op0=ALU.mult,
                                                                              in1=X[n][:, 0:W], op1=ALU.add), reads=[b_raw, b_pp, bx], writes=[bx])
                    P.op("dve", lambda e, n=n, c0=c0: e.scalar_tensor_tensor(out=X[n][:, 0:W], in0=raw[n][:, 2:W + 2], scalar=pp[:, c0 + 2:c0 + 3], op0=ALU.mult,
                                                                              in1=X[n][:, 0:W], op1=ALU.add), reads=[b_raw, b_pp, bx], writes=[bx, b_X])
                pb0, bpb0 = self.PBk(0), self.bPBk(0)
                for h in (0, 64):
                    P.op("pe", lambda e, h=h: e.matmul(out=pb0[h:h + 64, 0:W], lhsT=la[h:h + 64, :], rhs=X["ad"][h:h + 64, 0:W], start=True, stop=True),
                         reads=[b_pp, b_X], writes=[bpb0])
                P.op("act", lambda e: e.activation(out=E["a"][:, 0:W], in_=pb0[:, 0:W], func=AF.Sigmoid, bias=pp[:, 16:17]),
                     reads=[bpb0, b_pp], writes=[b_E["a"]])
                P.op("dve", lambda e: e.tensor_scalar(out=E["tmp"][:, 0:W], in0=E["a"][:, 0:W], scalar1=-1.0, scalar2=pp[:, 18:19], op0=ALU.add, op1=ALU.mult),
                     reads=[b_E["a"], b_pp], writes=[b_E["tmp"]])
                P.op("dve", lambda e: e.scalar_tensor_tensor(out=E["kd"][:, 0:W], in0=E["tmp"][:, 0:W], scalar=1.0, op0=ALU.add, in1=X["k"][:, 0:W], op1=ALU.mult),
                     reads=[b_E["tmp"], b_X], writes=[b_E["kd"]])

        def rwkv_steps(st, kind, pi, slot):
            C = RwkvCtx(st, kind, pi, slot, True)
            T, F, pp, lw = C.T, C.F, C.pp, C.lw
            raw, t1, X, E = C.raw, C.t1, C.X, C.E
            b_raw, b_t1, b_X, b_E, b_pp = C.b_raw, C.b_t1, C.b_X, C.b_E, C.b_pp
            PBk, bPBk = C.PBk, C.bPBk
            NCH = T // 64
            steps = []
            c = 0
            while c < NCH:
                nj = min(8, NCH - c)
                steps.append((c, nj))
                c += nj
            ST = F("ST", [128, 64]); b_ST = Buf()
            RST = F("RST", [128, 512])
            P.op("dve", lambda e: e.memset(ST[:], 0.0), writes=[b_ST])
            P.op("dve", lambda e: e.memset(RST[:], 1.0), writes=[b_pp])
            P.op("dve", lambda e: e.memset(RST[:].rearrange("p (j t) -> p j t", t=64)[:, :, 0:1], 0.0), writes=[b_pp])
            GE = F("GE", [128, 8]); b_GE = Buf()
            BK = F("BK", [128, 8, 2, 64]); b_BK = Buf()
            AR = F("AR", [128, 8, 2, 64]); b_AR = Buf()
            BKH = F("BKH", [128, 8, 2, 64]); b_BKH = Buf()
            VV = F("VV", [128, 8, 2, 64]); b_VV = Buf()
            MX = F("MX", [128, 16, 128]); b_MX = Buf()
            Nm = [F("Nm%d" % i, [128, 8, 64], BF16) for i in range(2)]; b_Nm = [Buf(), Buf()]
            Mm = [F("Mm%d" % i, [128, 8, 64], BF16) for i in range(2)]; b_Mm = [Buf(), Buf()]
            Rm = F("Rm", [128, 8, 64]); b_Rm = Buf()
            Rb = F("Rb", [128, 8, 64], BF16); b_Rb = Buf()
            BKHs = F("BKHs", [128, 16, 64]); b_BKHs = Buf()
            V2s = F("V2s", [128, 16, 64]); b_V2s = Buf()
            BVs = F("BVs", [128, 8, 64]); b_BVs = Buf()
            YVs = F("YVs", [128, 8, 64]); b_YVs = Buf()
            KVs = F("KVs", [128, 8, 64]); b_KVs = Buf()
            Ws = F("Ws", [128, 64]); b_Ws = Buf()
            Us = F("Us", [128, 64]); b_Us = Buf()
            ST2 = F("ST2", [128, 64]); b_ST2 = Buf()
            Yst = [F("Yst%d" % i, [128, 512]) for i in range(1)] * 2; b_Yst = [Buf()] * 2
            E["tw"], b_E["tw"] = t1, b_t1
            E["gexc"], b_E["gexc"] = E["d1"], b_E["d1"]
            E["gtail"], b_E["gtail"] = E["d2"], b_E["d2"]
            E["kk2"], b_E["kk2"] = E["tmp"], b_E["tmp"]
            pending = []
            nsteps_run = len(steps) if dbg.get("rwkv_steps") is None else dbg["rwkv_steps"]
            yield

            def do_step(g):
                cf, nj = steps[g]
                W = nj * 64
                f0 = cf * 64
                b0 = T - f0 - W
                if g == 0:
                    C.load_raw(W, f0, b0)
                while pending:
                    pending.pop(0)()
                C.prep_common(W)
                yield
                P.op("act", lambda e: e.activation(out=E["tw"][:, 0:W], in_=X["wd"][:, 0:W], func=AF.Tanh), reads=[b_X, b_t1], writes=[b_E["tw"]])
                for h in (0, 64):
                    P.op("pe", lambda e, h=h: e.matmul(out=PBk(1)[h:h + 64, 0:W], lhsT=lw[h:h + 64, :], rhs=E["tw"][h:h + 64, 0:W], start=True, stop=True),
                         reads=[b_pp, b_E["tw"]], writes=[bPBk(1)])
                P.op("act", lambda e: e.activation(out=E["sg"][:, 0:W], in_=PBk(1)[:, 0:W], func=AF.Sigmoid, bias=pp[:, 15:16]),
                     reads=[bPBk(1), b_pp], writes=[b_E["sg"]])
                P.op("dve", lambda e: e.tensor_tensor_scan(out=E["cs"][:, 0:W], data0=RST[:, 0:W], data1=E["sg"][:, 0:W], initial=0.0, op0=ALU.mult, op1=ALU.add),
                     reads=[b_E["sg"], b_pp], writes=[b_E["cs"]])
                cs3 = E["cs"][:, 0:W].rearrange("p (j t) -> p j t", t=64)
                ceb = cs3[:, :, 63:64].to_broadcast([128, nj, 64])
                x3 = E["x"][:, 0:W].rearrange("p (j t) -> p j t", t=64)
                P.op("dve", lambda e: e.tensor_copy(out=E["x"][0:64, 0:W], in_=E["cs"][0:64, 0:W]), reads=[b_E["cs"]], writes=[b_E["x"]])
                P.op("dve", lambda e: e.tensor_tensor(out=x3[64:128], in0=ceb[64:128], in1=cs3[64:128], op=ALU.subtract), reads=[b_E["cs"]], writes=[b_E["x"]])
                P.op("dve", lambda e: e.tensor_tensor(out=E["x"][64:128, 0:W], in0=E["x"][64:128, 0:W], in1=E["sg"][64:128, 0:W], op=ALU.add),
                     reads=[b_E["x"], b_E["sg"]], writes=[b_E["x"]])
                yield
                P.op("dve", lambda e: e.tensor_tensor(out=E["d1"][:, 0:W], in0=E["x"][:, 0:W], in1=E["sg"][:, 0:W], op=ALU.subtract),
                     reads=[b_E["x"], b_E["sg"]], writes=[b_E["d1"]])
                d23 = E["d2"][:, 0:W].rearrange("p (j t) -> p j t", t=64)
                P.op("dve", lambda e: e.tensor_tensor(out=d23, in0=ceb, in1=x3, op=ALU.subtract), reads=[b_E["cs"], b_E["x"]], writes=[b_E["d2"]])
                P.op("act", lambda e: e.activation(out=E["ginc"][:, 0:W], in_=E["x"][:, 0:W], func=AF.Exp, scale=-K0), reads=[b_E["x"]], writes=[b_E["ginc"]])
                P.op("act", lambda e: e.activation(out=E["ginv"][:, 0:W], in_=E["x"][:, 0:W], func=AF.Exp, scale=K0), reads=[b_E["x"]], writes=[b_E["ginv"]])
                P.op("act", lambda e: e.activation(out=E["gexc"][:, 0:W], in_=E["d1"][:, 0:W], func=AF.Exp, scale=-K0), reads=[b_E["d1"]], writes=[b_E["gexc"]])
                P.op("act", lambda e: e.activation(out=E["gtail"][:, 0:W], in_=E["d2"][:, 0:W], func=AF.Exp, scale=-K0), reads=[b_E["d2"]], writes=[b_E["gtail"]])
                P.op("act", lambda e: e.activation(out=GE[:, 0:nj], in_=cs3[:, :, 63], func=AF.Exp, scale=-K0), reads=[b_E["cs"]], writes=[b_GE])
                yield
                P.op("dve", lambda e: e.tensor_scalar(out=E["kk"][:, 0:W], in0=X["k"][:, 0:W], scalar1=pp[:, 17:18], scalar2=None, op0=ALU.mult),
                     reads=[b_X, b_pp], writes=[b_E["kk"]])
                P.op("act", lambda e: e.activation(out=E["kk2"][:, 0:W], in_=E["kk"][:, 0:W], func=AF.Square), reads=[b_E["kk"], b_E["tmp"]], writes=[b_E["kk2"]])
                P.op("pe", lambda e: e.matmul(out=PBk(2)[:, 0:W], lhsT=BLK, rhs=E["kk2"][:, 0:W], start=True, stop=True), reads=[b_cst, b_E["kk2"]], writes=[bPBk(2)])
                P.op("dve", lambda e: e.tensor_scalar(out=E["rn"][:, 0:W], in0=PBk(2)[:, 0:W], scalar1=1e-24, scalar2=None, op0=ALU.max),
                     reads=[bPBk(2)], writes=[b_E["rn"]])
                P.op("act", lambda e: e.activation(out=E["rn"][:, 0:W], in_=E["rn"][:, 0:W], func=AF.Sqrt), reads=[b_E["rn"]], writes=[b_E["rn"]])
                P.op("dve", lambda e: e.reciprocal(out=E["rn"][:, 0:W], in_=E["rn"][:, 0:W]), reads=[b_E["rn"]], writes=[b_E["rn"]])
                P.op("dve", lambda e: e.tensor_tensor(out=E["kap"][:, 0:W], in0=E["kk"][:, 0:W], in1=E["rn"][:, 0:W], op=ALU.mult),
                     reads=[b_E["kk"], b_E["rn"]], writes=[b_E["kap"]])
                P.op("dve", lambda e: e.tensor_tensor(out=E["kapa"][:, 0:W], in0=E["kap"][:, 0:W], in1=E["a"][:, 0:W], op=ALU.mult),
                     reads=[b_E["kap"], b_E["a"]], writes=[b_E["kapa"]])
                yield

                def v4(tns, slot_):
                    return tns[:, 0:nj, slot_, :]

                def e3(n):
                    return E[n][:, 0:W].rearrange("p (j t) -> p j t", t=64)
                x3r = X["r"][:, 0:W].rearrange("p (j t) -> p j t", t=64)
                x3v = X["v"][:, 0:W].rearrange("p (j t) -> p j t", t=64)
                P.op("dve", lambda e: e.scalar_tensor_tensor(out=v4(AR, 0), in0=e3("kap"), scalar=-1.0, op0=ALU.mult, in1=e3("gexc"), op1=ALU.mult),
                     reads=[b_E["kap"], b_E["gexc"]], writes=[b_AR])
                P.op("dve", lambda e: e.tensor_tensor(out=v4(AR, 1), in0=x3r, in1=e3("ginc"), op=ALU.mult), reads=[b_X, b_E["ginc"]], writes=[b_AR])

                def halves(tns, n_beta, n_k, g_, b_dst, eng):
                    for (h, sb_, sk_) in ((0, 0, 1), (64, 1, 0)):
                        P.op(eng, lambda e, h=h, sb_=sb_: e.tensor_tensor(out=tns[h:h + 64, 0:nj, sb_, :], in0=e3(n_beta)[h:h + 64], in1=e3(g_)[h:h + 64], op=ALU.mult),
                             reads=[b_E[n_beta], b_E[g_]], writes=[b_dst])
                        P.op(eng, lambda e, h=h, sk_=sk_: e.tensor_tensor(out=tns[h:h + 64, 0:nj, sk_, :], in0=e3(n_k)[h:h + 64], in1=e3(g_)[h:h + 64], op=ALU.mult),
                             reads=[b_E[n_k], b_E[g_]], writes=[b_dst])
                halves(BK, "kapa", "kd", "ginv", b_BK, "dve")
                halves(BKH, "kapa", "kd", "gtail", b_BKH, "pool")
                P.op("pool", lambda e: e.tensor_copy(out=v4(VV, 0), in_=x3v), reads=[b_X], writes=[b_VV])
                P.op("pool", lambda e: e.tensor_copy(out=v4(VV, 1), in_=x3v), reads=[b_X], writes=[b_VV])
                if g + 1 < nsteps_run:
                    cf2, nj2 = steps[g + 1]
                    C.load_raw(nj2 * 64, cf2 * 64, T - cf2 * 64 - nj2 * 64)
                yield

                def uidx(d, j):
                    return d * 8 + j
                SBm = {0: 0, 1: 1}
                for d in (0, 1):
                    h = 64 * d
                    msk = (MF if d == 0 else MB)
                    for j0 in range(0, nj, 4):
                        n_in = min(4, nj - j0)
                        bk = 2 * d + (j0 // 4) % 2
                        for jo in range(n_in):
                            j = j0 + jo
                            off = jo * 128
                            lhs = BK[h:h + 64, j, :, :].rearrange("p a t -> p (a t)")
                            rhs = AR[h:h + 64, j, :, :].rearrange("p a t -> p (a t)")
                            P.op("pe", lambda e, bk=bk, off=off, lhs=lhs, rhs=rhs: e.matmul(out=PBk(bk)[:, off:off + 128], lhsT=lhs, rhs=rhs, start=True, stop=True),
                                 reads=[b_BK, b_AR], writes=[bPBk(bk)])
                        u0 = uidx(d, j0)
                        P.op("dve", lambda e, bk=bk, n_in=n_in, u0=u0, msk=msk: e.tensor_tensor(
                            out=MX[:, u0:u0 + n_in, :], in0=PBk(bk)[:, 0:n_in * 128].rearrange("p (u t) -> p u t", t=128),
                            in1=msk.unsqueeze(1).to_broadcast([128, n_in, 128]), op=ALU.mult), reads=[bPBk(bk), b_cst], writes=[b_MX])
                    yield
                for d in (0, 1):
                    h = 64 * d
                    for j in range(nj):
                        lhs = AR[h:h + 64, j, 0, :]
                        rhs = BK[h:h + 64, j, SBm[d], :]
                        P.op("pe", lambda e, j=j, h=h, lhs=lhs, rhs=rhs: e.matmul(out=PBk(0)[h:h + 64, j * 64:(j + 1) * 64], lhsT=lhs, rhs=rhs, start=True, stop=True),
                             reads=[b_AR, b_BK], writes=[bPBk(0)])
                P.op("dve", lambda e: e.tensor_tensor(
                    out=Nm[0][:, 0:nj, :], in0=PBk(0)[:, 0:nj * 64].rearrange("p (u t) -> p u t", t=64),
                    in1=cst[:, 384:448].unsqueeze(1).to_broadcast([128, nj, 64]), op=ALU.mult), reads=[bPBk(0), b_cst], writes=[b_Nm[0]])
                for d in (0, 1):
                    h = 64 * d
                    P.op("dve", lambda e, d=d, h=h: e.tensor_tensor(out=Rm[h:h + 64, 0:nj, :], in0=MX[h:h + 64, d * 8:d * 8 + nj, 0:64],
                                                                    in1=cst[h:h + 64, h:h + 64].unsqueeze(1).to_broadcast([64, nj, 64]), op=ALU.add),
                         reads=[b_MX, b_cst], writes=[b_Rm])
                    P.op("pool", lambda e, d=d, h=h: e.tensor_copy(out=Mm[0][h:h + 64, 0:nj, :], in_=MX[h:h + 64, d * 8:d * 8 + nj, 0:64]), reads=[b_MX], writes=[b_Mm[0]])
                    P.op("act", lambda e, h=h: e.copy(out=Rb[h:h + 64, 0:nj, :], in_=Rm[h:h + 64, 0:nj, :]), reads=[b_Rm], writes=[b_Rb])
                yield

                def prod(bank0, lhs_fn, rhs_fn, reads, evac, rev=False):
                    bk = bank0 // 2
                    for d in (0, 1):
                        h = 64 * d
                        for j in range(nj):
                            lhs = lhs_fn(d, h, j)
                            rhs = rhs_fn(d, h, j)
                            jp = (nj - 1 - j) if (rev and d == 1) else j
                            P.op("pe", lambda e, jp=jp, h=h, lhs=lhs, rhs=rhs: e.matmul(out=PBk(bk)[h:h + 64, jp * 64:(jp + 1) * 64], lhsT=lhs, rhs=rhs, start=True, stop=True),
                                 reads=reads, writes=[bPBk(bk)])
                    evac(bk, PBk(bk)[:, 0:nj * 64].rearrange("p (u t) -> p u t", t=64))

                def ev_copy(dst_t, b_dst, eng):
                    def f(bk, pv):
                        dst = dst_t[:, 0:nj, :]
                        if eng == "act":
                            P.op("act", lambda e: e.copy(out=dst, in_=pv), reads=[bPBk(bk)], writes=[b_dst])
                        else:
                            P.op("dve", lambda e: e.tensor_copy(out=dst, in_=pv), reads=[bPBk(bk)], writes=[b_dst])
                    return f

                def ev_acc(dst_t, b_dst):
                    def f(bk, pv):
                        dst = dst_t[:, 0:nj, :]
                        P.op("dve", lambda e: e.tensor_tensor(out=dst, in0=dst, in1=pv, op=ALU.add), reads=[bPBk(bk), b_dst], writes=[b_dst])
                    return f

                cur = 0
                for k in range(1, 6):
                    nxt = 1 - cur
                    Nc, Mc, Nn, Mn = Nm[cur], Mm[cur], Nm[nxt], Mm[nxt]
                    if k <= 4:
                        prod(0, lambda d, h, j, Nc=Nc: Nc[h:h + 64, j, :], lambda d, h, j, Mc=Mc: Mc[h:h + 64, j, :], [b_Nm[cur], b_Mm[cur]], ev_copy(Mn, b_Mm[nxt], "act"))
                    prod(2, lambda d, h, j, Mc=Mc: Mc[h:h + 64, j, :], lambda d, h, j, Nc=Nc: Nc[h:h + 64, j, :], [b_Mm[cur], b_Nm[cur]], ev_copy(Nn, b_Nm[nxt], "dve"))
                    yield
                    prod(4, lambda d, h, j, Nn=Nn: Nn[h:h + 64, j, :], lambda d, h, j: Rb[h:h + 64, j, :], [b_Nm[nxt], b_Rb], ev_acc(Rm, b_Rm))
                    if k < 5:
                        P.op("act", lambda e: e.copy(out=Rb[:, 0:nj, :], in_=Rm[:, 0:nj, :]), reads=[b_Rm], writes=[b_Rb])
                    yield
                    cur = nxt
                for (srct, b_src, dst, b_dst, bks) in ((BKH, b_BKH, BKHs, b_BKHs, (0, 1)), (VV, b_VV, V2s, b_V2s, (2, 3))):
                    for d in (0, 1):
                        h = 64 * d
                        bk = bks[d]
                        for j in range(nj):
                            P.op("pe", lambda e, h=h, j=j, bk=bk, srct=srct: e.transpose(out=PBk(bk)[:, j * 64:(j + 1) * 64], in_=srct[h:h + 64, j, :, :].rearrange("p a t -> p (a t)"),
                                                                                        identity=cst[h:h + 64, h:h + 64]),
                                 reads=[b_src, b_cst], writes=[bPBk(bk)])
                        P.op("act", lambda e, d=d, bk=bk, dst=dst: e.copy(out=dst[:, d * 8:d * 8 + nj, :], in_=PBk(bk)[:, 0:nj * 64].rearrange("p (u t) -> p u t", t=64)),
                             reads=[bPBk(bk)], writes=[b_dst])
                    yield

                def hv_of(h):
                    return 64 - h
                prod(0, lambda d, h, j: MX[hv_of(h):hv_of(h) + 64, uidx(d, j), 0:64], lambda d, h, j: V2s[hv_of(h):hv_of(h) + 64, uidx(d, j), :],
                     [b_MX, b_V2s], ev_copy(BVs, b_BVs, "act"), rev=True)
                prod(2, lambda d, h, j: V2s[hv_of(h):hv_of(h) + 64, uidx(d, j), :], lambda d, h, j: MX[hv_of(h):hv_of(h) + 64, uidx(d, j), 64:128],
                     [b_MX, b_V2s], ev_copy(YVs, b_YVs, "dve"))
                yield
                prod(4, lambda d, h, j: BKHs[hv_of(h):hv_of(h) + 64, uidx(d, j), :], lambda d, h, j: V2s[hv_of(h):hv_of(h) + 64, uidx(d, j), :],
                     [b_BKHs, b_V2s], ev_copy(KVs, b_KVs, "act"), rev=True)
                yield
                for i in range(nj):
                    jj = {0: i, 1: nj - 1 - i}
                    for d in (0, 1):
                        h = 64 * d
                        j = jj[d]
                        P.op("pe", lambda e, h=h, j=j: e.matmul(out=PBk(0)[h:h + 64, 0:64], lhsT=AR[h:h + 64, j, 0, :], rhs=ST[h:h + 64, :], start=True, stop=True),
                             reads=[b_AR, b_ST], writes=[bPBk(0)])
                    P.op("dve", lambda e, i=i: e.tensor_tensor(out=Ws[:, :], in0=PBk(0)[:, 0:64], in1=BVs[:, i, :], op=ALU.add),
                         reads=[bPBk(0), b_BVs], writes=[b_Ws])
                    for d in (0, 1):
                        h = 64 * d
                        j = jj[d]
                        P.op("dve", lambda e, h=h, j=j, i=i: e.scalar_tensor_tensor(out=ST2[h:h + 64, :], in0=ST[h:h + 64, :], scalar=GE[h:h + 64, j:j + 1], op0=ALU.mult,
                                                                                  in1=KVs[h:h + 64, i, :], op1=ALU.add), reads=[b_ST, b_GE, b_KVs], writes=[b_ST2])
                    yield
                    for d in (0, 1):
                        h = 64 * d
                        j = jj[d]
                        P.op("pe", lambda e, h=h, j=j: e.matmul(out=PBk(0)[h:h + 64, 64:128], lhsT=Rm[h:h + 64, j, :], rhs=Ws[h:h + 64, :], start=True, stop=True),
                             reads=[b_Rm, b_Ws], writes=[bPBk(0)])
                    P.op("act", lambda e: e.copy(out=Us[:, :], in_=PBk(0)[:, 64:128]), reads=[bPBk(0)], writes=[b_Us])
                    yield
                    for d in (0, 1):
                        h = 64 * d
                        j = jj[d]
                        u = uidx(d, j)
                        P.op("pe", lambda e, h=h, u=u: e.matmul(out=PBk(0)[h:h + 64, 128:192], lhsT=BKHs[h:h + 64, u, :], rhs=Us[h:h + 64, :], start=True, stop=True),
                             reads=[b_BKHs, b_Us], writes=[bPBk(0)])
                        P.op("pe", lambda e, h=h, j=j, d=d: e.matmul(out=PBk(2 + d)[h:h + 64, j * 64:(j + 1) * 64], lhsT=ST[h:h + 64, :], rhs=AR[h:h + 64, j, 1, :], start=True, stop=False),
                             reads=[b_ST, b_AR], writes=[bPBk(2 + d)])
                        P.op("pe", lambda e, h=h, j=j, u=u, d=d: e.matmul(out=PBk(2 + d)[h:h + 64, j * 64:(j + 1) * 64], lhsT=Us[h:h + 64, :], rhs=MX[h:h + 64, u, 64:128], start=False, stop=True),
                             reads=[b_Us, b_MX], writes=[bPBk(2 + d)])
                    P.op("dve", lambda e: e.tensor_tensor(out=ST[:, :], in0=ST2[:, :], in1=PBk(0)[:, 128:192], op=ALU.add),
                         reads=[b_ST2, bPBk(0)], writes=[b_ST])
                    yield
                ys = g % 2
                for d in (0, 1):
                    h = 64 * d
                    P.op("dve", lambda e, h=h, d=d: e.tensor_tensor(out=Yst[ys][h:h + 64, 0:W].rearrange("p (j t) -> p j t", t=64),
                                                                    in0=PBk(2 + d)[h:h + 64, 0:W].rearrange("p (j t) -> p j t", t=64),
                                                                    in1=YVs[h:h + 64, 0:nj, :], op=ALU.add), reads=[bPBk(2 + d), b_YVs], writes=[b_Yst[ys]])
                def flush(ys=ys, f0=f0, b0=b0, W=W):
                    P.dma(lambda e: e.dma_start(out=YD[kind][pi, 0:64, f0:f0 + W], in_=Yst[ys][0:64, 0:W]), reads=[b_Yst[ys]], writes=[b_YD])
                    P.dma(lambda e: e.dma_start(out=YD[kind][pi, 64:128, b0:b0 + W], in_=Yst[ys][64:128, 0:W]), reads=[b_Yst[ys]], writes=[b_YD])
                pending.append(flush)
                yield

            for g in range(nsteps_run):
                yield from do_step(g)
            while pending:
                pending.pop(0)()

        def rwkv_stage_e(st, kind, pi, slot):
            C = RwkvCtx(st, kind, pi, slot, False)
            T, F, pp, lg = C.T, C.F, C.pp, C.lg
            X, E, t1 = C.X, C.E, C.t1
            b_X, b_E, b_pp, b_t1 = C.b_X, C.b_E, C.b_pp, C.b_t1
            PBk, bPBk = C.PBk, C.bPBk
            WW = 512
            rawg = F("rawg", [128, WW + 2]); b_rawg = Buf()
            Xg = F("Xg", [128, WW]); b_Xg = Buf()
            Yt = F("Yt", [128, WW]); b_Yt = Buf()
            Ys = F("Ysum", [64, WW]); b_Ys = Buf()
            Dd = F("Dd", [64, WW]); b_Dd = Buf()
            D2 = F("D2", [64, WW]); b_D2 = Buf()
            Rs = F("Rs", [64, WW]); b_Rs = Buf()
            Oa = [F("Oa%d" % i, [64, WW], BF16) for i in range(2)]; b_Oa = [Buf(), Buf()]
            tiles = [(t0, 512) for t0 in range(0, 2048, 512)]
            pending = []
            yield

            def do_tile(ti, t0, W):
                C.load_raw(W, t0, t0)
                P.dma(lambda e: e.dma_start(out=rawg[:, 0:W + 2], in_=C.paL[128:256, t0:t0 + W + 2]), reads=[b_PAO], writes=[b_rawg])
                P.dma(lambda e: e.dma_start(out=Yt[:, 0:W], in_=YDO[kind][pi, :, t0:t0 + W]), reads=[b_YDO], writes=[b_Yt])
                while pending:
                    pending.pop(0)()
                C.prep_common(W)
                yield
                P.op("act", lambda e: e.activation(out=t1[:, 0:W], in_=rawg[:, 1:W + 1], func=AF.Identity, scale=pp[:, 22:23]), reads=[b_rawg, b_pp], writes=[b_t1])
                P.op("dve", lambda e: e.scalar_tensor_tensor(out=t1[:, 0:W], in0=rawg[:, 0:W], scalar=pp[:, 23:24], op0=ALU.mult, in1=t1[:, 0:W], op1=ALU.add),
                     reads=[b_rawg, b_pp, b_t1], writes=[b_t1])
                P.op("dve", lambda e: e.scalar_tensor_tensor(out=Xg[:, 0:W], in0=rawg[:, 2:W + 2], scalar=pp[:, 24:25], op0=ALU.mult, in1=t1[:, 0:W], op1=ALU.add),
                     reads=[b_rawg, b_pp, b_t1], writes=[b_Xg])
                P.op("act", lambda e: e.activation(out=Xg[:, 0:W], in_=Xg[:, 0:W], func=AF.Sigmoid), reads=[b_Xg], writes=[b_Xg])
                P.op("dve", lambda e: e.scalar_tensor_tensor(out=E["tmp"][:, 0:W], in0=X["r"][:, 0:W], scalar=pp[:, 19:20], op0=ALU.mult, in1=E["kd"][:, 0:W], op1=ALU.mult),
                     reads=[b_X, b_pp, b_E["kd"]], writes=[b_E["tmp"]])
                P.op("pe", lambda e: e.matmul(out=PBk(1)[0:64, 0:W], lhsT=ONES[:, 0:64], rhs=E["tmp"][:, 0:W], start=True, stop=True), reads=[b_cst, b_E["tmp"]], writes=[bPBk(1)])
                P.op("pe", lambda e: e.matmul(out=PBk(2)[0:64, 0:W], lhsT=SEL, rhs=Yt[:, 0:W], start=True, stop=True), reads=[b_cst, b_Yt], writes=[bPBk(2)])
                P.op("act", lambda e: e.copy(out=Ys[:, 0:W], in_=PBk(2)[0:64, 0:W]), reads=[bPBk(2)], writes=[b_Ys])
                yield
                P.op("pe", lambda e: e.matmul(out=PBk(3)[0:64, 0:W], lhsT=O64, rhs=Ys[:, 0:W], start=True, stop=True), reads=[b_cst, b_Ys], writes=[bPBk(3)])
                P.op("dve", lambda e: e.tensor_tensor(out=Dd[:, 0:W], in0=Ys[:, 0:W], in1=PBk(3)[0:64, 0:W], op=ALU.subtract), reads=[b_Ys, bPBk(3)], writes=[b_Dd])
                P.op("act", lambda e: e.activation(out=D2[:, 0:W], in_=Dd[:, 0:W], func=AF.Square), reads=[b_Dd], writes=[b_D2])
                P.op("pe", lambda e: e.matmul(out=PBk(3)[0:64, 0:W], lhsT=O64, rhs=D2[:, 0:W], start=True, stop=True), reads=[b_cst, b_D2], writes=[bPBk(3)])
                P.op("act", lambda e: e.activation(out=Rs[:, 0:W], in_=PBk(3)[0:64, 0:W], func=AF.Sqrt, bias=epsb[0:64, 2:3]), reads=[bPBk(3), b_cst], writes=[b_Rs])
                yield
                P.op("dve", lambda e: e.reciprocal(out=Rs[:, 0:W], in_=Rs[:, 0:W]), reads=[b_Rs], writes=[b_Rs])
                P.op("dve", lambda e: e.tensor_tensor(out=Dd[:, 0:W], in0=Dd[:, 0:W], in1=Rs[:, 0:W], op=ALU.mult), reads=[b_Dd, b_Rs], writes=[b_Dd])
                P.op("dve", lambda e: e.tensor_scalar(out=Dd[:, 0:W], in0=Dd[:, 0:W], scalar1=pp[0:64, 20:21], scalar2=pp[0:64, 21:22], op0=ALU.mult, op1=ALU.add),
                     reads=[b_Dd, b_pp], writes=[b_Dd])
                P.op("dve", lambda e: e.tensor_tensor(out=D2[:, 0:W], in0=PBk(1)[0:64, 0:W], in1=X["v"][0:64, 0:W], op=ALU.mult), reads=[bPBk(1), b_X], writes=[b_D2])
                P.op("dve", lambda e: e.tensor_tensor(out=Dd[:, 0:W], in0=Dd[:, 0:W], in1=D2[:, 0:W], op=ALU.add), reads=[b_Dd, b_D2], writes=[b_Dd])
                P.op("pe", lambda e: e.matmul(out=PBk(0)[0:64, 0:W], lhsT=lg[:, :], rhs=Xg[:, 0:W], start=True, stop=True), reads=[b_pp, b_Xg], writes=[bPBk(0)])
                ob = ti % 2
                P.op("dve", lambda e: e.tensor_tensor(out=Oa[ob][:, 0:W], in0=Dd[:, 0:W], in1=PBk(0)[0:64, 0:W], op=ALU.mult), reads=[b_Dd, bPBk(0)], writes=[b_Oa[ob]])
                def flush(ob=ob, t0=t0, W=W):
                    P.dma(lambda e: e.dma_start(out=OAO[kind][pi, :, t0:t0 + W], in_=Oa[ob][:, 0:W]), reads=[b_Oa[ob]], writes=[b_OAO])
                pending.append(flush)
                yield

            if not dbg.get("skip_stage_e"):
                for ti, (t0, W) in enumerate(tiles):
                    yield from do_tile(ti, t0, W)
                while pending:
                    pending.pop(0)()

        cast_state = {"done": not (do_cast and do_post and do_rwkv and dbg.get("cast_overlap", True))}

        def rwkv_group(kind, plist, stage):
            with ExitStack() as st:
                if stage == 0:
                    gens = [rwkv_steps(st, kind, p, i) for i, p in enumerate(plist)]
                    if not cast_state["done"]:
                        cast_state["done"] = True
                        gens.append(cast_stream(st))
                    run_streams(gens)
                else:
                    run_streams([rwkv_stage_e(st, kind, p, i) for i, p in enumerate(plist)])
                P.barrier()

        kinds = dbg.get("kinds", [0, 1])
        for kind in kinds:
            if do_p1:
                p1_phase(kind)
                rotate_phase(kind)
            if do_attn:
                attn_phase(kind, pieces_run)
            if do_rwkv:
                ns = dbg.get("streams", 2)
                for i in range(0, len(pieces_run), ns):
                    rwkv_group(kind, pieces_run[i:i + ns], 0)
                P.barrier()
                own_y_phase(kind)
                for i in range(0, len(pieces_run), ns):
                    rwkv_group(kind, pieces_run[i:i + ns], 1)
            P.barrier()

        def post_phase(kind, xsrc, x_row0, yout, tg):
            with ExitStack() as st:
                def F(name, shape, dt=F32):
                    return sb(st, tg + name, shape, dt)
                wg = F("wg", [128, 8, 2048], BF16); wua = F("wua", [128, 4, D], BF16); wub = F("wub", [128, 4, D], BF16)
                wo = [F("wo%d" % i, [128, 8, 128], BF16) for i in range(2)]; b_wo = [Buf(), Buf()]
                b_w = Buf()
                P.dma(lambda e: e.dma_start(out=wg[:], in_=wg_b.rearrange("(k p) m -> p k m", p=128)), reads=[b_wsc], writes=[b_w])
                P.dma(lambda e: e.dma_start(out=wua[:], in_=wua_b.rearrange("(k p) m -> p k m", p=128)), reads=[b_wsc], writes=[b_w])
                P.dma(lambda e: e.dma_start(out=wub[:], in_=wub_b.rearrange("(k p) m -> p k m", p=128)), reads=[b_wsc], writes=[b_w])
                w1 = [F("w1_%d" % i, [128, 8, 256], BF16) for i in range(2)]; b_w1 = [Buf(), Buf()]
                w2 = [F("w2_%d" % i, [128, 32, 128], BF16) for i in range(2)]; b_w2 = [Buf(), Buf()]
                gfin = F("gfin", [128, D]); slg = F("slg", [128, 1]); lamt = F("lamt", [128, 4, 64]); lamv = F("lamv", [128, 8])
                b_c = Buf()
                P.dma(lambda e: e.dma_start(out=gfin[:], in_=gfin_d), writes=[b_c])
                P.dma(lambda e: e.dma_start(out=slg[:], in_=slg_d), writes=[b_c])
                P.dma(lambda e: e.dma_start(out=lamt[:], in_=lam_d), writes=[b_c])
                P.op("dve", lambda e: e.tensor_tensor(out=lamt[:, 0, :], in0=lamt[:, 0, :], in1=lamt[:, 1, :], op=ALU.mult), reads=[b_c], writes=[b_c])
                P.op("dve", lambda e: e.tensor_tensor(out=lamt[:, 2, :], in0=lamt[:, 2, :], in1=lamt[:, 3, :], op=ALU.mult), reads=[b_c], writes=[b_c])
                P.op("dve", lambda e: e.tensor_reduce(out=lamv[:, 0:1], in_=lamt[:, 0, :], op=ALU.add, axis=mybir.AxisListType.X), reads=[b_c], writes=[b_c])
                P.op("dve", lambda e: e.tensor_reduce(out=lamv[:, 1:2], in_=lamt[:, 2, :], op=ALU.add, axis=mybir.AxisListType.X), reads=[b_c], writes=[b_c])
                P.op("act", lambda e: e.activation(out=lamv[:, 2:4], in_=lamv[:, 0:2], func=AF.Exp), reads=[b_c], writes=[b_c])
                P.op("dve", lambda e: e.tensor_tensor(out=lamv[:, 4:5], in0=lamv[:, 3:4], in1=lamv[:, 2:3], op=ALU.subtract), reads=[b_c], writes=[b_c])
                P.op("dve", lambda e: e.tensor_scalar(out=lamv[:, 4:5], in0=lamv[:, 4:5], scalar1=-LAMBDA_INIT, scalar2=None, op0=ALU.add), reads=[b_c], writes=[b_c])
                P.op("dve", lambda e: e.tensor_scalar(out=slg[:], in0=slg[:], scalar1=1.0 - LAMBDA_INIT, scalar2=None, op0=ALU.mult), reads=[b_c], writes=[b_c])
                xres = F("xres", [128, 4, D]); b_xres = Buf()
                h2 = F("h2", [128, 4, D]); b_h2 = Buf()
                sq_sh = F("sqsh", [128, D])
                tmp = [(sq_sh, F("ss%d" % i, [128, 4]), F("xn%d" % i, [128, D], BF16), Buf()) for i in range(2)]
                nT = F("nT", [128, 8, 512], BF16); b_nT = Buf()
                oaT = F("oaT", [128, 4, 512], BF16); b_oaT = Buf()
                AB = F("AB", [128, 2, 512], BF16); b_AB = Buf()
                Dh = F("Dh", [128, 512]); b_Dh = Buf()
                D2h = F("D2h", [128, 512]); b_D2h = Buf()
                rsh = F("rsh", [128, 512]); b_rsh = Buf()
                obT = F("obT", [128, 4, 512], BF16); b_obT = Buf()
                sg = [F("sg%d" % i, [128, 512]) for i in range(2)]; b_sg = [Buf(), Buf()]
                ma = F("ma", [128, 512]); b_ma = Buf()
                mT = F("mT", [128, 8, 512], BF16); b_mT = Buf()
                ao = [F("ao%d" % i, [128, 512]) for i in range(2)]; b_ao = [Buf(), Buf()]
                hT = F("hT", [128, 32, 512], BF16); b_hT = Buf()
                hx = [F("hx%d" % i, [128, 512]) for i in range(2)]; b_hx = [Buf(), Buf()]
                yo = xres; b_yo = b_xres
                sqf = tmp[0][0]; ssf = F("ssf", [128, 4]); b_f = tmp[0][3]
                wcnt = {"w1": 0, "w2": 0}

                def do_tt(tt):
                    tk0 = tt * 512
                    for s in range(4):
                        P.dma(lambda e, s=s: e.dma_start(out=xres[:, s, :], in_=xsrc[x_row0 + tk0 + s * 128:x_row0 + tk0 + (s + 1) * 128, :]), writes=[b_xres])
                    for q in range(4):
                        for hh in range(2):
                            P.dma(lambda e, q=q, hh=hh: e.dma_start(out=oaT[hh * 64:(hh + 1) * 64, q, :], in_=OAO[kind][2 * q + hh, :, tk0:tk0 + 512]),
                                  reads=[b_OAO], writes=[b_oaT])
                    for s in range(4):
                        norm_T2(tmp[s % 2], xres[:, s, :], b_xres, nT[:, :, s * 128:(s + 1) * 128], b_nT, s % 2)
                    for hh in range(4):
                        for mm in range(2):
                            P.dma(lambda e, hh=hh, mm=mm: e.dma_start(out=AB[:, mm, :], in_=XR[kind][2 * hh + mm, :, tk0:tk0 + 512]), reads=[b_XR[kind]], writes=[b_AB])
                        P.op("dve", lambda e, hh=hh: e.scalar_tensor_tensor(out=Dh[:], in0=AB[:, 1, :], scalar=lamv[:, 4:5], op0=ALU.mult, in1=AB[:, 0, :], op1=ALU.add),
                             reads=[b_AB, b_c], writes=[b_Dh])
                        P.op("act", lambda e: e.activation(out=D2h[:], in_=Dh[:], func=AF.Square), reads=[b_Dh], writes=[b_D2h])
                        P.op("pe", lambda e: e.matmul(out=PB[2][:, :], lhsT=O128, rhs=D2h[:], start=True, stop=True), reads=[b_cst, b_D2h], writes=[bPB[2]])
                        P.op("act", lambda e: e.activation(out=rsh[:], in_=PB[2][:, :], func=AF.Sqrt, bias=epsb[:, 1:2]), reads=[bPB[2], b_cst], writes=[b_rsh])
                        P.op("dve", lambda e: e.reciprocal(out=rsh[:], in_=rsh[:]), reads=[b_rsh], writes=[b_rsh])
                        P.op("dve", lambda e, hh=hh: e.scalar_tensor_tensor(out=obT[:, hh, :], in0=Dh[:], scalar=slg[:, 0:1], op0=ALU.mult, in1=rsh[:], op1=ALU.mult),
                             reads=[b_Dh, b_rsh, b_c], writes=[b_obT])
                    for mo in range(8):
                        for br, (wu, src, b_src) in enumerate(((wua, oaT, b_oaT), (wub, obT, b_obT))):
                            gb = 3 + br
                            ub = 5 + br
                            for k in range(8):
                                P.op("pe", lambda e, k=k, mo=mo, br=br, gb=gb: e.matmul(out=PB[gb][:, :], lhsT=wg[:, k, br * 1024 + mo * 128:br * 1024 + (mo + 1) * 128], rhs=nT[:, k, :],
                                                                                        start=(k == 0), stop=(k == 7)), reads=[b_w, b_nT], writes=[bPB[gb]])
                            P.op("act", lambda e, br=br, gb=gb: e.activation(out=sg[br][:], in_=PB[gb][:, :], func=AF.Sigmoid), reads=[bPB[gb]], writes=[b_sg[br]])
                            for k in range(4):
                                P.op("pe", lambda e, k=k, mo=mo, wu=wu, src=src, ub=ub: e.matmul(out=PB[ub][:, :], lhsT=wu[:, k, mo * 128:(mo + 1) * 128], rhs=src[:, k, :],
                                                                                                 start=(k == 0), stop=(k == 3)), reads=[b_w, b_src], writes=[bPB[ub]])
                        P.op("dve", lambda e: e.tensor_tensor(out=ma[:], in0=sg[0][:], in1=PB[5][:, :], op=ALU.mult), reads=[b_sg[0], bPB[5]], writes=[b_ma])
                        P.op("dve", lambda e: e.tensor_tensor(out=sg[1][:], in0=sg[1][:], in1=PB[6][:, :], op=ALU.mult), reads=[b_sg[1], bPB[6]], writes=[b_sg[1]])
                        P.op("pool", lambda e, mo=mo: e.tensor_tensor(out=mT[:, mo, :], in0=ma[:], in1=sg[1][:], op=ALU.add), reads=[b_ma, b_sg[1]], writes=[b_mT])
                    for mo in range(8):
                        ab = mo % 2
                        P.dma(lambda e, ab=ab, mo=mo: e.dma_start(out=wo[ab][:], in_=wout_b[:, mo * 128:(mo + 1) * 128].rearrange("(k p) m -> p k m", p=128)),
                              reads=[b_wsc], writes=[b_wo[ab]])
                        for k in range(8):
                            P.op("pe", lambda e, k=k, ab=ab: e.matmul(out=PB[3][:, :], lhsT=wo[ab][:, k, :], rhs=mT[:, k, :], start=(k == 0), stop=(k == 7)),
                                 reads=[b_wo[ab], b_mT], writes=[bPB[3]])
                        P.op("act", lambda e, ab=ab: e.copy(out=ao[ab][:], in_=PB[3][:, :]), reads=[bPB[3]], writes=[b_ao[ab]])
                        for s in range(4):
                            P.op("pe", lambda e, s=s, ab=ab: e.transpose(out=PB[4][:, s * 128:(s + 1) * 128], in_=ao[ab][:, s * 128:(s + 1) * 128], identity=ident),
                                 reads=[b_ao[ab], b_cst], writes=[bPB[4]])
                        P.op("dve", lambda e, mo=mo: e.tensor_tensor(out=h2[:, :, mo * 128:(mo + 1) * 128], in0=xres[:, :, mo * 128:(mo + 1) * 128],
                                                                     in1=PB[4][:, :].rearrange("p (s f) -> p s f", s=4), op=ALU.add), reads=[b_xres, bPB[4]], writes=[b_h2])
                    for s in range(4):
                        norm_T2(tmp[s % 2], h2[:, s, :], b_h2, nT[:, :, s * 128:(s + 1) * 128], b_nT, s % 2)
                    for mg in range(16):
                        wb = wcnt["w1"] % 2
                        wcnt["w1"] += 1
                        P.dma(lambda e, wb=wb, mg=mg: e.dma_start(out=w1[wb][:], in_=wf1_b[:, mg * 256:(mg + 1) * 256].rearrange("(k p) m -> p k m", p=128)),
                              reads=[b_wsc], writes=[b_w1[wb]])
                        for mc in range(2):
                            hb_ = mc % 2
                            pbk = 3 + (mc % 2)
                            for k in range(8):
                                P.op("pe", lambda e, k=k, mc=mc, wb=wb, pbk=pbk: e.matmul(out=PB[pbk][:, :], lhsT=w1[wb][:, k, mc * 128:(mc + 1) * 128], rhs=nT[:, k, :],
                                                                                          start=(k == 0), stop=(k == 7)), reads=[b_w1[wb], b_nT], writes=[bPB[pbk]])
                            P.op("act", lambda e, hb_=hb_, pbk=pbk: e.copy(out=hx[hb_][:], in_=PB[pbk][:, :]), reads=[bPB[pbk]], writes=[b_hx[hb_]])
                            P.op("dve", lambda e, hb_=hb_, mg=mg, mc=mc: e.scalar_tensor_tensor(out=hT[:, mg * 2 + mc, :], in0=hx[hb_][:], scalar=0.0, op0=ALU.max, in1=hx[hb_][:], op1=ALU.mult),
                                 reads=[b_hx[hb_]], writes=[b_hT])
                    for mg in range(8):
                        wb = wcnt["w2"] % 2
                        wcnt["w2"] += 1
                        P.dma(lambda e, wb=wb, mg=mg: e.dma_start(out=w2[wb][:], in_=wf2_b[:, mg * 128:(mg + 1) * 128].rearrange("(k p) m -> p k m", p=128)),
                              reads=[b_wsc], writes=[b_w2[wb]])
                        for mc in range(1):
                            mo = mg
                            ab = mo % 2
                            pbk = 5 + (mg % 2)
                            for k in range(32):
                                P.op("pe", lambda e, k=k, mc=mc, wb=wb, pbk=pbk: e.matmul(out=PB[pbk][:, :], lhsT=w2[wb][:, k, mc * 128:(mc + 1) * 128], rhs=hT[:, k, :],
                                                                                          start=(k == 0), stop=(k == 31)), reads=[b_w2[wb], b_hT], writes=[bPB[pbk]])
                            P.op("act", lambda e, ab=ab, pbk=pbk: e.copy(out=ao[ab][:], in_=PB[pbk][:, :]), reads=[bPB[pbk]], writes=[b_ao[ab]])
                            for s in range(4):
                                P.op("pe", lambda e, s=s, ab=ab: e.transpose(out=PB[7][:, s * 128:(s + 1) * 128], in_=ao[ab][:, s * 128:(s + 1) * 128], identity=ident),
                                     reads=[b_ao[ab], b_cst], writes=[bPB[7]])
                            P.op("dve", lambda e, mo=mo: e.tensor_tensor(out=yo[:, :, mo * 128:(mo + 1) * 128], in0=h2[:, :, mo * 128:(mo + 1) * 128],
                                                                         in1=PB[7][:, :].rearrange("p (s f) -> p s f", s=4), op=ALU.add), reads=[b_h2, bPB[7]], writes=[b_yo])
                    for s in range(4):
                        P.op("act", lambda e, s=s: e.activation(out=sqf[:], in_=yo[:, s, :], func=AF.Square, accum_out=ssf[:, 0:1]), reads=[b_yo], writes=[b_f])
                        P.op("act", lambda e: e.activation(out=ssf[:, 1:2], in_=ssf[:, 0:1], func=AF.Sqrt, scale=1.0 / D, bias=epsb[:, 0:1]), reads=[b_f, b_cst], writes=[b_f])
                        P.op("dve", lambda e: e.reciprocal(out=ssf[:, 2:3], in_=ssf[:, 1:2]), reads=[b_f], writes=[b_f])
                        P.op("dve", lambda e, s=s: e.scalar_tensor_tensor(out=yo[:, s, :], in0=yo[:, s, :], scalar=ssf[:, 2:3], op0=ALU.mult, in1=gfin[:], op1=ALU.mult),
                             reads=[b_yo, b_f, b_c], writes=[b_yo])
                        P.dma(lambda e, s=s: e.dma_start(out=yout[tk0 + s * 128:tk0 + (s + 1) * 128, :], in_=yo[:, s, :]), reads=[b_yo])

                for tt in range(dbg.get("post_tiles", 4)):
                    do_tt(tt)
                P.barrier()

        if do_post:
            if 0 in kinds:
                post_phase(0, xpp, 0, yp, "pp")
            if 1 in kinds:
                post_phase(1, hs, 16, ys, "ps")

        P.finish()
        P.emit(top)
    return nc


def prepare_inputs(inputs):
    f = lambda a: np.ascontiguousarray(np.asarray(a, dtype=np.float32))
    x_prompt, x_sample, meta = f(inputs["x_prompt"]), f(inputs["x_sample"]), f(inputs["meta_tokens"])
    w_in = f(inputs["w_in"])[0]
    mu_p, mu_n = f(inputs["mu_prev"])[0], f(inputs["mu_next"])[0]
    w0, w_up, a0, a_up, g_up = f(inputs["w0"])[0], f(inputs["w_up"])[0], f(inputs["a0"])[0], f(inputs["a_up"])[0], f(inputs["g_up"])[0]
    k_k, k_a, r_k = f(inputs["k_k"])[0], f(inputs["k_a"])[0], f(inputs["r_k"])[0].reshape(-1)
    ln_w, ln_b = f(inputs["ln_x_w"])[0], f(inputs["ln_x_b"])[0]
    hp = np.zeros((T_P, D), np.float32)
    hp[0:16] = meta
    hp[16:16 + 16384] = x_prompt[0]
    cst = _consts()
    gm = np.ascontiguousarray(f(inputs["g_mix"])[0].reshape(8, 128).T)
    gfc = np.ascontiguousarray(f(inputs["g_ffn"])[0].reshape(8, 128).T)
    gfin = np.ascontiguousarray(np.broadcast_to(f(inputs["g_final"])[None, :], (128, D)))
    slg = np.ascontiguousarray(f(inputs["subln_g"])[0].reshape(128, 1))
    lam = np.stack([f(inputs["lam_q1"])[0], f(inputs["lam_k1"])[0], f(inputs["lam_q2"])[0], f(inputs["lam_k2"])[0]])
    lam = np.ascontiguousarray(np.broadcast_to(lam[None], (128, 4, 64)))
    wg = np.ascontiguousarray(w_in[:, 3328:5376])

    def piece(ha, hb, m):
        cols = np.concatenate([np.arange(ha * 64, ha * 64 + 64), 512 + np.arange(ha * 64, ha * 64 + 64), 1024 + np.arange(ha * 64, ha * 64 + 64),
                               np.arange(1536, 1792),
                               1792 + hb * 128 + m * 64 + np.arange(64), 1792 + 512 + hb * 128 + m * 64 + np.arange(64),
                               1792 + 1024 + hb * 128 + np.arange(128)])
        W = w_in[:, cols]
        pp = np.zeros((128, NPP), np.float32)
        rc = cols[0:448]
        for gi in range(5):
            cc = rc[gi * 64:(gi + 1) * 64]
            for half in (0, 64):
                pp[half:half + 64, 3 * gi] = 1.0 - mu_p[cc] - mu_n[cc]
                pp[half:half + 64, 3 * gi + 1] = mu_p[cc]
                pp[half:half + 64, 3 * gi + 2] = mu_n[cc]
        cg = rc[320:448]
        pp[:, 22] = 1.0 - mu_p[cg] - mu_n[cg]
        pp[:, 23] = mu_p[cg]
        pp[:, 24] = mu_n[cg]
        ch = np.arange(ha * 64, ha * 64 + 64)
        for d in (0, 1):
            pp[d * 64:(d + 1) * 64, 15] = w0[d][ch]
            pp[d * 64:(d + 1) * 64, 16] = a0[d][ch]
            pp[d * 64:(d + 1) * 64, 17] = k_k[ch]
            pp[d * 64:(d + 1) * 64, 18] = k_a[ch]
            pp[d * 64:(d + 1) * 64, 19] = r_k[ch]
            pp[d * 64:(d + 1) * 64, 20] = ln_w[ch]
            pp[d * 64:(d + 1) * 64, 21] = ln_b[ch]
        lw = np.concatenate([w_up[0][:, ch], w_up[1][:, ch]], axis=0)
        la = np.concatenate([a_up[0][:, ch], a_up[1][:, ch]], axis=0)
        lg = g_up[:, ch]
        return W, pp, lw, la, lg

    spieces = [piece(p, p // 2, p % 2) for p in range(8)]
    stat = [_alibi_static(hb) for hb in range(4)]
    shared = dict(hp=hp, wg=wg, wua=f(inputs["w_up_a"])[0], wub=f(inputs["w_up_b"])[0], wout=f(inputs["w_out"])[0],
                  wf1=f(inputs["w_ff1"])[0], wf2=f(inputs["w_ff2"])[0], gm=gm, gf=gfc, gfin=gfin, slg=slg, lam=lam, cst=cst,
                  wpc=np.stack([p[0] for p in spieces]), pp=np.stack([p[1] for p in spieces]), lw=np.stack([p[2] for p in spieces]),
                  la=np.stack([p[3] for p in spieces]), lg=np.stack([p[4] for p in spieces]),
                  dm=np.stack([s[0] for s in stat]), qb=np.stack([s[1] for s in stat]))
    in_maps = []
    for c in range(NCORE):
        hs = np.zeros((T_S, D), np.float32)
        hs[0:16] = meta
        hs[16:16 + 2048] = x_sample[c]
        bc, ks, kval = _alibi_core(c)
        m = dict(shared)
        m["hs"] = hs
        m["xpp"] = np.ascontiguousarray(x_prompt[0, c * 2048:(c + 1) * 2048])
        m["bc"] = bc
        m["ks"] = ks
        m["kval"] = kval
        in_maps.append(m)
    return in_maps


_NC_CACHE = {}


def kernel(**inputs):
    in_maps = prepare_inputs(inputs)
    if "nc" not in _NC_CACHE:
        _NC_CACHE["nc"] = build_program()
    nc = _NC_CACHE["nc"]
    res = run_bass_kernel_spmd(nc, in_maps, core_ids=list(range(NCORE)))
    y_prompt = np.concatenate([np.asarray(r["yp"], dtype=np.float32) for r in res.results], axis=0)[None]
    y_sample = np.stack([np.asarray(r["ys"], dtype=np.float32) for r in res.results], axis=0)
    return (y_prompt, y_sample)
```
